# Optimizing a Trainium2 kernel written in Bass

```python
import math
import jax, jax.numpy as jnp
from jax import lax
import numpy as np

D_MODEL = 1024
BATCH = 8
SEQ = 2048
DEPTH = 1
DEC_BATCH = 128
DEC_SEQ = 4
PAST_LEN = 16384
PAGE_SIZE = 128

D_LRU = D_MODEL
LRU_HEADS = 16
LRU_HEAD_DIM = D_LRU // LRU_HEADS
LRU_CONV = 4
LRU_C = 8.0
S5_GROUP = 16
D_S5 = D_MODEL // 2
S5_GROUPS = D_S5 // S5_GROUP
S5_STATE = 64
D_FF = 3 * D_MODEL
FFN_CONV = 3
D_IN = D_LRU + D_S5 + 2 * D_MODEL
ALPHA = (2.0 * DEPTH) ** 0.25
BETA = (8.0 * DEPTH) ** -0.25
LN_EPS = 1e-5

kernel_name = 'hybrid_rglru_s5_convffn_deepnorm_step'


def layer_norm(x, g, b):
    xf = x.astype(jnp.float32)
    mu = jnp.mean(xf, axis=-1, keepdims=True)
    var = jnp.mean(jnp.square(xf - mu), axis=-1, keepdims=True)
    return ((xf - mu) * lax.rsqrt(var + LN_EPS) * g + b).astype(x.dtype)


def causal_dwconv(x, buf, w, b):
    width = w.shape[0]
    L = x.shape[1]
    xx = jnp.concatenate([buf.astype(x.dtype), x], axis=1)
    out = b + xx[:, 0:L] * w[0]
    for k in range(1, width):
        out = out + xx[:, k:k + L] * w[k]
    return out.astype(x.dtype), xx[:, xx.shape[1] - (width - 1):]


def _lin_combine(c1, c2):
    a1, b1 = c1
    a2, b2 = c2
    return a1 * a2, a2 * b1 + b2


def _cplx_combine(c1, c2):
    ar1, ai1, br1, bi1 = c1
    ar2, ai2, br2, bi2 = c2
    ar = ar1 * ar2 - ai1 * ai2
    ai = ar1 * ai2 + ai1 * ar2
    br = ar2 * br1 - ai2 * bi1 + br2
    bi = ar2 * bi1 + ai2 * br1 + bi2
    return ar, ai, br, bi


def rg_lru(x, h0, w_a, b_a, w_x, b_x, lam):
    B_, L, _ = x.shape
    xf = x.astype(jnp.float32)
    xh = xf.reshape(B_, L, LRU_HEADS, LRU_HEAD_DIM)
    r = jax.nn.sigmoid(jnp.einsum('blhi,hij->blhj', xh, w_a.astype(jnp.float32)).reshape(B_, L, D_LRU) + b_a)
    i = jax.nn.sigmoid(jnp.einsum('blhi,hij->blhj', xh, w_x.astype(jnp.float32)).reshape(B_, L, D_LRU) + b_x)
    log_a = LRU_C * r * jax.nn.log_sigmoid(lam.astype(jnp.float32))
    a = jnp.exp(log_a)
    mult = jnp.sqrt(jnp.maximum(-jnp.expm1(2.0 * log_a), 0.0))
    bterm = mult * (i * xf)
    a_cum, h = lax.associative_scan(_lin_combine, (a, bterm), axis=1)
    h = h + a_cum * h0.astype(jnp.float32)[:, None]
    return h, h[:, -1]


def s5_ssm(u, h0_re, h0_im, a_re, a_im, log_dt, b_re, b_im, c_re, c_im, d):
    B_, L, _ = u.shape
    f32 = jnp.float32
    uf = u.astype(f32).reshape(B_, L, S5_GROUPS, S5_GROUP)
    a_re = a_re.astype(f32)
    a_im = a_im.astype(f32)
    dt = jnp.exp(log_dt.astype(f32))[:, None]
    mag = jnp.exp(a_re * dt)
    ab_re = mag * jnp.cos(a_im * dt)
    ab_im = mag * jnp.sin(a_im * dt)
    nr = ab_re - 1.0
    ni = ab_im
    den = a_re * a_re + a_im * a_im
    coef_re = (nr * a_re + ni * a_im) / den
    coef_im = (ni * a_re - nr * a_im) / den
    b_re = b_re.astype(f32)
    b_im = b_im.astype(f32)
    bb_re = coef_re[..., None] * b_re - coef_im[..., None] * b_im
    bb_im = coef_re[..., None] * b_im + coef_im[..., None] * b_re
    bu_re = jnp.einsum('blgc,gpc->blgp', uf, bb_re)
    bu_im = jnp.einsum('blgc,gpc->blgp', uf, bb_im)
    ar = jnp.broadcast_to(ab_re, bu_re.shape)
    ai = jnp.broadcast_to(ab_im, bu_re.shape)
    acr, aci, hr, hi = lax.associative_scan(_cplx_combine, (ar, ai, bu_re, bu_im), axis=1)
    h0r = h0_re.astype(f32)[:, None]
    h0i = h0_im.astype(f32)[:, None]
    hr = hr + acr * h0r - aci * h0i
    hi = hi + acr * h0i + aci * h0r
    y = jnp.einsum('blgp,gcp->blgc', hr, c_re.astype(f32)) - jnp.einsum('blgp,gcp->blgc', hi, c_im.astype(f32))
    y = y.reshape(B_, L, D_S5) + d * u.astype(f32)
    return y, hr[:, -1], hi[:, -1]


def trunk_layer(x, lru_conv_buf, lru_h0, s5_h0_re, s5_h0_im, ffn_conv_buf,
                w_in, lru_conv_w, lru_conv_b, lru_wa, lru_ba, lru_wx, lru_bx, lru_lambda,
                s5_a_re, s5_a_im, s5_log_dt, s5_b_re, s5_b_im, s5_c_re, s5_c_im, s5_d,
                w_glu, w_out, ln1_g, ln1_b, w_up, ffn_conv_w, ffn_conv_b, w_down, ln2_g, ln2_b):
    proj = x @ w_in
    x_lru = proj[..., :D_LRU]
    u_s5 = proj[..., D_LRU:D_LRU + D_S5]
    gates = jax.nn.sigmoid(proj[..., D_LRU + D_S5:].astype(jnp.float32))
    g_lru = gates[..., :D_MODEL]
    g_s5 = gates[..., D_MODEL:]
    xc, new_lru_conv = causal_dwconv(x_lru, lru_conv_buf, lru_conv_w, lru_conv_b)
    h_lru, new_lru_h = rg_lru(xc, lru_h0, lru_wa, lru_ba, lru_wx, lru_bx, lru_lambda)
    y_s5, new_s5_re, new_s5_im = s5_ssm(u_s5, s5_h0_re, s5_h0_im, s5_a_re, s5_a_im, s5_log_dt,
                                        s5_b_re, s5_b_im, s5_c_re, s5_c_im, s5_d)
    z = jax.nn.gelu(y_s5).astype(x.dtype)
    glu = z @ w_glu
    s5_out = glu[..., :D_MODEL].astype(jnp.float32) * jax.nn.sigmoid(glu[..., D_MODEL:].astype(jnp.float32))
    merged = (g_lru * h_lru + g_s5 * s5_out).astype(x.dtype)
    mix = merged @ w_out
    x = layer_norm(ALPHA * x + mix, ln1_g, ln1_b)
    up = x @ w_up
    a = up[..., :D_FF]
    gate = up[..., D_FF:]
    ac, new_ffn_conv = causal_dwconv(a, ffn_conv_buf, ffn_conv_w, ffn_conv_b)
    f = (jax.nn.gelu(ac) * gate) @ w_down
    x = layer_norm(ALPHA * x + f, ln2_g, ln2_b)
    return x, new_lru_conv, new_lru_h, new_s5_re, new_s5_im, new_ffn_conv


def setup_inputs(seed: int = 0) -> dict:
    key = jax.random.key(seed)
    ks = jax.random.split(key, 40)
    f32 = jnp.float32

    def nrm(k, shape, s):
        return jax.random.normal(k, shape, f32) * s

    a_base = jax.random.uniform(ks[7], (DEPTH, D_LRU), f32, minval=0.9, maxval=0.999)
    sig = a_base ** (1.0 / LRU_C)
    lru_lambda = jnp.log(sig) - jnp.log1p(-sig)
    n_idx = jnp.arange(S5_STATE, dtype=f32)
    s5_log_dt = jax.random.uniform(ks[10], (DEPTH, S5_GROUPS), f32,
                                   minval=math.log(1e-3), maxval=math.log(1e-1))
    return {
        'x_prompt': nrm(ks[0], (BATCH, SEQ, D_MODEL), 1.0),
        'x_sample': nrm(ks[1], (DEC_BATCH, DEC_SEQ, D_MODEL), 1.0),
        'state_lru_conv': nrm(ks[2], (DEPTH, DEC_BATCH, LRU_CONV - 1, D_LRU), 1.0),
        'state_lru_h': nrm(ks[3], (DEPTH, DEC_BATCH, D_LRU), 0.5),
        'state_s5_re': nrm(ks[4], (DEPTH, DEC_BATCH, S5_GROUPS, S5_STATE), 0.1),
        'state_s5_im': nrm(ks[5], (DEPTH, DEC_BATCH, S5_GROUPS, S5_STATE), 0.1),
        'state_ffn_conv': nrm(ks[6], (DEPTH, DEC_BATCH, FFN_CONV - 1, D_FF), 1.0),
        'w_in': nrm(ks[11], (DEPTH, D_MODEL, D_IN), D_MODEL ** -0.5),
        'lru_conv_w': nrm(ks[12], (DEPTH, LRU_CONV, D_LRU), LRU_CONV ** -0.5),
        'lru_conv_b': nrm(ks[13], (DEPTH, D_LRU), 0.02),
        'lru_wa': nrm(ks[14], (DEPTH, LRU_HEADS, LRU_HEAD_DIM, LRU_HEAD_DIM), LRU_HEAD_DIM ** -0.5),
        'lru_ba': nrm(ks[15], (DEPTH, D_LRU), 0.02),
        'lru_wx': nrm(ks[16], (DEPTH, LRU_HEADS, LRU_HEAD_DIM, LRU_HEAD_DIM), LRU_HEAD_DIM ** -0.5),
        'lru_bx': nrm(ks[17], (DEPTH, D_LRU), 0.02),
        'lru_lambda': lru_lambda,
        's5_a_re': -0.5 + nrm(ks[8], (DEPTH, S5_GROUPS, S5_STATE), 0.01),
        's5_a_im': math.pi * n_idx + nrm(ks[9], (DEPTH, S5_GROUPS, S5_STATE), 0.01),
        's5_log_dt': s5_log_dt,
        's5_b_re': nrm(ks[18], (DEPTH, S5_GROUPS, S5_STATE, S5_GROUP), (2.0 * S5_GROUP) ** -0.5),
        's5_b_im': nrm(ks[19], (DEPTH, S5_GROUPS, S5_STATE, S5_GROUP), (2.0 * S5_GROUP) ** -0.5),
        's5_c_re': nrm(ks[20], (DEPTH, S5_GROUPS, S5_GROUP, S5_STATE), S5_STATE ** -0.5),
        's5_c_im': nrm(ks[21], (DEPTH, S5_GROUPS, S5_GROUP, S5_STATE), S5_STATE ** -0.5),
        's5_d': nrm(ks[22], (DEPTH, D_S5), 1.0),
        'w_glu': nrm(ks[23], (DEPTH, D_S5, 2 * D_MODEL), D_S5 ** -0.5),
        'w_out': nrm(ks[24], (DEPTH, D_MODEL, D_MODEL), BETA * D_MODEL ** -0.5),
        'ln1_g': 1.0 + nrm(ks[25], (DEPTH, D_MODEL), 0.02),
        'ln1_b': nrm(ks[26], (DEPTH, D_MODEL), 0.02),
        'w_up': nrm(ks[27], (DEPTH, D_MODEL, 2 * D_FF), D_MODEL ** -0.5),
        'ffn_conv_w': nrm(ks[28], (DEPTH, FFN_CONV, D_FF), FFN_CONV ** -0.5),
        'ffn_conv_b': nrm(ks[29], (DEPTH, D_FF), 0.02),
        'w_down': nrm(ks[30], (DEPTH, D_FF, D_MODEL), BETA * D_FF ** -0.5),
        'ln2_g': 1.0 + nrm(ks[31], (DEPTH, D_MODEL), 0.02),
        'ln2_b': nrm(ks[32], (DEPTH, D_MODEL), 0.02),
    }


def reference(x_prompt, x_sample, state_lru_conv, state_lru_h, state_s5_re, state_s5_im, state_ffn_conv,
              w_in, lru_conv_w, lru_conv_b, lru_wa, lru_ba, lru_wx, lru_bx, lru_lambda,
              s5_a_re, s5_a_im, s5_log_dt, s5_b_re, s5_b_im, s5_c_re, s5_c_im, s5_d,
              w_glu, w_out, ln1_g, ln1_b, w_up, ffn_conv_w, ffn_conv_b, w_down, ln2_g, ln2_b):
    n_p = x_prompt.shape[0]
    xp = x_prompt
    xs = x_sample
    p_conv, p_h, p_re, p_im, p_ffn = [], [], [], [], []
    s_conv, s_h, s_re, s_im, s_ffn = [], [], [], [], []
    for l in range(DEPTH):
        w = (w_in[l], lru_conv_w[l], lru_conv_b[l], lru_wa[l], lru_ba[l], lru_wx[l], lru_bx[l], lru_lambda[l],
             s5_a_re[l], s5_a_im[l], s5_log_dt[l], s5_b_re[l], s5_b_im[l], s5_c_re[l], s5_c_im[l], s5_d[l],
             w_glu[l], w_out[l], ln1_g[l], ln1_b[l], w_up[l], ffn_conv_w[l], ffn_conv_b[l], w_down[l],
             ln2_g[l], ln2_b[l])
        zc = jnp.zeros((n_p, LRU_CONV - 1, D_LRU), xp.dtype)
        zh = jnp.zeros((n_p, D_LRU), jnp.float32)
        zs = jnp.zeros((n_p, S5_GROUPS, S5_STATE), jnp.float32)
        zf = jnp.zeros((n_p, FFN_CONV - 1, D_FF), xp.dtype)
        xp, c1, h1, r1, i1, f1 = trunk_layer(xp, zc, zh, zs, zs, zf, *w)
        p_conv.append(c1); p_h.append(h1); p_re.append(r1); p_im.append(i1); p_ffn.append(f1)
        xs, c2, h2, r2, i2, f2 = trunk_layer(xs, state_lru_conv[l], state_lru_h[l], state_s5_re[l],
                                             state_s5_im[l], state_ffn_conv[l], *w)
        s_conv.append(c2); s_h.append(h2); s_re.append(r2); s_im.append(i2); s_ffn.append(f2)
    return (xp, xs,
            jnp.stack(p_conv), jnp.stack(p_h), jnp.stack(p_re), jnp.stack(p_im), jnp.stack(p_ffn),
            jnp.stack(s_conv), jnp.stack(s_h), jnp.stack(s_re), jnp.stack(s_im), jnp.stack(s_ffn))
```

```python
import math
import contextlib
import numpy as np
import concourse.bass as bass
import concourse.mybir as mybir
from concourse.bass_utils import run_bass_kernel_spmd

F32 = mybir.dt.float32
BF16 = mybir.dt.bfloat16
AF = mybir.ActivationFunctionType
ALU = mybir.AluOpType

ENGINES = ("pe", "act", "dve", "pool", "sp")
NCORES = 8
TP = 2048
NSQ = 16
NS = 64
T = TP + NS
TAIL0 = TP - 3
NTAIL = T - TAIL0
D = 1024
DFF = 3072
ALPHA = 2.0 ** 0.25
LN_EPS = 1e-5
LCH = 8
NCH = TP // LCH
NLEV = 8
PAD = 128
PI = math.pi
GK = math.sqrt(2.0 / math.pi)
GC = 0.044715


class Prog:
    def __init__(self, nc):
        self.nc = nc
        self.streams = {e: [] for e in ENGINES}
        self.count = {}
        self.waited = {e: {} for e in ENGINES}
        self.last_write = {}
        self.readers = {}
        self.sem_names = set()
        self.epoch = "0"
        self.pending = {}

    def barrier(self):
        snap = list(self.count.items())
        for e in ENGINES:
            self.pending.setdefault(e, []).extend(snap)

    def op(self, eng, fn, reads=(), writes=(), inc=True, sem=None, amount=1):
        if sem is None:
            sem = "s_%s_%s" % (eng, self.epoch)
        self.sem_names.add(sem)
        deps = list(self.pending.pop(eng, ()))
        for k in reads:
            ev = self.last_write.get(k)
            if ev is not None:
                deps.append(ev)
        for k in writes:
            ev = self.last_write.get(k)
            if ev is not None:
                deps.append(ev)
            deps.extend(self.readers.get(k, ()))
        waits = {}
        for (s, v) in deps:
            if eng == "pe" and s.startswith("s_pe_"):
                continue
            if self.waited[eng].get(s, 0) >= v:
                continue
            if waits.get(s, 0) < v:
                waits[s] = v
        for s, v in waits.items():
            self.waited[eng][s] = v
        cur = self.count.get(sem, 0)
        val = cur + amount
        if inc:
            self.count[sem] = val
        ev = (sem, val)
        self.streams[eng].append((fn, list(waits.items()), (sem, amount) if inc else None))
        for k in reads:
            self.readers.setdefault(k, []).append(ev)
        for k in writes:
            self.last_write[k] = ev
            self.readers[k] = []
        return ev

    def dma(self, queue, out, in_, sem, reads=(), writes=(), **kw):
        def fn(e):
            return e.dma_start(out=out, in_=in_, **kw)
        return self.op(queue, fn, reads=reads, writes=writes, inc=True, sem=sem, amount=16)

    def emit(self, final_waits=()):
        nc = self.nc
        names = sorted(self.sem_names)
        with contextlib.ExitStack() as st:
            sems = {n: st.enter_context(nc.semaphore(n)) for n in names}
            block = st.enter_context(nc.Block())
            streams = self.streams

            def run(engh, lst, last):
                for fn, waits, inc in lst:
                    for s, v in waits:
                        engh.wait_ge(sems[s], v)
                    ins = fn(engh)
                    if inc is not None:
                        ins.then_inc(sems[inc[0]], inc[1])
                if last:
                    for s, v in final_waits:
                        engh.wait_ge(sems[s], v)

            @block.tensor
            def _(e):
                run(e, streams["pe"], False)

            @block.scalar
            def _(e):
                run(e, streams["act"], False)

            @block.vector
            def _(e):
                run(e, streams["dve"], False)

            @block.gpsimd
            def _(e):
                run(e, streams["pool"], False)

            @block.sync
            def _(e):
                run(e, streams["sp"], True)


class Arena:
    def __init__(self, tensor, words):
        self.t = tensor
        self.words = words
        self.pos = 0
        self.peak = 0

    def take(self, shape, dt=F32):
        n = 1
        for s in shape[1:]:
            n *= s
        esz = 4 if dt == F32 else 2
        w = (n * esz + 3) // 4
        w = (w + 7) // 8 * 8
        assert self.pos + w <= self.words, "arena overflow: need %d have %d" % (self.pos + w, self.words)
        v = self.t[0:shape[0], self.pos:self.pos + w]
        self.pos += w
        self.peak = max(self.peak, self.pos)
        if dt != F32:
            v = v.bitcast(dt)
        v = v[:, 0:n]
        if len(shape) > 2:
            names = " ".join("d%d" % i for i in range(len(shape) - 1))
            kw = {"d%d" % i: shape[i + 1] for i in range(len(shape) - 2)}
            v = v.rearrange("p (%s) -> p %s" % (names, names), **kw)
        return v


class StopBuild(Exception):
    pass


class Builder:
    def chk_stop(self, name, reads=()):
        if self.stop_after == name:
            self.p.barrier()
            d = self.dout("dbg_stop", [128, 4])
            t = self.R0.take([128, 4])
            self.MS("dve", t, 1.0, "stoptile")
            self.out_dma(d, t, ["stoptile"])
            raise StopBuild()

    def __init__(self, debug=(), stop_after=None):
        self.debug = set(debug)
        self.stop_after = stop_after
        self.nc = bass.Bass("TRN2", target_bir_lowering=False)
        self.p = Prog(self.nc)
        self.st = contextlib.ExitStack()
        self.out_sems = {}
        self.nout = 0
        self.free_banks = list(range(8))
        self.nbank = 0
        self.ntmp = 0

    def din(self, name, shape):
        return self.nc.dram_tensor(name, list(shape), F32, kind="ExternalInput").ap()

    def dout(self, name, shape, dt=F32):
        return self.nc.dram_tensor(name, list(shape), dt, kind="ExternalOutput").ap()

    def sb(self, name, shape, dt=F32):
        t = self.st.enter_context(self.nc.sbuf_tensor(name, list(shape), dt))
        return t[:]

    def out_dma(self, out, in_, reads, queue="sp", **kw):
        sem = "o_%d" % (self.nout % 8)
        self.nout += 1
        self.p.dma(queue, out, in_, sem, reads=reads, **kw)
        self.out_sems[sem] = self.p.count[sem]

    def tap(self, name, ap, reads):
        if name not in self.debug:
            return
        d = self.dout("dbg_" + name, list(ap.shape), ap.dtype)
        self.out_dma(d, ap, reads)

    def bank(self):
        b = self.free_banks[self.nbank % len(self.free_banks)]
        self.nbank += 1
        return b

    def bk(self, b):
        return self.ps[b // 2][:, b % 2, :]

    def TT(self, eng, out, a, b_, op, r, w):
        self.p.op(eng, lambda e: e.tensor_tensor(out=out, in0=a, in1=b_, op=op), reads=r, writes=[w])

    def TS(self, eng, out, a, s1, op0, r, w, s2=None, op1=None):
        if op1 is None:
            self.p.op(eng, lambda e: e.tensor_scalar(out=out, in0=a, scalar1=s1, scalar2=None, op0=op0), reads=r, writes=[w])
        else:
            self.p.op(eng, lambda e: e.tensor_scalar(out=out, in0=a, scalar1=s1, scalar2=s2, op0=op0, op1=op1), reads=r, writes=[w])

    def STT(self, eng, out, a, sc, b_, op0, op1, r, w):
        self.p.op(eng, lambda e: e.scalar_tensor_tensor(out=out, in0=a, scalar=sc, in1=b_, op0=op0, op1=op1), reads=r, writes=[w])

    def ACT(self, out, a, func, r, w, scale=1.0, bias=0.0):
        self.p.op("act", lambda e: e.activation(out=out, in_=a, func=func, scale=scale, bias=bias), reads=r, writes=[w])

    def CP(self, eng, out, a, r, w):
        if eng == "act":
            self.p.op("act", lambda e: e.copy(out=out, in_=a), reads=r, writes=[w])
        else:
            self.p.op(eng, lambda e: e.tensor_copy(out=out, in_=a), reads=r, writes=[w])

    def MS(self, eng, out, val, w, r=()):
        self.p.op(eng, lambda e: e.memset(out, val), reads=list(r), writes=[w])

    def build(self):
        nc, p = self.nc, self.p
        din = self.din
        I = {}
        for name, shape in (("x", [T, D]), ("st_lru_conv", [NSQ * 3, D]), ("st_lru_h", [NSQ, D]), ("st_s5_re", [NSQ, 2048]),
                            ("st_s5_im", [NSQ, 2048]), ("st_ffn_conv", [NSQ * 2, DFF]), ("w_in", [D, 3584]),
                            ("lru_conv_w", [4, D]), ("lru_conv_b", [D]), ("lru_wa", [16, 64, 64]), ("lru_ba", [D]),
                            ("lru_wx", [16, 64, 64]), ("lru_bx", [D]), ("lru_lambda", [D]), ("s5_a_re", [32, 64]),
                            ("s5_a_im", [32, 64]), ("s5_log_dt", [32]), ("s5_b_re", [32, 64, 16]), ("s5_b_im", [32, 64, 16]),
                            ("s5_c_re", [32, 16, 64]), ("s5_c_im", [32, 16, 64]), ("s5_d", [512]), ("w_glu", [512, 2048]),
                            ("w_out", [D, D]), ("ln1_g", [D]), ("ln1_b", [D]), ("w_up", [D, 2 * DFF]), ("ffn_conv_w", [3, DFF]),
                            ("ffn_conv_b", [DFF]), ("w_down", [DFF, D]), ("ln2_g", [D]), ("ln2_b", [D])):
            I[name] = din(name, shape)
        self.I = I
        O = {}
        O["y"] = self.dout("y", [T, D])
        O["lru_conv"] = self.dout("o_lru_conv", [NTAIL, D])
        O["lru_h"] = self.dout("o_lru_h", [17, D])
        O["s5_re"] = self.dout("o_s5_re", [17, 2048])
        O["s5_im"] = self.dout("o_s5_im", [17, 2048])
        O["ffn_conv"] = self.dout("o_ffn_conv", [NTAIL, DFF])
        self.O = O
        self.x1_scr = nc.dram_tensor("x1_scr", [T, D], F32, kind="Internal").ap()

        self.ps = [self.st.enter_context(nc.psum_tensor("ps%d" % i, [128, 2, 512], F32)) for i in range(4)]
        R0W = 17664
        R1W = 35456
        self.R0 = Arena(self.st.enter_context(nc.sbuf_tensor("R0", [128, R0W], F32)), R0W)
        self.R1 = Arena(self.st.enter_context(nc.sbuf_tensor("R1", [128, R1W], F32)), R1W)
        R0, R1 = self.R0, self.R1

        ident_f = R0.take([128, 128]); ident_b = R0.take([128, 128], BF16)
        self.ident_f, self.ident_b = ident_f, ident_b
        self.MS("pool", ident_f, 0.0, "ident_f")
        p.op("pool", lambda e: e.affine_select(out=ident_f, in_=ident_f, pattern=[[-1, 128]],
                                               compare_op=ALU.not_equal, fill=1.0, base=0, channel_multiplier=1),
             reads=["ident_f"], writes=["ident_f"])
        self.CP("pool", ident_b, ident_f, ["ident_f"], "ident_b")

        self.PIECES = [(0, 512), (512, 512), (1024, 512), (1536, 512), (2048, 64)]
        xT = R0.take([128, 8, T], BF16)
        self.xT = xT
        xb = [R0.take([128, D], BF16) for i in range(3)]
        ntt = (T + 127) // 128
        self.free_banks = [3, 4, 5, 6, 7]
        for tt in range(ntt):
            r0 = tt * 128
            rows = min(128, T - r0)
            slot = tt % 3
            p.dma("pool", xb[slot][0:rows, :], I["x"][r0:r0 + rows, :], "d_xb%d" % slot, writes=[("xb", slot)])
            b = self.bank()
            pt = self.bk(b).bitcast(BF16)
            for k in range(8):
                p.op("pe", lambda e, k=k, pt=pt, slot=slot, rows=rows: e.transpose(
                    pt[:, k * 128:k * 128 + rows], xb[slot][0:rows, k * 128:(k + 1) * 128], ident_b[0:rows, 0:rows]),
                    reads=[("xb", slot), "ident_b"], writes=[("bank", b)], inc=(k == 7))
            src = pt.rearrange("p (k c) -> p k c", c=128)[:, :, 0:rows]
            self.CP("act" if tt % 2 == 0 else "dve", xT[:, :, r0:r0 + rows], src, [("bank", b)], ("xT", tt))
        self.tap("xT", xT[:, 0, :], [("xT", tt) for tt in range(ntt)])
        if self.stop_after == "p0":
            return self.finish()

        zblk = R0.take([128, 4224])
        self.z2 = zblk.bitcast(BF16)[:, 0:4 * T].rearrange("p (a t) -> p a t", a=4)
        self.lnc = zblk[:, 0:4096].rearrange("p (a n) -> p a n", a=4)
        self.Hfin = R0.take([128, 2, 16, 17])
        self.dS5 = R0.take([128, 4])
        mark0 = R1.pos
        u_bf = R1.take([128, 4, T], BF16)
        try:
            self.s5_prep()
        except StopBuild:
            return self.finish()
        self.tap("W1", self.W1[:, 0, :, :, :], self.W1_keys)
        self.tap("W4a", self.W4a[:, 0, :, :, :], self.W4_keys)
        self.tap("W4b", self.W4b[:, 0, :, :, :], self.W4_keys)
        self.tap("h0S", self.h0S, ["h0S"])
        self.tap("mur", self.mur, self.mu_keys)
        if self.stop_after == "prep":
            return self.finish()
        wslot_u = [R1.take([128, 8, 128], BF16) for i in range(2)]
        for qh in range(4):
            slot = qh % 2
            p.dma("pool", wslot_u[slot], I["w_in"][:, 1024 + 128 * qh:1024 + 128 * (qh + 1)].rearrange("(k p) n -> p k n", p=128),
                  "d_wu%d" % slot, writes=[("wslot_u", slot)])
            for (c0, w) in self.PIECES:
                b = self.bank()
                for k in range(8):
                    p.op("pe", lambda e, k=k, b=b, slot=slot, c0=c0, w=w: e.matmul(
                        self.bk(b)[:, 0:w], wslot_u[slot][:, k, :], xT[:, k, c0:c0 + w], start=(k == 0), stop=(k == 7)),
                        reads=[("wslot_u", slot)] + self.xT_keys(c0, w), writes=[("bank", b)], inc=(k == 7))
                self.CP("act", u_bf[:, qh, c0:c0 + w], self.bk(b)[:, 0:w], [("bank", b)], ("u_bf", qh, c0))
        self.tap("u_bf", u_bf[:, 0, :], [("u_bf", 0, c0) for c0, _ in self.PIECES])
        if self.stop_after == "u":
            return self.finish()
        self.s5_main(u_bf)
        self.tap("z2", self.z2[:, 0, :], [("z2", 0, c0) for c0, _ in self.PIECES])
        self.tap("Hfin", self.Hfin, [("Hfin", q) for q in range(16)] + [("Hfin0", q) for q in range(16)])
        hk = [("Hfin", q) for q in range(16)] + [("Hfin0", q) for q in range(16)]
        for part, nm in ((0, "s5_re"), (1, "s5_im")):
            v = O[nm].rearrange("b (q tp) -> q tp b", tp=128)
            for q in range(16):
                self.out_dma(v[q], self.Hfin[:, part, q, :], hk, allow_slow_non_contiguous=True)
        if self.stop_after == "s5":
            return self.finish()
        p.barrier()
        R1.pos = mark0
        p.epoch = "1"
        self.lru_stage()
        if self.stop_after == "lru":
            return self.finish()
        p.barrier()
        R1.pos = self.merged_end
        p.epoch = "2"
        self.mix_stage()
        if self.stop_after == "mix":
            return self.finish()
        p.barrier()
        R1.pos = 0
        p.epoch = "3"
        self.ffn_stage()
        return self.finish()

    def xT_keys(self, c0, w):
        return [("xT", tt) for tt in range(c0 // 128, (c0 + w - 1) // 128 + 1)]

    def s5_prep(self):
        nc, p, I = self.nc, self.p, self.I
        R0, R1 = self.R0, self.R1
        TT, TS, STT, ACT, CP, MS = self.TT, self.TS, self.STT, self.ACT, self.CP, self.MS
        NCK = dict(allow_slow_non_contiguous=True)
        self.W1 = R1.take([128, 4, LCH, 2, 128], BF16)
        self.W4a = R1.take([128, 16, LCH, 2, 32], BF16)
        self.W4b = R1.take([128, 16, LCH, 2, 32], BF16)
        self.h0S = R1.take([128, 2, 16, 16]); self.h0S_bf = R1.take([128, 2, 16, 16], BF16)
        W1, W4a, W4b = self.W1, self.W4a, self.W4b
        RS = Arena(R1.take([128, 2304]), 2304)
        mark1 = R1.pos
        shB = [128, 4, 128]
        BTr = R1.take(shB); BTi = R1.take(shB); aBr = R1.take(shB); aBi = R1.take(shB); ldB = R1.take([128, 4, 2])
        MS("pool", BTr, 0.0, "BTr"); MS("pool", BTi, 0.0, "BTi")
        for (dst, nm, key) in ((BTr, "s5_b_re", "BTr"), (BTi, "s5_b_im", "BTi")):
            v = I[nm].rearrange("(qh ql two) p c -> ql two qh c p", qh=4, ql=4, two=2)
            for ql in range(4):
                for two in range(2):
                    p0 = 32 * ql + 16 * two
                    for qh in range(4):
                        p.dma("sp", dst[p0:p0 + 16, qh, 64 * two:64 * two + 64], v[ql, two, qh], "d_prep", writes=[key], **NCK)
        for (dst, nm, key) in ((aBr, "s5_a_re", "aBr"), (aBi, "s5_a_im", "aBi")):
            v = I[nm].rearrange("(qh ql two) p -> ql qh (two p)", qh=4, ql=4, two=2)
            for ql in range(4):
                p.dma("sp", dst[32 * ql:32 * ql + 32, :, :], v[ql].partition_broadcast(32), "d_prep", writes=[key])
        v = I["s5_log_dt"].rearrange("(qh ql two) -> ql qh two", qh=4, ql=4, two=2)
        for ql in range(4):
            p.dma("sp", ldB[32 * ql:32 * ql + 32, :, :], v[ql].partition_broadcast(32), "d_prep", writes=["ldB"], **NCK)
        shS = [128, 16]
        aSr = R1.take(shS); aSi = R1.take(shS); ldS = R1.take(shS)
        p.dma("sp", aSr, I["s5_a_re"].rearrange("(q two) p -> (two p) q", two=2), "d_prep", writes=["aSr"], **NCK)
        p.dma("sp", aSi, I["s5_a_im"].rearrange("(q two) p -> (two p) q", two=2), "d_prep", writes=["aSi"], **NCK)
        v = I["s5_log_dt"].rearrange("(q two) -> two q", two=2)
        for two in range(2):
            p.dma("sp", ldS[64 * two:64 * two + 64, :], v[two].partition_broadcast(64), "d_prep", writes=["ldS"], **NCK)
        sh3 = [128, 16, 32]
        CTr = R1.take(sh3); CTi = R1.take(sh3)
        MS("pool", CTr, 0.0, "CTr"); MS("pool", CTi, 0.0, "CTi")
        for (dst, nm, key) in ((CTr, "s5_c_re", "CTr"), (CTi, "s5_c_im", "CTi")):
            v = I[nm].rearrange("(q two) c p -> two q p c", two=2)
            for two in range(2):
                for q in range(16):
                    p.dma("sp", dst[64 * two:64 * two + 64, q, 16 * two:16 * two + 16], v[two, q], "d_prep", writes=[key], **NCK)
        p.dma("sp", self.dS5, I["s5_d"].rearrange("(t p) -> p t", p=128), "d_prep", writes=["dS5"], **NCK)
        tot = ("d_prep", p.count["d_prep"])
        for k_ in ("BTr", "BTi", "aBr", "aBi", "ldB", "aSr", "aSi", "ldS", "CTr", "CTi", "dS5"):
            p.last_write[k_] = tot
        self.chk_stop("prepA")
        for part, nm in ((0, "st_s5_re"), (1, "st_s5_im")):
            v = I[nm].rearrange("b (q tp) -> q tp b", tp=128)
            for q in range(16):
                p.dma("sp", self.h0S[:, part, q, :], v[q], "d_h0", writes=["h0S"], **NCK)
        p.last_write["h0S"] = ("d_h0", p.count["d_h0"])
        CP("dve", self.h0S_bf, self.h0S, ["h0S"], "h0S_bf")
        self.chk_stop("prepB")
        I32 = mybir.dt.int32

        def sincos_base(theta, t, kf, A, C2, cs, sn, k_th, k_t, k_kf, k_A, k_C2, k_cs, k_sn):
            TS("dve", t, theta, 1.0 / (2 * PI), ALU.mult, [k_th], k_t, s2=16.0, op1=ALU.add)
            CP("dve", kf.bitcast(I32), t, [k_t], k_kf)
            CP("dve", kf, kf.bitcast(I32), [k_kf], k_kf)
            TT("dve", t, t, kf, ALU.subtract, [k_t, k_kf], k_t)
            ACT(A, t, AF.Sin, [k_t], k_A, scale=PI)
            ACT(C2, t, AF.Sin, [k_t], k_C2, scale=PI / 2)
            TT("dve", C2, C2, C2, ALU.mult, [k_C2], k_C2)
            TS("dve", C2, C2, -2.0, ALU.mult, [k_C2], k_C2, s2=1.0, op1=ALU.add)
            STT("dve", sn, A, 2.0, C2, ALU.mult, ALU.mult, [k_A, k_C2], k_sn)
            TT("dve", cs, A, A, ALU.mult, [k_A], k_cs)
            TS("dve", cs, cs, -2.0, ALU.mult, [k_cs], k_cs, s2=1.0, op1=ALU.add)

        tA = R1.take(shB); tB = R1.take(shB); tC = R1.take(shB); tD = R1.take(shB); tE = R1.take(shB)
        tF = R1.take(shB); tG = R1.take(shB)
        dtB = R1.take([128, 4, 2])
        ACT(dtB, ldB, AF.Exp, ["ldB"], "dtB")
        dtB_b = dtB.unsqueeze(3).to_broadcast([128, 4, 2, 64])

        def v4(t):
            return t.rearrange("p a (b c) -> p a b c", b=2)

        TT("dve", v4(tA), v4(aBr), dtB_b, ALU.mult, ["aBr", "dtB"], "tA")
        TT("dve", v4(tB), v4(aBi), dtB_b, ALU.mult, ["aBi", "dtB"], "tB")
        ACT(tC, tA, AF.Exp, ["tA"], "tC")
        tH = R1.take(shB); tI = R1.take(shB); c1B = R1.take(shB); s1B = R1.take(shB)
        sincos_base(tB, tF, tG, tH, tI, c1B, s1B, "tB", "tF", "tG", "tH", "tI", "c1B", "s1B")
        CP("pool", tD, c1B, ["c1B"], "tD"); CP("pool", tE, s1B, ["s1B"], "tE")
        TT("dve", tD, tC, tD, ALU.mult, ["tC", "tD"], "tD")
        TT("dve", tE, tC, tE, ALU.mult, ["tC", "tE"], "tE")
        TS("dve", tC, tD, -1.0, ALU.add, ["tD"], "tC")
        TT("dve", tF, aBr, aBr, ALU.mult, ["aBr"], "tF")
        TT("dve", tG, aBi, aBi, ALU.mult, ["aBi"], "tG")
        TT("dve", tF, tF, tG, ALU.add, ["tF", "tG"], "tF")
        p.op("dve", lambda e: e.reciprocal(out=tF, in_=tF), reads=["tF"], writes=["tF"])
        TT("dve", tA, tC, aBr, ALU.mult, ["tC", "aBr", "tA"], "tA")
        TT("dve", tG, tE, aBi, ALU.mult, ["tE", "aBi"], "tG")
        TT("dve", tA, tA, tG, ALU.add, ["tA", "tG"], "tA")
        TT("dve", tA, tA, tF, ALU.mult, ["tA", "tF"], "tA")
        TT("dve", tG, tE, aBr, ALU.mult, ["tE", "aBr"], "tG")
        TT("dve", tD, tC, aBi, ALU.mult, ["tC", "aBi"], "tD")
        TT("dve", tG, tG, tD, ALU.subtract, ["tG", "tD"], "tG")
        TT("dve", tG, tG, tF, ALU.mult, ["tG", "tF"], "tG")
        TT("dve", tC, tA, BTr, ALU.mult, ["tA", "BTr"], "tC")
        TT("dve", tD, tG, BTi, ALU.mult, ["tG", "BTi"], "tD")
        TT("dve", tC, tC, tD, ALU.subtract, ["tC", "tD"], "tC")
        TT("dve", tE, tA, BTi, ALU.mult, ["tA", "BTi"], "tE")
        TT("dve", tD, tG, BTr, ALU.mult, ["tG", "BTr"], "tD")
        TT("dve", tE, tE, tD, ALU.add, ["tE", "tD"], "tE")
        MS("dve", tG, 1.0, "tG", r=["tG"]); MS("dve", tF, 0.0, "tF", r=["tF"])
        for s in range(LCH):
            TT("pool", tA, tG, tC, ALU.mult, ["tG", "tC"], "tA")
            TT("pool", tD, tF, tE, ALU.mult, ["tF", "tE"], "tD")
            TT("pool", W1[:, :, s, 0, :], tA, tD, ALU.add, ["tA", "tD"], ("W1", s, 0))
            TT("pool", tA, tG, tE, ALU.mult, ["tG", "tE"], "tA")
            TT("pool", tD, tF, tC, ALU.mult, ["tF", "tC"], "tD")
            TT("pool", W1[:, :, s, 1, :], tA, tD, ALU.subtract, ["tA", "tD"], ("W1", s, 1))
            if s < LCH - 1:
                TT("dve", tH, tG, c1B, ALU.mult, ["tG", "c1B"], "tH")
                TT("dve", tI, tF, s1B, ALU.mult, ["tF", "s1B"], "tI")
                TT("dve", tH, tH, tI, ALU.subtract, ["tH", "tI"], "tH")
                TT("dve", tI, tF, c1B, ALU.mult, ["tF", "c1B"], "tI")
                TT("dve", tF, tG, s1B, ALU.mult, ["tG", "s1B"], "tF")
                TT("dve", tF, tF, tI, ALU.add, ["tF", "tI"], "tF")
                CP("dve", tG, tH, ["tH"], "tG")
        self.W1_keys = [("W1", s, part) for s in range(LCH) for part in range(2)]

        self.chk_stop("prepC")
        def tS():
            return RS.take(shS)

        dtS = tS(); adtS = tS(); thS = tS()
        ACT(dtS, ldS, AF.Exp, ["ldS"], "dtS")
        TT("dve", adtS, aSr, dtS, ALU.mult, ["aSr", "dtS"], "adtS")
        TT("dve", thS, aSi, dtS, ALU.mult, ["aSi", "dtS"], "thS")
        self.rhoS = tS()
        ACT(self.rhoS, adtS, AF.Exp, ["adtS"], "rhoS")
        cosS = []; sinS = []; nsinS = []; lamr = [None]; lami = [None]
        c1S = tS(); s1S = tS(); w1 = tS(); w2 = tS(); w3 = tS(); w4 = tS()
        sincos_base(thS, w1, w2, w3, w4, c1S, s1S, "thS", "Sw1", "Sw2", "Sw3", "Sw4", "S1cs", "S1sn")
        for s in range(LCH + 1):
            if s == 0:
                cs = tS(); sn = tS()
                MS("dve", cs, 1.0, "S0cs"); MS("dve", sn, 0.0, "S0sn")
            elif s == 1:
                cs, sn = c1S, s1S
            else:
                cs = tS(); sn = tS(); ua = tS(); ub = tS()
                pc, ps_ = cosS[s - 1], sinS[s - 1]
                kpc, kps = "S%dcs" % (s - 1), "S%dsn" % (s - 1)
                TT("dve", ua, pc, c1S, ALU.mult, [kpc, "S1cs"], "Sua%d" % s)
                TT("dve", ub, ps_, s1S, ALU.mult, [kps, "S1sn"], "Sub%d" % s)
                TT("dve", cs, ua, ub, ALU.subtract, ["Sua%d" % s, "Sub%d" % s], "S%dcs" % s)
                TT("dve", ua, ps_, c1S, ALU.mult, [kps, "S1cs", "S%dcs" % s], "Sua%d" % s)
                TT("dve", ub, pc, s1S, ALU.mult, [kpc, "S1sn", "S%dcs" % s], "Sub%d" % s)
                TT("dve", sn, ua, ub, ALU.add, ["Sua%d" % s, "Sub%d" % s], "S%dsn" % s)
            cosS.append(cs); sinS.append(sn)
            ns = tS()
            TS("dve", ns, sn, -1.0, ALU.mult, ["S%dsn" % s], "nS%dsn" % s)
            nsinS.append(ns)
            if s >= 1:
                r = tS(); a = tS(); b_ = tS()
                ACT(r, adtS, AF.Exp, ["adtS"], "rpow%d" % s, scale=float(s))
                TT("dve", a, r, cs, ALU.mult, ["rpow%d" % s, "S%dcs" % s], "lamr%d" % s)
                TT("dve", b_, r, sn, ALU.mult, ["rpow%d" % s, "S%dsn" % s], "lami%d" % s)
                lamr.append(a); lami.append(b_)
        self.cosS, self.sinS, self.nsinS, self.lamr, self.lami = cosS, sinS, nsinS, lamr, lami
        self.nlami4 = tS()
        TS("dve", self.nlami4, lami[4], -1.0, ALU.mult, ["lami4"], "nlami4")
        ta = R1.take(sh3); tb = R1.take(sh3)

        def bc(t):
            return t.unsqueeze(2).to_broadcast(sh3)

        for s in range(LCH):
            for (Wt, cr_t, ci_t, kr, ki, nm) in (
                (W4a, cosS[s], sinS[s], "S%dcs" % s, "S%dsn" % s, "W4a"),
                (W4b, lamr[s + 1], lami[s + 1], "lamr%d" % (s + 1), "lami%d" % (s + 1), "W4b"),
            ):
                TT("dve", ta, CTr, bc(cr_t), ALU.mult, ["CTr", kr], "w4ta")
                TT("dve", tb, CTi, bc(ci_t), ALU.mult, ["CTi", ki], "w4tb")
                TT("dve", Wt[:, :, s, 0, :], ta, tb, ALU.subtract, ["w4ta", "w4tb"], (nm, s, 0))
                TT("dve", ta, CTr, bc(ci_t), ALU.mult, ["CTr", ki], "w4ta")
                TT("dve", tb, CTi, bc(cr_t), ALU.mult, ["CTi", kr], "w4tb")
                TT("dve", ta, ta, tb, ALU.add, ["w4ta", "w4tb"], "w4ta")
                TS("dve", Wt[:, :, s, 1, :], ta, -1.0, ALU.mult, ["w4ta"], (nm, s, 1))
        self.W4_keys = [(nm, s, part) for nm in ("W4a", "W4b") for s in range(LCH) for part in range(2)]
        self.mur = RS.take([128, 16, NLEV]); self.mui = RS.take([128, 16, NLEV]); self.muni = RS.take([128, 16, NLEV])
        angc = RS.take([128, 16, NLEV]); angs = RS.take([128, 16, NLEV]); rmag = RS.take([128, 16, NLEV])
        CP("dve", angc[:, :, 0], cosS[LCH], ["S%dcs" % LCH], ("angc", 0))
        CP("dve", angs[:, :, 0], sinS[LCH], ["S%dsn" % LCH], ("angs", 0))
        sa = tS(); sb_ = tS()
        for j in range(NLEV):
            if j >= 1:
                TT("dve", sa, angc[:, :, j - 1], angc[:, :, j - 1], ALU.mult, [("angc", j - 1)], "sq_a")
                TT("dve", sb_, angs[:, :, j - 1], angs[:, :, j - 1], ALU.mult, [("angs", j - 1)], "sq_b")
                TT("dve", angc[:, :, j], sa, sb_, ALU.subtract, ["sq_a", "sq_b"], ("angc", j))
                TT("dve", sa, angc[:, :, j - 1], angs[:, :, j - 1], ALU.mult, [("angc", j - 1), ("angs", j - 1)], "sq_a")
                TS("dve", angs[:, :, j], sa, 2.0, ALU.mult, ["sq_a"], ("angs", j))
            ACT(rmag[:, :, j], adtS, AF.Exp, ["adtS"], ("rmag", j), scale=float(LCH * (1 << j)))
            TT("dve", self.mur[:, :, j], rmag[:, :, j], angc[:, :, j], ALU.mult, [("rmag", j), ("angc", j)], ("mur", j))
            TT("dve", self.mui[:, :, j], rmag[:, :, j], angs[:, :, j], ALU.mult, [("rmag", j), ("angs", j)], ("mui", j))
        for j in range(NLEV):
            TS("dve", self.muni[:, :, j], self.mui[:, :, j], -1.0, ALU.mult, [("mui", j)], ("muni", j))
        self.mu_keys = [(nm, j) for nm in ("mur", "mui", "muni") for j in range(NLEV)]
        self.rp8 = RS.take([128, 16, LCH]); self.rp4 = RS.take([128, 16, 4])
        CP("dve", self.rp8, self.rhoS.unsqueeze(2).to_broadcast([128, 16, LCH]), ["rhoS"], "rp8")
        MS("dve", self.rp8[:, :, 0:1], 0.0, "rp8", r=["rp8"])
        CP("dve", self.rp4, self.rhoS.unsqueeze(2).to_broadcast([128, 16, 4]), ["rhoS"], "rp4")
        MS("dve", self.rp4[:, :, 0:1], 0.0, "rp4", r=["rp4"])
        p.barrier()
        R1.pos = mark1
    def s5_main(self, u_bf):
        nc, p = self.nc, self.p
        R0, R1 = self.R0, self.R1
        TT, TS, STT, ACT, CP, MS = self.TT, self.TS, self.STT, self.ACT, self.CP, self.MS
        W1, W4a, W4b = self.W1, self.W4a, self.W4b
        cosS, sinS, nsinS, lamr, lami = self.cosS, self.sinS, self.nsinS, self.lamr, self.lami
        z2 = self.z2
        NSET = 2
        pat = [R1.take([128, 2, 512]) for i in range(NSET)]
        pats = [R1.take([128, 2, 64]) for i in range(NSET)]
        gzf = [R1.take([128, 2, 512]) for i in range(2)]
        gzfs = [R1.take([128, 2, 64]) for i in range(2)]
        gzb = [R1.take([128, 2, T], BF16) for i in range(NSET)]
        HA = [R1.take([128, 2, PAD + NCH]) for i in range(NSET)]
        HB = [R1.take([128, 2, PAD + NCH]) for i in range(NSET)]
        Hpb = [R1.take([128, 2, NCH], BF16) for i in range(NSET)]
        for i in range(NSET):
            MS("pool", HA[i], 0.0, ("HA", i))
            MS("pool", HB[i], 0.0, ("HB", i))
        yf = R1.take([128, 512]); gq = R1.take([128, 512]); ga = R1.take([128, 512]); gt = gq
        VP = self.ps[0]
        SVB = 2
        YB = {0: 4, 512: 5, 1024: 6, 1536: 7}
        PIECES = self.PIECES
        nseg = [0]

        def stageA(q):
            qh, ql = q // 4, q % 4
            hb = q % NSET
            kw = dict(tile_position=(96, 0)) if ql == 3 else {}
            CP("pool", pat[hb].rearrange("p a (c s) -> p (a c) s", s=LCH),
               self.rp8[:, q:q + 1, :].to_broadcast([128, 2 * 512 // LCH, LCH]), ["rp8"], ("pat", hb))
            CP("pool", pats[hb].rearrange("p a (c s) -> p (a c) s", s=4),
               self.rp4[:, q:q + 1, :].to_broadcast([128, 2 * 64 // 4, 4]), ["rp4"], ("pats", hb))
            for (c0, w) in PIECES:
                samp = (w == 64)
                L = 4 if samp else LCH
                sl = nseg[0] % 2
                nseg[0] += 1
                if samp:
                    vre = self.bk(SVB)[:, 0:64]; vim = self.bk(SVB)[:, 64:128]
                    vkeys = [("bank", SVB)]
                else:
                    vre = VP[:, 0, :]; vim = VP[:, 1, :]
                    vkeys = [("bank", 0), ("bank", 1)]
                nmm = 0
                for s in range(L):
                    for part in range(2):
                        nmm += 1
                        outv = (vre, vim)[part][:, s:w:L]
                        p.op("pe", lambda e, outv=outv, s=s, part=part, qh=qh, ql=ql, c0=c0, w=w, L=L, kw=kw: e.matmul(
                            outv, W1[32 * ql:32 * ql + 32, qh, s, part, :],
                            u_bf[32 * ql:32 * ql + 32, qh, c0 + s:c0 + w:L], start=True, stop=True, **kw),
                            reads=self.W1_keys + [("u_bf", qh, c0)], writes=vkeys, inc=(nmm == 2 * L))
                if not samp:
                    go = gzf[sl]; gkey = ("gzf", sl)
                    p.op("dve", lambda e, go=go, hb=hb: e.tensor_tensor_scan(
                        out=go.rearrange("p a c -> p (a c)"), data0=pat[hb].rearrange("p a c -> p (a c)"),
                        data1=VP[:].rearrange("p a c -> p (a c)"), initial=0.0, op0=ALU.mult, op1=ALU.add),
                        reads=[("pat", hb)] + vkeys, writes=[gkey])
                else:
                    go = gzfs[sl]; gkey = ("gzfs", sl)
                    p.op("dve", lambda e, go=go, hb=hb: e.tensor_tensor_scan(
                        out=go.rearrange("p a c -> p (a c)"), data0=pats[hb].rearrange("p a c -> p (a c)"),
                        data1=self.bk(SVB)[:, 0:128], initial=0.0, op0=ALU.mult, op1=ALU.add),
                        reads=[("pats", hb)] + vkeys, writes=[gkey])
                CP("act", gzb[hb][:, :, c0:c0 + w], go, [gkey], ("gzb", hb, c0))
                if not samp:
                    k0 = PAD + c0 // LCH
                    nchunk = w // LCH
                    er = go[:, 0, LCH - 1:512:LCH]; ei = go[:, 1, LCH - 1:512:LCH]
                    c7 = cosS[LCH - 1][:, q:q + 1]; s7 = sinS[LCH - 1][:, q:q + 1]; ns7 = nsinS[LCH - 1][:, q:q + 1]
                    kk = [gkey, "S%dcs" % (LCH - 1), "S%dsn" % (LCH - 1), "nS%dsn" % (LCH - 1)]
                    hk = ("HA", hb)
                    dr = HA[hb][:, 0, k0:k0 + nchunk]; di = HA[hb][:, 1, k0:k0 + nchunk]
                    TS("dve", dr, er, c7, ALU.mult, kk, hk)
                    STT("dve", dr, ei, ns7, dr, ALU.mult, ALU.add, kk + [hk], hk)
                    TS("dve", di, er, s7, ALU.mult, kk + [hk], hk)
                    STT("dve", di, ei, c7, di, ALU.mult, ALU.add, kk + [hk], hk)
                else:
                    er = go[:, 0, 3:64:4]; ei = go[:, 1, 3:64:4]
                    c3 = cosS[3][:, q:q + 1]; s3 = sinS[3][:, q:q + 1]; ns3 = nsinS[3][:, q:q + 1]
                    l4r = lamr[4][:, q:q + 1]; l4i = lami[4][:, q:q + 1]; nl4i = self.nlami4[:, q:q + 1]
                    kk = [gkey, "S3cs", "S3sn", "nS3sn", "lamr4", "lami4", "nlami4", "h0S"]
                    fr = self.Hfin[:, 0, q, 1:17]; fi = self.Hfin[:, 1, q, 1:17]
                    h0r = self.h0S[:, 0, q, :]; h0i = self.h0S[:, 1, q, :]
                    fk = ("Hfin", q)
                    TS("dve", fr, er, c3, ALU.mult, kk, fk)
                    STT("dve", fr, ei, ns3, fr, ALU.mult, ALU.add, kk + [fk], fk)
                    STT("dve", fr, h0r, l4r, fr, ALU.mult, ALU.add, kk + [fk], fk)
                    STT("dve", fr, h0i, nl4i, fr, ALU.mult, ALU.add, kk + [fk], fk)
                    TS("dve", fi, er, s3, ALU.mult, kk + [fk], fk)
                    STT("dve", fi, ei, c3, fi, ALU.mult, ALU.add, kk + [fk], fk)
                    STT("dve", fi, h0r, l4i, fi, ALU.mult, ALU.add, kk + [fk], fk)
                    STT("dve", fi, h0i, l4r, fi, ALU.mult, ALU.add, kk + [fk], fk)

        def stageB(q):
            hb = q % NSET
            src, dst = HA[hb], HB[hb]
            skey, dkey = ("HA", hb), ("HB", hb)
            for j in range(NLEV):
                d = 1 << j
                mr = self.mur[:, q, j:j + 1]; mi = self.mui[:, q, j:j + 1]; mni = self.muni[:, q, j:j + 1]
                mk = [("mur", j), ("mui", j), ("muni", j)]
                S0 = src[:, 0, PAD:PAD + NCH]; S1 = src[:, 1, PAD:PAD + NCH]
                Z0 = src[:, 0, PAD - d:PAD + NCH - d]; Z1 = src[:, 1, PAD - d:PAD + NCH - d]
                D0 = dst[:, 0, PAD:PAD + NCH]; D1 = dst[:, 1, PAD:PAD + NCH]
                STT("dve", D0, Z0, mr, S0, ALU.mult, ALU.add, [skey] + mk, dkey)
                STT("dve", D0, Z1, mni, D0, ALU.mult, ALU.add, [skey, dkey] + mk, dkey)
                STT("dve", D1, Z0, mi, S1, ALU.mult, ALU.add, [skey, dkey] + mk, dkey)
                STT("dve", D1, Z1, mr, D1, ALU.mult, ALU.add, [skey, dkey] + mk, dkey)
                src, dst = dst, src
                skey, dkey = dkey, skey
            CP("pool", Hpb[hb], HA[hb][:, :, PAD - 1:PAD - 1 + NCH], [("HA", hb)], ("Hpb", hb))
            CP("pool", self.Hfin[:, :, q, 0:1], HA[hb][:, :, PAD + NCH - 1:PAD + NCH], [("HA", hb)], ("Hfin0", q))

        def stageC(q):
            qh, ql = q // 4, q % 4
            hb = q % NSET
            kw = dict(tile_position=(0, 96)) if ql == 3 else {}
            for (c0, w) in PIECES:
                samp = (w == 64)
                L = 4 if samp else LCH
                if samp:
                    ybank = SVB
                    yall = self.bk(SVB)[:, 128:192]
                else:
                    ybank = YB[c0]
                    yall = self.bk(ybank)
                for s in range(L):
                    outv = yall[32 * ql:32 * ql + 32, s:w:L]
                    if samp:
                        hr = self.h0S_bf[:, 0, q, :]; hi = self.h0S_bf[:, 1, q, :]
                        hkeys = ["h0S_bf"]
                    else:
                        k0 = c0 // LCH
                        hr = Hpb[hb][:, 0, k0:k0 + w // L]; hi = Hpb[hb][:, 1, k0:k0 + w // L]
                        hkeys = [("Hpb", hb)]
                    ops = [
                        (W4a[:, q, s, 0, :], gzb[hb][:, 0, c0 + s:c0 + w:L]),
                        (W4a[:, q, s, 1, :], gzb[hb][:, 1, c0 + s:c0 + w:L]),
                        (W4b[:, q, s, 0, :], hr),
                        (W4b[:, q, s, 1, :], hi),
                    ]
                    for i, (lh, rh) in enumerate(ops):
                        last = (i == 3 and s == L - 1)
                        p.op("pe", lambda e, outv=outv, lh=lh, rh=rh, i=i, kw=kw: e.matmul(
                            outv, lh, rh, start=(i == 0), stop=(i == 3), **kw),
                            reads=self.W4_keys + [("gzb", hb, c0)] + hkeys, writes=[("bank", ybank)], inc=last)

        def stageY(qh):
            for (c0, w) in PIECES:
                samp = (w == 64)
                if samp:
                    ybank = SVB; ysrc = self.bk(SVB)[:, 128:192]
                else:
                    ybank = YB[c0]; ysrc = self.bk(ybank)
                yv = yf[:, 0:w]; qv = gq[:, 0:w]; av = ga[:, 0:w]; tv = gt[:, 0:w]
                STT("dve", yv, u_bf[:, qh, c0:c0 + w], self.dS5[:, qh:qh + 1], ysrc, ALU.mult, ALU.add,
                    [("u_bf", qh, c0), "dS5", ("bank", ybank)], "s5yf")
                if qh == 0:
                    self.tap("ys5_%d" % c0, yv, ["s5yf"])
                self.gelu2(yv, qv, av, tv, z2[:, qh, c0:c0 + w], "s5yf", "s5gq", "s5ga", "s5gq", ("z2", qh, c0))

        for qh in range(4):
            qs = [4 * qh + i for i in range(4)]
            stageA(qs[0]); stageB(qs[0])
            for i in range(1, 4):
                stageA(qs[i])
                stageC(qs[i - 1])
                stageB(qs[i])
            stageC(qs[3])
            stageY(qh)

    def gelu2(self, yv, qv, av, tv, outv, ky, kq, ka, kt, kout):
        self.ACT(qv, yv, AF.Square, [ky], kq)
        self.STT("dve", av, qv, GK * GC, yv, ALU.mult, ALU.mult, [kq, ky], ka)
        self.STT("dve", av, yv, GK, av, ALU.mult, ALU.add, [ky, ka], ka)
        self.ACT(tv, av, AF.Tanh, [ka], kt)
        self.STT("dve", outv, tv, 1.0, yv, ALU.add, ALU.mult, [kt, ky], kout)

    def lru_stage(self):
        nc, p, I, O = self.nc, self.p, self.I, self.O
        R0, R1 = self.R0, self.R1
        TT, TS, STT, ACT, CP, MS = self.TT, self.TS, self.STT, self.ACT, self.CP, self.MS
        NCK = dict(allow_slow_non_contiguous=True)
        xT, z2 = self.xT, self.z2
        PIECES = self.PIECES
        self.free_banks = list(range(8))
        cwT = R0.take([128, 4, 8]); cbT = R0.take([128, 8]); baT = R0.take([128, 8]); bxT = R0.take([128, 8])
        lamT = R0.take([128, 8]); sc8 = R0.take([128, 8]); hsc8 = R0.take([128, 8]); nq = R0.take([128, 8])
        for k in range(4):
            p.dma("sp", cwT[:, k, :], I["lru_conv_w"][k].rearrange("(t p) -> p t", p=128), "d_lc", writes=["cwT"], **NCK)
        for (dst, nm, key) in ((cbT, "lru_conv_b", "cbT"), (baT, "lru_ba", "baT"), (bxT, "lru_bx", "bxT"), (lamT, "lru_lambda", "lamT")):
            p.dma("sp", dst, I[nm].rearrange("(t p) -> p t", p=128), "d_lc", writes=[key], **NCK)
        xs_state = R0.take([128, 8, 48]); h0T = R0.take([128, 8, 16])
        for j in range(8):
            p.dma("sp", xs_state[:, j, :], I["st_lru_conv"][:, 128 * j:128 * (j + 1)].rearrange("r p -> p r"), "d_lc",
                  writes=["xs_state"], **NCK)
            p.dma("sp", h0T[:, j, :], I["st_lru_h"][:, 128 * j:128 * (j + 1)].rearrange("r p -> p r"), "d_lc", writes=["h0T"], **NCK)
        tot = ("d_lc", p.count["d_lc"])
        for k_ in ("cwT", "cbT", "baT", "bxT", "lamT", "xs_state", "h0T"):
            p.last_write[k_] = tot
        TS("dve", baT, baT, 0.5, ALU.mult, ["baT"], "baT")
        TS("dve", bxT, bxT, 0.5, ALU.mult, ["bxT"], "bxT")
        ACT(sc8, lamT, AF.Exp, ["lamT"], "sc8", scale=-1.0)
        ACT(sc8, sc8, AF.Ln, ["sc8"], "sc8", bias=1.0)
        TS("dve", hsc8, sc8, -4.0, ALU.mult, ["sc8"], "hsc8")
        TS("dve", sc8, sc8, -8.0, ALU.mult, ["sc8", "hsc8"], "sc8")
        Wg = R0.take([128, 8, 2, 128], BF16)
        MS("pool", Wg, 0.0, "Wg")
        for gi, nm in ((0, "lru_wa"), (1, "lru_wx")):
            v = I[nm].rearrange("(j two) i o -> two i j o", two=2)
            for par in range(2):
                p.dma("pool", Wg[64 * par:64 * par + 64, :, gi, 64 * par:64 * par + 64], v[par], "d_wg", writes=["Wg"])
        p.last_write["Wg"] = ("d_wg", p.count["d_wg"])
        hfinL = R0.take([128, 8, 17])
        self.merged2 = R1.take([128, 8, T], BF16)
        merged2 = self.merged2
        self.merged_end = R1.pos
        xl_sb = R1.take([128, 3 + TP]); xs_sb = R1.take([128, 16, 7])
        abuf = R1.take([128, T]); a2buf = R1.take([128, T]); ixbuf = R1.take([128, T])
        hbuf = [R1.take([128, T]) for i in range(2)]
        tail_sb = R1.take([NTAIL, D])
        MS("pool", xl_sb[:, 0:3], 0.0, ("xl", -1))
        NP = 2
        ytmp = [R1.take([128, 512]) for i in range(NP)]
        xc = [R1.take([128, 512]) for i in range(NP)]
        xcb = [R1.take([128, 512], BF16) for i in range(NP)]
        rp_ = [R1.take([128, 512]) for i in range(NP)]
        ip_ = [R1.take([128, 512]) for i in range(NP)]
        glp = [R1.take([128, 512]) for i in range(NP)]
        gsp = [R1.take([128, 512]) for i in range(NP)]
        gbp = [R1.take([128, 512]) for i in range(NP)]
        t1p = [R1.take([128, 512]) for i in range(NP)]
        t16 = R1.take([128, 16])
        wsl = [R1.take([128, 8, 128], BF16) for i in range(2)]
        wsg = [R1.take([128, 8, 2, 128], BF16) for i in range(2)]
        wgl = [R1.take([128, 4, 2, 128], BF16) for i in range(2)]
        npc = [0]

        def load_w(j):
            sl = j % 2
            wv = lambda c: I["w_in"][:, c:c + 128].rearrange("(k p) n -> p k n", p=128)
            gv = lambda c: I["w_glu"][:, c:c + 128].rearrange("(k p) n -> p k n", p=128)
            p.dma("pool", wsl[sl], wv(128 * j), "d_wsl%d" % sl, writes=[("wsl", sl)])
            p.dma("pool", wsg[sl][:, :, 0, :], wv(1536 + 128 * j), "d_wsg%d_0" % sl, writes=[("wsg", sl, 0)])
            p.dma("pool", wsg[sl][:, :, 1, :], wv(2560 + 128 * j), "d_wsg%d_1" % sl, writes=[("wsg", sl, 1)])
            p.dma("pool", wgl[sl][:, :, 0, :], gv(128 * j), "d_wgl%d_0" % sl, writes=[("wgl", sl, 0)])
            p.dma("pool", wgl[sl][:, :, 1, :], gv(1024 + 128 * j), "d_wgl%d_1" % sl, writes=[("wgl", sl, 1)])

        for j in range(8):
            sl = j % 2
            hb = hbuf[sl]
            if j == 0:
                load_w(0)
            if j + 1 < 8:
                load_w(j + 1)
            CP("pool", xs_sb[:, :, 0:3], xs_state[:, j, :].rearrange("p (b k) -> p b k", k=3), ["xs_state"], ("xs", "st"))
            cw = [cwT[:, k, j:j + 1] for k in range(4)]
            for pi, (c0, w) in enumerate(PIECES):
                samp = (w == 64)
                i2 = npc[0] % NP
                npc[0] += 1
                b = self.bank()
                for k in range(8):
                    p.op("pe", lambda e, k=k, b=b, sl=sl, c0=c0, w=w: e.matmul(
                        self.bk(b)[:, 0:w], wsl[sl][:, k, :], xT[:, k, c0:c0 + w], start=(k == 0), stop=(k == 7)),
                        reads=[("wsl", sl)] + self.xT_keys(c0, w), writes=[("bank", b)], inc=(k == 7))
                ps = self.bk(b)[:, 0:w]
                yv = ytmp[i2][:, 0:w]; xcv = xc[i2][:, 0:w]
                ACT(yv, ps, AF.Identity, [("bank", b), "cwT", "cbT"], ("ytmp", i2), scale=cw[3], bias=cbT[:, j:j + 1])
                if not samp:
                    CP("act", xl_sb[:, 3 + c0:3 + c0 + w], ps, [("bank", b)], ("xl", pi))
                    xk = [("xl", pi), ("xl", pi - 1)]
                    STT("dve", yv, xl_sb[:, c0 + 2:c0 + 2 + w], cw[2], yv, ALU.mult, ALU.add, xk + [("ytmp", i2), "cwT"], ("ytmp", i2))
                    STT("dve", yv, xl_sb[:, c0 + 1:c0 + 1 + w], cw[1], yv, ALU.mult, ALU.add, xk + [("ytmp", i2), "cwT"], ("ytmp", i2))
                    STT("dve", xcv, xl_sb[:, c0:c0 + w], cw[0], yv, ALU.mult, ALU.add, xk + [("ytmp", i2), "cwT"], ("xc", i2))
                else:
                    CP("act", xs_sb[:, :, 3:7], ps.rearrange("p (b s) -> p b s", s=4), [("bank", b)], ("xs", "new"))
                    xk = [("xs", "st"), ("xs", "new")]
                    y3 = yv.rearrange("p (b s) -> p b s", s=4); xc3 = xcv.rearrange("p (b s) -> p b s", s=4)
                    STT("dve", y3, xs_sb[:, :, 2:6], cw[2], y3, ALU.mult, ALU.add, xk + [("ytmp", i2), "cwT"], ("ytmp", i2))
                    STT("dve", y3, xs_sb[:, :, 1:5], cw[1], y3, ALU.mult, ALU.add, xk + [("ytmp", i2), "cwT"], ("ytmp", i2))
                    STT("dve", xc3, xs_sb[:, :, 0:4], cw[0], y3, ALU.mult, ALU.add, xk + [("ytmp", i2), "cwT"], ("xc", i2))
                if j == 0:
                    self.tap("xc_%d" % c0, xcv, [("xc", i2)])
                CP("pool", xcb[i2][:, 0:w], xcv, [("xc", i2)], ("xcb", i2))
                ba_ = self.bank(); bx_ = self.bank()
                p.op("pe", lambda e, ba_=ba_, j=j, i2=i2, w=w: e.matmul(self.bk(ba_)[:, 0:w], Wg[:, j, 0, :], xcb[i2][:, 0:w], start=True, stop=True),
                     reads=["Wg", ("xcb", i2)], writes=[("bank", ba_)])
                p.op("pe", lambda e, bx_=bx_, j=j, i2=i2, w=w: e.matmul(self.bk(bx_)[:, 0:w], Wg[:, j, 1, :], xcb[i2][:, 0:w], start=True, stop=True),
                     reads=["Wg", ("xcb", i2)], writes=[("bank", bx_)])
                rv = rp_[i2][:, 0:w]; iv = ip_[i2][:, 0:w]
                ACT(rv, self.bk(ba_)[:, 0:w], AF.Tanh, [("bank", ba_), "baT"], ("rp", i2), scale=0.5, bias=baT[:, j:j + 1])
                ACT(iv, self.bk(bx_)[:, 0:w], AF.Tanh, [("bank", bx_), "bxT"], ("ip", i2), scale=0.5, bias=bxT[:, j:j + 1])
                ACT(abuf[:, c0:c0 + w], rv, AF.Exp, [("rp", i2), "hsc8"], ("abuf", pi), scale=hsc8[:, j:j + 1], bias=hsc8[:, j:j + 1])
                ACT(a2buf[:, c0:c0 + w], rv, AF.Exp, [("rp", i2), "sc8"], ("a2buf", pi), scale=sc8[:, j:j + 1], bias=sc8[:, j:j + 1])
                STT("dve", ixbuf[:, c0:c0 + w], iv, 1.0, xcv, ALU.add, ALU.mult, [("ip", i2), ("xc", i2)], ("ixbuf", pi))
            allp = lambda nm: [(nm, pi) for pi in range(len(PIECES))]
            ACT(a2buf, a2buf, AF.Sqrt, allp("a2buf"), "mh", scale=-0.25, bias=0.25)
            TT("dve", ixbuf, a2buf, ixbuf, ALU.mult, ["mh"] + allp("ixbuf"), "bterm")
            TT("dve", t16, abuf[:, TP:T:4], h0T[:, j, :], ALU.mult, allp("abuf") + ["h0T"], "t16")
            TT("dve", ixbuf[:, TP:T:4], ixbuf[:, TP:T:4], t16, ALU.add, ["bterm", "t16"], "bterm")
            MS("dve", abuf[:, TP:T:4], 0.0, "afix", r=allp("abuf") + ["t16"])
            p.op("dve", lambda e, hb=hb: e.tensor_tensor_scan(out=hb, data0=abuf, data1=ixbuf, initial=0.0, op0=ALU.mult, op1=ALU.add),
                 reads=allp("abuf") + ["afix", "bterm"], writes=[("hbuf", sl)])
            for nm in ("abuf", "a2buf", "ixbuf"):
                for pi in range(len(PIECES)):
                    p.readers.setdefault((nm, pi), []).append(p.last_write[("hbuf", sl)])
            if j == 0:
                self.tap("hbuf", hb, [("hbuf", sl)])
            CP("pool", hfinL[:, j, 0:1], hb[:, TP - 1:TP], [("hbuf", sl)], ("hfinL", j, 0))
            CP("pool", hfinL[:, j, 1:17], hb[:, TP + 3:T:4], [("hbuf", sl)], ("hfinL", j, 1))
            b = self.bank()
            for k in range(8):
                p.op("pe", lambda e, k=k, b=b, sl=sl: e.matmul(self.bk(b)[0:NTAIL, 0:128], xT[:, k, TAIL0:T], wsl[sl][:, k, :],
                                                               start=(k == 0), stop=(k == 7)),
                     reads=[("wsl", sl)] + self.xT_keys(TAIL0, NTAIL), writes=[("bank", b)], inc=(k == 7))
            CP("act", tail_sb[:, 128 * j:128 * (j + 1)], self.bk(b)[0:NTAIL, 0:128], [("bank", b)], ("tail", j))
            for pi, (c0, w) in enumerate(PIECES):
                i2 = npc[0] % NP
                npc[0] += 1
                bl = self.bank(); bs = self.bank(); bga = self.bank(); bgb = self.bank()
                for (bb, gi) in ((bl, 0), (bs, 1)):
                    for k in range(8):
                        p.op("pe", lambda e, k=k, bb=bb, gi=gi, sl=sl, c0=c0, w=w: e.matmul(
                            self.bk(bb)[:, 0:w], wsg[sl][:, k, gi, :], xT[:, k, c0:c0 + w], start=(k == 0), stop=(k == 7)),
                            reads=[("wsg", sl, gi)] + self.xT_keys(c0, w), writes=[("bank", bb)], inc=(k == 7))
                for (bb, gi) in ((bga, 0), (bgb, 1)):
                    for k in range(4):
                        p.op("pe", lambda e, k=k, bb=bb, gi=gi, sl=sl, c0=c0, w=w: e.matmul(
                            self.bk(bb)[:, 0:w], wgl[sl][:, k, gi, :], z2[:, k, c0:c0 + w], start=(k == 0), stop=(k == 3)),
                            reads=[("wgl", sl, gi)] + [("z2", k, c0)], writes=[("bank", bb)], inc=(k == 3))
                glv = glp[i2][:, 0:w]; gsv = gsp[i2][:, 0:w]; gbv = gbp[i2][:, 0:w]; t1v = t1p[i2][:, 0:w]
                ACT(glv, self.bk(bl)[:, 0:w], AF.Tanh, [("bank", bl)], ("glp", i2), scale=0.5)
                ACT(gsv, self.bk(bs)[:, 0:w], AF.Tanh, [("bank", bs)], ("gsp", i2), scale=0.5)
                ACT(gbv, self.bk(bgb)[:, 0:w], AF.Tanh, [("bank", bgb)], ("gbp", i2), scale=0.25)
                STT("dve", t1v, gbv, 1.0, self.bk(bga)[:, 0:w], ALU.add, ALU.mult, [("gbp", i2), ("bank", bga)], ("t1p", i2))
                STT("dve", t1v, gsv, 1.0, t1v, ALU.add, ALU.mult, [("gsp", i2), ("t1p", i2)], ("t1p", i2))
                STT("dve", glv, glv, 1.0, hb[:, c0:c0 + w], ALU.add, ALU.mult, [("glp", i2), ("hbuf", sl)], ("glp", i2))
                STT("dve", merged2[:, j, c0:c0 + w], t1v, 0.25, glv, ALU.mult, ALU.add, [("t1p", i2), ("glp", i2)], ("merged2", j, c0))
        self.tap("merged2", merged2[:, 0, :], [("merged2", 0, c0) for c0, _ in PIECES])
        self.out_dma(O["lru_conv"], tail_sb, [("tail", j) for j in range(8)])
        for j in range(8):
            self.out_dma(O["lru_h"][:, 128 * j:128 * (j + 1)].rearrange("b p -> p b"), hfinL[:, j, :],
                         [("hfinL", j, 0), ("hfinL", j, 1)], allow_slow_non_contiguous=True)

    def ln_tile(self, ps_flat, res, gB, bB, ytok, outv, rows, kps, kres, kg, kb, ky, kout, st6, mv, sd):
        p = self.p
        TT, TS, STT, ACT, CP = self.TT, self.TS, self.STT, self.ACT, self.CP
        yv = ytok[0:rows, :]
        p.op("act", lambda e: e.activation(out=yv, in_=ps_flat[0:rows, :], func=AF.Copy, scale=0.5), reads=kps, writes=[ky])
        STT("dve", yv, res[0:rows, :], ALPHA, yv, ALU.mult, ALU.add, [kres, ky], ky)
        for h in range(2):
            p.op("dve", lambda e, h=h: e.bn_stats(out=st6[0:rows, h, :], in_=yv[:, 512 * h:512 * (h + 1)]), reads=[ky], writes=[ky + ("st", h)])
        p.op("dve", lambda e: e.bn_aggr(out=mv[0:rows, :], in_=st6[0:rows, :, :].rearrange("p a b -> p (a b)")),
             reads=[ky + ("st", 0), ky + ("st", 1)], writes=[ky + ("mv",)])
        ACT(sd[0:rows, :], mv[0:rows, 1:2], AF.Sqrt, [ky + ("mv",)], ky + ("sd",), bias=LN_EPS)
        p.op("dve", lambda e: e.reciprocal(out=sd[0:rows, :], in_=sd[0:rows, :]), reads=[ky + ("sd",)], writes=[ky + ("sd",)])
        TS("dve", yv, yv, mv[0:rows, 0:1], ALU.subtract, [ky, ky + ("mv",), ky + ("sd",)], ky, s2=sd[0:rows, 0:1], op1=ALU.mult)
        TT("pool", yv, yv, gB[0:rows, :], ALU.mult, [ky, kg], ky)
        TT("dve", outv[0:rows, :], yv, bB[0:rows, :], ALU.add, [ky, kb], kout)

    def mix_stage(self):
        nc, p, I, O = self.nc, self.p, self.I, self.O
        R0, R1 = self.R0, self.R1
        TT, TS, STT, ACT, CP, MS = self.TT, self.TS, self.STT, self.ACT, self.CP, self.MS
        merged2 = self.merged2
        x1T = self.xT
        self.x1T = x1T
        lnc = self.lnc
        for i, nm in enumerate(("ln1_g", "ln1_b", "ln2_g", "ln2_b")):
            p.dma("sp", lnc[:, i, :], I[nm].partition_broadcast(128), "d_lnc%d" % i, writes=[("lnc", i)])
        wout = R1.take([128, 8, D], BF16)
        for h in range(2):
            p.dma("pool", wout[:, :, 512 * h:512 * (h + 1)], I["w_out"][:, 512 * h:512 * (h + 1)].rearrange("(k p) n -> p k n", p=128),
                  "d_wout%d" % h, writes=[("wout", h)])
        NB = 2
        xtok = [R1.take([128, D]) for i in range(NB)]
        ytok = [R1.take([128, D]) for i in range(NB)]
        x1tok = [R1.take([128, D]) for i in range(NB)]
        x1b = [R1.take([128, D], BF16) for i in range(NB)]
        st6 = [R1.take([128, 2, 6]) for i in range(NB)]
        mv = [R1.take([128, 2]) for i in range(NB)]
        sd = [R1.take([128, 1]) for i in range(NB)]
        ntt = (T + 127) // 128
        self.free_banks = [4, 5, 6, 7]
        for tt in range(ntt):
            r0 = tt * 128
            rows = min(128, T - r0)
            s = tt % NB
            pp = tt % 2
            p.dma("sp", xtok[s][0:rows, :], I["x"][r0:r0 + rows, :], "d_xtok%d" % s, writes=[("xtok", s)])
            psf = self.ps[pp][:].rearrange("p a c -> p (a c)")
            for h in range(2):
                for k in range(8):
                    p.op("pe", lambda e, k=k, h=h, pp=pp, r0=r0, rows=rows: e.matmul(
                        self.ps[pp][0:rows, h, :], merged2[:, k, r0:r0 + rows], wout[:, k, 512 * h:512 * (h + 1)],
                        start=(k == 0), stop=(k == 7)),
                        reads=[("wout", h)] + [("merged2", k, c0) for (c0, w) in self.PIECES if c0 <= r0 < c0 + w],
                        writes=[("bank", 2 * pp + h)], inc=(k == 7))
            self.ln_tile(psf, xtok[s], lnc[:, 0, :], lnc[:, 1, :], ytok[s], x1tok[s], rows,
                         [("bank", 2 * pp), ("bank", 2 * pp + 1)], ("xtok", s), ("lnc", 0), ("lnc", 1), ("ytok", s), ("x1tok", s),
                         st6[s], mv[s], sd[s])
            if tt == 0:
                self.tap("x1tok", x1tok[s], [("x1tok", s)])
            p.dma("sp", self.x1_scr[r0:r0 + rows, :], x1tok[s][0:rows, :], "d_x1w%d" % s, reads=[("x1tok", s)], writes=[("x1scr", tt)])
            CP("act", x1b[s][0:rows, :], x1tok[s][0:rows, :], [("x1tok", s)], ("x1b", s))
            b = self.bank()
            pt = self.bk(b).bitcast(BF16)
            for k in range(8):
                p.op("pe", lambda e, k=k, pt=pt, s=s, rows=rows: e.transpose(
                    pt[:, k * 128:k * 128 + rows], x1b[s][0:rows, k * 128:(k + 1) * 128], self.ident_b[0:rows, 0:rows]),
                    reads=[("x1b", s), "ident_b"], writes=[("bank", b)], inc=(k == 7))
            src = pt.rearrange("p (k c) -> p k c", c=128)[:, :, 0:rows]
            CP("act", x1T[:, :, r0:r0 + rows], src, [("bank", b)], ("x1T", tt))
        self.tap("x1T", x1T[:, 0, :], [("x1T", tt) for tt in range(ntt)])

    def x1T_keys(self, c0, w):
        return [("x1T", tt) for tt in range(c0 // 128, (c0 + w - 1) // 128 + 1)]

    def ffn_stage(self):
        nc, p, I, O = self.nc, self.p, self.I, self.O
        R0, R1 = self.R0, self.R1
        TT, TS, STT, ACT, CP, MS = self.TT, self.TS, self.STT, self.ACT, self.CP, self.MS
        NCK = dict(allow_slow_non_contiguous=True)
        x1T, lnc = self.x1T, self.lnc
        NJ = DFF // 128
        QW = 576
        fwT = R1.take([128, 3, NJ]); fbT = R1.take([128, NJ])
        for k in range(3):
            p.dma("sp", fwT[:, k, :], I["ffn_conv_w"][k].rearrange("(t p) -> p t", p=128), "d_fc", writes=["fwT"], **NCK)
        p.dma("sp", fbT, I["ffn_conv_b"].rearrange("(t p) -> p t", p=128), "d_fc", writes=["fbT"], **NCK)
        fs_state = R1.take([128, NJ, 32])
        for j in range(NJ):
            p.dma("sp", fs_state[:, j, :], I["st_ffn_conv"][:, 128 * j:128 * (j + 1)].rearrange("r p -> p r"), "d_fc",
                  writes=["fs_state"], **NCK)
        tot = ("d_fc", p.count["d_fc"])
        for k_ in ("fwT", "fbT", "fs_state"):
            p.last_write[k_] = tot
        wdn = R1.take([128, NJ, D], BF16)
        for c in range(6):
            p.dma("pool", wdn[:, 4 * c:4 * c + 4, :], I["w_down"][512 * c:512 * (c + 1), :].rearrange("(k p) n -> p k n", p=128),
                  "d_wdn%d" % c, writes=[("wdn", c)])
        Gq = R1.take([128, NJ, QW], BF16)
        NSL = 3
        wup = [R1.take([128, 8, 2, 128], BF16) for i in range(NSL)]
        halo = R1.take([128, NJ, 2])
        MS("pool", halo, 0.0, "halo_init")
        a_sb = [R1.take([128, 2 + 512]) for i in range(2)]
        as_sb = R1.take([128, 16, 6])
        NT = 2
        y0 = [R1.take([128, 512]) for i in range(NT)]
        qq = [R1.take([128, 512]) for i in range(NT)]
        ag = [R1.take([128, 512]) for i in range(NT)]
        tailf = [R1.take([NTAIL, 512]) for i in range(2)]
        NB = 2
        x1tok = [R1.take([128, D]) for i in range(NB)]
        ytok = [R1.take([128, D]) for i in range(NB)]
        otok = [R1.take([128, D]) for i in range(NB)]
        st6 = [R1.take([128, 2, 6]) for i in range(NB)]
        mv = [R1.take([128, 2]) for i in range(NB)]
        sd = [R1.take([128, 1]) for i in range(NB)]
        nld = [0]

        def load_wup(j):
            sl = nld[0] % NSL
            nld[0] += 1
            wv = lambda c: I["w_up"][:, c:c + 128].rearrange("(k p) n -> p k n", p=128)
            p.dma("pool", wup[sl][:, :, 0, :], wv(128 * j), "d_wup%d_0" % sl, writes=[("wup", sl, 0)])
            p.dma("pool", wup[sl][:, :, 1, :], wv(DFF + 128 * j), "d_wup%d_1" % sl, writes=[("wup", sl, 1)])
            return sl

        npc = [0]
        ntile = [0]
        for n in range(4):
            q0 = 512 * n
            pieces = [(q0, 512, 0)] + ([(TP, NS, 512)] if n == 3 else [])
            self.free_banks = [4, 5, 6, 7]
            pending = [load_wup(0), load_wup(1)]
            for j in range(NJ):
                sl = pending.pop(0)
                if j + 2 < NJ:
                    pending.append(load_wup(j + 2))
                fw = [fwT[:, k, j:j + 1] for k in range(3)]
                for (c0, w, lc0) in pieces:
                    samp = (w == NS)
                    i2 = npc[0] % NT
                    npc[0] += 1
                    bA = self.bank(); bG = self.bank()
                    for (bb, gi) in ((bA, 0), (bG, 1)):
                        for k in range(8):
                            p.op("pe", lambda e, k=k, bb=bb, gi=gi, sl=sl, c0=c0, w=w: e.matmul(
                                self.bk(bb)[:, 0:w], wup[sl][:, k, gi, :], x1T[:, k, c0:c0 + w], start=(k == 0), stop=(k == 7)),
                                reads=[("wup", sl, gi)] + self.x1T_keys(c0, w), writes=[("bank", bb)], inc=(k == 7))
                    aps = self.bk(bA)[:, 0:w]; gps = self.bk(bG)[:, 0:w]
                    yv = y0[i2][:, 0:w]; qv = qq[i2][:, 0:w]; av = ag[i2][:, 0:w]
                    ACT(yv, aps, AF.Identity, [("bank", bA), "fwT", "fbT"], ("y0", i2), scale=fw[2], bias=fbT[:, j:j + 1])
                    if not samp:
                        ab = a_sb[i2]
                        CP("act", ab[:, 2:2 + w], aps, [("bank", bA)], ("a_sb", i2))
                        CP("pool", ab[:, 0:2], halo[:, j, :], ["halo_init", ("halo", j)], ("a_sbh", i2))
                        ak = [("a_sb", i2), ("a_sbh", i2), ("y0", i2), "fwT"]
                        STT("dve", yv, ab[:, 1:1 + w], fw[1], yv, ALU.mult, ALU.add, ak, ("y0", i2))
                        STT("dve", yv, ab[:, 0:w], fw[0], yv, ALU.mult, ALU.add, ak, ("y0", i2))
                        p.op("pool", lambda e, ab=ab, j=j, w=w: e.tensor_copy(out=halo[:, j, :], in_=ab[:, w:w + 2]),
                             reads=[("a_sb", i2), ("a_sbh", i2)], writes=[("halo", j)])
                    else:
                        CP("act", as_sb[:, :, 2:6], aps.rearrange("p (b s) -> p b s", s=4), [("bank", bA)], ("as_sb", "new"))
                        CP("pool", as_sb[:, :, 0:2], fs_state[:, j, :].rearrange("p (b k) -> p b k", k=2), ["fs_state"], ("as_sb", "st"))
                        ak = [("as_sb", "new"), ("as_sb", "st"), ("y0", i2), "fwT"]
                        y3 = yv.rearrange("p (b s) -> p b s", s=4)
                        STT("dve", y3, as_sb[:, :, 1:5], fw[1], y3, ALU.mult, ALU.add, ak, ("y0", i2))
                        STT("dve", y3, as_sb[:, :, 0:4], fw[0], y3, ALU.mult, ALU.add, ak, ("y0", i2))
                    ACT(qv, yv, AF.Square, [("y0", i2)], ("qq", i2))
                    TS("pool", qv, qv, GC, ALU.mult, [("qq", i2)], ("qq", i2), s2=1.0, op1=ALU.add)
                    TT("pool", av, qv, yv, ALU.mult, [("qq", i2), ("y0", i2)], ("ag", i2))
                    ACT(qv, av, AF.Tanh, [("ag", i2)], ("qq", i2), scale=GK)
                    STT("dve", av, qv, 1.0, yv, ALU.add, ALU.mult, [("qq", i2), ("y0", i2)], ("ag", i2))
                    TT("dve", Gq[:, j, lc0:lc0 + w], av, gps, ALU.mult, [("ag", i2), ("bank", bG)], ("Gq", j, lc0))
                if n == 3:
                    b = self.bank()
                    tb = (j // 4) % 2
                    for k in range(8):
                        p.op("pe", lambda e, k=k, b=b, sl=sl: e.matmul(self.bk(b)[0:NTAIL, 0:128], x1T[:, k, TAIL0:T], wup[sl][:, k, 0, :],
                                                                       start=(k == 0), stop=(k == 7)),
                             reads=[("wup", sl, 0)] + self.x1T_keys(TAIL0, NTAIL), writes=[("bank", b)], inc=(k == 7))
                    CP("act", tailf[tb][:, 128 * (j % 4):128 * (j % 4 + 1)], self.bk(b)[0:NTAIL, 0:128], [("bank", b)], ("tailf", tb, j % 4))
                    if j % 4 == 3:
                        sem = "o_tf%d" % tb
                        p.dma("sp", O["ffn_conv"][:, 512 * (j // 4):512 * (j // 4 + 1)], tailf[tb], sem,
                              reads=[("tailf", tb, i) for i in range(4)])
                        self.out_sems[sem] = p.count[sem]
            if n == 0:
                self.tap("Gq", Gq[:, 0, :], [("Gq", 0, 0)])
            tts = [4 * n + i for i in range(4)] + ([16] if n == 3 else [])
            for tt in tts:
                r0 = tt * 128
                rows = min(128, T - r0)
                lc = r0 - q0 if tt < 16 else 512
                s = ntile[0] % NB
                pp = ntile[0] % 2
                ntile[0] += 1
                p.dma("sp", x1tok[s][0:rows, :], self.x1_scr[r0:r0 + rows, :], "d_x1r%d" % s, reads=[("x1scr", tt)], writes=[("x1tok2", s)])
                psf = self.ps[pp][:].rearrange("p a c -> p (a c)")
                gkeys = [("Gq", j, 512 if tt == 16 else 0) for j in range(NJ)]
                for h in range(2):
                    for k in range(NJ):
                        p.op("pe", lambda e, k=k, h=h, pp=pp, lc=lc, rows=rows: e.matmul(
                            self.ps[pp][0:rows, h, :], Gq[:, k, lc:lc + rows], wdn[:, k, 512 * h:512 * (h + 1)],
                            start=(k == 0), stop=(k == NJ - 1)),
                            reads=[("wdn", k // 4), ("Gq", k, 512 if tt == 16 else 0)], writes=[("bank", 2 * pp + h)], inc=(k == NJ - 1))
                self.ln_tile(psf, x1tok[s], lnc[:, 2, :], lnc[:, 3, :], ytok[s], otok[s], rows,
                             [("bank", 2 * pp), ("bank", 2 * pp + 1)], ("x1tok2", s), ("lnc", 2), ("lnc", 3), ("ytok2", s), ("otok", s),
                             st6[s], mv[s], sd[s])
                sem = "o_y%d" % s
                p.dma("sp", O["y"][r0:r0 + rows, :], otok[s][0:rows, :], sem, reads=[("otok", s)])
                self.out_sems[sem] = p.count[sem]

    def finish(self):
        fw = [(s, v) for s, v in self.out_sems.items()]
        self.p.emit(final_waits=fw)
        self.st.close()
        print("arena peaks: R0 %d/%d words, R1 %d/%d words" % (self.R0.peak, self.R0.words, self.R1.peak, self.R1.words))
        return self.nc


def shard_inputs(inputs, c):
    f = lambda a: np.ascontiguousarray(a, dtype=np.float32)
    m = {}
    m["x"] = f(np.concatenate([inputs["x_prompt"][c], inputs["x_sample"][NSQ * c:NSQ * (c + 1)].reshape(NS, D)], axis=0))
    m["st_lru_conv"] = f(inputs["state_lru_conv"][0, NSQ * c:NSQ * (c + 1)].reshape(NSQ * 3, D))
    m["st_lru_h"] = f(inputs["state_lru_h"][0, NSQ * c:NSQ * (c + 1)])
    m["st_s5_re"] = f(inputs["state_s5_re"][0, NSQ * c:NSQ * (c + 1)].reshape(NSQ, 2048))
    m["st_s5_im"] = f(inputs["state_s5_im"][0, NSQ * c:NSQ * (c + 1)].reshape(NSQ, 2048))
    m["st_ffn_conv"] = f(inputs["state_ffn_conv"][0, NSQ * c:NSQ * (c + 1)].reshape(NSQ * 2, DFF))
    for k in ("w_in", "lru_conv_w", "lru_conv_b", "lru_wa", "lru_ba", "lru_wx", "lru_bx", "lru_lambda", "s5_a_re", "s5_a_im",
              "s5_log_dt", "s5_b_re", "s5_b_im", "s5_c_re", "s5_c_im", "s5_d", "w_glu", "w_out", "ln1_g", "ln1_b", "w_up",
              "ffn_conv_w", "ffn_conv_b", "w_down", "ln2_g", "ln2_b"):
        m[k] = f(inputs[k][0])
    return m


_NC_CACHE = {}


def _get_nc():
    if "nc" not in _NC_CACHE:
        _NC_CACHE["nc"] = Builder().build()
    return _NC_CACHE["nc"]


def kernel(**inputs):
    nc = _get_nc()
    in_maps = [shard_inputs(inputs, c) for c in range(NCORES)]
    res = run_bass_kernel_spmd(nc, in_maps, core_ids=list(range(NCORES)))
    R = res.results
    B = NCORES
    y_p = np.zeros((B, TP, D), np.float32); y_s = np.zeros((B * NSQ, 4, D), np.float32)
    p_conv = np.zeros((1, B, 3, D), np.float32); p_h = np.zeros((1, B, D), np.float32)
    p_re = np.zeros((1, B, 32, 64), np.float32); p_im = np.zeros((1, B, 32, 64), np.float32)
    p_ffn = np.zeros((1, B, 2, DFF), np.float32)
    s_conv = np.zeros((1, B * NSQ, 3, D), np.float32); s_h = np.zeros((1, B * NSQ, D), np.float32)
    s_re = np.zeros((1, B * NSQ, 32, 64), np.float32); s_im = np.zeros((1, B * NSQ, 32, 64), np.float32)
    s_ffn = np.zeros((1, B * NSQ, 2, DFF), np.float32)
    for c in range(B):
        r = R[c]
        sl = slice(NSQ * c, NSQ * (c + 1))
        y_p[c] = r["y"][0:TP]
        y_s[sl] = r["y"][TP:].reshape(NSQ, 4, D)
        lc = r["o_lru_conv"]
        p_conv[0, c] = lc[0:3]
        s_conv[0, sl] = lc[3:].reshape(NSQ, 4, D)[:, 1:4]
        p_h[0, c] = r["o_lru_h"][0]
        s_h[0, sl] = r["o_lru_h"][1:17]
        p_re[0, c] = r["o_s5_re"][0].reshape(32, 64)
        p_im[0, c] = r["o_s5_im"][0].reshape(32, 64)
        s_re[0, sl] = r["o_s5_re"][1:17].reshape(NSQ, 32, 64)
        s_im[0, sl] = r["o_s5_im"][1:17].reshape(NSQ, 32, 64)
        fc = r["o_ffn_conv"]
        p_ffn[0, c] = fc[1:3]
        s_ffn[0, sl] = fc[3:].reshape(NSQ, 4, DFF)[:, 2:4]
    return (y_p, y_s, p_conv, p_h, p_re, p_im, p_ffn, s_conv, s_h, s_re, s_im, s_ffn)
```

```python
import math
import contextlib
import numpy as np
import concourse.bass as bass
import concourse.mybir as mybir
from concourse.bass_utils import run_bass_kernel_spmd

F32 = mybir.dt.float32
BF16 = mybir.dt.bfloat16
AF = mybir.ActivationFunctionType
ALU = mybir.AluOpType

ENGINES = ("pe", "act", "dve", "pool", "sp")
NCORES = 8
TP = 2048
NSQ = 16
NS = 64
T = TP + NS
TAIL0 = TP - 3
NTAIL = T - TAIL0
D = 1024
DFF = 3072
ALPHA = 2.0 ** 0.25
LN_EPS = 1e-5
LCH = 8
NCH = TP // LCH
NLEV = 8
PAD = 128
PI = math.pi
GK = math.sqrt(2.0 / math.pi)
GC = 0.044715


class Prog:
    def __init__(self, nc):
        self.nc = nc
        self.streams = {e: [] for e in ENGINES}
        self.count = {}
        self.waited = {e: {} for e in ENGINES}
        self.last_write = {}
        self.readers = {}
        self.sem_names = set()
        self.epoch = "0"
        self.pending = {}

    def barrier(self):
        snap = [(s, v) for s, v in self.count.items() if not s.startswith("o_")]
        for e in ENGINES:
            self.pending.setdefault(e, []).extend(snap)

    def op(self, eng, fn, reads=(), writes=(), inc=True, sem=None, amount=1):
        if sem is None:
            sem = "s_%s_%s" % (eng, self.epoch)
        self.sem_names.add(sem)
        deps = list(self.pending.pop(eng, ()))
        for k in reads:
            ev = self.last_write.get(k)
            if ev is not None:
                deps.append(ev)
        for k in writes:
            ev = self.last_write.get(k)
            if ev is not None:
                deps.append(ev)
            deps.extend(self.readers.get(k, ()))
        waits = {}
        for (s, v) in deps:
            if eng == "pe" and s.startswith("s_pe_"):
                continue
            if self.waited[eng].get(s, 0) >= v:
                continue
            if waits.get(s, 0) < v:
                waits[s] = v
        for s, v in waits.items():
            self.waited[eng][s] = v
        cur = self.count.get(sem, 0)
        val = cur + amount
        if inc:
            self.count[sem] = val
        ev = (sem, val)
        self.streams[eng].append((fn, list(waits.items()), (sem, amount) if inc else None))
        for k in reads:
            self.readers.setdefault(k, []).append(ev)
        for k in writes:
            self.last_write[k] = ev
            self.readers[k] = []
        return ev

    def dma(self, queue, out, in_, sem, reads=(), writes=(), **kw):
        def fn(e):
            return e.dma_start(out=out, in_=in_, **kw)
        return self.op(queue, fn, reads=reads, writes=writes, inc=True, sem=sem, amount=16)

    def emit(self, final_waits=()):
        nc = self.nc
        names = sorted(self.sem_names)
        with contextlib.ExitStack() as st:
            sems = {n: st.enter_context(nc.semaphore(n)) for n in names}
            block = st.enter_context(nc.Block())
            streams = self.streams

            def run(engh, lst, last):
                for fn, waits, inc in lst:
                    for s, v in waits:
                        engh.wait_ge(sems[s], v)
                    ins = fn(engh)
                    if inc is not None:
                        ins.then_inc(sems[inc[0]], inc[1])
                if last:
                    for s, v in final_waits:
                        engh.wait_ge(sems[s], v)

            @block.tensor
            def _(e):
                run(e, streams["pe"], False)

            @block.scalar
            def _(e):
                run(e, streams["act"], False)

            @block.vector
            def _(e):
                run(e, streams["dve"], False)

            @block.gpsimd
            def _(e):
                run(e, streams["pool"], False)

            @block.sync
            def _(e):
                run(e, streams["sp"], True)


class Arena:
    def __init__(self, tensor, words):
        self.t = tensor
        self.words = words
        self.pos = 0
        self.peak = 0

    def take(self, shape, dt=F32):
        n = 1
        for s in shape[1:]:
            n *= s
        esz = 4 if dt == F32 else 2
        w = (n * esz + 3) // 4
        w = (w + 7) // 8 * 8
        assert self.pos + w <= self.words, "arena overflow: need %d have %d" % (self.pos + w, self.words)
        v = self.t[0:shape[0], self.pos:self.pos + w]
        self.pos += w
        self.peak = max(self.peak, self.pos)
        if dt != F32:
            v = v.bitcast(dt)
        v = v[:, 0:n]
        if len(shape) > 2:
            names = " ".join("d%d" % i for i in range(len(shape) - 1))
            kw = {"d%d" % i: shape[i + 1] for i in range(len(shape) - 2)}
            v = v.rearrange("p (%s) -> p %s" % (names, names), **kw)
        return v


class StopBuild(Exception):
    pass


class Builder:
    def chk_stop(self, name, reads=()):
        if self.stop_after == name:
            self.p.barrier()
            d = self.dout("dbg_stop", [128, 4])
            t = self.R0.take([128, 4])
            self.MS("dve", t, 1.0, "stoptile")
            self.out_dma(d, t, ["stoptile"])
            raise StopBuild()

    def __init__(self, debug=(), stop_after=None):
        self.debug = set(debug)
        self.stop_after = stop_after
        self.nc = bass.Bass("TRN2", target_bir_lowering=False)
        self.p = Prog(self.nc)
        self.st = contextlib.ExitStack()
        self.out_sems = {}
        self.nout = 0
        self.free_banks = list(range(8))
        self.nbank = 0
        self.ntmp = 0

    def din(self, name, shape):
        return self.nc.dram_tensor(name, list(shape), F32, kind="ExternalInput").ap()

    def dout(self, name, shape, dt=F32):
        return self.nc.dram_tensor(name, list(shape), dt, kind="ExternalOutput").ap()

    def sb(self, name, shape, dt=F32):
        t = self.st.enter_context(self.nc.sbuf_tensor(name, list(shape), dt))
        return t[:]

    def out_dma(self, out, in_, reads, queue="sp", **kw):
        sem = "o_%d" % (self.nout % 8)
        self.nout += 1
        self.p.dma(queue, out, in_, sem, reads=reads, **kw)
        self.out_sems[sem] = self.p.count[sem]

    def tap(self, name, ap, reads):
        if name not in self.debug:
            return
        d = self.dout("dbg_" + name, list(ap.shape), ap.dtype)
        self.out_dma(d, ap, reads)

    def bank(self):
        b = self.free_banks[self.nbank % len(self.free_banks)]
        self.nbank += 1
        return b

    def bk(self, b):
        return self.ps[b // 2][:, b % 2, :]

    def TT(self, eng, out, a, b_, op, r, w):
        self.p.op(eng, lambda e: e.tensor_tensor(out=out, in0=a, in1=b_, op=op), reads=r, writes=[w])

    def TS(self, eng, out, a, s1, op0, r, w, s2=None, op1=None):
        if op1 is None:
            self.p.op(eng, lambda e: e.tensor_scalar(out=out, in0=a, scalar1=s1, scalar2=None, op0=op0), reads=r, writes=[w])
        else:
            self.p.op(eng, lambda e: e.tensor_scalar(out=out, in0=a, scalar1=s1, scalar2=s2, op0=op0, op1=op1), reads=r, writes=[w])

    def STT(self, eng, out, a, sc, b_, op0, op1, r, w):
        self.p.op(eng, lambda e: e.scalar_tensor_tensor(out=out, in0=a, scalar=sc, in1=b_, op0=op0, op1=op1), reads=r, writes=[w])

    def ACT(self, out, a, func, r, w, scale=1.0, bias=0.0):
        self.p.op("act", lambda e: e.activation(out=out, in_=a, func=func, scale=scale, bias=bias), reads=r, writes=[w])

    def CP(self, eng, out, a, r, w):
        if eng == "act":
            self.p.op("act", lambda e: e.copy(out=out, in_=a), reads=r, writes=[w])
        else:
            self.p.op(eng, lambda e: e.tensor_copy(out=out, in_=a), reads=r, writes=[w])

    def MS(self, eng, out, val, w, r=()):
        self.p.op(eng, lambda e: e.memset(out, val), reads=list(r), writes=[w])

    def build(self):
        nc, p = self.nc, self.p
        din = self.din
        I = {}
        for name, shape in (("x", [T, D]), ("st_lru_conv", [NSQ * 3, D]), ("st_lru_h", [NSQ, D]), ("st_s5_re", [NSQ, 2048]),
                            ("st_s5_im", [NSQ, 2048]), ("st_ffn_conv", [NSQ * 2, DFF]), ("w_in", [D, 3584]),
                            ("lru_conv_w", [4, D]), ("lru_conv_b", [D]), ("lru_wa", [16, 64, 64]), ("lru_ba", [D]),
                            ("lru_wx", [16, 64, 64]), ("lru_bx", [D]), ("lru_lambda", [D]), ("s5_a_re", [32, 64]),
                            ("s5_a_im", [32, 64]), ("s5_log_dt", [32]), ("s5_b_re", [32, 64, 16]), ("s5_b_im", [32, 64, 16]),
                            ("s5_c_re", [32, 16, 64]), ("s5_c_im", [32, 16, 64]), ("s5_d", [512]), ("w_glu", [512, 2048]),
                            ("w_out", [D, D]), ("ln1_g", [D]), ("ln1_b", [D]), ("w_up", [D, 2 * DFF]), ("ffn_conv_w", [3, DFF]),
                            ("ffn_conv_b", [DFF]), ("w_down", [DFF, D]), ("ln2_g", [D]), ("ln2_b", [D])):
            I[name] = din(name, shape)
        self.I = I
        O = {}
        O["y"] = self.dout("y", [T, D])
        O["lru_conv"] = self.dout("o_lru_conv", [NTAIL, D])
        O["lru_h"] = self.dout("o_lru_h", [128, 8, 17])
        O["s5"] = self.dout("o_s5", [128, 2, 16, 17])
        O["ffn_conv"] = self.dout("o_ffn_conv", [NTAIL, DFF])
        self.O = O
        self.x1_scr = nc.dram_tensor("x1_scr", [T, D], F32, kind="Internal").ap()

        self.ps = [self.st.enter_context(nc.psum_tensor("ps%d" % i, [128, 2, 512], F32)) for i in range(4)]
        R0W = 17664
        R1W = 35456
        self.R0 = Arena(self.st.enter_context(nc.sbuf_tensor("R0", [128, R0W], F32)), R0W)
        self.R1 = Arena(self.st.enter_context(nc.sbuf_tensor("R1", [128, R1W], F32)), R1W)
        R0, R1 = self.R0, self.R1

        ident_f = R0.take([128, 128]); ident_b = R0.take([128, 128], BF16)
        self.ident_f, self.ident_b = ident_f, ident_b
        self.MS("pool", ident_f, 0.0, "ident_f")
        p.op("pool", lambda e: e.affine_select(out=ident_f, in_=ident_f, pattern=[[-1, 128]],
                                               compare_op=ALU.not_equal, fill=1.0, base=0, channel_multiplier=1),
             reads=["ident_f"], writes=["ident_f"])
        self.CP("pool", ident_b, ident_f, ["ident_f"], "ident_b")

        self.PIECES = [(0, 512), (512, 512), (1024, 512), (1536, 512), (2048, 64)]
        xT = R0.take([128, 8, T], BF16)
        self.xT = xT
        zblk = R0.take([128, 4224])
        self.z2 = zblk.bitcast(BF16)[:, 0:4 * T].rearrange("p (a t) -> p a t", a=4)
        self.lnc = zblk[:, 0:4096].rearrange("p (a n) -> p a n", a=4)
        self.Hfin = R0.take([128, 2, 16, 17])
        self.LRUc = R0.take([128, 8, 72])
        self.Fc = R0.take([128, DFF // 128, 36])
        mark0 = R1.pos
        u_bf = R1.take([128, 4, T], BF16)
        self.W1 = R1.take([128, 4, LCH, 2, 128], BF16)
        self.W4a = R1.take([128, 16, LCH, 2, 32], BF16)
        self.W4b = R1.take([128, 16, LCH, 2, 32], BF16)
        self.h0S = R1.take([128, 2, 16, 16]); self.h0S_bf = R1.take([128, 2, 16, 16], BF16)
        self.RS = Arena(R1.take([128, 2304]), 2304)
        mark1 = R1.pos
        self.free_banks = [4, 5, 6, 7]
        self.param_loads()
        xb = [R1.take([128, D], BF16) for i in range(2)]
        ntt = (T + 127) // 128
        for tt in range(ntt):
            r0 = tt * 128
            rows = min(128, T - r0)
            slot = tt % 2
            p.dma("pool", xb[slot][0:rows, :], I["x"][r0:r0 + rows, :], "d_xb%d" % slot, writes=[("xb", slot)])
            b = self.bank()
            pt = self.bk(b).bitcast(BF16)
            for k in range(8):
                p.op("pe", lambda e, k=k, pt=pt, slot=slot, rows=rows: e.transpose(
                    pt[:, k * 128:k * 128 + rows], xb[slot][0:rows, k * 128:(k + 1) * 128], ident_b[0:rows, 0:rows]),
                    reads=[("xb", slot), "ident_b"], writes=[("bank", b)], inc=(k == 7))
            src = pt.rearrange("p (k c) -> p k c", c=128)[:, :, 0:rows]
            self.CP("act" if tt % 2 == 0 else "dve", xT[:, :, r0:r0 + rows], src, [("bank", b)], ("xT", tt))
            if tt == 3:
                self.param_transposes()
        self.tap("xT", xT[:, 0, :], [("xT", tt) for tt in range(ntt)])
        if self.stop_after == "p0":
            return self.finish()
        try:
            self.s5_prep()
        except StopBuild:
            return self.finish()
        self.tap("W1", self.W1[:, 0, :, :, :], self.W1_keys)
        self.tap("W4a", self.W4a[:, 0, :, :, :], self.W4_keys)
        self.tap("W4b", self.W4b[:, 0, :, :, :], self.W4_keys)
        self.tap("h0S", self.h0S, self.h0S_keys)
        self.tap("mur", self.mur, self.mu_keys)
        if self.stop_after == "prep":
            return self.finish()
        p.barrier()
        R1.pos = mark1
        wslot_u = [R1.take([128, 8, 128], BF16) for i in range(2)]
        for qh in range(4):
            slot = qh % 2
            p.dma("pool", wslot_u[slot], I["w_in"][:, 1024 + 128 * qh:1024 + 128 * (qh + 1)].rearrange("(k p) n -> p k n", p=128),
                  "d_wu%d" % slot, writes=[("wslot_u", slot)])
            for (c0, w) in self.PIECES:
                b = self.bank()
                for k in range(8):
                    p.op("pe", lambda e, k=k, b=b, slot=slot, c0=c0, w=w: e.matmul(
                        self.bk(b)[:, 0:w], wslot_u[slot][:, k, :], xT[:, k, c0:c0 + w], start=(k == 0), stop=(k == 7)),
                        reads=[("wslot_u", slot)] + self.xT_keys(c0, w), writes=[("bank", b)], inc=(k == 7))
                self.CP("act", u_bf[:, qh, c0:c0 + w], self.bk(b)[:, 0:w], [("bank", b)], ("u_bf", qh, c0))
        self.tap("u_bf", u_bf[:, 0, :], [("u_bf", 0, c0) for c0, _ in self.PIECES])
        if self.stop_after == "u":
            return self.finish()
        self.s5_main(u_bf)
        self.tap("z2", self.z2[:, 0, :], [("z2", 0, c0) for c0, _ in self.PIECES])
        self.tap("Hfin", self.Hfin, [("Hfin", q) for q in range(16)] + [("Hfin0", q) for q in range(16)])
        hk = [("Hfin", q) for q in range(16)] + [("Hfin0", q) for q in range(16)]
        self.out_dma(O["s5"], self.Hfin, hk)
        if self.stop_after == "s5":
            return self.finish()
        p.barrier()
        R1.pos = mark0
        p.epoch = "1"
        self.lru_stage()
        if self.stop_after == "lru":
            return self.finish()
        p.barrier()
        R1.pos = self.merged_end
        p.epoch = "2"
        self.mix_stage()
        if self.stop_after == "mix":
            return self.finish()
        p.barrier()
        R1.pos = 0
        p.epoch = "3"
        self.ffn_stage()
        return self.finish()

    def xT_keys(self, c0, w):
        return [("xT", tt) for tt in range(c0 // 128, (c0 + w - 1) // 128 + 1)]

    def param_loads(self):
        nc, p, I = self.nc, self.p, self.I
        R1 = self.R1
        MS = self.MS
        NCK = dict(allow_slow_non_contiguous=True)
        NJ = DFF // 128
        self.LF = R1.take([128, D + DFF])
        Lp = self.LF[:, 0:D]; Fp = self.LF[:, D:D + DFF]
        hs1 = R1.take([128, 2048])
        hsp = [hs1, hs1]
        self.Lp, self.Fp, self.hsp = Lp, Fp, hsp
        MS("pool", Lp, 0.0, "Lp"); MS("pool", Fp, 0.0, "Fp"); MS("pool", hs1, 0.0, "hsp")
        row = lambda nm: I[nm].rearrange("(o n) -> o n", o=1)
        for (r0, r1, src) in ((0, 48, I["st_lru_conv"]), (48, 64, I["st_lru_h"]), (64, 68, I["lru_conv_w"]), (68, 69, row("lru_conv_b")),
                              (69, 70, row("lru_ba")), (70, 71, row("lru_bx")), (71, 72, row("lru_lambda"))):
            p.dma("sp", Lp[r0:r1, :], src, "d_Lp", writes=["Lp"])
        for (r0, r1, src) in ((0, 32, I["st_ffn_conv"]), (32, 35, I["ffn_conv_w"]), (35, 36, row("ffn_conv_b"))):
            p.dma("sp", Fp[r0:r1, :], src, "d_Fp", writes=["Fp"])
        self.hs_loaded = False
        sh3 = [128, 16, 32]
        self.Bnr = R1.take(sh3); self.Bni = R1.take(sh3)
        self.Cnr = R1.take([128, 4, 128]); self.Cni = R1.take([128, 4, 128])
        for t_, k_ in ((self.Bnr, "Bnr"), (self.Bni, "Bni"), (self.Cnr, "Cnr"), (self.Cni, "Cni")):
            MS("pool", t_, 0.0, k_)
        for (dst, nm, key) in ((self.Bnr, "s5_b_re", "Bnr"), (self.Bni, "s5_b_im", "Bni")):
            v = I[nm].rearrange("(q two) p c -> two p q c", two=2)
            for two in range(2):
                p.dma("sp", dst[64 * two:64 * two + 64, :, 16 * two:16 * two + 16], v[two], "d_prep", writes=[key])
        for (dst, nm, key) in ((self.Cnr, "s5_c_re", "Cnr"), (self.Cni, "s5_c_im", "Cni")):
            v = I[nm].rearrange("(qh ql two) c p -> ql two c qh p", qh=4, ql=4, two=2)
            for ql in range(4):
                for two in range(2):
                    p0 = 32 * ql + 16 * two
                    p.dma("sp", dst[p0:p0 + 16, :, 64 * two:64 * two + 64], v[ql, two], "d_prep", writes=[key])
        shS = [128, 16]
        self.aSr = R1.take(shS); self.aSi = R1.take(shS); self.ldS = R1.take(shS)
        p.dma("sp", self.aSr, I["s5_a_re"].rearrange("(q two) p -> (two p) q", two=2), "d_prep", writes=["aSr"], **NCK)
        p.dma("sp", self.aSi, I["s5_a_im"].rearrange("(q two) p -> (two p) q", two=2), "d_prep", writes=["aSi"], **NCK)
        v = I["s5_log_dt"].rearrange("(q two) -> two q", two=2)
        for two in range(2):
            p.dma("sp", self.ldS[64 * two:64 * two + 64, :], v[two].partition_broadcast(64), "d_prep", writes=["ldS"], **NCK)
        self.dS5 = self.R0.take([128, 4])
        p.dma("sp", self.dS5, I["s5_d"].rearrange("(t p) -> p t", p=128), "d_prep", writes=["dS5"], **NCK)
        for sem, keys in (("d_Lp", ["Lp"]), ("d_Fp", ["Fp"]),
                          ("d_prep", ["Bnr", "Bni", "Cnr", "Cni", "aSr", "aSi", "ldS", "dS5"])):
            for k_ in keys:
                p.last_write[k_] = (sem, p.count[sem])

    def tr_group(self, srcs, evac):
        p = self.p
        for g0 in range(0, len(srcs), 4):
            grp = srcs[g0:g0 + 4]
            b = self.bank()
            bv = self.bk(b).rearrange("p (a c) -> p a c", a=4)
            for i, (ap, keys) in enumerate(grp):
                p.op("pe", lambda e, i=i, ap=ap, bv=bv: e.transpose(bv[:, i, :], ap, self.ident_f),
                     reads=list(keys) + ["ident_f"], writes=[("bank", b)], inc=(i == len(grp) - 1))
            evac(bv, g0, len(grp), b)

    def param_transposes(self):
        p = self.p
        CP = self.CP
        NJ = DFF // 128
        Lp, Fp, hsp = self.Lp, self.Fp, self.hsp

        def ev_L(bv, g0, n, b):
            CP("act", self.LRUc[:, g0:g0 + n, :], bv[:, 0:n, 0:72], [("bank", b)], ("LRUc", g0 // 4))
        self.tr_group([(Lp[:, 128 * j:128 * (j + 1)], ["Lp"]) for j in range(8)], ev_L)

        def ev_F(bv, g0, n, b):
            CP("dve", self.Fc[:, g0:g0 + n, :], bv[:, 0:n, 0:36], [("bank", b)], ("Fc", g0 // 4))
        self.tr_group([(Fp[:, 128 * j:128 * (j + 1)], ["Fp"]) for j in range(NJ)], ev_F)
        self.LRUc_keys = [("LRUc", i) for i in range(2)]
        self.Fc_keys = [("Fc", i) for i in range(NJ // 4)]
        for part in range(2):
            p.dma("sp", hsp[part][0:16, :], self.I[("st_s5_re", "st_s5_im")[part]], "d_hsp", writes=["hsp"])
            def ev_h(bv, g0, n, b, part=part):
                CP("act", self.h0S[:, part, g0:g0 + n, :], bv[:, 0:n, 0:16], [("bank", b)], ("h0S", part, g0 // 4))
            self.tr_group([(hsp[part][:, 128 * q:128 * (q + 1)], ["hsp"]) for q in range(16)], ev_h)
        self.h0S_keys = [("h0S", part, i) for part in range(2) for i in range(4)]
        CP("dve", self.h0S_bf, self.h0S, self.h0S_keys, "h0S_bf")
        sh3 = [128, 16, 32]
        self.CTr = self.R1.take(sh3); self.CTi = self.R1.take(sh3)
        for (src, dst, k_src, k_dst) in ((self.Cnr, self.CTr, "Cnr", "CTr"), (self.Cni, self.CTi, "Cni", "CTi")):
            def ev_c(bv, g0, n, b, dst=dst, k_dst=k_dst):
                CP("dve", dst[:, 4 * g0:4 * (g0 + n), :].rearrange("p (a q) c -> p a (q c)", a=n), bv[:, 0:n, :], [("bank", b)], (k_dst, g0))
            self.tr_group([(src[:, qh, :], [k_src]) for qh in range(4)], ev_c)
        self.CT_keys = {"CTr": [("CTr", 0)], "CTi": [("CTi", 0)]}

    def s5_prep(self):
        nc, p, I = self.nc, self.p, self.I
        R0, R1, RS = self.R0, self.R1, self.RS
        TT, TS, STT, ACT, CP, MS = self.TT, self.TS, self.STT, self.ACT, self.CP, self.MS
        W1, W4a, W4b = self.W1, self.W4a, self.W4b
        aSr, aSi, ldS = self.aSr, self.aSi, self.ldS
        CTr, CTi = self.CTr, self.CTi
        kCTr, kCTi = self.CT_keys["CTr"], self.CT_keys["CTi"]
        shS = [128, 16]
        sh3 = [128, 16, 32]
        I32 = mybir.dt.int32

        def sincos_base(theta, t, kf, A, C2, cs, sn, k_th, k_t, k_kf, k_A, k_C2, k_cs, k_sn):
            TS("dve", t, theta, 1.0 / (2 * PI), ALU.mult, [k_th], k_t, s2=16.0, op1=ALU.add)
            CP("dve", kf.bitcast(I32), t, [k_t], k_kf)
            CP("dve", kf, kf.bitcast(I32), [k_kf], k_kf)
            TT("dve", t, t, kf, ALU.subtract, [k_t, k_kf], k_t)
            ACT(A, t, AF.Sin, [k_t], k_A, scale=PI)
            ACT(C2, t, AF.Sin, [k_t], k_C2, scale=PI / 2)
            TT("dve", C2, C2, C2, ALU.mult, [k_C2], k_C2)
            TS("dve", C2, C2, -2.0, ALU.mult, [k_C2], k_C2, s2=1.0, op1=ALU.add)
            STT("dve", sn, A, 2.0, C2, ALU.mult, ALU.mult, [k_A, k_C2], k_sn)
            TT("dve", cs, A, A, ALU.mult, [k_A], k_cs)
            TS("dve", cs, cs, -2.0, ALU.mult, [k_cs], k_cs, s2=1.0, op1=ALU.add)

        def tS():
            return RS.take(shS)

        dtS = tS(); adtS = tS(); thS = tS()
        ACT(dtS, ldS, AF.Exp, ["ldS"], "dtS")
        TT("dve", adtS, aSr, dtS, ALU.mult, ["aSr", "dtS"], "adtS")
        TT("dve", thS, aSi, dtS, ALU.mult, ["aSi", "dtS"], "thS")
        self.rhoS = tS()
        ACT(self.rhoS, adtS, AF.Exp, ["adtS"], "rhoS")
        cosS = []; sinS = []; nsinS = []; lamr = [None]; lami = [None]
        c1S = tS(); s1S = tS(); w1 = tS(); w2 = tS(); w3 = tS(); w4 = tS()
        sincos_base(thS, w1, w2, w3, w4, c1S, s1S, "thS", "Sw1", "Sw2", "Sw3", "Sw4", "S1cs", "S1sn")
        for s in range(LCH + 1):
            if s == 0:
                cs = tS(); sn = tS()
                MS("dve", cs, 1.0, "S0cs"); MS("dve", sn, 0.0, "S0sn")
            elif s == 1:
                cs, sn = c1S, s1S
            else:
                cs = tS(); sn = tS(); ua = tS(); ub = tS()
                pc, ps_ = cosS[s - 1], sinS[s - 1]
                kpc, kps = "S%dcs" % (s - 1), "S%dsn" % (s - 1)
                TT("dve", ua, pc, c1S, ALU.mult, [kpc, "S1cs"], "Sua%d" % s)
                TT("dve", ub, ps_, s1S, ALU.mult, [kps, "S1sn"], "Sub%d" % s)
                TT("dve", cs, ua, ub, ALU.subtract, ["Sua%d" % s, "Sub%d" % s], "S%dcs" % s)
                TT("dve", ua, ps_, c1S, ALU.mult, [kps, "S1cs", "S%dcs" % s], "Sua%d" % s)
                TT("dve", ub, pc, s1S, ALU.mult, [kpc, "S1sn", "S%dcs" % s], "Sub%d" % s)
                TT("dve", sn, ua, ub, ALU.add, ["Sua%d" % s, "Sub%d" % s], "S%dsn" % s)
            cosS.append(cs); sinS.append(sn)
            ns = tS()
            TS("dve", ns, sn, -1.0, ALU.mult, ["S%dsn" % s], "nS%dsn" % s)
            nsinS.append(ns)
            if s >= 1:
                r = tS(); a = tS(); b_ = tS()
                ACT(r, adtS, AF.Exp, ["adtS"], "rpow%d" % s, scale=float(s))
                TT("dve", a, r, cs, ALU.mult, ["rpow%d" % s, "S%dcs" % s], "lamr%d" % s)
                TT("dve", b_, r, sn, ALU.mult, ["rpow%d" % s, "S%dsn" % s], "lami%d" % s)
                lamr.append(a); lami.append(b_)
        self.cosS, self.sinS, self.nsinS, self.lamr, self.lami = cosS, sinS, nsinS, lamr, lami
        self.nlami4 = tS()
        TS("dve", self.nlami4, lami[4], -1.0, ALU.mult, ["lami4"], "nlami4")
        ta = R1.take(sh3); tb = R1.take(sh3)

        def bc(t):
            return t.unsqueeze(2).to_broadcast(sh3)

        for s in range(LCH):
            for (Wt, cr_t, ci_t, kr, ki, nm) in (
                (W4a, cosS[s], sinS[s], "S%dcs" % s, "S%dsn" % s, "W4a"),
                (W4b, lamr[s + 1], lami[s + 1], "lamr%d" % (s + 1), "lami%d" % (s + 1), "W4b"),
            ):
                TT("dve", ta, CTr, bc(cr_t), ALU.mult, kCTr + [kr], "w4ta")
                TT("dve", tb, CTi, bc(ci_t), ALU.mult, kCTi + [ki], "w4tb")
                TT("dve", Wt[:, :, s, 0, :], ta, tb, ALU.subtract, ["w4ta", "w4tb"], (nm, s, 0))
                TT("dve", ta, CTr, bc(ci_t), ALU.mult, kCTr + [ki], "w4ta")
                TT("dve", tb, CTi, bc(cr_t), ALU.mult, kCTi + [kr], "w4tb")
                TT("dve", ta, ta, tb, ALU.add, ["w4ta", "w4tb"], "w4ta")
                TS("dve", Wt[:, :, s, 1, :], ta, -1.0, ALU.mult, ["w4ta"], (nm, s, 1))
        Bnr, Bni = self.Bnr, self.Bni
        nr = tS(); den = tS(); u1 = tS(); cfr = tS(); cfi = tS()
        TS("dve", nr, lamr[1], -1.0, ALU.add, ["lamr1"], "nr")
        TT("dve", den, aSr, aSr, ALU.mult, ["aSr"], "den")
        TT("dve", u1, aSi, aSi, ALU.mult, ["aSi"], "u1")
        TT("dve", den, den, u1, ALU.add, ["den", "u1"], "den")
        p.op("dve", lambda e: e.reciprocal(out=den, in_=den), reads=["den"], writes=["den"])
        TT("dve", cfr, nr, aSr, ALU.mult, ["nr", "aSr"], "cfr")
        TT("dve", u1, lami[1], aSi, ALU.mult, ["lami1", "aSi", "den"], "u1")
        TT("dve", cfr, cfr, u1, ALU.add, ["cfr", "u1"], "cfr")
        TT("dve", cfr, cfr, den, ALU.mult, ["cfr", "den"], "cfr")
        TT("dve", cfi, lami[1], aSr, ALU.mult, ["lami1", "aSr"], "cfi")
        TT("dve", u1, nr, aSi, ALU.mult, ["nr", "aSi", "cfr"], "u1")
        TT("dve", cfi, cfi, u1, ALU.subtract, ["cfi", "u1"], "cfi")
        TT("dve", cfi, cfi, den, ALU.mult, ["cfi", "den"], "cfi")
        BbR = R1.take(sh3); BbI = R1.take(sh3)
        TT("pool", ta, Bnr, bc(cfr), ALU.mult, ["Bnr", "cfr", "w4ta"], "w4ta")
        TT("pool", tb, Bni, bc(cfi), ALU.mult, ["Bni", "cfi", "w4tb"], "w4tb")
        TT("pool", BbR, ta, tb, ALU.subtract, ["w4ta", "w4tb"], "BbR")
        TT("pool", ta, Bni, bc(cfr), ALU.mult, ["Bni", "cfr"], "w4ta")
        TT("pool", tb, Bnr, bc(cfi), ALU.mult, ["Bnr", "cfi"], "w4tb")
        TT("pool", BbI, ta, tb, ALU.add, ["w4ta", "w4tb"], "BbI")
        W1S = self.LF.bitcast(BF16).rearrange("p (a s b c) -> p a s b c", a=4, s=LCH, b=2)
        CP("pool", W1S[:, 0, 0, 0, 0:2], W1S[:, 0, 0, 0, 0:2], ["Lp", "Fp"], "LFfree")
        p.last_write["Lp"] = p.last_write["LFfree"]; p.last_write["Fp"] = p.last_write["LFfree"]
        tc_ = R1.take(sh3); td_ = R1.take(sh3)
        v4 = lambda t: t.rearrange("p (a b) c -> p a (b c)", a=4)
        for s in range(LCH):
            kc, ks = "S%dcs" % s, "S%dsn" % s
            TT("pool", tc_, BbR, bc(cosS[s]), ALU.mult, ["BbR", kc], "w1tc")
            TT("pool", td_, BbI, bc(sinS[s]), ALU.mult, ["BbI", ks], "w1td")
            TT("pool", W1S[:, :, s, 0, :], v4(tc_), v4(td_), ALU.add, ["w1tc", "w1td", "LFfree"], ("W1S", s, 0))
            TT("pool", tc_, BbI, bc(cosS[s]), ALU.mult, ["BbI", kc], "w1tc")
            TT("pool", td_, BbR, bc(sinS[s]), ALU.mult, ["BbR", ks], "w1td")
            TT("pool", W1S[:, :, s, 1, :], v4(tc_), v4(td_), ALU.subtract, ["w1tc", "w1td", "LFfree"], ("W1S", s, 1))
        for qh in range(4):
            for h in range(2):
                b = self.bank()
                pt = self.bk(b).bitcast(BF16)
                n = 0
                for s in range(4 * h, 4 * h + 4):
                    for part in range(2):
                        p.op("pe", lambda e, pt=pt, n=n, qh=qh, s=s, part=part: e.transpose(
                            pt[:, 128 * n:128 * (n + 1)], W1S[:, qh, s, part, :], self.ident_b),
                            reads=[("W1S", s, part), "ident_b"], writes=[("bank", b)], inc=(n == 7))
                        n += 1
                CP("act", W1[:, qh, 4 * h:4 * h + 4, :, :].rearrange("p a b c -> p (a b c)"), pt, [("bank", b)], ("W1", qh, h))
        self.W1_keys = [("W1", qh, h) for qh in range(4) for h in range(2)]
        self.W4_keys = [(nm, s, part) for nm in ("W4a", "W4b") for s in range(LCH) for part in range(2)]
        self.mur = RS.take([128, 16, NLEV]); self.mui = RS.take([128, 16, NLEV]); self.muni = RS.take([128, 16, NLEV])
        angc = RS.take([128, 16, NLEV]); angs = RS.take([128, 16, NLEV]); rmag = RS.take([128, 16, NLEV])
        CP("dve", angc[:, :, 0], cosS[LCH], ["S%dcs" % LCH], ("angc", 0))
        CP("dve", angs[:, :, 0], sinS[LCH], ["S%dsn" % LCH], ("angs", 0))
        sa = tS(); sb_ = tS()
        for j in range(NLEV):
            if j >= 1:
                TT("dve", sa, angc[:, :, j - 1], angc[:, :, j - 1], ALU.mult, [("angc", j - 1)], "sq_a")
                TT("dve", sb_, angs[:, :, j - 1], angs[:, :, j - 1], ALU.mult, [("angs", j - 1)], "sq_b")
                TT("dve", angc[:, :, j], sa, sb_, ALU.subtract, ["sq_a", "sq_b"], ("angc", j))
                TT("dve", sa, angc[:, :, j - 1], angs[:, :, j - 1], ALU.mult, [("angc", j - 1), ("angs", j - 1)], "sq_a")
                TS("dve", angs[:, :, j], sa, 2.0, ALU.mult, ["sq_a"], ("angs", j))
            ACT(rmag[:, :, j], adtS, AF.Exp, ["adtS"], ("rmag", j), scale=float(LCH * (1 << j)))
            TT("dve", self.mur[:, :, j], rmag[:, :, j], angc[:, :, j], ALU.mult, [("rmag", j), ("angc", j)], ("mur", j))
            TT("dve", self.mui[:, :, j], rmag[:, :, j], angs[:, :, j], ALU.mult, [("rmag", j), ("angs", j)], ("mui", j))
        for j in range(NLEV):
            TS("dve", self.muni[:, :, j], self.mui[:, :, j], -1.0, ALU.mult, [("mui", j)], ("muni", j))
        self.mu_keys = [(nm, j) for nm in ("mur", "mui", "muni") for j in range(NLEV)]
        self.rp8 = RS.take([128, 16, LCH]); self.rp4 = RS.take([128, 16, 4])
        CP("dve", self.rp8, self.rhoS.unsqueeze(2).to_broadcast([128, 16, LCH]), ["rhoS"], "rp8")
        MS("dve", self.rp8[:, :, 0:1], 0.0, "rp8", r=["rp8"])
        CP("dve", self.rp4, self.rhoS.unsqueeze(2).to_broadcast([128, 16, 4]), ["rhoS"], "rp4")
        MS("dve", self.rp4[:, :, 0:1], 0.0, "rp4", r=["rp4"])
    def s5_main(self, u_bf):
        nc, p = self.nc, self.p
        R0, R1 = self.R0, self.R1
        TT, TS, STT, ACT, CP, MS = self.TT, self.TS, self.STT, self.ACT, self.CP, self.MS
        W1, W4a, W4b = self.W1, self.W4a, self.W4b
        cosS, sinS, nsinS, lamr, lami = self.cosS, self.sinS, self.nsinS, self.lamr, self.lami
        z2 = self.z2
        NSET = 2
        pat = [R1.take([128, 2, 512]) for i in range(NSET)]
        pats = [R1.take([128, 2, 64]) for i in range(NSET)]
        gzf = [R1.take([128, 2, 512]) for i in range(2)]
        gzfs = [R1.take([128, 2, 64]) for i in range(2)]
        gzb = [R1.take([128, 2, T], BF16) for i in range(NSET)]
        HA = [R1.take([128, 2, PAD + NCH]) for i in range(NSET)]
        HB = [R1.take([128, 2, PAD + NCH]) for i in range(NSET)]
        Hpb = [R1.take([128, 2, NCH], BF16) for i in range(NSET)]
        for i in range(NSET):
            MS("pool", HA[i], 0.0, ("HA", i))
            MS("pool", HB[i], 0.0, ("HB", i))
        yf = R1.take([128, 512]); gq = R1.take([128, 512]); ga = R1.take([128, 512]); gt = gq
        VP = self.ps[0]
        SVB = 2
        YB = {0: 4, 512: 5, 1024: 6, 1536: 7}
        PIECES = self.PIECES
        nseg = [0]

        def stageA(q):
            qh, ql = q // 4, q % 4
            hb = q % NSET
            kw = dict(tile_position=(96, 0)) if ql == 3 else {}
            CP("pool", pat[hb].rearrange("p a (c s) -> p (a c) s", s=LCH),
               self.rp8[:, q:q + 1, :].to_broadcast([128, 2 * 512 // LCH, LCH]), ["rp8"], ("pat", hb))
            CP("pool", pats[hb].rearrange("p a (c s) -> p (a c) s", s=4),
               self.rp4[:, q:q + 1, :].to_broadcast([128, 2 * 64 // 4, 4]), ["rp4"], ("pats", hb))
            for (c0, w) in PIECES:
                samp = (w == 64)
                L = 4 if samp else LCH
                sl = nseg[0] % 2
                nseg[0] += 1
                if samp:
                    vre = self.bk(SVB)[:, 0:64]; vim = self.bk(SVB)[:, 64:128]
                    vkeys = [("bank", SVB)]
                else:
                    vre = VP[:, 0, :]; vim = VP[:, 1, :]
                    vkeys = [("bank", 0), ("bank", 1)]
                nmm = 0
                for s in range(L):
                    for part in range(2):
                        nmm += 1
                        outv = (vre, vim)[part][:, s:w:L]
                        p.op("pe", lambda e, outv=outv, s=s, part=part, qh=qh, ql=ql, c0=c0, w=w, L=L, kw=kw: e.matmul(
                            outv, W1[32 * ql:32 * ql + 32, qh, s, part, :],
                            u_bf[32 * ql:32 * ql + 32, qh, c0 + s:c0 + w:L], start=True, stop=True, **kw),
                            reads=self.W1_keys + [("u_bf", qh, c0)], writes=vkeys, inc=(nmm == 2 * L))
                if not samp:
                    go = gzf[sl]; gkey = ("gzf", sl)
                    p.op("dve", lambda e, go=go, hb=hb: e.tensor_tensor_scan(
                        out=go.rearrange("p a c -> p (a c)"), data0=pat[hb].rearrange("p a c -> p (a c)"),
                        data1=VP[:].rearrange("p a c -> p (a c)"), initial=0.0, op0=ALU.mult, op1=ALU.add),
                        reads=[("pat", hb)] + vkeys, writes=[gkey])
                else:
                    go = gzfs[sl]; gkey = ("gzfs", sl)
                    p.op("dve", lambda e, go=go, hb=hb: e.tensor_tensor_scan(
                        out=go.rearrange("p a c -> p (a c)"), data0=pats[hb].rearrange("p a c -> p (a c)"),
                        data1=self.bk(SVB)[:, 0:128], initial=0.0, op0=ALU.mult, op1=ALU.add),
                        reads=[("pats", hb)] + vkeys, writes=[gkey])
                CP("act", gzb[hb][:, :, c0:c0 + w], go, [gkey], ("gzb", hb, c0))
                if not samp:
                    k0 = PAD + c0 // LCH
                    nchunk = w // LCH
                    er = go[:, 0, LCH - 1:512:LCH]; ei = go[:, 1, LCH - 1:512:LCH]
                    c7 = cosS[LCH - 1][:, q:q + 1]; s7 = sinS[LCH - 1][:, q:q + 1]; ns7 = nsinS[LCH - 1][:, q:q + 1]
                    kk = [gkey, "S%dcs" % (LCH - 1), "S%dsn" % (LCH - 1), "nS%dsn" % (LCH - 1)]
                    hk = ("HA", hb)
                    dr = HA[hb][:, 0, k0:k0 + nchunk]; di = HA[hb][:, 1, k0:k0 + nchunk]
                    TS("dve", dr, er, c7, ALU.mult, kk, hk)
                    STT("dve", dr, ei, ns7, dr, ALU.mult, ALU.add, kk + [hk], hk)
                    TS("dve", di, er, s7, ALU.mult, kk + [hk], hk)
                    STT("dve", di, ei, c7, di, ALU.mult, ALU.add, kk + [hk], hk)
                else:
                    er = go[:, 0, 3:64:4]; ei = go[:, 1, 3:64:4]
                    c3 = cosS[3][:, q:q + 1]; s3 = sinS[3][:, q:q + 1]; ns3 = nsinS[3][:, q:q + 1]
                    l4r = lamr[4][:, q:q + 1]; l4i = lami[4][:, q:q + 1]; nl4i = self.nlami4[:, q:q + 1]
                    kk = [gkey, "S3cs", "S3sn", "nS3sn", "lamr4", "lami4", "nlami4"] + self.h0S_keys
                    fr = self.Hfin[:, 0, q, 1:17]; fi = self.Hfin[:, 1, q, 1:17]
                    h0r = self.h0S[:, 0, q, :]; h0i = self.h0S[:, 1, q, :]
                    fk = ("Hfin", q)
                    TS("dve", fr, er, c3, ALU.mult, kk, fk)
                    STT("dve", fr, ei, ns3, fr, ALU.mult, ALU.add, kk + [fk], fk)
                    STT("dve", fr, h0r, l4r, fr, ALU.mult, ALU.add, kk + [fk], fk)
                    STT("dve", fr, h0i, nl4i, fr, ALU.mult, ALU.add, kk + [fk], fk)
                    TS("dve", fi, er, s3, ALU.mult, kk + [fk], fk)
                    STT("dve", fi, ei, c3, fi, ALU.mult, ALU.add, kk + [fk], fk)
                    STT("dve", fi, h0r, l4i, fi, ALU.mult, ALU.add, kk + [fk], fk)
                    STT("dve", fi, h0i, l4r, fi, ALU.mult, ALU.add, kk + [fk], fk)

        def stageB(q):
            hb = q % NSET
            src, dst = HA[hb], HB[hb]
            skey, dkey = ("HA", hb), ("HB", hb)
            for j in range(NLEV):
                d = 1 << j
                mr = self.mur[:, q, j:j + 1]; mi = self.mui[:, q, j:j + 1]; mni = self.muni[:, q, j:j + 1]
                mk = [("mur", j), ("mui", j), ("muni", j)]
                S0 = src[:, 0, PAD:PAD + NCH]; S1 = src[:, 1, PAD:PAD + NCH]
                Z0 = src[:, 0, PAD - d:PAD + NCH - d]; Z1 = src[:, 1, PAD - d:PAD + NCH - d]
                D0 = dst[:, 0, PAD:PAD + NCH]; D1 = dst[:, 1, PAD:PAD + NCH]
                STT("dve", D0, Z0, mr, S0, ALU.mult, ALU.add, [skey] + mk, dkey)
                STT("dve", D0, Z1, mni, D0, ALU.mult, ALU.add, [skey, dkey] + mk, dkey)
                STT("dve", D1, Z0, mi, S1, ALU.mult, ALU.add, [skey, dkey] + mk, dkey)
                STT("dve", D1, Z1, mr, D1, ALU.mult, ALU.add, [skey, dkey] + mk, dkey)
                src, dst = dst, src
                skey, dkey = dkey, skey
            CP("pool", Hpb[hb], HA[hb][:, :, PAD - 1:PAD - 1 + NCH], [("HA", hb)], ("Hpb", hb))
            CP("pool", self.Hfin[:, :, q, 0:1], HA[hb][:, :, PAD + NCH - 1:PAD + NCH], [("HA", hb)], ("Hfin0", q))

        def stageC(q):
            qh, ql = q // 4, q % 4
            hb = q % NSET
            kw = dict(tile_position=(0, 96)) if ql == 3 else {}
            for (c0, w) in PIECES:
                samp = (w == 64)
                L = 4 if samp else LCH
                if samp:
                    ybank = SVB
                    yall = self.bk(SVB)[:, 128:192]
                else:
                    ybank = YB[c0]
                    yall = self.bk(ybank)
                for s in range(L):
                    outv = yall[32 * ql:32 * ql + 32, s:w:L]
                    if samp:
                        hr = self.h0S_bf[:, 0, q, :]; hi = self.h0S_bf[:, 1, q, :]
                        hkeys = ["h0S_bf"]
                    else:
                        k0 = c0 // LCH
                        hr = Hpb[hb][:, 0, k0:k0 + w // L]; hi = Hpb[hb][:, 1, k0:k0 + w // L]
                        hkeys = [("Hpb", hb)]
                    ops = [
                        (W4a[:, q, s, 0, :], gzb[hb][:, 0, c0 + s:c0 + w:L]),
                        (W4a[:, q, s, 1, :], gzb[hb][:, 1, c0 + s:c0 + w:L]),
                        (W4b[:, q, s, 0, :], hr),
                        (W4b[:, q, s, 1, :], hi),
                    ]
                    for i, (lh, rh) in enumerate(ops):
                        last = (i == 3 and s == L - 1)
                        p.op("pe", lambda e, outv=outv, lh=lh, rh=rh, i=i, kw=kw: e.matmul(
                            outv, lh, rh, start=(i == 0), stop=(i == 3), **kw),
                            reads=self.W4_keys + [("gzb", hb, c0)] + hkeys, writes=[("bank", ybank)], inc=last)

        def stageY(qh):
            for (c0, w) in PIECES:
                samp = (w == 64)
                if samp:
                    ybank = SVB; ysrc = self.bk(SVB)[:, 128:192]
                else:
                    ybank = YB[c0]; ysrc = self.bk(ybank)
                yv = yf[:, 0:w]; qv = gq[:, 0:w]; av = ga[:, 0:w]; tv = gt[:, 0:w]
                STT("dve", yv, u_bf[:, qh, c0:c0 + w], self.dS5[:, qh:qh + 1], ysrc, ALU.mult, ALU.add,
                    [("u_bf", qh, c0), "dS5", ("bank", ybank)], "s5yf")
                if qh == 0:
                    self.tap("ys5_%d" % c0, yv, ["s5yf"])
                self.gelu2(yv, qv, av, tv, z2[:, qh, c0:c0 + w], "s5yf", "s5gq", "s5ga", "s5gq", ("z2", qh, c0))

        for qh in range(4):
            qs = [4 * qh + i for i in range(4)]
            stageA(qs[0]); stageB(qs[0])
            for i in range(1, 4):
                stageA(qs[i])
                stageC(qs[i - 1])
                stageB(qs[i])
            stageC(qs[3])
            stageY(qh)

    def gelu2(self, yv, qv, av, tv, outv, ky, kq, ka, kt, kout):
        self.ACT(qv, yv, AF.Square, [ky], kq)
        self.STT("dve", av, qv, GK * GC, yv, ALU.mult, ALU.mult, [kq, ky], ka)
        self.STT("dve", av, yv, GK, av, ALU.mult, ALU.add, [ky, ka], ka)
        self.ACT(tv, av, AF.Tanh, [ka], kt)
        self.STT("dve", outv, tv, 1.0, yv, ALU.add, ALU.mult, [kt, ky], kout)

    def lru_stage(self):
        nc, p, I, O = self.nc, self.p, self.I, self.O
        R0, R1 = self.R0, self.R1
        TT, TS, STT, ACT, CP, MS = self.TT, self.TS, self.STT, self.ACT, self.CP, self.MS
        NCK = dict(allow_slow_non_contiguous=True)
        xT, z2 = self.xT, self.z2
        PIECES = self.PIECES
        self.free_banks = list(range(8))
        LRUc = self.LRUc
        LK = self.LRUc_keys
        baT = R0.take([128, 8]); bxT = R0.take([128, 8]); sc8 = R0.take([128, 8]); hsc8 = R0.take([128, 8])
        TS("dve", baT, LRUc[:, :, 69], 0.5, ALU.mult, LK, "baT")
        TS("dve", bxT, LRUc[:, :, 70], 0.5, ALU.mult, LK, "bxT")
        ACT(sc8, LRUc[:, :, 71], AF.Exp, LK, "sc8", scale=-1.0)
        ACT(sc8, sc8, AF.Ln, ["sc8"], "sc8", bias=1.0)
        TS("dve", hsc8, sc8, -4.0, ALU.mult, ["sc8"], "hsc8")
        TS("dve", sc8, sc8, -8.0, ALU.mult, ["sc8", "hsc8"], "sc8")
        Wg = R0.take([128, 8, 2, 128], BF16)
        MS("pool", Wg, 0.0, "Wg")
        for gi, nm in ((0, "lru_wa"), (1, "lru_wx")):
            v = I[nm].rearrange("(j two) i o -> two i j o", two=2)
            for par in range(2):
                p.dma("pool", Wg[64 * par:64 * par + 64, :, gi, 64 * par:64 * par + 64], v[par], "d_wg", writes=["Wg"])
        p.last_write["Wg"] = ("d_wg", p.count["d_wg"])
        hfinL = R0.take([128, 8, 17])
        self.merged2 = R1.take([128, 8, T], BF16)
        merged2 = self.merged2
        self.merged_end = R1.pos
        xl_sb = R1.take([128, 3 + TP]); xs_sb = R1.take([128, 16, 7])
        abuf = R1.take([128, T]); a2buf = R1.take([128, T]); ixbuf = R1.take([128, T])
        hbuf = [R1.take([128, T]) for i in range(2)]
        tail_sb = R1.take([NTAIL, D])
        MS("pool", xl_sb[:, 0:3], 0.0, ("xl", -1))
        NP = 2
        ytmp = [R1.take([128, 512]) for i in range(NP)]
        xc = [R1.take([128, 512]) for i in range(NP)]
        xcb = [R1.take([128, 512], BF16) for i in range(NP)]
        rp_ = [R1.take([128, 512]) for i in range(NP)]
        ip_ = [R1.take([128, 512]) for i in range(NP)]
        glp = [R1.take([128, 512]) for i in range(NP)]
        gsp = [R1.take([128, 512]) for i in range(NP)]
        gbp = [R1.take([128, 512]) for i in range(NP)]
        t1p = [R1.take([128, 512]) for i in range(NP)]
        t16 = R1.take([128, 16])
        wsl = [R1.take([128, 8, 128], BF16) for i in range(2)]
        wsg = [R1.take([128, 8, 2, 128], BF16) for i in range(2)]
        wgl = [R1.take([128, 4, 2, 128], BF16) for i in range(2)]
        npc = [0]

        def load_w(j):
            sl = j % 2
            wv = lambda c: I["w_in"][:, c:c + 128].rearrange("(k p) n -> p k n", p=128)
            gv = lambda c: I["w_glu"][:, c:c + 128].rearrange("(k p) n -> p k n", p=128)
            p.dma("pool", wsl[sl], wv(128 * j), "d_wsl%d" % sl, writes=[("wsl", sl)])
            p.dma("pool", wsg[sl][:, :, 0, :], wv(1536 + 128 * j), "d_wsg%d_0" % sl, writes=[("wsg", sl, 0)])
            p.dma("pool", wsg[sl][:, :, 1, :], wv(2560 + 128 * j), "d_wsg%d_1" % sl, writes=[("wsg", sl, 1)])
            p.dma("pool", wgl[sl][:, :, 0, :], gv(128 * j), "d_wgl%d_0" % sl, writes=[("wgl", sl, 0)])
            p.dma("pool", wgl[sl][:, :, 1, :], gv(1024 + 128 * j), "d_wgl%d_1" % sl, writes=[("wgl", sl, 1)])

        for j in range(8):
            sl = j % 2
            hb = hbuf[sl]
            if j == 0:
                load_w(0)
            if j + 1 < 8:
                load_w(j + 1)
            CP("pool", xs_sb[:, :, 0:3], LRUc[:, j, 0:48].rearrange("p (b k) -> p b k", k=3), LK, ("xs", "st"))
            cw = [LRUc[:, j, 64 + k:65 + k] for k in range(4)]
            cbj = LRUc[:, j, 68:69]
            for pi, (c0, w) in enumerate(PIECES):
                samp = (w == 64)
                i2 = npc[0] % NP
                npc[0] += 1
                b = self.bank()
                for k in range(8):
                    p.op("pe", lambda e, k=k, b=b, sl=sl, c0=c0, w=w: e.matmul(
                        self.bk(b)[:, 0:w], wsl[sl][:, k, :], xT[:, k, c0:c0 + w], start=(k == 0), stop=(k == 7)),
                        reads=[("wsl", sl)] + self.xT_keys(c0, w), writes=[("bank", b)], inc=(k == 7))
                ps = self.bk(b)[:, 0:w]
                yv = ytmp[i2][:, 0:w]; xcv = xc[i2][:, 0:w]
                ACT(yv, ps, AF.Identity, [("bank", b)] + LK, ("ytmp", i2), scale=cw[3], bias=cbj)
                if not samp:
                    CP("act", xl_sb[:, 3 + c0:3 + c0 + w], ps, [("bank", b)], ("xl", pi))
                    xk = [("xl", pi), ("xl", pi - 1)]
                    STT("dve", yv, xl_sb[:, c0 + 2:c0 + 2 + w], cw[2], yv, ALU.mult, ALU.add, xk + [("ytmp", i2)], ("ytmp", i2))
                    STT("dve", yv, xl_sb[:, c0 + 1:c0 + 1 + w], cw[1], yv, ALU.mult, ALU.add, xk + [("ytmp", i2)], ("ytmp", i2))
                    STT("dve", xcv, xl_sb[:, c0:c0 + w], cw[0], yv, ALU.mult, ALU.add, xk + [("ytmp", i2)], ("xc", i2))
                else:
                    CP("act", xs_sb[:, :, 3:7], ps.rearrange("p (b s) -> p b s", s=4), [("bank", b)], ("xs", "new"))
                    xk = [("xs", "st"), ("xs", "new")]
                    y3 = yv.rearrange("p (b s) -> p b s", s=4); xc3 = xcv.rearrange("p (b s) -> p b s", s=4)
                    STT("dve", y3, xs_sb[:, :, 2:6], cw[2], y3, ALU.mult, ALU.add, xk + [("ytmp", i2)], ("ytmp", i2))
                    STT("dve", y3, xs_sb[:, :, 1:5], cw[1], y3, ALU.mult, ALU.add, xk + [("ytmp", i2)], ("ytmp", i2))
                    STT("dve", xc3, xs_sb[:, :, 0:4], cw[0], y3, ALU.mult, ALU.add, xk + [("ytmp", i2)], ("xc", i2))
                if j == 0:
                    self.tap("xc_%d" % c0, xcv, [("xc", i2)])
                CP("pool", xcb[i2][:, 0:w], xcv, [("xc", i2)], ("xcb", i2))
                ba_ = self.bank(); bx_ = self.bank()
                p.op("pe", lambda e, ba_=ba_, j=j, i2=i2, w=w: e.matmul(self.bk(ba_)[:, 0:w], Wg[:, j, 0, :], xcb[i2][:, 0:w], start=True, stop=True),
                     reads=["Wg", ("xcb", i2)], writes=[("bank", ba_)])
                p.op("pe", lambda e, bx_=bx_, j=j, i2=i2, w=w: e.matmul(self.bk(bx_)[:, 0:w], Wg[:, j, 1, :], xcb[i2][:, 0:w], start=True, stop=True),
                     reads=["Wg", ("xcb", i2)], writes=[("bank", bx_)])
                rv = rp_[i2][:, 0:w]; iv = ip_[i2][:, 0:w]
                ACT(rv, self.bk(ba_)[:, 0:w], AF.Tanh, [("bank", ba_), "baT"], ("rp", i2), scale=0.5, bias=baT[:, j:j + 1])
                ACT(iv, self.bk(bx_)[:, 0:w], AF.Tanh, [("bank", bx_), "bxT"], ("ip", i2), scale=0.5, bias=bxT[:, j:j + 1])
                ACT(abuf[:, c0:c0 + w], rv, AF.Exp, [("rp", i2), "hsc8"], ("abuf", pi), scale=hsc8[:, j:j + 1], bias=hsc8[:, j:j + 1])
                ACT(a2buf[:, c0:c0 + w], rv, AF.Exp, [("rp", i2), "sc8"], ("a2buf", pi), scale=sc8[:, j:j + 1], bias=sc8[:, j:j + 1])
                STT("dve", ixbuf[:, c0:c0 + w], iv, 1.0, xcv, ALU.add, ALU.mult, [("ip", i2), ("xc", i2)], ("ixbuf", pi))
            allp = lambda nm: [(nm, pi) for pi in range(len(PIECES))]
            ACT(a2buf, a2buf, AF.Sqrt, allp("a2buf"), "mh", scale=-0.25, bias=0.25)
            TT("dve", ixbuf, a2buf, ixbuf, ALU.mult, ["mh"] + allp("ixbuf"), "bterm")
            TT("dve", t16, abuf[:, TP:T:4], LRUc[:, j, 48:64], ALU.mult, allp("abuf") + LK, "t16")
            TT("dve", ixbuf[:, TP:T:4], ixbuf[:, TP:T:4], t16, ALU.add, ["bterm", "t16"], "bterm")
            MS("dve", abuf[:, TP:T:4], 0.0, "afix", r=allp("abuf") + ["t16"])
            p.op("dve", lambda e, hb=hb: e.tensor_tensor_scan(out=hb, data0=abuf, data1=ixbuf, initial=0.0, op0=ALU.mult, op1=ALU.add),
                 reads=allp("abuf") + ["afix", "bterm"], writes=[("hbuf", sl)])
            for nm in ("abuf", "a2buf", "ixbuf"):
                for pi in range(len(PIECES)):
                    p.readers.setdefault((nm, pi), []).append(p.last_write[("hbuf", sl)])
            if j == 0:
                self.tap("hbuf", hb, [("hbuf", sl)])
            CP("pool", hfinL[:, j, 0:1], hb[:, TP - 1:TP], [("hbuf", sl)], ("hfinL", j, 0))
            CP("pool", hfinL[:, j, 1:17], hb[:, TP + 3:T:4], [("hbuf", sl)], ("hfinL", j, 1))
            b = self.bank()
            for k in range(8):
                p.op("pe", lambda e, k=k, b=b, sl=sl: e.matmul(self.bk(b)[0:NTAIL, 0:128], xT[:, k, TAIL0:T], wsl[sl][:, k, :],
                                                               start=(k == 0), stop=(k == 7)),
                     reads=[("wsl", sl)] + self.xT_keys(TAIL0, NTAIL), writes=[("bank", b)], inc=(k == 7))
            CP("act", tail_sb[:, 128 * j:128 * (j + 1)], self.bk(b)[0:NTAIL, 0:128], [("bank", b)], ("tail", j))
            for pi, (c0, w) in enumerate(PIECES):
                i2 = npc[0] % NP
                npc[0] += 1
                bl = self.bank(); bs = self.bank(); bga = self.bank(); bgb = self.bank()
                for (bb, gi) in ((bl, 0), (bs, 1)):
                    for k in range(8):
                        p.op("pe", lambda e, k=k, bb=bb, gi=gi, sl=sl, c0=c0, w=w: e.matmul(
                            self.bk(bb)[:, 0:w], wsg[sl][:, k, gi, :], xT[:, k, c0:c0 + w], start=(k == 0), stop=(k == 7)),
                            reads=[("wsg", sl, gi)] + self.xT_keys(c0, w), writes=[("bank", bb)], inc=(k == 7))
                for (bb, gi) in ((bga, 0), (bgb, 1)):
                    for k in range(4):
                        p.op("pe", lambda e, k=k, bb=bb, gi=gi, sl=sl, c0=c0, w=w: e.matmul(
                            self.bk(bb)[:, 0:w], wgl[sl][:, k, gi, :], z2[:, k, c0:c0 + w], start=(k == 0), stop=(k == 3)),
                            reads=[("wgl", sl, gi)] + [("z2", k, c0)], writes=[("bank", bb)], inc=(k == 3))
                glv = glp[i2][:, 0:w]; gsv = gsp[i2][:, 0:w]; gbv = gbp[i2][:, 0:w]; t1v = t1p[i2][:, 0:w]
                ACT(glv, self.bk(bl)[:, 0:w], AF.Tanh, [("bank", bl)], ("glp", i2), scale=0.5)
                ACT(gsv, self.bk(bs)[:, 0:w], AF.Tanh, [("bank", bs)], ("gsp", i2), scale=0.5)
                ACT(gbv, self.bk(bgb)[:, 0:w], AF.Tanh, [("bank", bgb)], ("gbp", i2), scale=0.25)
                STT("dve", t1v, gbv, 1.0, self.bk(bga)[:, 0:w], ALU.add, ALU.mult, [("gbp", i2), ("bank", bga)], ("t1p", i2))
                STT("dve", t1v, gsv, 1.0, t1v, ALU.add, ALU.mult, [("gsp", i2), ("t1p", i2)], ("t1p", i2))
                STT("dve", glv, glv, 1.0, hb[:, c0:c0 + w], ALU.add, ALU.mult, [("glp", i2), ("hbuf", sl)], ("glp", i2))
                STT("dve", merged2[:, j, c0:c0 + w], t1v, 0.25, glv, ALU.mult, ALU.add, [("t1p", i2), ("glp", i2)], ("merged2", j, c0))
        self.tap("merged2", merged2[:, 0, :], [("merged2", 0, c0) for c0, _ in PIECES])
        self.out_dma(O["lru_conv"], tail_sb, [("tail", j) for j in range(8)])
        self.out_dma(O["lru_h"], hfinL, [("hfinL", j, i) for j in range(8) for i in range(2)])

    def ln_tile(self, ps_flat, res, gB, bB, ytok, outv, rows, kps, kres, kg, kb, ky, kout, st6, mv, sd):
        p = self.p
        TT, TS, STT, ACT, CP = self.TT, self.TS, self.STT, self.ACT, self.CP
        yv = ytok[0:rows, :]
        p.op("act", lambda e: e.activation(out=yv, in_=ps_flat[0:rows, :], func=AF.Copy, scale=0.5), reads=kps, writes=[ky])
        STT("dve", yv, res[0:rows, :], ALPHA, yv, ALU.mult, ALU.add, [kres, ky], ky)
        for h in range(2):
            p.op("dve", lambda e, h=h: e.bn_stats(out=st6[0:rows, h, :], in_=yv[:, 512 * h:512 * (h + 1)]), reads=[ky], writes=[ky + ("st", h)])
        p.op("dve", lambda e: e.bn_aggr(out=mv[0:rows, :], in_=st6[0:rows, :, :].rearrange("p a b -> p (a b)")),
             reads=[ky + ("st", 0), ky + ("st", 1)], writes=[ky + ("mv",)])
        ACT(sd[0:rows, :], mv[0:rows, 1:2], AF.Sqrt, [ky + ("mv",)], ky + ("sd",), bias=LN_EPS)
        p.op("dve", lambda e: e.reciprocal(out=sd[0:rows, :], in_=sd[0:rows, :]), reads=[ky + ("sd",)], writes=[ky + ("sd",)])
        TS("dve", yv, yv, mv[0:rows, 0:1], ALU.subtract, [ky, ky + ("mv",), ky + ("sd",)], ky, s2=sd[0:rows, 0:1], op1=ALU.mult)
        TT("pool", yv, yv, gB[0:rows, :], ALU.mult, [ky, kg], ky)
        TT("dve", outv[0:rows, :], yv, bB[0:rows, :], ALU.add, [ky, kb], kout)

    def mix_stage(self):
        nc, p, I, O = self.nc, self.p, self.I, self.O
        R0, R1 = self.R0, self.R1
        TT, TS, STT, ACT, CP, MS = self.TT, self.TS, self.STT, self.ACT, self.CP, self.MS
        merged2 = self.merged2
        x1T = self.xT
        self.x1T = x1T
        lnc = self.lnc
        for i, nm in enumerate(("ln1_g", "ln1_b", "ln2_g", "ln2_b")):
            p.dma("sp", lnc[:, i, :], I[nm].partition_broadcast(128), "d_lnc%d" % i, writes=[("lnc", i)])
        wout = R1.take([128, 8, D], BF16)
        for h in range(2):
            p.dma("pool", wout[:, :, 512 * h:512 * (h + 1)], I["w_out"][:, 512 * h:512 * (h + 1)].rearrange("(k p) n -> p k n", p=128),
                  "d_wout%d" % h, writes=[("wout", h)])
        NB = 2
        xtok = [R1.take([128, D]) for i in range(NB)]
        ytok = [R1.take([128, D]) for i in range(NB)]
        x1tok = [R1.take([128, D]) for i in range(NB)]
        x1b = [R1.take([128, D], BF16) for i in range(NB)]
        st6 = [R1.take([128, 2, 6]) for i in range(NB)]
        mv = [R1.take([128, 2]) for i in range(NB)]
        sd = [R1.take([128, 1]) for i in range(NB)]
        ntt = (T + 127) // 128
        self.free_banks = [4, 5, 6, 7]
        for tt in range(ntt):
            r0 = tt * 128
            rows = min(128, T - r0)
            s = tt % NB
            pp = tt % 2
            p.dma("sp", xtok[s][0:rows, :], I["x"][r0:r0 + rows, :], "d_xtok%d" % s, writes=[("xtok", s)])
            psf = self.ps[pp][:].rearrange("p a c -> p (a c)")
            for h in range(2):
                for k in range(8):
                    p.op("pe", lambda e, k=k, h=h, pp=pp, r0=r0, rows=rows: e.matmul(
                        self.ps[pp][0:rows, h, :], merged2[:, k, r0:r0 + rows], wout[:, k, 512 * h:512 * (h + 1)],
                        start=(k == 0), stop=(k == 7)),
                        reads=[("wout", h)] + [("merged2", k, c0) for (c0, w) in self.PIECES if c0 <= r0 < c0 + w],
                        writes=[("bank", 2 * pp + h)], inc=(k == 7))
            self.ln_tile(psf, xtok[s], lnc[:, 0, :], lnc[:, 1, :], ytok[s], x1tok[s], rows,
                         [("bank", 2 * pp), ("bank", 2 * pp + 1)], ("xtok", s), ("lnc", 0), ("lnc", 1), ("ytok", s), ("x1tok", s),
                         st6[s], mv[s], sd[s])
            if tt == 0:
                self.tap("x1tok", x1tok[s], [("x1tok", s)])
            p.dma("sp", self.x1_scr[r0:r0 + rows, :], x1tok[s][0:rows, :], "d_x1w%d" % s, reads=[("x1tok", s)], writes=[("x1scr", tt)])
            CP("act", x1b[s][0:rows, :], x1tok[s][0:rows, :], [("x1tok", s)], ("x1b", s))
            b = self.bank()
            pt = self.bk(b).bitcast(BF16)
            for k in range(8):
                p.op("pe", lambda e, k=k, pt=pt, s=s, rows=rows: e.transpose(
                    pt[:, k * 128:k * 128 + rows], x1b[s][0:rows, k * 128:(k + 1) * 128], self.ident_b[0:rows, 0:rows]),
                    reads=[("x1b", s), "ident_b"], writes=[("bank", b)], inc=(k == 7))
            src = pt.rearrange("p (k c) -> p k c", c=128)[:, :, 0:rows]
            CP("act", x1T[:, :, r0:r0 + rows], src, [("bank", b)], ("x1T", tt))
        self.tap("x1T", x1T[:, 0, :], [("x1T", tt) for tt in range(ntt)])

    def x1T_keys(self, c0, w):
        return [("x1T", tt) for tt in range(c0 // 128, (c0 + w - 1) // 128 + 1)]

    def ffn_stage(self):
        nc, p, I, O = self.nc, self.p, self.I, self.O
        R0, R1 = self.R0, self.R1
        TT, TS, STT, ACT, CP, MS = self.TT, self.TS, self.STT, self.ACT, self.CP, self.MS
        NCK = dict(allow_slow_non_contiguous=True)
        x1T, lnc = self.x1T, self.lnc
        NJ = DFF // 128
        QW = 576
        Fc = self.Fc
        FK = self.Fc_keys
        wdn = R1.take([128, NJ, D], BF16)
        for c in range(6):
            p.dma("pool", wdn[:, 4 * c:4 * c + 4, :], I["w_down"][512 * c:512 * (c + 1), :].rearrange("(k p) n -> p k n", p=128),
                  "d_wdn%d" % c, writes=[("wdn", c)])
        Gq = R1.take([128, NJ, QW], BF16)
        NSL = 3
        wup = [R1.take([128, 8, 2, 128], BF16) for i in range(NSL)]
        halo = R1.take([128, NJ, 2])
        MS("pool", halo, 0.0, "halo_init")
        NT = 3
        a_sb = [R1.take([128, 2 + 512]) for i in range(NT)]
        as_sb = R1.take([128, 16, 6])
        y0 = [R1.take([128, 512]) for i in range(NT)]
        qq = [R1.take([128, 512]) for i in range(NT)]
        ag = [R1.take([128, 512]) for i in range(NT)]
        tailf = [R1.take([NTAIL, 512]) for i in range(2)]
        NB = 2
        x1tok = [R1.take([128, D]) for i in range(NB)]
        ytok = [R1.take([128, D]) for i in range(NB)]
        otok = ytok
        st6 = [R1.take([128, 2, 6]) for i in range(NB)]
        mv = [R1.take([128, 2]) for i in range(NB)]
        sd = [R1.take([128, 1]) for i in range(NB)]
        nld = [0]

        def load_wup(j):
            sl = nld[0] % NSL
            nld[0] += 1
            wv = lambda c: I["w_up"][:, c:c + 128].rearrange("(k p) n -> p k n", p=128)
            p.dma("pool", wup[sl][:, :, 0, :], wv(128 * j), "d_wup%d_0" % sl, writes=[("wup", sl, 0)])
            p.dma("pool", wup[sl][:, :, 1, :], wv(DFF + 128 * j), "d_wup%d_1" % sl, writes=[("wup", sl, 1)])
            return sl

        npc = [0]
        ntile = [0]
        for n in range(4):
            q0 = 512 * n
            pieces = [(q0, 512, 0)] + ([(TP, NS, 512)] if n == 3 else [])
            self.free_banks = list(range(8))
            pending = [load_wup(0), load_wup(1)]
            for j in range(NJ):
                sl = pending.pop(0)
                if j + 2 < NJ:
                    pending.append(load_wup(j + 2))
                fw = [Fc[:, j, 32 + k:33 + k] for k in range(3)]
                for (c0, w, lc0) in pieces:
                    samp = (w == NS)
                    i2 = npc[0] % NT
                    npc[0] += 1
                    bA = self.bank(); bG = self.bank()
                    for (bb, gi) in ((bA, 0), (bG, 1)):
                        for k in range(8):
                            p.op("pe", lambda e, k=k, bb=bb, gi=gi, sl=sl, c0=c0, w=w: e.matmul(
                                self.bk(bb)[:, 0:w], wup[sl][:, k, gi, :], x1T[:, k, c0:c0 + w], start=(k == 0), stop=(k == 7)),
                                reads=[("wup", sl, gi)] + self.x1T_keys(c0, w), writes=[("bank", bb)], inc=(k == 7))
                    aps = self.bk(bA)[:, 0:w]; gps = self.bk(bG)[:, 0:w]
                    yv = y0[i2][:, 0:w]; qv = qq[i2][:, 0:w]; av = ag[i2][:, 0:w]
                    ACT(yv, aps, AF.Identity, [("bank", bA)] + FK, ("y0", i2), scale=fw[2], bias=Fc[:, j, 35:36])
                    if not samp:
                        ab = a_sb[i2]
                        CP("act", ab[:, 2:2 + w], aps, [("bank", bA)], ("a_sb", i2))
                        CP("pool", ab[:, 0:2], halo[:, j, :], ["halo_init", ("halo", j)], ("a_sbh", i2))
                        ak = [("a_sb", i2), ("a_sbh", i2), ("y0", i2)]
                        STT("dve", yv, ab[:, 1:1 + w], fw[1], yv, ALU.mult, ALU.add, ak, ("y0", i2))
                        STT("dve", yv, ab[:, 0:w], fw[0], yv, ALU.mult, ALU.add, ak, ("y0", i2))
                        p.op("pool", lambda e, ab=ab, j=j, w=w: e.tensor_copy(out=halo[:, j, :], in_=ab[:, w:w + 2]),
                             reads=[("a_sb", i2), ("a_sbh", i2)], writes=[("halo", j)])
                    else:
                        CP("act", as_sb[:, :, 2:6], aps.rearrange("p (b s) -> p b s", s=4), [("bank", bA)], ("as_sb", "new"))
                        CP("pool", as_sb[:, :, 0:2], Fc[:, j, 0:32].rearrange("p (b k) -> p b k", k=2), FK, ("as_sb", "st"))
                        ak = [("as_sb", "new"), ("as_sb", "st"), ("y0", i2)]
                        y3 = yv.rearrange("p (b s) -> p b s", s=4)
                        STT("dve", y3, as_sb[:, :, 1:5], fw[1], y3, ALU.mult, ALU.add, ak, ("y0", i2))
                        STT("dve", y3, as_sb[:, :, 0:4], fw[0], y3, ALU.mult, ALU.add, ak, ("y0", i2))
                    ACT(qv, yv, AF.Square, [("y0", i2)], ("qq", i2))
                    TS("pool", qv, qv, GC, ALU.mult, [("qq", i2)], ("qq", i2), s2=1.0, op1=ALU.add)
                    TT("pool", av, qv, yv, ALU.mult, [("qq", i2), ("y0", i2)], ("ag", i2))
                    ACT(qv, av, AF.Tanh, [("ag", i2)], ("qq", i2), scale=GK)
                    STT("dve", av, qv, 1.0, yv, ALU.add, ALU.mult, [("qq", i2), ("y0", i2)], ("ag", i2))
                    TT("dve", Gq[:, j, lc0:lc0 + w], av, gps, ALU.mult, [("ag", i2), ("bank", bG)], ("Gq", j, lc0))
                if n == 3:
                    b = self.bank()
                    tb = (j // 4) % 2
                    for k in range(8):
                        p.op("pe", lambda e, k=k, b=b, sl=sl: e.matmul(self.bk(b)[0:NTAIL, 0:128], x1T[:, k, TAIL0:T], wup[sl][:, k, 0, :],
                                                                       start=(k == 0), stop=(k == 7)),
                             reads=[("wup", sl, 0)] + self.x1T_keys(TAIL0, NTAIL), writes=[("bank", b)], inc=(k == 7))
                    CP("act", tailf[tb][:, 128 * (j % 4):128 * (j % 4 + 1)], self.bk(b)[0:NTAIL, 0:128], [("bank", b)], ("tailf", tb, j % 4))
                    if j % 4 == 3:
                        sem = "o_tf%d" % tb
                        p.dma("sp", O["ffn_conv"][:, 512 * (j // 4):512 * (j // 4 + 1)], tailf[tb], sem,
                              reads=[("tailf", tb, i) for i in range(4)])
                        self.out_sems[sem] = p.count[sem]
            if n == 0:
                self.tap("Gq", Gq[:, 0, :], [("Gq", 0, 0)])
            tts = [4 * n + i for i in range(4)] + ([16] if n == 3 else [])
            for tt in tts:
                r0 = tt * 128
                rows = min(128, T - r0)
                lc = r0 - q0 if tt < 16 else 512
                s = ntile[0] % NB
                pp = ntile[0] % 2
                ntile[0] += 1
                p.dma("sp", x1tok[s][0:rows, :], self.x1_scr[r0:r0 + rows, :], "d_x1r%d" % s, reads=[("x1scr", tt)], writes=[("x1tok2", s)])
                psf = self.ps[pp][:].rearrange("p a c -> p (a c)")
                gkeys = [("Gq", j, 512 if tt == 16 else 0) for j in range(NJ)]
                for h in range(2):
                    for k in range(NJ):
                        p.op("pe", lambda e, k=k, h=h, pp=pp, lc=lc, rows=rows: e.matmul(
                            self.ps[pp][0:rows, h, :], Gq[:, k, lc:lc + rows], wdn[:, k, 512 * h:512 * (h + 1)],
                            start=(k == 0), stop=(k == NJ - 1)),
                            reads=[("wdn", k // 4), ("Gq", k, 512 if tt == 16 else 0)], writes=[("bank", 2 * pp + h)], inc=(k == NJ - 1))
                self.ln_tile(psf, x1tok[s], lnc[:, 2, :], lnc[:, 3, :], ytok[s], otok[s], rows,
                             [("bank", 2 * pp), ("bank", 2 * pp + 1)], ("x1tok2", s), ("lnc", 2), ("lnc", 3), ("ytok2", s), ("ytok2", s),
                             st6[s], mv[s], sd[s])
                sem = "o_y%d" % s
                p.dma("sp", O["y"][r0:r0 + rows, :], otok[s][0:rows, :], sem, reads=[("ytok2", s)])
                self.out_sems[sem] = p.count[sem]

    def finish(self):
        fw = [(s, v) for s, v in self.out_sems.items()]
        self.p.emit(final_waits=fw)
        self.st.close()
        print("arena peaks: R0 %d/%d words, R1 %d/%d words" % (self.R0.peak, self.R0.words, self.R1.peak, self.R1.words))
        return self.nc


def shard_inputs(inputs, c):
    f = lambda a: np.ascontiguousarray(a, dtype=np.float32)
    m = {}
    m["x"] = f(np.concatenate([inputs["x_prompt"][c], inputs["x_sample"][NSQ * c:NSQ * (c + 1)].reshape(NS, D)], axis=0))
    m["st_lru_conv"] = f(inputs["state_lru_conv"][0, NSQ * c:NSQ * (c + 1)].reshape(NSQ * 3, D))
    m["st_lru_h"] = f(inputs["state_lru_h"][0, NSQ * c:NSQ * (c + 1)])
    m["st_s5_re"] = f(inputs["state_s5_re"][0, NSQ * c:NSQ * (c + 1)].reshape(NSQ, 2048))
    m["st_s5_im"] = f(inputs["state_s5_im"][0, NSQ * c:NSQ * (c + 1)].reshape(NSQ, 2048))
    m["st_ffn_conv"] = f(inputs["state_ffn_conv"][0, NSQ * c:NSQ * (c + 1)].reshape(NSQ * 2, DFF))
    for k in ("w_in", "lru_conv_w", "lru_conv_b", "lru_wa", "lru_ba", "lru_wx", "lru_bx", "lru_lambda", "s5_a_re", "s5_a_im",
              "s5_log_dt", "s5_b_re", "s5_b_im", "s5_c_re", "s5_c_im", "s5_d", "w_glu", "w_out", "ln1_g", "ln1_b", "w_up",
              "ffn_conv_w", "ffn_conv_b", "w_down", "ln2_g", "ln2_b"):
        m[k] = f(inputs[k][0])
    return m


_NC_CACHE = {}


def _get_nc():
    if "nc" not in _NC_CACHE:
        _NC_CACHE["nc"] = Builder().build()
    return _NC_CACHE["nc"]


def kernel(**inputs):
    nc = _get_nc()
    in_maps = [shard_inputs(inputs, c) for c in range(NCORES)]
    res = run_bass_kernel_spmd(nc, in_maps, core_ids=list(range(NCORES)))
    R = res.results
    B = NCORES
    y_p = np.zeros((B, TP, D), np.float32); y_s = np.zeros((B * NSQ, 4, D), np.float32)
    p_conv = np.zeros((1, B, 3, D), np.float32); p_h = np.zeros((1, B, D), np.float32)
    p_re = np.zeros((1, B, 32, 64), np.float32); p_im = np.zeros((1, B, 32, 64), np.float32)
    p_ffn = np.zeros((1, B, 2, DFF), np.float32)
    s_conv = np.zeros((1, B * NSQ, 3, D), np.float32); s_h = np.zeros((1, B * NSQ, D), np.float32)
    s_re = np.zeros((1, B * NSQ, 32, 64), np.float32); s_im = np.zeros((1, B * NSQ, 32, 64), np.float32)
    s_ffn = np.zeros((1, B * NSQ, 2, DFF), np.float32)
    for c in range(B):
        r = R[c]
        sl = slice(NSQ * c, NSQ * (c + 1))
        y_p[c] = r["y"][0:TP]
        y_s[sl] = r["y"][TP:].reshape(NSQ, 4, D)
        lc = r["o_lru_conv"]
        p_conv[0, c] = lc[0:3]
        s_conv[0, sl] = lc[3:].reshape(NSQ, 4, D)[:, 1:4]
        lh = r["o_lru_h"].transpose(2, 1, 0).reshape(17, D)
        p_h[0, c] = lh[0]
        s_h[0, sl] = lh[1:17]
        s5 = r["o_s5"].reshape(2, 64, 2, 16, 17).transpose(2, 4, 3, 0, 1)
        s5 = s5.reshape(2, 17, 32, 64)
        p_re[0, c] = s5[0, 0]
        p_im[0, c] = s5[1, 0]
        s_re[0, sl] = s5[0, 1:17]
        s_im[0, sl] = s5[1, 1:17]
        fc = r["o_ffn_conv"]
        p_ffn[0, c] = fc[1:3]
        s_ffn[0, sl] = fc[3:].reshape(NSQ, 4, DFF)[:, 2:4]
    return (y_p, y_s, p_conv, p_h, p_re, p_im, p_ffn, s_conv, s_h, s_re, s_im, s_ffn)
```

```python
import math
import contextlib
import numpy as np
import concourse.bass as bass
import concourse.mybir as mybir
from concourse.bass_utils import run_bass_kernel_spmd

F32 = mybir.dt.float32
BF16 = mybir.dt.bfloat16
AF = mybir.ActivationFunctionType
ALU = mybir.AluOpType

ENGINES = ("pe", "act", "dve", "pool", "sp")
NCORES = 8
TP = 2048
NSQ = 16
NS = 64
T = TP + NS
TAIL0 = TP - 3
NTAIL = T - TAIL0
D = 1024
DFF = 3072
ALPHA = 2.0 ** 0.25
LN_EPS = 1e-5
LCH = 8
NCH = TP // LCH
NLEV = 8
PAD = 128
PI = math.pi
GK = math.sqrt(2.0 / math.pi)
GC = 0.044715


class Prog:
    def __init__(self, nc):
        self.nc = nc
        self.streams = {e: [] for e in ENGINES}
        self.count = {}
        self.waited = {e: {} for e in ENGINES}
        self.last_write = {}
        self.readers = {}
        self.sem_names = set()
        self.epoch = "0"
        self.pending = {}

    def barrier(self):
        snap = [(s, v) for s, v in self.count.items() if not s.startswith("o_")]
        for e in ENGINES:
            self.pending.setdefault(e, []).extend(snap)

    def op(self, eng, fn, reads=(), writes=(), inc=True, sem=None, amount=1):
        if sem is None:
            sem = "s_%s_%s" % (eng, self.epoch)
        self.sem_names.add(sem)
        deps = list(self.pending.pop(eng, ()))
        for k in reads:
            ev = self.last_write.get(k)
            if ev is not None:
                deps.append(ev)
        for k in writes:
            ev = self.last_write.get(k)
            if ev is not None:
                deps.append(ev)
            deps.extend(self.readers.get(k, ()))
        waits = {}
        for (s, v) in deps:
            if eng == "pe" and s.startswith("s_pe_"):
                continue
            if self.waited[eng].get(s, 0) >= v:
                continue
            if waits.get(s, 0) < v:
                waits[s] = v
        for s, v in waits.items():
            self.waited[eng][s] = v
        cur = self.count.get(sem, 0)
        val = cur + amount
        if inc:
            self.count[sem] = val
        ev = (sem, val)
        self.streams[eng].append((fn, list(waits.items()), (sem, amount) if inc else None))
        for k in reads:
            self.readers.setdefault(k, []).append(ev)
        for k in writes:
            self.last_write[k] = ev
            self.readers[k] = []
        return ev

    def dma(self, queue, out, in_, sem, reads=(), writes=(), **kw):
        def fn(e):
            return e.dma_start(out=out, in_=in_, **kw)
        return self.op(queue, fn, reads=reads, writes=writes, inc=True, sem=sem, amount=16)

    def emit(self, final_waits=()):
        nc = self.nc
        names = sorted(self.sem_names)
        with contextlib.ExitStack() as st:
            sems = {n: st.enter_context(nc.semaphore(n)) for n in names}
            block = st.enter_context(nc.Block())
            streams = self.streams

            def run(engh, lst, last):
                for fn, waits, inc in lst:
                    for s, v in waits:
                        engh.wait_ge(sems[s], v)
                    ins = fn(engh)
                    if inc is not None:
                        ins.then_inc(sems[inc[0]], inc[1])
                if last:
                    for s, v in final_waits:
                        engh.wait_ge(sems[s], v)

            @block.tensor
            def _(e):
                run(e, streams["pe"], False)

            @block.scalar
            def _(e):
                run(e, streams["act"], False)

            @block.vector
            def _(e):
                run(e, streams["dve"], False)

            @block.gpsimd
            def _(e):
                run(e, streams["pool"], False)

            @block.sync
            def _(e):
                run(e, streams["sp"], True)


class Arena:
    def __init__(self, tensor, words):
        self.t = tensor
        self.words = words
        self.pos = 0
        self.peak = 0

    def take(self, shape, dt=F32):
        n = 1
        for s in shape[1:]:
            n *= s
        esz = 4 if dt == F32 else 2
        w = (n * esz + 3) // 4
        w = (w + 7) // 8 * 8
        assert self.pos + w <= self.words, "arena overflow: need %d have %d" % (self.pos + w, self.words)
        v = self.t[0:shape[0], self.pos:self.pos + w]
        self.pos += w
        self.peak = max(self.peak, self.pos)
        if dt != F32:
            v = v.bitcast(dt)
        v = v[:, 0:n]
        if len(shape) > 2:
            names = " ".join("d%d" % i for i in range(len(shape) - 1))
            kw = {"d%d" % i: shape[i + 1] for i in range(len(shape) - 2)}
            v = v.rearrange("p (%s) -> p %s" % (names, names), **kw)
        return v


class StopBuild(Exception):
    pass


class Builder:
    def chk_stop(self, name, reads=()):
        if self.stop_after == name:
            self.p.barrier()
            d = self.dout("dbg_stop", [128, 4])
            t = self.R0.take([128, 4])
            self.MS("dve", t, 1.0, "stoptile")
            self.out_dma(d, t, ["stoptile"])
            raise StopBuild()

    def __init__(self, debug=(), stop_after=None):
        self.debug = set(debug)
        self.stop_after = stop_after
        self.nc = bass.Bass("TRN2", target_bir_lowering=False)
        self.p = Prog(self.nc)
        self.st = contextlib.ExitStack()
        self.out_sems = {}
        self.nout = 0
        self.free_banks = list(range(8))
        self.nbank = 0
        self.ntmp = 0

    def din(self, name, shape):
        return self.nc.dram_tensor(name, list(shape), F32, kind="ExternalInput").ap()

    def dout(self, name, shape, dt=F32):
        return self.nc.dram_tensor(name, list(shape), dt, kind="ExternalOutput").ap()

    def sb(self, name, shape, dt=F32):
        t = self.st.enter_context(self.nc.sbuf_tensor(name, list(shape), dt))
        return t[:]

    def out_dma(self, out, in_, reads, queue="sp", **kw):
        sem = "o_%d" % (self.nout % 8)
        self.nout += 1
        self.p.dma(queue, out, in_, sem, reads=reads, **kw)
        self.out_sems[sem] = self.p.count[sem]

    def tap(self, name, ap, reads):
        if name not in self.debug:
            return
        d = self.dout("dbg_" + name, list(ap.shape), ap.dtype)
        self.out_dma(d, ap, reads)

    def bank(self):
        b = self.free_banks[self.nbank % len(self.free_banks)]
        self.nbank += 1
        return b

    def bk(self, b):
        return self.ps[b // 2][:, b % 2, :]

    def TT(self, eng, out, a, b_, op, r, w):
        self.p.op(eng, lambda e: e.tensor_tensor(out=out, in0=a, in1=b_, op=op), reads=r, writes=[w])

    def TS(self, eng, out, a, s1, op0, r, w, s2=None, op1=None):
        if op1 is None:
            self.p.op(eng, lambda e: e.tensor_scalar(out=out, in0=a, scalar1=s1, scalar2=None, op0=op0), reads=r, writes=[w])
        else:
            self.p.op(eng, lambda e: e.tensor_scalar(out=out, in0=a, scalar1=s1, scalar2=s2, op0=op0, op1=op1), reads=r, writes=[w])

    def STT(self, eng, out, a, sc, b_, op0, op1, r, w):
        self.p.op(eng, lambda e: e.scalar_tensor_tensor(out=out, in0=a, scalar=sc, in1=b_, op0=op0, op1=op1), reads=r, writes=[w])

    def ACT(self, out, a, func, r, w, scale=1.0, bias=0.0):
        self.p.op("act", lambda e: e.activation(out=out, in_=a, func=func, scale=scale, bias=bias), reads=r, writes=[w])

    def CP(self, eng, out, a, r, w):
        if eng == "act":
            self.p.op("act", lambda e: e.copy(out=out, in_=a), reads=r, writes=[w])
        else:
            self.p.op(eng, lambda e: e.tensor_copy(out=out, in_=a), reads=r, writes=[w])

    def MS(self, eng, out, val, w, r=()):
        self.p.op(eng, lambda e: e.memset(out, val), reads=list(r), writes=[w])

    def build(self):
        nc, p = self.nc, self.p
        din = self.din
        I = {}
        for name, shape in (("x", [T, D]), ("st_lru_conv", [NSQ * 3, D]), ("st_lru_h", [NSQ, D]), ("st_s5_re", [NSQ, 2048]),
                            ("st_s5_im", [NSQ, 2048]), ("st_ffn_conv", [NSQ * 2, DFF]), ("w_in", [D, 3584]),
                            ("lru_conv_w", [4, D]), ("lru_conv_b", [D]), ("lru_wa", [16, 64, 64]), ("lru_ba", [D]),
                            ("lru_wx", [16, 64, 64]), ("lru_bx", [D]), ("lru_lambda", [D]), ("s5_a_re", [32, 64]),
                            ("s5_a_im", [32, 64]), ("s5_log_dt", [32]), ("s5_b_re", [32, 64, 16]), ("s5_b_im", [32, 64, 16]),
                            ("s5_c_re", [32, 16, 64]), ("s5_c_im", [32, 16, 64]), ("s5_d", [512]), ("w_glu", [512, 2048]),
                            ("w_out", [D, D]), ("ln1_g", [D]), ("ln1_b", [D]), ("w_up", [D, 2 * DFF]), ("ffn_conv_w", [3, DFF]),
                            ("ffn_conv_b", [DFF]), ("w_down", [DFF, D]), ("ln2_g", [D]), ("ln2_b", [D])):
            I[name] = din(name, shape)
        self.I = I
        O = {}
        O["y"] = self.dout("y", [T, D])
        O["lru_conv"] = self.dout("o_lru_conv", [NTAIL, D])
        O["lru_h"] = self.dout("o_lru_h", [128, 8, 17])
        O["s5"] = self.dout("o_s5", [128, 2, 16, 17])
        O["ffn_conv"] = self.dout("o_ffn_conv", [NTAIL, DFF])
        self.O = O
        self.x1_scr = nc.dram_tensor("x1_scr", [T, D], F32, kind="Internal").ap()

        self.ps = [self.st.enter_context(nc.psum_tensor("ps%d" % i, [128, 2, 512], F32)) for i in range(4)]
        R0W = 17664
        R1W = 35456
        self.R0 = Arena(self.st.enter_context(nc.sbuf_tensor("R0", [128, R0W], F32)), R0W)
        self.R1 = Arena(self.st.enter_context(nc.sbuf_tensor("R1", [128, R1W], F32)), R1W)
        R0, R1 = self.R0, self.R1

        ident_f = R0.take([128, 128]); ident_b = R0.take([128, 128], BF16)
        self.ident_f, self.ident_b = ident_f, ident_b
        self.MS("pool", ident_f, 0.0, "ident_f")
        p.op("pool", lambda e: e.affine_select(out=ident_f, in_=ident_f, pattern=[[-1, 128]],
                                               compare_op=ALU.not_equal, fill=1.0, base=0, channel_multiplier=1),
             reads=["ident_f"], writes=["ident_f"])
        self.CP("pool", ident_b, ident_f, ["ident_f"], "ident_b")

        self.PIECES = [(0, 512), (512, 512), (1024, 512), (1536, 512), (2048, 64)]
        xT = R0.take([128, 8, T], BF16)
        self.xT = xT
        zblk = R0.take([128, 4224])
        self.z2 = zblk.bitcast(BF16)[:, 0:4 * T].rearrange("p (a t) -> p a t", a=4)
        self.lnc = zblk[:, 0:4096].rearrange("p (a n) -> p a n", a=4)
        self.Hfin = R0.take([128, 2, 16, 17])
        self.LRUc = R0.take([128, 8, 72])
        self.Fc = R0.take([128, DFF // 128, 36])
        mark0 = R1.pos
        u_bf = R1.take([128, 4, T], BF16)
        self.W1 = R1.take([128, 4, LCH, 2, 128], BF16)
        self.W4a = R1.take([128, 16, LCH, 2, 32], BF16)
        self.W4b = R1.take([128, 16, LCH, 2, 32], BF16)
        self.h0S = R1.take([128, 2, 16, 16]); self.h0S_bf = R1.take([128, 2, 16, 16], BF16)
        self.RS = Arena(R1.take([128, 2304]), 2304)
        mark1 = R1.pos
        self.free_banks = [4, 5, 6, 7]
        self.param_loads()
        xb = [R1.take([128, D], BF16) for i in range(2)]
        ntt = (T + 127) // 128
        for tt in range(ntt):
            r0 = tt * 128
            rows = min(128, T - r0)
            slot = tt % 2
            p.dma("pool", xb[slot][0:rows, :], I["x"][r0:r0 + rows, :], "d_xb%d" % slot, writes=[("xb", slot)])
            b = self.bank()
            pt = self.bk(b).bitcast(BF16)
            for k in range(8):
                p.op("pe", lambda e, k=k, pt=pt, slot=slot, rows=rows: e.transpose(
                    pt[:, k * 128:k * 128 + rows], xb[slot][0:rows, k * 128:(k + 1) * 128], ident_b[0:rows, 0:rows]),
                    reads=[("xb", slot), "ident_b"], writes=[("bank", b)], inc=(k == 7))
            src = pt.rearrange("p (k c) -> p k c", c=128)[:, :, 0:rows]
            self.CP("act" if tt % 2 == 0 else "dve", xT[:, :, r0:r0 + rows], src, [("bank", b)], ("xT", tt))
            if tt == 3:
                self.param_transposes()
        self.tap("xT", xT[:, 0, :], [("xT", tt) for tt in range(ntt)])
        if self.stop_after == "p0":
            return self.finish()
        try:
            self.s5_prep()
        except StopBuild:
            return self.finish()
        self.tap("W1", self.W1[:, 0, :, :, :], self.W1_keys)
        self.tap("W4a", self.W4a[:, 0, :, :, :], self.W4_keys)
        self.tap("W4b", self.W4b[:, 0, :, :, :], self.W4_keys)
        self.tap("h0S", self.h0S, self.h0S_keys)
        self.tap("mur", self.mur, self.mu_keys)
        if self.stop_after == "prep":
            return self.finish()
        p.barrier()
        R1.pos = mark1
        wslot_u = [R1.take([128, 8, 128], BF16) for i in range(2)]
        for qh in range(4):
            slot = qh % 2
            p.dma("pool", wslot_u[slot], I["w_in"][:, 1024 + 128 * qh:1024 + 128 * (qh + 1)].rearrange("(k p) n -> p k n", p=128),
                  "d_wu%d" % slot, writes=[("wslot_u", slot)])
            for (c0, w) in self.PIECES:
                b = self.bank()
                for k in range(8):
                    p.op("pe", lambda e, k=k, b=b, slot=slot, c0=c0, w=w: e.matmul(
                        self.bk(b)[:, 0:w], wslot_u[slot][:, k, :], xT[:, k, c0:c0 + w], start=(k == 0), stop=(k == 7)),
                        reads=[("wslot_u", slot)] + self.xT_keys(c0, w), writes=[("bank", b)], inc=(k == 7))
                self.CP("act", u_bf[:, qh, c0:c0 + w], self.bk(b)[:, 0:w], [("bank", b)], ("u_bf", qh, c0))
        self.tap("u_bf", u_bf[:, 0, :], [("u_bf", 0, c0) for c0, _ in self.PIECES])
        if self.stop_after == "u":
            return self.finish()
        self.s5_main(u_bf)
        self.tap("z2", self.z2[:, 0, :], [("z2", 0, c0) for c0, _ in self.PIECES])
        self.tap("Hfin", self.Hfin, [("Hfin", q) for q in range(16)] + [("Hfin0", q) for q in range(16)])
        hk = [("Hfin", q) for q in range(16)] + [("Hfin0", q) for q in range(16)]
        self.out_dma(O["s5"], self.Hfin, hk)
        if self.stop_after == "s5":
            return self.finish()
        p.barrier()
        R1.pos = mark0
        p.epoch = "1"
        self.lru_stage()
        if self.stop_after == "lru":
            return self.finish()
        p.barrier()
        R1.pos = self.merged_end
        p.epoch = "2"
        self.mix_stage()
        if self.stop_after == "mix":
            return self.finish()
        p.barrier()
        R1.pos = 0
        p.epoch = "3"
        self.ffn_stage()
        return self.finish()

    def xT_keys(self, c0, w):
        return [("xT", tt) for tt in range(c0 // 128, (c0 + w - 1) // 128 + 1)]

    def param_loads(self):
        nc, p, I = self.nc, self.p, self.I
        R1 = self.R1
        MS = self.MS
        NCK = dict(allow_slow_non_contiguous=True)
        NJ = DFF // 128
        self.LF = R1.take([128, D + DFF])
        Lp = self.LF[:, 0:D]; Fp = self.LF[:, D:D + DFF]
        hs1 = R1.take([128, 2048])
        hsp = [hs1, hs1]
        self.Lp, self.Fp, self.hsp = Lp, Fp, hsp
        MS("pool", Lp, 0.0, "Lp"); MS("pool", Fp, 0.0, "Fp"); MS("pool", hs1, 0.0, "hsp")
        row = lambda nm: I[nm].rearrange("(o n) -> o n", o=1)
        for (r0, r1, src) in ((0, 48, I["st_lru_conv"]), (48, 64, I["st_lru_h"]), (64, 68, I["lru_conv_w"]), (68, 69, row("lru_conv_b")),
                              (69, 70, row("lru_ba")), (70, 71, row("lru_bx")), (71, 72, row("lru_lambda"))):
            p.dma("sp", Lp[r0:r1, :], src, "d_Lp", writes=["Lp"])
        for (r0, r1, src) in ((0, 32, I["st_ffn_conv"]), (32, 35, I["ffn_conv_w"]), (35, 36, row("ffn_conv_b"))):
            p.dma("sp", Fp[r0:r1, :], src, "d_Fp", writes=["Fp"])
        self.hs_loaded = False
        sh3 = [128, 16, 32]
        self.Bnr = R1.take(sh3); self.Bni = R1.take(sh3)
        self.Cnr = R1.take([128, 4, 128]); self.Cni = R1.take([128, 4, 128])
        for t_, k_ in ((self.Bnr, "Bnr"), (self.Bni, "Bni"), (self.Cnr, "Cnr"), (self.Cni, "Cni")):
            MS("pool", t_, 0.0, k_)
        for (dst, nm, key) in ((self.Bnr, "s5_b_re", "Bnr"), (self.Bni, "s5_b_im", "Bni")):
            v = I[nm].rearrange("(q two) p c -> two p q c", two=2)
            for two in range(2):
                p.dma("sp", dst[64 * two:64 * two + 64, :, 16 * two:16 * two + 16], v[two], "d_prep", writes=[key])
        for (dst, nm, key) in ((self.Cnr, "s5_c_re", "Cnr"), (self.Cni, "s5_c_im", "Cni")):
            v = I[nm].rearrange("(qh ql two) c p -> ql two c qh p", qh=4, ql=4, two=2)
            for ql in range(4):
                for two in range(2):
                    p0 = 32 * ql + 16 * two
                    p.dma("sp", dst[p0:p0 + 16, :, 64 * two:64 * two + 64], v[ql, two], "d_prep", writes=[key])
        shS = [128, 16]
        self.aSr = R1.take(shS); self.aSi = R1.take(shS); self.ldS = R1.take(shS)
        p.dma("sp", self.aSr, I["s5_a_re"].rearrange("(q two) p -> (two p) q", two=2), "d_prep", writes=["aSr"], **NCK)
        p.dma("sp", self.aSi, I["s5_a_im"].rearrange("(q two) p -> (two p) q", two=2), "d_prep", writes=["aSi"], **NCK)
        v = I["s5_log_dt"].rearrange("(q two) -> two q", two=2)
        for two in range(2):
            p.dma("sp", self.ldS[64 * two:64 * two + 64, :], v[two].partition_broadcast(64), "d_prep", writes=["ldS"], **NCK)
        self.dS5 = self.R0.take([128, 4])
        p.dma("sp", self.dS5, I["s5_d"].rearrange("(t p) -> p t", p=128), "d_prep", writes=["dS5"], **NCK)
        for sem, keys in (("d_Lp", ["Lp"]), ("d_Fp", ["Fp"]),
                          ("d_prep", ["Bnr", "Bni", "Cnr", "Cni", "aSr", "aSi", "ldS", "dS5"])):
            for k_ in keys:
                p.last_write[k_] = (sem, p.count[sem])

    def tr_group(self, srcs, evac):
        p = self.p
        for g0 in range(0, len(srcs), 4):
            grp = srcs[g0:g0 + 4]
            b = self.bank()
            bv = self.bk(b).rearrange("p (a c) -> p a c", a=4)
            for i, (ap, keys) in enumerate(grp):
                p.op("pe", lambda e, i=i, ap=ap, bv=bv: e.transpose(bv[:, i, :], ap, self.ident_f),
                     reads=list(keys) + ["ident_f"], writes=[("bank", b)], inc=(i == len(grp) - 1))
            evac(bv, g0, len(grp), b)

    def param_transposes(self):
        p = self.p
        CP = self.CP
        NJ = DFF // 128
        Lp, Fp, hsp = self.Lp, self.Fp, self.hsp

        def ev_L(bv, g0, n, b):
            CP("act", self.LRUc[:, g0:g0 + n, :], bv[:, 0:n, 0:72], [("bank", b)], ("LRUc", g0 // 4))
        self.tr_group([(Lp[:, 128 * j:128 * (j + 1)], ["Lp"]) for j in range(8)], ev_L)

        def ev_F(bv, g0, n, b):
            CP("dve", self.Fc[:, g0:g0 + n, :], bv[:, 0:n, 0:36], [("bank", b)], ("Fc", g0 // 4))
        self.tr_group([(Fp[:, 128 * j:128 * (j + 1)], ["Fp"]) for j in range(NJ)], ev_F)
        self.LRUc_keys = [("LRUc", i) for i in range(2)]
        self.Fc_keys = [("Fc", i) for i in range(NJ // 4)]
        for part in range(2):
            p.dma("sp", hsp[part][0:16, :], self.I[("st_s5_re", "st_s5_im")[part]], "d_hsp", writes=["hsp"])
            def ev_h(bv, g0, n, b, part=part):
                CP("act", self.h0S[:, part, g0:g0 + n, :], bv[:, 0:n, 0:16], [("bank", b)], ("h0S", part, g0 // 4))
            self.tr_group([(hsp[part][:, 128 * q:128 * (q + 1)], ["hsp"]) for q in range(16)], ev_h)
        self.h0S_keys = [("h0S", part, i) for part in range(2) for i in range(4)]
        CP("dve", self.h0S_bf, self.h0S, self.h0S_keys, "h0S_bf")
        sh3 = [128, 16, 32]
        self.CTr = self.R1.take(sh3); self.CTi = self.R1.take(sh3)
        for (src, dst, k_src, k_dst) in ((self.Cnr, self.CTr, "Cnr", "CTr"), (self.Cni, self.CTi, "Cni", "CTi")):
            def ev_c(bv, g0, n, b, dst=dst, k_dst=k_dst):
                CP("dve", dst[:, 4 * g0:4 * (g0 + n), :].rearrange("p (a q) c -> p a (q c)", a=n), bv[:, 0:n, :], [("bank", b)], (k_dst, g0))
            self.tr_group([(src[:, qh, :], [k_src]) for qh in range(4)], ev_c)
        self.CT_keys = {"CTr": [("CTr", 0)], "CTi": [("CTi", 0)]}

    def s5_prep(self):
        nc, p, I = self.nc, self.p, self.I
        R0, R1, RS = self.R0, self.R1, self.RS
        TT, TS, STT, ACT, CP, MS = self.TT, self.TS, self.STT, self.ACT, self.CP, self.MS
        W1, W4a, W4b = self.W1, self.W4a, self.W4b
        aSr, aSi, ldS = self.aSr, self.aSi, self.ldS
        CTr, CTi = self.CTr, self.CTi
        kCTr, kCTi = self.CT_keys["CTr"], self.CT_keys["CTi"]
        shS = [128, 16]
        sh3 = [128, 16, 32]
        I32 = mybir.dt.int32

        def sincos_base(theta, t, kf, A, C2, cs, sn, k_th, k_t, k_kf, k_A, k_C2, k_cs, k_sn):
            TS("dve", t, theta, 1.0 / (2 * PI), ALU.mult, [k_th], k_t, s2=16.0, op1=ALU.add)
            CP("dve", kf.bitcast(I32), t, [k_t], k_kf)
            CP("dve", kf, kf.bitcast(I32), [k_kf], k_kf)
            TT("dve", t, t, kf, ALU.subtract, [k_t, k_kf], k_t)
            ACT(A, t, AF.Sin, [k_t], k_A, scale=PI)
            ACT(C2, t, AF.Sin, [k_t], k_C2, scale=PI / 2)
            TT("dve", C2, C2, C2, ALU.mult, [k_C2], k_C2)
            TS("dve", C2, C2, -2.0, ALU.mult, [k_C2], k_C2, s2=1.0, op1=ALU.add)
            STT("dve", sn, A, 2.0, C2, ALU.mult, ALU.mult, [k_A, k_C2], k_sn)
            TT("dve", cs, A, A, ALU.mult, [k_A], k_cs)
            TS("dve", cs, cs, -2.0, ALU.mult, [k_cs], k_cs, s2=1.0, op1=ALU.add)

        def tS():
            return RS.take(shS)

        dtS = tS(); adtS = tS(); thS = tS()
        ACT(dtS, ldS, AF.Exp, ["ldS"], "dtS")
        TT("dve", adtS, aSr, dtS, ALU.mult, ["aSr", "dtS"], "adtS")
        TT("dve", thS, aSi, dtS, ALU.mult, ["aSi", "dtS"], "thS")
        self.rhoS = tS()
        ACT(self.rhoS, adtS, AF.Exp, ["adtS"], "rhoS")
        cosS = []; sinS = []; nsinS = []; lamr = [None]; lami = [None]
        c1S = tS(); s1S = tS(); w1 = tS(); w2 = tS(); w3 = tS(); w4 = tS()
        sincos_base(thS, w1, w2, w3, w4, c1S, s1S, "thS", "Sw1", "Sw2", "Sw3", "Sw4", "S1cs", "S1sn")
        for s in range(LCH + 1):
            if s == 0:
                cs = tS(); sn = tS()
                MS("dve", cs, 1.0, "S0cs"); MS("dve", sn, 0.0, "S0sn")
            elif s == 1:
                cs, sn = c1S, s1S
            else:
                cs = tS(); sn = tS(); ua = tS(); ub = tS()
                pc, ps_ = cosS[s - 1], sinS[s - 1]
                kpc, kps = "S%dcs" % (s - 1), "S%dsn" % (s - 1)
                TT("dve", ua, pc, c1S, ALU.mult, [kpc, "S1cs"], "Sua%d" % s)
                TT("dve", ub, ps_, s1S, ALU.mult, [kps, "S1sn"], "Sub%d" % s)
                TT("dve", cs, ua, ub, ALU.subtract, ["Sua%d" % s, "Sub%d" % s], "S%dcs" % s)
                TT("dve", ua, ps_, c1S, ALU.mult, [kps, "S1cs", "S%dcs" % s], "Sua%d" % s)
                TT("dve", ub, pc, s1S, ALU.mult, [kpc, "S1sn", "S%dcs" % s], "Sub%d" % s)
                TT("dve", sn, ua, ub, ALU.add, ["Sua%d" % s, "Sub%d" % s], "S%dsn" % s)
            cosS.append(cs); sinS.append(sn)
            ns = tS()
            TS("dve", ns, sn, -1.0, ALU.mult, ["S%dsn" % s], "nS%dsn" % s)
            nsinS.append(ns)
            if s >= 1:
                r = tS(); a = tS(); b_ = tS()
                ACT(r, adtS, AF.Exp, ["adtS"], "rpow%d" % s, scale=float(s))
                TT("dve", a, r, cs, ALU.mult, ["rpow%d" % s, "S%dcs" % s], "lamr%d" % s)
                TT("dve", b_, r, sn, ALU.mult, ["rpow%d" % s, "S%dsn" % s], "lami%d" % s)
                lamr.append(a); lami.append(b_)
        self.cosS, self.sinS, self.nsinS, self.lamr, self.lami = cosS, sinS, nsinS, lamr, lami
        self.nlami4 = tS()
        TS("dve", self.nlami4, lami[4], -1.0, ALU.mult, ["lami4"], "nlami4")
        ta = R1.take(sh3); tb = R1.take(sh3)

        def bc(t):
            return t.unsqueeze(2).to_broadcast(sh3)

        for s in range(LCH):
            for (Wt, cr_t, ci_t, kr, ki, nm) in (
                (W4a, cosS[s], sinS[s], "S%dcs" % s, "S%dsn" % s, "W4a"),
                (W4b, lamr[s + 1], lami[s + 1], "lamr%d" % (s + 1), "lami%d" % (s + 1), "W4b"),
            ):
                TT("dve", ta, CTr, bc(cr_t), ALU.mult, kCTr + [kr], "w4ta")
                TT("dve", tb, CTi, bc(ci_t), ALU.mult, kCTi + [ki], "w4tb")
                TT("dve", Wt[:, :, s, 0, :], ta, tb, ALU.subtract, ["w4ta", "w4tb"], (nm, s, 0))
                TT("dve", ta, CTr, bc(ci_t), ALU.mult, kCTr + [ki], "w4ta")
                TT("dve", tb, CTi, bc(cr_t), ALU.mult, kCTi + [kr], "w4tb")
                TT("dve", ta, ta, tb, ALU.add, ["w4ta", "w4tb"], "w4ta")
                TS("dve", Wt[:, :, s, 1, :], ta, -1.0, ALU.mult, ["w4ta"], (nm, s, 1))
        Bnr, Bni = self.Bnr, self.Bni
        nr = tS(); den = tS(); u1 = tS(); cfr = tS(); cfi = tS()
        TS("dve", nr, lamr[1], -1.0, ALU.add, ["lamr1"], "nr")
        TT("dve", den, aSr, aSr, ALU.mult, ["aSr"], "den")
        TT("dve", u1, aSi, aSi, ALU.mult, ["aSi"], "u1")
        TT("dve", den, den, u1, ALU.add, ["den", "u1"], "den")
        p.op("dve", lambda e: e.reciprocal(out=den, in_=den), reads=["den"], writes=["den"])
        TT("dve", cfr, nr, aSr, ALU.mult, ["nr", "aSr"], "cfr")
        TT("dve", u1, lami[1], aSi, ALU.mult, ["lami1", "aSi", "den"], "u1")
        TT("dve", cfr, cfr, u1, ALU.add, ["cfr", "u1"], "cfr")
        TT("dve", cfr, cfr, den, ALU.mult, ["cfr", "den"], "cfr")
        TT("dve", cfi, lami[1], aSr, ALU.mult, ["lami1", "aSr"], "cfi")
        TT("dve", u1, nr, aSi, ALU.mult, ["nr", "aSi", "cfr"], "u1")
        TT("dve", cfi, cfi, u1, ALU.subtract, ["cfi", "u1"], "cfi")
        TT("dve", cfi, cfi, den, ALU.mult, ["cfi", "den"], "cfi")
        BbR = R1.take(sh3); BbI = R1.take(sh3)
        TT("pool", ta, Bnr, bc(cfr), ALU.mult, ["Bnr", "cfr", "w4ta"], "w4ta")
        TT("pool", tb, Bni, bc(cfi), ALU.mult, ["Bni", "cfi", "w4tb"], "w4tb")
        TT("pool", BbR, ta, tb, ALU.subtract, ["w4ta", "w4tb"], "BbR")
        TT("pool", ta, Bni, bc(cfr), ALU.mult, ["Bni", "cfr"], "w4ta")
        TT("pool", tb, Bnr, bc(cfi), ALU.mult, ["Bnr", "cfi"], "w4tb")
        TT("pool", BbI, ta, tb, ALU.add, ["w4ta", "w4tb"], "BbI")
        W1S = self.LF.bitcast(BF16).rearrange("p (a s b c) -> p a s b c", a=4, s=LCH, b=2)
        CP("pool", W1S[:, 0, 0, 0, 0:2], W1S[:, 0, 0, 0, 0:2], ["Lp", "Fp"], "LFfree")
        p.last_write["Lp"] = p.last_write["LFfree"]; p.last_write["Fp"] = p.last_write["LFfree"]
        tc_ = R1.take(sh3); td_ = R1.take(sh3)
        v4 = lambda t: t.rearrange("p (a b) c -> p a (b c)", a=4)
        for s in range(LCH):
            kc, ks = "S%dcs" % s, "S%dsn" % s
            TT("pool", tc_, BbR, bc(cosS[s]), ALU.mult, ["BbR", kc], "w1tc")
            TT("pool", td_, BbI, bc(sinS[s]), ALU.mult, ["BbI", ks], "w1td")
            TT("pool", W1S[:, :, s, 0, :], v4(tc_), v4(td_), ALU.add, ["w1tc", "w1td", "LFfree"], ("W1S", s, 0))
            TT("pool", tc_, BbI, bc(cosS[s]), ALU.mult, ["BbI", kc], "w1tc")
            TT("pool", td_, BbR, bc(sinS[s]), ALU.mult, ["BbR", ks], "w1td")
            TT("pool", W1S[:, :, s, 1, :], v4(tc_), v4(td_), ALU.subtract, ["w1tc", "w1td", "LFfree"], ("W1S", s, 1))
        for qh in range(4):
            for h in range(2):
                b = self.bank()
                pt = self.bk(b).bitcast(BF16)
                n = 0
                for s in range(4 * h, 4 * h + 4):
                    for part in range(2):
                        p.op("pe", lambda e, pt=pt, n=n, qh=qh, s=s, part=part: e.transpose(
                            pt[:, 128 * n:128 * (n + 1)], W1S[:, qh, s, part, :], self.ident_b),
                            reads=[("W1S", s, part), "ident_b"], writes=[("bank", b)], inc=(n == 7))
                        n += 1
                CP("act", W1[:, qh, 4 * h:4 * h + 4, :, :].rearrange("p a b c -> p (a b c)"), pt, [("bank", b)], ("W1", qh, h))
        self.W1_keys = [("W1", qh, h) for qh in range(4) for h in range(2)]
        self.W4_keys = [(nm, s, part) for nm in ("W4a", "W4b") for s in range(LCH) for part in range(2)]
        self.mur = RS.take([128, 16, NLEV]); self.mui = RS.take([128, 16, NLEV]); self.muni = RS.take([128, 16, NLEV])
        angc = RS.take([128, 16, NLEV]); angs = RS.take([128, 16, NLEV]); rmag = RS.take([128, 16, NLEV])
        CP("dve", angc[:, :, 0], cosS[LCH], ["S%dcs" % LCH], ("angc", 0))
        CP("dve", angs[:, :, 0], sinS[LCH], ["S%dsn" % LCH], ("angs", 0))
        sa = tS(); sb_ = tS()
        for j in range(NLEV):
            if j >= 1:
                TT("dve", sa, angc[:, :, j - 1], angc[:, :, j - 1], ALU.mult, [("angc", j - 1)], "sq_a")
                TT("dve", sb_, angs[:, :, j - 1], angs[:, :, j - 1], ALU.mult, [("angs", j - 1)], "sq_b")
                TT("dve", angc[:, :, j], sa, sb_, ALU.subtract, ["sq_a", "sq_b"], ("angc", j))
                TT("dve", sa, angc[:, :, j - 1], angs[:, :, j - 1], ALU.mult, [("angc", j - 1), ("angs", j - 1)], "sq_a")
                TS("dve", angs[:, :, j], sa, 2.0, ALU.mult, ["sq_a"], ("angs", j))
            ACT(rmag[:, :, j], adtS, AF.Exp, ["adtS"], ("rmag", j), scale=float(LCH * (1 << j)))
            TT("dve", self.mur[:, :, j], rmag[:, :, j], angc[:, :, j], ALU.mult, [("rmag", j), ("angc", j)], ("mur", j))
            TT("dve", self.mui[:, :, j], rmag[:, :, j], angs[:, :, j], ALU.mult, [("rmag", j), ("angs", j)], ("mui", j))
        for j in range(NLEV):
            TS("dve", self.muni[:, :, j], self.mui[:, :, j], -1.0, ALU.mult, [("mui", j)], ("muni", j))
        self.mu_keys = [(nm, j) for nm in ("mur", "mui", "muni") for j in range(NLEV)]
        self.rp8 = RS.take([128, 16, LCH]); self.rp4 = RS.take([128, 16, 4])
        CP("dve", self.rp8, self.rhoS.unsqueeze(2).to_broadcast([128, 16, LCH]), ["rhoS"], "rp8")
        MS("dve", self.rp8[:, :, 0:1], 0.0, "rp8", r=["rp8"])
        CP("dve", self.rp4, self.rhoS.unsqueeze(2).to_broadcast([128, 16, 4]), ["rhoS"], "rp4")
        MS("dve", self.rp4[:, :, 0:1], 0.0, "rp4", r=["rp4"])
    def s5_main(self, u_bf):
        nc, p = self.nc, self.p
        R0, R1 = self.R0, self.R1
        TT, TS, STT, ACT, CP, MS = self.TT, self.TS, self.STT, self.ACT, self.CP, self.MS
        W1, W4a, W4b = self.W1, self.W4a, self.W4b
        cosS, sinS, nsinS, lamr, lami = self.cosS, self.sinS, self.nsinS, self.lamr, self.lami
        z2 = self.z2
        NSET = 2
        pat = [R1.take([128, 2, 512]) for i in range(NSET)]
        pats = [R1.take([128, 2, 64]) for i in range(NSET)]
        gzf = [R1.take([128, 2, 512]) for i in range(2)]
        gzfs = [R1.take([128, 2, 64]) for i in range(2)]
        gzb = [R1.take([128, 2, T], BF16) for i in range(NSET)]
        HA = [R1.take([128, 2, PAD + NCH]) for i in range(NSET)]
        HB = [R1.take([128, 2, PAD + NCH]) for i in range(NSET)]
        Hpb = [R1.take([128, 2, NCH], BF16) for i in range(NSET)]
        for i in range(NSET):
            MS("pool", HA[i], 0.0, ("HA", i))
            MS("pool", HB[i], 0.0, ("HB", i))
        yf = R1.take([128, 512]); gq = R1.take([128, 512]); ga = R1.take([128, 512]); gt = gq
        VP = self.ps[0]
        SVB = 2
        YB = {0: 4, 512: 5, 1024: 6, 1536: 7}
        PIECES = self.PIECES
        nseg = [0]

        def stageA(q):
            qh, ql = q // 4, q % 4
            hb = q % NSET
            kw = dict(tile_position=(96, 0)) if ql == 3 else {}
            CP("pool", pat[hb].rearrange("p a (c s) -> p (a c) s", s=LCH),
               self.rp8[:, q:q + 1, :].to_broadcast([128, 2 * 512 // LCH, LCH]), ["rp8"], ("pat", hb))
            CP("pool", pats[hb].rearrange("p a (c s) -> p (a c) s", s=4),
               self.rp4[:, q:q + 1, :].to_broadcast([128, 2 * 64 // 4, 4]), ["rp4"], ("pats", hb))
            for (c0, w) in PIECES:
                samp = (w == 64)
                L = 4 if samp else LCH
                sl = nseg[0] % 2
                nseg[0] += 1
                if samp:
                    vre = self.bk(SVB)[:, 0:64]; vim = self.bk(SVB)[:, 64:128]
                    vkeys = [("bank", SVB)]
                else:
                    vre = VP[:, 0, :]; vim = VP[:, 1, :]
                    vkeys = [("bank", 0), ("bank", 1)]
                nmm = 0
                for s in range(L):
                    for part in range(2):
                        nmm += 1
                        outv = (vre, vim)[part][:, s:w:L]
                        p.op("pe", lambda e, outv=outv, s=s, part=part, qh=qh, ql=ql, c0=c0, w=w, L=L, kw=kw: e.matmul(
                            outv, W1[32 * ql:32 * ql + 32, qh, s, part, :],
                            u_bf[32 * ql:32 * ql + 32, qh, c0 + s:c0 + w:L], start=True, stop=True, **kw),
                            reads=self.W1_keys + [("u_bf", qh, c0)], writes=vkeys, inc=(nmm == 2 * L))
                if not samp:
                    go = gzf[sl]; gkey = ("gzf", sl)
                    p.op("dve", lambda e, go=go, hb=hb: e.tensor_tensor_scan(
                        out=go.rearrange("p a c -> p (a c)"), data0=pat[hb].rearrange("p a c -> p (a c)"),
                        data1=VP[:].rearrange("p a c -> p (a c)"), initial=0.0, op0=ALU.mult, op1=ALU.add),
                        reads=[("pat", hb)] + vkeys, writes=[gkey])
                else:
                    go = gzfs[sl]; gkey = ("gzfs", sl)
                    p.op("dve", lambda e, go=go, hb=hb: e.tensor_tensor_scan(
                        out=go.rearrange("p a c -> p (a c)"), data0=pats[hb].rearrange("p a c -> p (a c)"),
                        data1=self.bk(SVB)[:, 0:128], initial=0.0, op0=ALU.mult, op1=ALU.add),
                        reads=[("pats", hb)] + vkeys, writes=[gkey])
                CP("act", gzb[hb][:, :, c0:c0 + w], go, [gkey], ("gzb", hb, c0))
                if not samp:
                    k0 = PAD + c0 // LCH
                    nchunk = w // LCH
                    er = go[:, 0, LCH - 1:512:LCH]; ei = go[:, 1, LCH - 1:512:LCH]
                    c7 = cosS[LCH - 1][:, q:q + 1]; s7 = sinS[LCH - 1][:, q:q + 1]; ns7 = nsinS[LCH - 1][:, q:q + 1]
                    kk = [gkey, "S%dcs" % (LCH - 1), "S%dsn" % (LCH - 1), "nS%dsn" % (LCH - 1)]
                    hk = ("HA", hb)
                    dr = HA[hb][:, 0, k0:k0 + nchunk]; di = HA[hb][:, 1, k0:k0 + nchunk]
                    TS("dve", dr, er, c7, ALU.mult, kk, hk)
                    STT("dve", dr, ei, ns7, dr, ALU.mult, ALU.add, kk + [hk], hk)
                    TS("dve", di, er, s7, ALU.mult, kk + [hk], hk)
                    STT("dve", di, ei, c7, di, ALU.mult, ALU.add, kk + [hk], hk)
                else:
                    er = go[:, 0, 3:64:4]; ei = go[:, 1, 3:64:4]
                    c3 = cosS[3][:, q:q + 1]; s3 = sinS[3][:, q:q + 1]; ns3 = nsinS[3][:, q:q + 1]
                    l4r = lamr[4][:, q:q + 1]; l4i = lami[4][:, q:q + 1]; nl4i = self.nlami4[:, q:q + 1]
                    kk = [gkey, "S3cs", "S3sn", "nS3sn", "lamr4", "lami4", "nlami4"] + self.h0S_keys
                    fr = self.Hfin[:, 0, q, 1:17]; fi = self.Hfin[:, 1, q, 1:17]
                    h0r = self.h0S[:, 0, q, :]; h0i = self.h0S[:, 1, q, :]
                    fk = ("Hfin", q)
                    TS("dve", fr, er, c3, ALU.mult, kk, fk)
                    STT("dve", fr, ei, ns3, fr, ALU.mult, ALU.add, kk + [fk], fk)
                    STT("dve", fr, h0r, l4r, fr, ALU.mult, ALU.add, kk + [fk], fk)
                    STT("dve", fr, h0i, nl4i, fr, ALU.mult, ALU.add, kk + [fk], fk)
                    TS("dve", fi, er, s3, ALU.mult, kk + [fk], fk)
                    STT("dve", fi, ei, c3, fi, ALU.mult, ALU.add, kk + [fk], fk)
                    STT("dve", fi, h0r, l4i, fi, ALU.mult, ALU.add, kk + [fk], fk)
                    STT("dve", fi, h0i, l4r, fi, ALU.mult, ALU.add, kk + [fk], fk)

        def stageB(q):
            hb = q % NSET
            src, dst = HA[hb], HB[hb]
            skey, dkey = ("HA", hb), ("HB", hb)
            for j in range(NLEV):
                d = 1 << j
                mr = self.mur[:, q, j:j + 1]; mi = self.mui[:, q, j:j + 1]; mni = self.muni[:, q, j:j + 1]
                mk = [("mur", j), ("mui", j), ("muni", j)]
                S0 = src[:, 0, PAD:PAD + NCH]; S1 = src[:, 1, PAD:PAD + NCH]
                Z0 = src[:, 0, PAD - d:PAD + NCH - d]; Z1 = src[:, 1, PAD - d:PAD + NCH - d]
                D0 = dst[:, 0, PAD:PAD + NCH]; D1 = dst[:, 1, PAD:PAD + NCH]
                STT("dve", D0, Z0, mr, S0, ALU.mult, ALU.add, [skey] + mk, dkey)
                STT("dve", D0, Z1, mni, D0, ALU.mult, ALU.add, [skey, dkey] + mk, dkey)
                STT("dve", D1, Z0, mi, S1, ALU.mult, ALU.add, [skey, dkey] + mk, dkey)
                STT("dve", D1, Z1, mr, D1, ALU.mult, ALU.add, [skey, dkey] + mk, dkey)
                src, dst = dst, src
                skey, dkey = dkey, skey
            CP("pool", Hpb[hb], HA[hb][:, :, PAD - 1:PAD - 1 + NCH], [("HA", hb)], ("Hpb", hb))
            CP("pool", self.Hfin[:, :, q, 0:1], HA[hb][:, :, PAD + NCH - 1:PAD + NCH], [("HA", hb)], ("Hfin0", q))

        def stageC(q):
            qh, ql = q // 4, q % 4
            hb = q % NSET
            kw = dict(tile_position=(0, 96)) if ql == 3 else {}
            for (c0, w) in PIECES:
                samp = (w == 64)
                L = 4 if samp else LCH
                if samp:
                    ybank = SVB
                    yall = self.bk(SVB)[:, 128:192]
                else:
                    ybank = YB[c0]
                    yall = self.bk(ybank)
                for s in range(L):
                    outv = yall[32 * ql:32 * ql + 32, s:w:L]
                    if samp:
                        hr = self.h0S_bf[:, 0, q, :]; hi = self.h0S_bf[:, 1, q, :]
                        hkeys = ["h0S_bf"]
                    else:
                        k0 = c0 // LCH
                        hr = Hpb[hb][:, 0, k0:k0 + w // L]; hi = Hpb[hb][:, 1, k0:k0 + w // L]
                        hkeys = [("Hpb", hb)]
                    ops = [
                        (W4a[:, q, s, 0, :], gzb[hb][:, 0, c0 + s:c0 + w:L]),
                        (W4a[:, q, s, 1, :], gzb[hb][:, 1, c0 + s:c0 + w:L]),
                        (W4b[:, q, s, 0, :], hr),
                        (W4b[:, q, s, 1, :], hi),
                    ]
                    for i, (lh, rh) in enumerate(ops):
                        last = (i == 3 and s == L - 1)
                        p.op("pe", lambda e, outv=outv, lh=lh, rh=rh, i=i, kw=kw: e.matmul(
                            outv, lh, rh, start=(i == 0), stop=(i == 3), **kw),
                            reads=self.W4_keys + [("gzb", hb, c0)] + hkeys, writes=[("bank", ybank)], inc=last)

        def stageY(qh):
            for (c0, w) in PIECES:
                samp = (w == 64)
                if samp:
                    ybank = SVB; ysrc = self.bk(SVB)[:, 128:192]
                else:
                    ybank = YB[c0]; ysrc = self.bk(ybank)
                yv = yf[:, 0:w]; qv = gq[:, 0:w]; av = ga[:, 0:w]; tv = gt[:, 0:w]
                STT("dve", yv, u_bf[:, qh, c0:c0 + w], self.dS5[:, qh:qh + 1], ysrc, ALU.mult, ALU.add,
                    [("u_bf", qh, c0), "dS5", ("bank", ybank)], "s5yf")
                if qh == 0:
                    self.tap("ys5_%d" % c0, yv, ["s5yf"])
                self.gelu2(yv, qv, av, tv, z2[:, qh, c0:c0 + w], "s5yf", "s5gq", "s5ga", "s5gq", ("z2", qh, c0))

        for qh in range(4):
            qs = [4 * qh + i for i in range(4)]
            stageA(qs[0]); stageB(qs[0])
            for i in range(1, 4):
                stageA(qs[i])
                stageC(qs[i - 1])
                stageB(qs[i])
            stageC(qs[3])
            stageY(qh)

    def gelu2(self, yv, qv, av, tv, outv, ky, kq, ka, kt, kout):
        self.ACT(qv, yv, AF.Square, [ky], kq)
        self.STT("dve", av, qv, GK * GC, yv, ALU.mult, ALU.mult, [kq, ky], ka)
        self.STT("dve", av, yv, GK, av, ALU.mult, ALU.add, [ky, ka], ka)
        self.ACT(tv, av, AF.Tanh, [ka], kt)
        self.STT("dve", outv, tv, 1.0, yv, ALU.add, ALU.mult, [kt, ky], kout)

    def lru_stage(self):
        nc, p, I, O = self.nc, self.p, self.I, self.O
        R0, R1 = self.R0, self.R1
        TT, TS, STT, ACT, CP, MS = self.TT, self.TS, self.STT, self.ACT, self.CP, self.MS
        NCK = dict(allow_slow_non_contiguous=True)
        xT, z2 = self.xT, self.z2
        PIECES = self.PIECES
        self.free_banks = list(range(8))
        LRUc = self.LRUc
        LK = self.LRUc_keys
        baT = R0.take([128, 8]); bxT = R0.take([128, 8]); sc8 = R0.take([128, 8]); hsc8 = R0.take([128, 8])
        TS("dve", baT, LRUc[:, :, 69], 0.5, ALU.mult, LK, "baT")
        TS("dve", bxT, LRUc[:, :, 70], 0.5, ALU.mult, LK, "bxT")
        ACT(sc8, LRUc[:, :, 71], AF.Exp, LK, "sc8", scale=-1.0)
        ACT(sc8, sc8, AF.Ln, ["sc8"], "sc8", bias=1.0)
        TS("dve", hsc8, sc8, -4.0, ALU.mult, ["sc8"], "hsc8")
        TS("dve", sc8, sc8, -8.0, ALU.mult, ["sc8", "hsc8"], "sc8")
        Wg = R0.take([128, 8, 2, 128], BF16)
        MS("pool", Wg, 0.0, "Wg")
        for gi, nm in ((0, "lru_wa"), (1, "lru_wx")):
            v = I[nm].rearrange("(j two) i o -> two i j o", two=2)
            for par in range(2):
                p.dma("pool", Wg[64 * par:64 * par + 64, :, gi, 64 * par:64 * par + 64], v[par], "d_wg", writes=["Wg"])
        p.last_write["Wg"] = ("d_wg", p.count["d_wg"])
        hfinL = R0.take([128, 8, 17])
        self.merged2 = R1.take([128, 8, T], BF16)
        merged2 = self.merged2
        self.merged_end = R1.pos
        xl_sb = R1.take([128, 3 + TP]); xs_sb = R1.take([128, 16, 7])
        abuf = R1.take([128, T]); a2buf = R1.take([128, T]); ixbuf = R1.take([128, T])
        hbuf = [R1.take([128, T]) for i in range(2)]
        tail_sb = R1.take([NTAIL, D])
        MS("pool", xl_sb[:, 0:3], 0.0, ("xl", -1))
        NP = 2
        ytmp = [R1.take([128, 512]) for i in range(NP)]
        xc = [R1.take([128, 512]) for i in range(NP)]
        xcb = [R1.take([128, 512], BF16) for i in range(NP)]
        rp_ = [R1.take([128, 512]) for i in range(NP)]
        ip_ = [R1.take([128, 512]) for i in range(NP)]
        glp = [R1.take([128, 512]) for i in range(NP)]
        gsp = [R1.take([128, 512]) for i in range(NP)]
        gbp = [R1.take([128, 512]) for i in range(NP)]
        t1p = [R1.take([128, 512]) for i in range(NP)]
        t16 = R1.take([128, 16])
        wsl = [R1.take([128, 8, 128], BF16) for i in range(2)]
        wsg = [R1.take([128, 8, 2, 128], BF16) for i in range(2)]
        wgl = [R1.take([128, 4, 2, 128], BF16) for i in range(2)]
        npc = [0]

        def load_w(j):
            sl = j % 2
            wv = lambda c: I["w_in"][:, c:c + 128].rearrange("(k p) n -> p k n", p=128)
            gv = lambda c: I["w_glu"][:, c:c + 128].rearrange("(k p) n -> p k n", p=128)
            p.dma("pool", wsl[sl], wv(128 * j), "d_wsl%d" % sl, writes=[("wsl", sl)])
            p.dma("pool", wsg[sl][:, :, 0, :], wv(1536 + 128 * j), "d_wsg%d_0" % sl, writes=[("wsg", sl, 0)])
            p.dma("pool", wsg[sl][:, :, 1, :], wv(2560 + 128 * j), "d_wsg%d_1" % sl, writes=[("wsg", sl, 1)])
            p.dma("pool", wgl[sl][:, :, 0, :], gv(128 * j), "d_wgl%d_0" % sl, writes=[("wgl", sl, 0)])
            p.dma("pool", wgl[sl][:, :, 1, :], gv(1024 + 128 * j), "d_wgl%d_1" % sl, writes=[("wgl", sl, 1)])

        for j in range(8):
            sl = j % 2
            hb = hbuf[sl]
            if j == 0:
                load_w(0)
            if j + 1 < 8:
                load_w(j + 1)
            CP("pool", xs_sb[:, :, 0:3], LRUc[:, j, 0:48].rearrange("p (b k) -> p b k", k=3), LK, ("xs", "st"))
            cw = [LRUc[:, j, 64 + k:65 + k] for k in range(4)]
            cbj = LRUc[:, j, 68:69]
            for pi, (c0, w) in enumerate(PIECES):
                samp = (w == 64)
                i2 = npc[0] % NP
                npc[0] += 1
                b = self.bank()
                for k in range(8):
                    p.op("pe", lambda e, k=k, b=b, sl=sl, c0=c0, w=w: e.matmul(
                        self.bk(b)[:, 0:w], wsl[sl][:, k, :], xT[:, k, c0:c0 + w], start=(k == 0), stop=(k == 7)),
                        reads=[("wsl", sl)] + self.xT_keys(c0, w), writes=[("bank", b)], inc=(k == 7))
                ps = self.bk(b)[:, 0:w]
                yv = ytmp[i2][:, 0:w]; xcv = xc[i2][:, 0:w]
                ACT(yv, ps, AF.Identity, [("bank", b)] + LK, ("ytmp", i2), scale=cw[3], bias=cbj)
                if not samp:
                    CP("act", xl_sb[:, 3 + c0:3 + c0 + w], ps, [("bank", b)], ("xl", pi))
                    xk = [("xl", pi), ("xl", pi - 1)]
                    STT("dve", yv, xl_sb[:, c0 + 2:c0 + 2 + w], cw[2], yv, ALU.mult, ALU.add, xk + [("ytmp", i2)], ("ytmp", i2))
                    STT("dve", yv, xl_sb[:, c0 + 1:c0 + 1 + w], cw[1], yv, ALU.mult, ALU.add, xk + [("ytmp", i2)], ("ytmp", i2))
                    STT("dve", xcv, xl_sb[:, c0:c0 + w], cw[0], yv, ALU.mult, ALU.add, xk + [("ytmp", i2)], ("xc", i2))
                else:
                    CP("act", xs_sb[:, :, 3:7], ps.rearrange("p (b s) -> p b s", s=4), [("bank", b)], ("xs", "new"))
                    xk = [("xs", "st"), ("xs", "new")]
                    y3 = yv.rearrange("p (b s) -> p b s", s=4); xc3 = xcv.rearrange("p (b s) -> p b s", s=4)
                    STT("dve", y3, xs_sb[:, :, 2:6], cw[2], y3, ALU.mult, ALU.add, xk + [("ytmp", i2)], ("ytmp", i2))
                    STT("dve", y3, xs_sb[:, :, 1:5], cw[1], y3, ALU.mult, ALU.add, xk + [("ytmp", i2)], ("ytmp", i2))
                    STT("dve", xc3, xs_sb[:, :, 0:4], cw[0], y3, ALU.mult, ALU.add, xk + [("ytmp", i2)], ("xc", i2))
                if j == 0:
                    self.tap("xc_%d" % c0, xcv, [("xc", i2)])
                CP("pool", xcb[i2][:, 0:w], xcv, [("xc", i2)], ("xcb", i2))
                ba_ = self.bank(); bx_ = self.bank()
                p.op("pe", lambda e, ba_=ba_, j=j, i2=i2, w=w: e.matmul(self.bk(ba_)[:, 0:w], Wg[:, j, 0, :], xcb[i2][:, 0:w], start=True, stop=True),
                     reads=["Wg", ("xcb", i2)], writes=[("bank", ba_)])
                p.op("pe", lambda e, bx_=bx_, j=j, i2=i2, w=w: e.matmul(self.bk(bx_)[:, 0:w], Wg[:, j, 1, :], xcb[i2][:, 0:w], start=True, stop=True),
                     reads=["Wg", ("xcb", i2)], writes=[("bank", bx_)])
                rv = rp_[i2][:, 0:w]; iv = ip_[i2][:, 0:w]
                ACT(rv, self.bk(ba_)[:, 0:w], AF.Tanh, [("bank", ba_), "baT"], ("rp", i2), scale=0.5, bias=baT[:, j:j + 1])
                ACT(iv, self.bk(bx_)[:, 0:w], AF.Tanh, [("bank", bx_), "bxT"], ("ip", i2), scale=0.5, bias=bxT[:, j:j + 1])
                ACT(abuf[:, c0:c0 + w], rv, AF.Exp, [("rp", i2), "hsc8"], ("abuf", pi), scale=hsc8[:, j:j + 1], bias=hsc8[:, j:j + 1])
                ACT(a2buf[:, c0:c0 + w], rv, AF.Exp, [("rp", i2), "sc8"], ("a2buf", pi), scale=sc8[:, j:j + 1], bias=sc8[:, j:j + 1])
                STT("dve", ixbuf[:, c0:c0 + w], iv, 1.0, xcv, ALU.add, ALU.mult, [("ip", i2), ("xc", i2)], ("ixbuf", pi))
            allp = lambda nm: [(nm, pi) for pi in range(len(PIECES))]
            ACT(a2buf, a2buf, AF.Sqrt, allp("a2buf"), "mh", scale=-0.25, bias=0.25)
            TT("dve", ixbuf, a2buf, ixbuf, ALU.mult, ["mh"] + allp("ixbuf"), "bterm")
            TT("dve", t16, abuf[:, TP:T:4], LRUc[:, j, 48:64], ALU.mult, allp("abuf") + LK, "t16")
            TT("dve", ixbuf[:, TP:T:4], ixbuf[:, TP:T:4], t16, ALU.add, ["bterm", "t16"], "bterm")
            MS("dve", abuf[:, TP:T:4], 0.0, "afix", r=allp("abuf") + ["t16"])
            p.op("dve", lambda e, hb=hb: e.tensor_tensor_scan(out=hb, data0=abuf, data1=ixbuf, initial=0.0, op0=ALU.mult, op1=ALU.add),
                 reads=allp("abuf") + ["afix", "bterm"], writes=[("hbuf", sl)])
            for nm in ("abuf", "a2buf", "ixbuf"):
                for pi in range(len(PIECES)):
                    p.readers.setdefault((nm, pi), []).append(p.last_write[("hbuf", sl)])
            if j == 0:
                self.tap("hbuf", hb, [("hbuf", sl)])
            CP("pool", hfinL[:, j, 0:1], hb[:, TP - 1:TP], [("hbuf", sl)], ("hfinL", j, 0))
            CP("pool", hfinL[:, j, 1:17], hb[:, TP + 3:T:4], [("hbuf", sl)], ("hfinL", j, 1))
            b = self.bank()
            for k in range(8):
                p.op("pe", lambda e, k=k, b=b, sl=sl: e.matmul(self.bk(b)[0:NTAIL, 0:128], xT[:, k, TAIL0:T], wsl[sl][:, k, :],
                                                               start=(k == 0), stop=(k == 7)),
                     reads=[("wsl", sl)] + self.xT_keys(TAIL0, NTAIL), writes=[("bank", b)], inc=(k == 7))
            CP("act", tail_sb[:, 128 * j:128 * (j + 1)], self.bk(b)[0:NTAIL, 0:128], [("bank", b)], ("tail", j))
            for pi, (c0, w) in enumerate(PIECES):
                i2 = npc[0] % NP
                npc[0] += 1
                bl = self.bank(); bs = self.bank(); bga = self.bank(); bgb = self.bank()
                for (bb, gi) in ((bl, 0), (bs, 1)):
                    for k in range(8):
                        p.op("pe", lambda e, k=k, bb=bb, gi=gi, sl=sl, c0=c0, w=w: e.matmul(
                            self.bk(bb)[:, 0:w], wsg[sl][:, k, gi, :], xT[:, k, c0:c0 + w], start=(k == 0), stop=(k == 7)),
                            reads=[("wsg", sl, gi)] + self.xT_keys(c0, w), writes=[("bank", bb)], inc=(k == 7))
                for (bb, gi) in ((bga, 0), (bgb, 1)):
                    for k in range(4):
                        p.op("pe", lambda e, k=k, bb=bb, gi=gi, sl=sl, c0=c0, w=w: e.matmul(
                            self.bk(bb)[:, 0:w], wgl[sl][:, k, gi, :], z2[:, k, c0:c0 + w], start=(k == 0), stop=(k == 3)),
                            reads=[("wgl", sl, gi)] + [("z2", k, c0)], writes=[("bank", bb)], inc=(k == 3))
                glv = glp[i2][:, 0:w]; gsv = gsp[i2][:, 0:w]; gbv = gbp[i2][:, 0:w]; t1v = t1p[i2][:, 0:w]
                ACT(glv, self.bk(bl)[:, 0:w], AF.Tanh, [("bank", bl)], ("glp", i2), scale=0.5)
                ACT(gsv, self.bk(bs)[:, 0:w], AF.Tanh, [("bank", bs)], ("gsp", i2), scale=0.5)
                ACT(gbv, self.bk(bgb)[:, 0:w], AF.Tanh, [("bank", bgb)], ("gbp", i2), scale=0.25)
                STT("dve", t1v, gbv, 1.0, self.bk(bga)[:, 0:w], ALU.add, ALU.mult, [("gbp", i2), ("bank", bga)], ("t1p", i2))
                STT("dve", t1v, gsv, 1.0, t1v, ALU.add, ALU.mult, [("gsp", i2), ("t1p", i2)], ("t1p", i2))
                STT("dve", glv, glv, 1.0, hb[:, c0:c0 + w], ALU.add, ALU.mult, [("glp", i2), ("hbuf", sl)], ("glp", i2))
                STT("dve", merged2[:, j, c0:c0 + w], t1v, 0.25, glv, ALU.mult, ALU.add, [("t1p", i2), ("glp", i2)], ("merged2", j, c0))
        self.tap("merged2", merged2[:, 0, :], [("merged2", 0, c0) for c0, _ in PIECES])
        self.out_dma(O["lru_conv"], tail_sb, [("tail", j) for j in range(8)])
        self.out_dma(O["lru_h"], hfinL, [("hfinL", j, i) for j in range(8) for i in range(2)])

    def ln_tile(self, ps_flat, res, gB, bB, ytok, outv, rows, kps, kres, kg, kb, ky, kout, st6, mv, sd):
        p = self.p
        TT, TS, STT, ACT, CP = self.TT, self.TS, self.STT, self.ACT, self.CP
        yv = ytok[0:rows, :]
        p.op("act", lambda e: e.activation(out=yv, in_=ps_flat[0:rows, :], func=AF.Copy, scale=0.5), reads=kps, writes=[ky])
        STT("dve", yv, res[0:rows, :], ALPHA, yv, ALU.mult, ALU.add, [kres, ky], ky)
        for h in range(2):
            p.op("dve", lambda e, h=h: e.bn_stats(out=st6[0:rows, h, :], in_=yv[:, 512 * h:512 * (h + 1)]), reads=[ky], writes=[ky + ("st", h)])
        p.op("dve", lambda e: e.bn_aggr(out=mv[0:rows, :], in_=st6[0:rows, :, :].rearrange("p a b -> p (a b)")),
             reads=[ky + ("st", 0), ky + ("st", 1)], writes=[ky + ("mv",)])
        ACT(sd[0:rows, :], mv[0:rows, 1:2], AF.Sqrt, [ky + ("mv",)], ky + ("sd",), bias=LN_EPS)
        p.op("dve", lambda e: e.reciprocal(out=sd[0:rows, :], in_=sd[0:rows, :]), reads=[ky + ("sd",)], writes=[ky + ("sd",)])
        TS("dve", yv, yv, mv[0:rows, 0:1], ALU.subtract, [ky, ky + ("mv",), ky + ("sd",)], ky, s2=sd[0:rows, 0:1], op1=ALU.mult)
        TT("pool", yv, yv, gB[0:rows, :], ALU.mult, [ky, kg], ky)
        TT("dve", outv[0:rows, :], yv, bB[0:rows, :], ALU.add, [ky, kb], kout)

    def mix_stage(self):
        nc, p, I, O = self.nc, self.p, self.I, self.O
        R0, R1 = self.R0, self.R1
        TT, TS, STT, ACT, CP, MS = self.TT, self.TS, self.STT, self.ACT, self.CP, self.MS
        merged2 = self.merged2
        x1T = self.xT
        self.x1T = x1T
        lnc = self.lnc
        for i, nm in enumerate(("ln1_g", "ln1_b", "ln2_g", "ln2_b")):
            p.dma("sp", lnc[:, i, :], I[nm].partition_broadcast(128), "d_lnc%d" % i, writes=[("lnc", i)])
        wout = R1.take([128, 8, D], BF16)
        for h in range(2):
            p.dma("pool", wout[:, :, 512 * h:512 * (h + 1)], I["w_out"][:, 512 * h:512 * (h + 1)].rearrange("(k p) n -> p k n", p=128),
                  "d_wout%d" % h, writes=[("wout", h)])
        NB = 2
        xtok = [R1.take([128, D]) for i in range(NB)]
        ytok = [R1.take([128, D]) for i in range(NB)]
        x1tok = [R1.take([128, D]) for i in range(NB)]
        x1b = [R1.take([128, D], BF16) for i in range(NB)]
        st6 = [R1.take([128, 2, 6]) for i in range(NB)]
        mv = [R1.take([128, 2]) for i in range(NB)]
        sd = [R1.take([128, 1]) for i in range(NB)]
        ntt = (T + 127) // 128
        self.free_banks = [4, 5, 6, 7]
        for tt in range(ntt):
            r0 = tt * 128
            rows = min(128, T - r0)
            s = tt % NB
            pp = tt % 2
            p.dma("sp", xtok[s][0:rows, :], I["x"][r0:r0 + rows, :], "d_xtok%d" % s, writes=[("xtok", s)])
            psf = self.ps[pp][:].rearrange("p a c -> p (a c)")
            for h in range(2):
                for k in range(8):
                    p.op("pe", lambda e, k=k, h=h, pp=pp, r0=r0, rows=rows: e.matmul(
                        self.ps[pp][0:rows, h, :], merged2[:, k, r0:r0 + rows], wout[:, k, 512 * h:512 * (h + 1)],
                        start=(k == 0), stop=(k == 7)),
                        reads=[("wout", h)] + [("merged2", k, c0) for (c0, w) in self.PIECES if c0 <= r0 < c0 + w],
                        writes=[("bank", 2 * pp + h)], inc=(k == 7))
            self.ln_tile(psf, xtok[s], lnc[:, 0, :], lnc[:, 1, :], ytok[s], x1tok[s], rows,
                         [("bank", 2 * pp), ("bank", 2 * pp + 1)], ("xtok", s), ("lnc", 0), ("lnc", 1), ("ytok", s), ("x1tok", s),
                         st6[s], mv[s], sd[s])
            if tt == 0:
                self.tap("x1tok", x1tok[s], [("x1tok", s)])
            p.dma("sp", self.x1_scr[r0:r0 + rows, :], x1tok[s][0:rows, :], "d_x1w%d" % s, reads=[("x1tok", s)], writes=[("x1scr", tt)])
            CP("act", x1b[s][0:rows, :], x1tok[s][0:rows, :], [("x1tok", s)], ("x1b", s))
            b = self.bank()
            pt = self.bk(b).bitcast(BF16)
            for k in range(8):
                p.op("pe", lambda e, k=k, pt=pt, s=s, rows=rows: e.transpose(
                    pt[:, k * 128:k * 128 + rows], x1b[s][0:rows, k * 128:(k + 1) * 128], self.ident_b[0:rows, 0:rows]),
                    reads=[("x1b", s), "ident_b"], writes=[("bank", b)], inc=(k == 7))
            src = pt.rearrange("p (k c) -> p k c", c=128)[:, :, 0:rows]
            CP("act", x1T[:, :, r0:r0 + rows], src, [("bank", b)], ("x1T", tt))
        self.tap("x1T", x1T[:, 0, :], [("x1T", tt) for tt in range(ntt)])

    def x1T_keys(self, c0, w):
        return [("x1T", tt) for tt in range(c0 // 128, (c0 + w - 1) // 128 + 1)]

    def ffn_stage(self):
        nc, p, I, O = self.nc, self.p, self.I, self.O
        R0, R1 = self.R0, self.R1
        TT, TS, STT, ACT, CP, MS = self.TT, self.TS, self.STT, self.ACT, self.CP, self.MS
        NCK = dict(allow_slow_non_contiguous=True)
        x1T, lnc = self.x1T, self.lnc
        NJ = DFF // 128
        QW = 576
        Fc = self.Fc
        FK = self.Fc_keys
        wdn = R1.take([128, NJ, D], BF16)
        for c in range(6):
            p.dma("pool", wdn[:, 4 * c:4 * c + 4, :], I["w_down"][512 * c:512 * (c + 1), :].rearrange("(k p) n -> p k n", p=128),
                  "d_wdn%d" % c, writes=[("wdn", c)])
        Gq = R1.take([128, NJ, QW], BF16)
        NSL = 4
        wup = [R1.take([128, 8, 2, 128], BF16) for i in range(NSL)]
        halo = R1.take([128, NJ, 2])
        MS("pool", halo, 0.0, "halo_init")
        NY, NQ, NG = 5, 3, 2
        a_sb = [R1.take([128, 2 + 512]) for i in range(2)]
        as_sb = R1.take([128, 16, 6])
        y0 = [R1.take([128, 512]) for i in range(NY)]
        qq = [R1.take([128, 512]) for i in range(NQ)]
        ag = [R1.take([128, 512]) for i in range(NG)]
        tailf = [R1.take([NTAIL, 512]) for i in range(2)]
        NB = 2
        x1tok = [R1.take([128, D]) for i in range(NB)]
        ytok = [R1.take([128, D]) for i in range(NB)]
        otok = ytok
        st6 = [R1.take([128, 2, 6]) for i in range(NB)]
        mv = [R1.take([128, 2]) for i in range(NB)]
        sd = [R1.take([128, 1]) for i in range(NB)]
        nld = [0]

        def load_wup(j):
            sl = nld[0] % NSL
            nld[0] += 1
            wv = lambda c: I["w_up"][:, c:c + 128].rearrange("(k p) n -> p k n", p=128)
            p.dma("pool", wup[sl][:, :, 0, :], wv(128 * j), "d_wup%d_0" % sl, writes=[("wup", sl, 0)])
            p.dma("pool", wup[sl][:, :, 1, :], wv(DFF + 128 * j), "d_wup%d_1" % sl, writes=[("wup", sl, 1)])
            return sl

        npc = [0]
        ntile = [0]
        for n in range(4):
            q0 = 512 * n
            pieces = [(q0, 512, 0)] + ([(TP, NS, 512)] if n == 3 else [])
            RA = [0, 1]
            RG = [2, 3, 4, 5, 6]
            TAILB = 7
            items = []
            for j in range(NJ):
                for pi_, (c0, w, lc0) in enumerate(pieces):
                    items.append(dict(j=j, c0=c0, w=w, lc0=lc0, first=(pi_ == 0), last=(pi_ == len(pieces) - 1)))
            pending = [load_wup(0), load_wup(1), load_wup(2)]
            cur_sl = {}

            def S0(t, it):
                j, c0, w = it["j"], it["c0"], it["w"]
                if it["first"]:
                    cur_sl[j] = pending.pop(0)
                    if j + 3 < NJ:
                        pending.append(load_wup(j + 3))
                sl = cur_sl[j]
                bA = RA[t % 2]; bG = RG[t % 5]
                it["bA"], it["bG"] = bA, bG
                for (bb, gi) in ((bA, 0), (bG, 1)):
                    for k in range(8):
                        p.op("pe", lambda e, k=k, bb=bb, gi=gi, sl=sl, c0=c0, w=w: e.matmul(
                            self.bk(bb)[:, 0:w], wup[sl][:, k, gi, :], x1T[:, k, c0:c0 + w], start=(k == 0), stop=(k == 7)),
                            reads=[("wup", sl, gi)] + self.x1T_keys(c0, w), writes=[("bank", bb)], inc=(k == 7))
                if n == 3 and it["last"]:
                    for k in range(8):
                        p.op("pe", lambda e, k=k, sl=sl: e.matmul(self.bk(TAILB)[0:NTAIL, 0:128], x1T[:, k, TAIL0:T], wup[sl][:, k, 0, :],
                                                                  start=(k == 0), stop=(k == 7)),
                             reads=[("wup", sl, 0)] + self.x1T_keys(TAIL0, NTAIL), writes=[("bank", TAILB)], inc=(k == 7))

            def S1(t, it):
                j, c0, w, bA = it["j"], it["c0"], it["w"], it["bA"]
                samp = (w == NS)
                iy = t % NY; ia = t % 2
                fw = [Fc[:, j, 32 + k:33 + k] for k in range(3)]
                aps = self.bk(bA)[:, 0:w]
                yv = y0[iy][:, 0:w]
                ACT(yv, aps, AF.Identity, [("bank", bA)] + FK, ("y0", iy), scale=fw[2], bias=Fc[:, j, 35:36])
                if not samp:
                    ab = a_sb[ia]
                    CP("act", ab[:, 2:2 + w], aps, [("bank", bA)], ("a_sb", ia))
                    CP("dve", ab[:, 0:2], halo[:, j, :], ["halo_init", ("halo", j)], ("a_sbh", ia))
                    ak = [("a_sb", ia), ("a_sbh", ia), ("y0", iy)]
                    STT("dve", yv, ab[:, 1:1 + w], fw[1], yv, ALU.mult, ALU.add, ak, ("y0", iy))
                    STT("dve", yv, ab[:, 0:w], fw[0], yv, ALU.mult, ALU.add, ak, ("y0", iy))
                    CP("dve", halo[:, j, :], ab[:, w:w + 2], [("a_sb", ia), ("a_sbh", ia)], ("halo", j))
                else:
                    CP("act", as_sb[:, :, 2:6], aps.rearrange("p (b s) -> p b s", s=4), [("bank", bA)], ("as_sb", "new"))
                    CP("dve", as_sb[:, :, 0:2], Fc[:, j, 0:32].rearrange("p (b k) -> p b k", k=2), FK, ("as_sb", "st"))
                    ak = [("as_sb", "new"), ("as_sb", "st"), ("y0", iy)]
                    y3 = yv.rearrange("p (b s) -> p b s", s=4)
                    STT("dve", y3, as_sb[:, :, 1:5], fw[1], y3, ALU.mult, ALU.add, ak, ("y0", iy))
                    STT("dve", y3, as_sb[:, :, 0:4], fw[0], y3, ALU.mult, ALU.add, ak, ("y0", iy))
                if n == 3 and it["last"]:
                    tb = (j // 4) % 2
                    CP("act", tailf[tb][:, 128 * (j % 4):128 * (j % 4 + 1)], self.bk(TAILB)[0:NTAIL, 0:128], [("bank", TAILB)], ("tailf", tb, j % 4))
                    if j % 4 == 3:
                        sem = "o_tf%d" % tb
                        p.dma("sp", O["ffn_conv"][:, 512 * (j // 4):512 * (j // 4 + 1)], tailf[tb], sem,
                              reads=[("tailf", tb, i) for i in range(4)])
                        self.out_sems[sem] = p.count[sem]

            def S2(t, it):
                w = it["w"]
                iy = t % NY; iq = t % NQ
                yv = y0[iy][:, 0:w]; qv = qq[iq][:, 0:w]
                ACT(qv, yv, AF.Square, [("y0", iy)], ("qq", iq))
                ACT(qv, qv, AF.Identity, [("qq", iq)], ("qq", iq), scale=GC, bias=1.0)

            def S3(t, it):
                w = it["w"]
                iy = t % NY; iq = t % NQ; ig = t % NG
                TT("pool", ag[ig][:, 0:w], qq[iq][:, 0:w], y0[iy][:, 0:w], ALU.mult, [("qq", iq), ("y0", iy)], ("ag", ig))

            def S4(t, it):
                j, w, lc0, bG = it["j"], it["w"], it["lc0"], it["bG"]
                iy = t % NY; iq = t % NQ; ig = t % NG
                yv = y0[iy][:, 0:w]; qv = qq[iq][:, 0:w]; av = ag[ig][:, 0:w]
                gps = self.bk(bG)[:, 0:w]
                ACT(qv, av, AF.Tanh, [("ag", ig)], ("qq", iq), scale=GK)
                STT("dve", av, qv, 1.0, yv, ALU.add, ALU.mult, [("qq", iq), ("y0", iy)], ("ag", ig))
                TT("dve", Gq[:, j, lc0:lc0 + w], av, gps, ALU.mult, [("ag", ig), ("bank", bG)], ("Gq", j, lc0))

            N_ = len(items)
            stages = [(S4, 4), (S1, 1), (S2, 2), (S3, 3), (S0, 0)]
            for t in range(N_ + 4):
                for fn, lag in stages:
                    if 0 <= t - lag < N_:
                        fn(t - lag, items[t - lag])
            if n == 0:
                self.tap("Gq", Gq[:, 0, :], [("Gq", 0, 0)])
            tts = [4 * n + i for i in range(4)] + ([16] if n == 3 else [])
            for tt in tts:
                r0 = tt * 128
                rows = min(128, T - r0)
                lc = r0 - q0 if tt < 16 else 512
                s = ntile[0] % NB
                pp = ntile[0] % 2
                ntile[0] += 1
                p.dma("sp", x1tok[s][0:rows, :], self.x1_scr[r0:r0 + rows, :], "d_x1r%d" % s, reads=[("x1scr", tt)], writes=[("x1tok2", s)])
                psf = self.ps[pp][:].rearrange("p a c -> p (a c)")
                gkeys = [("Gq", j, 512 if tt == 16 else 0) for j in range(NJ)]
                for h in range(2):
                    for k in range(NJ):
                        p.op("pe", lambda e, k=k, h=h, pp=pp, lc=lc, rows=rows: e.matmul(
                            self.ps[pp][0:rows, h, :], Gq[:, k, lc:lc + rows], wdn[:, k, 512 * h:512 * (h + 1)],
                            start=(k == 0), stop=(k == NJ - 1)),
                            reads=[("wdn", k // 4), ("Gq", k, 512 if tt == 16 else 0)], writes=[("bank", 2 * pp + h)], inc=(k == NJ - 1))
                self.ln_tile(psf, x1tok[s], lnc[:, 2, :], lnc[:, 3, :], ytok[s], otok[s], rows,
                             [("bank", 2 * pp), ("bank", 2 * pp + 1)], ("x1tok2", s), ("lnc", 2), ("lnc", 3), ("ytok2", s), ("ytok2", s),
                             st6[s], mv[s], sd[s])
                sem = "o_y%d" % s
                p.dma("sp", O["y"][r0:r0 + rows, :], otok[s][0:rows, :], sem, reads=[("ytok2", s)])
                self.out_sems[sem] = p.count[sem]

    def finish(self):
        fw = [(s, v) for s, v in self.out_sems.items()]
        self.p.emit(final_waits=fw)
        self.st.close()
        print("arena peaks: R0 %d/%d words, R1 %d/%d words" % (self.R0.peak, self.R0.words, self.R1.peak, self.R1.words))
        return self.nc


def shard_inputs(inputs, c):
    f = lambda a: np.ascontiguousarray(a, dtype=np.float32)
    m = {}
    m["x"] = f(np.concatenate([inputs["x_prompt"][c], inputs["x_sample"][NSQ * c:NSQ * (c + 1)].reshape(NS, D)], axis=0))
    m["st_lru_conv"] = f(inputs["state_lru_conv"][0, NSQ * c:NSQ * (c + 1)].reshape(NSQ * 3, D))
    m["st_lru_h"] = f(inputs["state_lru_h"][0, NSQ * c:NSQ * (c + 1)])
    m["st_s5_re"] = f(inputs["state_s5_re"][0, NSQ * c:NSQ * (c + 1)].reshape(NSQ, 2048))
    m["st_s5_im"] = f(inputs["state_s5_im"][0, NSQ * c:NSQ * (c + 1)].reshape(NSQ, 2048))
    m["st_ffn_conv"] = f(inputs["state_ffn_conv"][0, NSQ * c:NSQ * (c + 1)].reshape(NSQ * 2, DFF))
    for k in ("w_in", "lru_conv_w", "lru_conv_b", "lru_wa", "lru_ba", "lru_wx", "lru_bx", "lru_lambda", "s5_a_re", "s5_a_im",
              "s5_log_dt", "s5_b_re", "s5_b_im", "s5_c_re", "s5_c_im", "s5_d", "w_glu", "w_out", "ln1_g", "ln1_b", "w_up",
              "ffn_conv_w", "ffn_conv_b", "w_down", "ln2_g", "ln2_b"):
        m[k] = f(inputs[k][0])
    return m


_NC_CACHE = {}


def _get_nc():
    if "nc" not in _NC_CACHE:
        _NC_CACHE["nc"] = Builder().build()
    return _NC_CACHE["nc"]


def kernel(**inputs):
    nc = _get_nc()
    in_maps = [shard_inputs(inputs, c) for c in range(NCORES)]
    res = run_bass_kernel_spmd(nc, in_maps, core_ids=list(range(NCORES)))
    R = res.results
    B = NCORES
    y_p = np.zeros((B, TP, D), np.float32); y_s = np.zeros((B * NSQ, 4, D), np.float32)
    p_conv = np.zeros((1, B, 3, D), np.float32); p_h = np.zeros((1, B, D), np.float32)
    p_re = np.zeros((1, B, 32, 64), np.float32); p_im = np.zeros((1, B, 32, 64), np.float32)
    p_ffn = np.zeros((1, B, 2, DFF), np.float32)
    s_conv = np.zeros((1, B * NSQ, 3, D), np.float32); s_h = np.zeros((1, B * NSQ, D), np.float32)
    s_re = np.zeros((1, B * NSQ, 32, 64), np.float32); s_im = np.zeros((1, B * NSQ, 32, 64), np.float32)
    s_ffn = np.zeros((1, B * NSQ, 2, DFF), np.float32)
    for c in range(B):
        r = R[c]
        sl = slice(NSQ * c, NSQ * (c + 1))
        y_p[c] = r["y"][0:TP]
        y_s[sl] = r["y"][TP:].reshape(NSQ, 4, D)
        lc = r["o_lru_conv"]
        p_conv[0, c] = lc[0:3]
        s_conv[0, sl] = lc[3:].reshape(NSQ, 4, D)[:, 1:4]
        lh = r["o_lru_h"].transpose(2, 1, 0).reshape(17, D)
        p_h[0, c] = lh[0]
        s_h[0, sl] = lh[1:17]
        s5 = r["o_s5"].reshape(2, 64, 2, 16, 17).transpose(2, 4, 3, 0, 1)
        s5 = s5.reshape(2, 17, 32, 64)
        p_re[0, c] = s5[0, 0]
        p_im[0, c] = s5[1, 0]
        s_re[0, sl] = s5[0, 1:17]
        s_im[0, sl] = s5[1, 1:17]
        fc = r["o_ffn_conv"]
        p_ffn[0, c] = fc[1:3]
        s_ffn[0, sl] = fc[3:].reshape(NSQ, 4, DFF)[:, 2:4]
    return (y_p, y_s, p_conv, p_h, p_re, p_im, p_ffn, s_conv, s_h, s_re, s_im, s_ffn)
```

```python
import math
import contextlib
import numpy as np
import concourse.bass as bass
import concourse.mybir as mybir
from concourse.bass_utils import run_bass_kernel_spmd

F32 = mybir.dt.float32
BF16 = mybir.dt.bfloat16
AF = mybir.ActivationFunctionType
ALU = mybir.AluOpType

ENGINES = ("pe", "act", "dve", "pool", "sp")
NCORES = 8
TP = 2048
NSQ = 16
NS = 64
T = TP + NS
TAIL0 = TP - 3
NTAIL = T - TAIL0
D = 1024
DFF = 3072
ALPHA = 2.0 ** 0.25
LN_EPS = 1e-5
LCH = 8
NCH = TP // LCH
NLEV = 8
PAD = 128
PI = math.pi
GK = math.sqrt(2.0 / math.pi)
GC = 0.044715


class Prog:
    def __init__(self, nc):
        self.nc = nc
        self.streams = {e: [] for e in ENGINES}
        self.count = {}
        self.waited = {e: {} for e in ENGINES}
        self.last_write = {}
        self.readers = {}
        self.sem_names = set()
        self.epoch = "0"
        self.pending = {}

    def barrier(self):
        snap = [(s, v) for s, v in self.count.items() if not s.startswith("o_")]
        for e in ENGINES:
            self.pending.setdefault(e, []).extend(snap)

    def op(self, eng, fn, reads=(), writes=(), inc=True, sem=None, amount=1):
        if sem is None:
            sem = "s_%s_%s" % (eng, self.epoch)
        self.sem_names.add(sem)
        deps = list(self.pending.pop(eng, ()))
        for k in reads:
            ev = self.last_write.get(k)
            if ev is not None:
                deps.append(ev)
        for k in writes:
            ev = self.last_write.get(k)
            if ev is not None:
                deps.append(ev)
            deps.extend(self.readers.get(k, ()))
        waits = {}
        for (s, v) in deps:
            if eng == "pe" and s.startswith("s_pe_"):
                continue
            if self.waited[eng].get(s, 0) >= v:
                continue
            if waits.get(s, 0) < v:
                waits[s] = v
        for s, v in waits.items():
            self.waited[eng][s] = v
        cur = self.count.get(sem, 0)
        val = cur + amount
        if inc:
            self.count[sem] = val
        ev = (sem, val)
        self.streams[eng].append((fn, list(waits.items()), (sem, amount) if inc else None))
        for k in reads:
            self.readers.setdefault(k, []).append(ev)
        for k in writes:
            self.last_write[k] = ev
            self.readers[k] = []
        return ev

    def dma(self, queue, out, in_, sem, reads=(), writes=(), **kw):
        def fn(e):
            return e.dma_start(out=out, in_=in_, **kw)
        return self.op(queue, fn, reads=reads, writes=writes, inc=True, sem=sem, amount=16)

    def emit(self, final_waits=()):
        nc = self.nc
        names = sorted(self.sem_names)
        with contextlib.ExitStack() as st:
            sems = {n: st.enter_context(nc.semaphore(n)) for n in names}
            block = st.enter_context(nc.Block())
            streams = self.streams

            def run(engh, lst, last):
                for fn, waits, inc in lst:
                    for s, v in waits:
                        engh.wait_ge(sems[s], v)
                    ins = fn(engh)
                    if inc is not None:
                        ins.then_inc(sems[inc[0]], inc[1])
                if last:
                    for s, v in final_waits:
                        engh.wait_ge(sems[s], v)

            @block.tensor
            def _(e):
                run(e, streams["pe"], False)

            @block.scalar
            def _(e):
                run(e, streams["act"], False)

            @block.vector
            def _(e):
                run(e, streams["dve"], False)

            @block.gpsimd
            def _(e):
                run(e, streams["pool"], False)

            @block.sync
            def _(e):
                run(e, streams["sp"], True)


class Arena:
    def __init__(self, tensor, words):
        self.t = tensor
        self.words = words
        self.pos = 0
        self.peak = 0

    def take(self, shape, dt=F32):
        n = 1
        for s in shape[1:]:
            n *= s
        esz = 4 if dt == F32 else 2
        w = (n * esz + 3) // 4
        w = (w + 7) // 8 * 8
        assert self.pos + w <= self.words, "arena overflow: need %d have %d" % (self.pos + w, self.words)
        v = self.t[0:shape[0], self.pos:self.pos + w]
        self.pos += w
        self.peak = max(self.peak, self.pos)
        if dt != F32:
            v = v.bitcast(dt)
        v = v[:, 0:n]
        if len(shape) > 2:
            names = " ".join("d%d" % i for i in range(len(shape) - 1))
            kw = {"d%d" % i: shape[i + 1] for i in range(len(shape) - 2)}
            v = v.rearrange("p (%s) -> p %s" % (names, names), **kw)
        return v


class StopBuild(Exception):
    pass


class Builder:
    def chk_stop(self, name, reads=()):
        if self.stop_after == name:
            self.p.barrier()
            d = self.dout("dbg_stop", [128, 4])
            t = self.R0.take([128, 4])
            self.MS("dve", t, 1.0, "stoptile")
            self.out_dma(d, t, ["stoptile"])
            raise StopBuild()

    def __init__(self, debug=(), stop_after=None):
        self.debug = set(debug)
        self.stop_after = stop_after
        self.nc = bass.Bass("TRN2", target_bir_lowering=False)
        self.p = Prog(self.nc)
        self.st = contextlib.ExitStack()
        self.out_sems = {}
        self.nout = 0
        self.free_banks = list(range(8))
        self.nbank = 0
        self.ntmp = 0

    def din(self, name, shape):
        return self.nc.dram_tensor(name, list(shape), F32, kind="ExternalInput").ap()

    def dout(self, name, shape, dt=F32):
        return self.nc.dram_tensor(name, list(shape), dt, kind="ExternalOutput").ap()

    def sb(self, name, shape, dt=F32):
        t = self.st.enter_context(self.nc.sbuf_tensor(name, list(shape), dt))
        return t[:]

    def out_dma(self, out, in_, reads, queue="sp", **kw):
        sem = "o_%d" % (self.nout % 8)
        self.nout += 1
        self.p.dma(queue, out, in_, sem, reads=reads, **kw)
        self.out_sems[sem] = self.p.count[sem]

    def tap(self, name, ap, reads):
        if name not in self.debug:
            return
        d = self.dout("dbg_" + name, list(ap.shape), ap.dtype)
        self.out_dma(d, ap, reads)

    def bank(self):
        b = self.free_banks[self.nbank % len(self.free_banks)]
        self.nbank += 1
        return b

    def bk(self, b):
        return self.ps[b // 2][:, b % 2, :]

    def TT(self, eng, out, a, b_, op, r, w):
        self.p.op(eng, lambda e: e.tensor_tensor(out=out, in0=a, in1=b_, op=op), reads=r, writes=[w])

    def TS(self, eng, out, a, s1, op0, r, w, s2=None, op1=None):
        if op1 is None:
            self.p.op(eng, lambda e: e.tensor_scalar(out=out, in0=a, scalar1=s1, scalar2=None, op0=op0), reads=r, writes=[w])
        else:
            self.p.op(eng, lambda e: e.tensor_scalar(out=out, in0=a, scalar1=s1, scalar2=s2, op0=op0, op1=op1), reads=r, writes=[w])

    def STT(self, eng, out, a, sc, b_, op0, op1, r, w):
        self.p.op(eng, lambda e: e.scalar_tensor_tensor(out=out, in0=a, scalar=sc, in1=b_, op0=op0, op1=op1), reads=r, writes=[w])

    def ACT(self, out, a, func, r, w, scale=1.0, bias=0.0):
        self.p.op("act", lambda e: e.activation(out=out, in_=a, func=func, scale=scale, bias=bias), reads=r, writes=[w])

    def CP(self, eng, out, a, r, w):
        if eng == "act":
            self.p.op("act", lambda e: e.copy(out=out, in_=a), reads=r, writes=[w])
        else:
            self.p.op(eng, lambda e: e.tensor_copy(out=out, in_=a), reads=r, writes=[w])

    def MS(self, eng, out, val, w, r=()):
        self.p.op(eng, lambda e: e.memset(out, val), reads=list(r), writes=[w])

    def build(self):
        nc, p = self.nc, self.p
        din = self.din
        I = {}
        for name, shape in (("x", [T, D]), ("st_lru_conv", [NSQ * 3, D]), ("st_lru_h", [NSQ, D]), ("st_s5_re", [NSQ, 2048]),
                            ("st_s5_im", [NSQ, 2048]), ("st_ffn_conv", [NSQ * 2, DFF]), ("w_in", [D, 3584]),
                            ("lru_conv_w", [4, D]), ("lru_conv_b", [D]), ("lru_wa", [16, 64, 64]), ("lru_ba", [D]),
                            ("lru_wx", [16, 64, 64]), ("lru_bx", [D]), ("lru_lambda", [D]), ("s5_a_re", [32, 64]),
                            ("s5_a_im", [32, 64]), ("s5_log_dt", [32]), ("s5_b_re", [32, 64, 16]), ("s5_b_im", [32, 64, 16]),
                            ("s5_c_re", [32, 16, 64]), ("s5_c_im", [32, 16, 64]), ("s5_d", [512]), ("w_glu", [512, 2048]),
                            ("w_out", [D, D]), ("ln1_g", [D]), ("ln1_b", [D]), ("w_up", [D, 2 * DFF]), ("ffn_conv_w", [3, DFF]),
                            ("ffn_conv_b", [DFF]), ("w_down", [DFF, D]), ("ln2_g", [D]), ("ln2_b", [D])):
            I[name] = din(name, shape)
        self.I = I
        O = {}
        O["y"] = self.dout("y", [T, D])
        O["lru_conv"] = self.dout("o_lru_conv", [NTAIL, D])
        O["lru_h"] = self.dout("o_lru_h", [128, 8, 17])
        O["s5"] = self.dout("o_s5", [128, 2, 16, 17])
        O["ffn_conv"] = self.dout("o_ffn_conv", [NTAIL, DFF])
        self.O = O
        self.x1_scr = nc.dram_tensor("x1_scr", [T, D], F32, kind="Internal").ap()

        self.ps = [self.st.enter_context(nc.psum_tensor("ps%d" % i, [128, 2, 512], F32)) for i in range(4)]
        R0W = 17664
        R1W = 35456
        self.R0 = Arena(self.st.enter_context(nc.sbuf_tensor("R0", [128, R0W], F32)), R0W)
        self.R1 = Arena(self.st.enter_context(nc.sbuf_tensor("R1", [128, R1W], F32)), R1W)
        R0, R1 = self.R0, self.R1

        ident_f = R0.take([128, 128]); ident_b = R0.take([128, 128], BF16)
        self.ident_f, self.ident_b = ident_f, ident_b
        self.MS("pool", ident_f, 0.0, "ident_f")
        p.op("pool", lambda e: e.affine_select(out=ident_f, in_=ident_f, pattern=[[-1, 128]],
                                               compare_op=ALU.not_equal, fill=1.0, base=0, channel_multiplier=1),
             reads=["ident_f"], writes=["ident_f"])
        self.CP("pool", ident_b, ident_f, ["ident_f"], "ident_b")

        self.PIECES = [(0, 512), (512, 512), (1024, 512), (1536, 512), (2048, 64)]
        xT = R0.take([128, 8, T], BF16)
        self.xT = xT
        zblk = R0.take([128, 4224])
        self.z2 = zblk.bitcast(BF16)[:, 0:4 * T].rearrange("p (a t) -> p a t", a=4)
        self.lnc = zblk[:, 0:4096].rearrange("p (a n) -> p a n", a=4)
        self.Hfin = R0.take([128, 2, 16, 17])
        self.LRUc = R0.take([128, 8, 72])
        self.Fc = R0.take([128, DFF // 128, 36])
        mark0 = R1.pos
        u_bf = R1.take([128, 4, T], BF16)
        self.W1 = R1.take([128, 4, LCH, 2, 128], BF16)
        self.W4a = R1.take([128, 16, LCH, 2, 32], BF16)
        self.W4b = R1.take([128, 16, LCH, 2, 32], BF16)
        self.h0S = R1.take([128, 2, 16, 16]); self.h0S_bf = R1.take([128, 2, 16, 16], BF16)
        self.RS = Arena(R1.take([128, 2304]), 2304)
        mark1 = R1.pos
        self.free_banks = [4, 5, 6, 7]
        self.param_loads()
        xb = [R1.take([128, D], BF16) for i in range(2)]
        ntt = (T + 127) // 128
        for tt in range(ntt):
            r0 = tt * 128
            rows = min(128, T - r0)
            slot = tt % 2
            p.dma("pool", xb[slot][0:rows, :], I["x"][r0:r0 + rows, :], "d_xb%d" % slot, writes=[("xb", slot)])
            b = self.bank()
            pt = self.bk(b).bitcast(BF16)
            for k in range(8):
                p.op("pe", lambda e, k=k, pt=pt, slot=slot, rows=rows: e.transpose(
                    pt[:, k * 128:k * 128 + rows], xb[slot][0:rows, k * 128:(k + 1) * 128], ident_b[0:rows, 0:rows]),
                    reads=[("xb", slot), "ident_b"], writes=[("bank", b)], inc=(k == 7))
            src = pt.rearrange("p (k c) -> p k c", c=128)[:, :, 0:rows]
            self.CP("act" if tt % 2 == 0 else "dve", xT[:, :, r0:r0 + rows], src, [("bank", b)], ("xT", tt))
            if tt == 3:
                self.param_transposes()
        self.tap("xT", xT[:, 0, :], [("xT", tt) for tt in range(ntt)])
        if self.stop_after == "p0":
            return self.finish()
        try:
            self.s5_prep()
        except StopBuild:
            return self.finish()
        self.tap("W1", self.W1[:, 0, :, :, :], self.W1_keys)
        self.tap("W4a", self.W4a[:, 0, :, :, :], self.W4_keys)
        self.tap("W4b", self.W4b[:, 0, :, :, :], self.W4_keys)
        self.tap("h0S", self.h0S, self.h0S_keys)
        self.tap("mur", self.mur, self.mu_keys)
        if self.stop_after == "prep":
            return self.finish()
        p.barrier()
        R1.pos = mark1
        wslot_u = [R1.take([128, 8, 128], BF16) for i in range(2)]
        for qh in range(4):
            slot = qh % 2
            p.dma("pool", wslot_u[slot], I["w_in"][:, 1024 + 128 * qh:1024 + 128 * (qh + 1)].rearrange("(k p) n -> p k n", p=128),
                  "d_wu%d" % slot, writes=[("wslot_u", slot)])
            for (c0, w) in self.PIECES:
                b = self.bank()
                for k in range(8):
                    p.op("pe", lambda e, k=k, b=b, slot=slot, c0=c0, w=w: e.matmul(
                        self.bk(b)[:, 0:w], wslot_u[slot][:, k, :], xT[:, k, c0:c0 + w], start=(k == 0), stop=(k == 7)),
                        reads=[("wslot_u", slot)] + self.xT_keys(c0, w), writes=[("bank", b)], inc=(k == 7))
                self.CP("act", u_bf[:, qh, c0:c0 + w], self.bk(b)[:, 0:w], [("bank", b)], ("u_bf", qh, c0))
        self.tap("u_bf", u_bf[:, 0, :], [("u_bf", 0, c0) for c0, _ in self.PIECES])
        if self.stop_after == "u":
            return self.finish()
        self.s5_main(u_bf)
        self.tap("z2", self.z2[:, 0, :], [("z2", 0, c0) for c0, _ in self.PIECES])
        self.tap("Hfin", self.Hfin, [("Hfin", q) for q in range(16)] + [("Hfin0", q) for q in range(16)])
        hk = [("Hfin", q) for q in range(16)] + [("Hfin0", q) for q in range(16)]
        self.out_dma(O["s5"], self.Hfin, hk)
        if self.stop_after == "s5":
            return self.finish()
        p.barrier()
        R1.pos = mark0
        p.epoch = "1"
        self.lru_stage()
        if self.stop_after == "lru":
            return self.finish()
        p.barrier()
        R1.pos = self.merged_end
        p.epoch = "2"
        self.mix_stage()
        if self.stop_after == "mix":
            return self.finish()
        p.barrier()
        R1.pos = 0
        p.epoch = "3"
        self.ffn_stage()
        return self.finish()

    def xT_keys(self, c0, w):
        return [("xT", tt) for tt in range(c0 // 128, (c0 + w - 1) // 128 + 1)]

    def param_loads(self):
        nc, p, I = self.nc, self.p, self.I
        R1 = self.R1
        MS = self.MS
        NCK = dict(allow_slow_non_contiguous=True)
        NJ = DFF // 128
        self.LF = R1.take([128, D + DFF])
        Lp = self.LF[:, 0:D]; Fp = self.LF[:, D:D + DFF]
        hs1 = R1.take([128, 2048])
        hsp = [hs1, hs1]
        self.Lp, self.Fp, self.hsp = Lp, Fp, hsp
        MS("pool", Lp, 0.0, "Lp"); MS("pool", Fp, 0.0, "Fp"); MS("pool", hs1, 0.0, "hsp")
        row = lambda nm: I[nm].rearrange("(o n) -> o n", o=1)
        for (r0, r1, src) in ((0, 48, I["st_lru_conv"]), (48, 64, I["st_lru_h"]), (64, 68, I["lru_conv_w"]), (68, 69, row("lru_conv_b")),
                              (69, 70, row("lru_ba")), (70, 71, row("lru_bx")), (71, 72, row("lru_lambda"))):
            p.dma("sp", Lp[r0:r1, :], src, "d_Lp", writes=["Lp"])
        for (r0, r1, src) in ((0, 32, I["st_ffn_conv"]), (32, 35, I["ffn_conv_w"]), (35, 36, row("ffn_conv_b"))):
            p.dma("sp", Fp[r0:r1, :], src, "d_Fp", writes=["Fp"])
        self.hs_loaded = False
        sh3 = [128, 16, 32]
        self.Bnr = R1.take(sh3); self.Bni = R1.take(sh3)
        self.Cnr = R1.take([128, 4, 128]); self.Cni = R1.take([128, 4, 128])
        for t_, k_ in ((self.Bnr, "Bnr"), (self.Bni, "Bni"), (self.Cnr, "Cnr"), (self.Cni, "Cni")):
            MS("pool", t_, 0.0, k_)
        for (dst, nm, key) in ((self.Bnr, "s5_b_re", "Bnr"), (self.Bni, "s5_b_im", "Bni")):
            v = I[nm].rearrange("(q two) p c -> two p q c", two=2)
            for two in range(2):
                p.dma("sp", dst[64 * two:64 * two + 64, :, 16 * two:16 * two + 16], v[two], "d_prep", writes=[key])
        for (dst, nm, key) in ((self.Cnr, "s5_c_re", "Cnr"), (self.Cni, "s5_c_im", "Cni")):
            v = I[nm].rearrange("(qh ql two) c p -> ql two c qh p", qh=4, ql=4, two=2)
            for ql in range(4):
                for two in range(2):
                    p0 = 32 * ql + 16 * two
                    p.dma("sp", dst[p0:p0 + 16, :, 64 * two:64 * two + 64], v[ql, two], "d_prep", writes=[key])
        shS = [128, 16]
        self.aSr = R1.take(shS); self.aSi = R1.take(shS); self.ldS = R1.take(shS)
        p.dma("sp", self.aSr, I["s5_a_re"].rearrange("(q two) p -> (two p) q", two=2), "d_prep", writes=["aSr"], **NCK)
        p.dma("sp", self.aSi, I["s5_a_im"].rearrange("(q two) p -> (two p) q", two=2), "d_prep", writes=["aSi"], **NCK)
        v = I["s5_log_dt"].rearrange("(q two) -> two q", two=2)
        for two in range(2):
            p.dma("sp", self.ldS[64 * two:64 * two + 64, :], v[two].partition_broadcast(64), "d_prep", writes=["ldS"], **NCK)
        self.dS5 = self.R0.take([128, 4])
        p.dma("sp", self.dS5, I["s5_d"].rearrange("(t p) -> p t", p=128), "d_prep", writes=["dS5"], **NCK)
        for sem, keys in (("d_Lp", ["Lp"]), ("d_Fp", ["Fp"]),
                          ("d_prep", ["Bnr", "Bni", "Cnr", "Cni", "aSr", "aSi", "ldS", "dS5"])):
            for k_ in keys:
                p.last_write[k_] = (sem, p.count[sem])

    def tr_group(self, srcs, evac):
        p = self.p
        for g0 in range(0, len(srcs), 4):
            grp = srcs[g0:g0 + 4]
            b = self.bank()
            bv = self.bk(b).rearrange("p (a c) -> p a c", a=4)
            for i, (ap, keys) in enumerate(grp):
                p.op("pe", lambda e, i=i, ap=ap, bv=bv: e.transpose(bv[:, i, :], ap, self.ident_f),
                     reads=list(keys) + ["ident_f"], writes=[("bank", b)], inc=(i == len(grp) - 1))
            evac(bv, g0, len(grp), b)

    def param_transposes(self):
        p = self.p
        CP = self.CP
        NJ = DFF // 128
        Lp, Fp, hsp = self.Lp, self.Fp, self.hsp

        def ev_L(bv, g0, n, b):
            CP("act", self.LRUc[:, g0:g0 + n, :], bv[:, 0:n, 0:72], [("bank", b)], ("LRUc", g0 // 4))
        self.tr_group([(Lp[:, 128 * j:128 * (j + 1)], ["Lp"]) for j in range(8)], ev_L)

        def ev_F(bv, g0, n, b):
            CP("dve", self.Fc[:, g0:g0 + n, :], bv[:, 0:n, 0:36], [("bank", b)], ("Fc", g0 // 4))
        self.tr_group([(Fp[:, 128 * j:128 * (j + 1)], ["Fp"]) for j in range(NJ)], ev_F)
        self.LRUc_keys = [("LRUc", i) for i in range(2)]
        self.Fc_keys = [("Fc", i) for i in range(NJ // 4)]
        for part in range(2):
            p.dma("sp", hsp[part][0:16, :], self.I[("st_s5_re", "st_s5_im")[part]], "d_hsp", writes=["hsp"])
            def ev_h(bv, g0, n, b, part=part):
                CP("act", self.h0S[:, part, g0:g0 + n, :], bv[:, 0:n, 0:16], [("bank", b)], ("h0S", part, g0 // 4))
            self.tr_group([(hsp[part][:, 128 * q:128 * (q + 1)], ["hsp"]) for q in range(16)], ev_h)
        self.h0S_keys = [("h0S", part, i) for part in range(2) for i in range(4)]
        CP("dve", self.h0S_bf, self.h0S, self.h0S_keys, "h0S_bf")
        sh3 = [128, 16, 32]
        self.CTr = self.R1.take(sh3); self.CTi = self.R1.take(sh3)
        for (src, dst, k_src, k_dst) in ((self.Cnr, self.CTr, "Cnr", "CTr"), (self.Cni, self.CTi, "Cni", "CTi")):
            def ev_c(bv, g0, n, b, dst=dst, k_dst=k_dst):
                CP("dve", dst[:, 4 * g0:4 * (g0 + n), :].rearrange("p (a q) c -> p a (q c)", a=n), bv[:, 0:n, :], [("bank", b)], (k_dst, g0))
            self.tr_group([(src[:, qh, :], [k_src]) for qh in range(4)], ev_c)
        self.CT_keys = {"CTr": [("CTr", 0)], "CTi": [("CTi", 0)]}

    def s5_prep(self):
        nc, p, I = self.nc, self.p, self.I
        R0, R1, RS = self.R0, self.R1, self.RS
        TT, TS, STT, ACT, CP, MS = self.TT, self.TS, self.STT, self.ACT, self.CP, self.MS
        W1, W4a, W4b = self.W1, self.W4a, self.W4b
        aSr, aSi, ldS = self.aSr, self.aSi, self.ldS
        CTr, CTi = self.CTr, self.CTi
        kCTr, kCTi = self.CT_keys["CTr"], self.CT_keys["CTi"]
        shS = [128, 16]
        sh3 = [128, 16, 32]
        I32 = mybir.dt.int32

        def sincos_base(theta, t, kf, A, C2, cs, sn, k_th, k_t, k_kf, k_A, k_C2, k_cs, k_sn):
            TS("dve", t, theta, 1.0 / (2 * PI), ALU.mult, [k_th], k_t, s2=16.0, op1=ALU.add)
            CP("dve", kf.bitcast(I32), t, [k_t], k_kf)
            CP("dve", kf, kf.bitcast(I32), [k_kf], k_kf)
            TT("dve", t, t, kf, ALU.subtract, [k_t, k_kf], k_t)
            ACT(A, t, AF.Sin, [k_t], k_A, scale=PI)
            ACT(C2, t, AF.Sin, [k_t], k_C2, scale=PI / 2)
            TT("dve", C2, C2, C2, ALU.mult, [k_C2], k_C2)
            TS("dve", C2, C2, -2.0, ALU.mult, [k_C2], k_C2, s2=1.0, op1=ALU.add)
            STT("dve", sn, A, 2.0, C2, ALU.mult, ALU.mult, [k_A, k_C2], k_sn)
            TT("dve", cs, A, A, ALU.mult, [k_A], k_cs)
            TS("dve", cs, cs, -2.0, ALU.mult, [k_cs], k_cs, s2=1.0, op1=ALU.add)

        def tS():
            return RS.take(shS)

        dtS = tS(); adtS = tS(); thS = tS()
        ACT(dtS, ldS, AF.Exp, ["ldS"], "dtS")
        TT("dve", adtS, aSr, dtS, ALU.mult, ["aSr", "dtS"], "adtS")
        TT("dve", thS, aSi, dtS, ALU.mult, ["aSi", "dtS"], "thS")
        self.rhoS = tS()
        ACT(self.rhoS, adtS, AF.Exp, ["adtS"], "rhoS")
        cosS = []; sinS = []; nsinS = []; lamr = [None]; lami = [None]
        c1S = tS(); s1S = tS(); w1 = tS(); w2 = tS(); w3 = tS(); w4 = tS()
        sincos_base(thS, w1, w2, w3, w4, c1S, s1S, "thS", "Sw1", "Sw2", "Sw3", "Sw4", "S1cs", "S1sn")
        for s in range(LCH + 1):
            if s == 0:
                cs = tS(); sn = tS()
                MS("dve", cs, 1.0, "S0cs"); MS("dve", sn, 0.0, "S0sn")
            elif s == 1:
                cs, sn = c1S, s1S
            else:
                cs = tS(); sn = tS(); ua = tS(); ub = tS()
                pc, ps_ = cosS[s - 1], sinS[s - 1]
                kpc, kps = "S%dcs" % (s - 1), "S%dsn" % (s - 1)
                TT("dve", ua, pc, c1S, ALU.mult, [kpc, "S1cs"], "Sua%d" % s)
                TT("dve", ub, ps_, s1S, ALU.mult, [kps, "S1sn"], "Sub%d" % s)
                TT("dve", cs, ua, ub, ALU.subtract, ["Sua%d" % s, "Sub%d" % s], "S%dcs" % s)
                TT("dve", ua, ps_, c1S, ALU.mult, [kps, "S1cs", "S%dcs" % s], "Sua%d" % s)
                TT("dve", ub, pc, s1S, ALU.mult, [kpc, "S1sn", "S%dcs" % s], "Sub%d" % s)
                TT("dve", sn, ua, ub, ALU.add, ["Sua%d" % s, "Sub%d" % s], "S%dsn" % s)
            cosS.append(cs); sinS.append(sn)
            ns = tS()
            TS("dve", ns, sn, -1.0, ALU.mult, ["S%dsn" % s], "nS%dsn" % s)
            nsinS.append(ns)
            if s >= 1:
                r = tS(); a = tS(); b_ = tS()
                ACT(r, adtS, AF.Exp, ["adtS"], "rpow%d" % s, scale=float(s))
                TT("dve", a, r, cs, ALU.mult, ["rpow%d" % s, "S%dcs" % s], "lamr%d" % s)
                TT("dve", b_, r, sn, ALU.mult, ["rpow%d" % s, "S%dsn" % s], "lami%d" % s)
                lamr.append(a); lami.append(b_)
        self.cosS, self.sinS, self.nsinS, self.lamr, self.lami = cosS, sinS, nsinS, lamr, lami
        self.nlami4 = tS()
        TS("dve", self.nlami4, lami[4], -1.0, ALU.mult, ["lami4"], "nlami4")
        ta = R1.take(sh3); tb = R1.take(sh3)

        def bc(t):
            return t.unsqueeze(2).to_broadcast(sh3)

        for s in range(LCH):
            for (Wt, cr_t, ci_t, kr, ki, nm) in (
                (W4a, cosS[s], sinS[s], "S%dcs" % s, "S%dsn" % s, "W4a"),
                (W4b, lamr[s + 1], lami[s + 1], "lamr%d" % (s + 1), "lami%d" % (s + 1), "W4b"),
            ):
                TT("dve", ta, CTr, bc(cr_t), ALU.mult, kCTr + [kr], "w4ta")
                TT("dve", tb, CTi, bc(ci_t), ALU.mult, kCTi + [ki], "w4tb")
                TT("dve", Wt[:, :, s, 0, :], ta, tb, ALU.subtract, ["w4ta", "w4tb"], (nm, s, 0))
                TT("dve", ta, CTr, bc(ci_t), ALU.mult, kCTr + [ki], "w4ta")
                TT("dve", tb, CTi, bc(cr_t), ALU.mult, kCTi + [kr], "w4tb")
                TT("dve", ta, ta, tb, ALU.add, ["w4ta", "w4tb"], "w4ta")
                TS("dve", Wt[:, :, s, 1, :], ta, -1.0, ALU.mult, ["w4ta"], (nm, s, 1))
        Bnr, Bni = self.Bnr, self.Bni
        nr = tS(); den = tS(); u1 = tS(); cfr = tS(); cfi = tS()
        TS("dve", nr, lamr[1], -1.0, ALU.add, ["lamr1"], "nr")
        TT("dve", den, aSr, aSr, ALU.mult, ["aSr"], "den")
        TT("dve", u1, aSi, aSi, ALU.mult, ["aSi"], "u1")
        TT("dve", den, den, u1, ALU.add, ["den", "u1"], "den")
        p.op("dve", lambda e: e.reciprocal(out=den, in_=den), reads=["den"], writes=["den"])
        TT("dve", cfr, nr, aSr, ALU.mult, ["nr", "aSr"], "cfr")
        TT("dve", u1, lami[1], aSi, ALU.mult, ["lami1", "aSi", "den"], "u1")
        TT("dve", cfr, cfr, u1, ALU.add, ["cfr", "u1"], "cfr")
        TT("dve", cfr, cfr, den, ALU.mult, ["cfr", "den"], "cfr")
        TT("dve", cfi, lami[1], aSr, ALU.mult, ["lami1", "aSr"], "cfi")
        TT("dve", u1, nr, aSi, ALU.mult, ["nr", "aSi", "cfr"], "u1")
        TT("dve", cfi, cfi, u1, ALU.subtract, ["cfi", "u1"], "cfi")
        TT("dve", cfi, cfi, den, ALU.mult, ["cfi", "den"], "cfi")
        BbR = R1.take(sh3); BbI = R1.take(sh3)
        TT("pool", ta, Bnr, bc(cfr), ALU.mult, ["Bnr", "cfr", "w4ta"], "w4ta")
        TT("pool", tb, Bni, bc(cfi), ALU.mult, ["Bni", "cfi", "w4tb"], "w4tb")
        TT("pool", BbR, ta, tb, ALU.subtract, ["w4ta", "w4tb"], "BbR")
        TT("pool", ta, Bni, bc(cfr), ALU.mult, ["Bni", "cfr"], "w4ta")
        TT("pool", tb, Bnr, bc(cfi), ALU.mult, ["Bnr", "cfi"], "w4tb")
        TT("pool", BbI, ta, tb, ALU.add, ["w4ta", "w4tb"], "BbI")
        W1S = self.LF.bitcast(BF16).rearrange("p (a s b c) -> p a s b c", a=4, s=LCH, b=2)
        MS("pool", W1S[:, 0, 0, 0, 0:2], 0.0, "LFfree", r=["Lp", "Fp"])
        p.last_write["Lp"] = p.last_write["LFfree"]; p.last_write["Fp"] = p.last_write["LFfree"]
        tc_ = R1.take(sh3); td_ = R1.take(sh3)
        v4 = lambda t: t.rearrange("p (a b) c -> p a (b c)", a=4)
        for s in range(LCH):
            kc, ks = "S%dcs" % s, "S%dsn" % s
            TT("pool", tc_, BbR, bc(cosS[s]), ALU.mult, ["BbR", kc], "w1tc")
            TT("pool", td_, BbI, bc(sinS[s]), ALU.mult, ["BbI", ks], "w1td")
            TT("pool", W1S[:, :, s, 0, :], v4(tc_), v4(td_), ALU.add, ["w1tc", "w1td", "LFfree"], ("W1S", s, 0))
            TT("pool", tc_, BbI, bc(cosS[s]), ALU.mult, ["BbI", kc], "w1tc")
            TT("pool", td_, BbR, bc(sinS[s]), ALU.mult, ["BbR", ks], "w1td")
            TT("pool", W1S[:, :, s, 1, :], v4(tc_), v4(td_), ALU.subtract, ["w1tc", "w1td", "LFfree"], ("W1S", s, 1))
        for qh in range(4):
            for h in range(2):
                b = self.bank()
                pt = self.bk(b).bitcast(BF16)
                n = 0
                for s in range(4 * h, 4 * h + 4):
                    for part in range(2):
                        p.op("pe", lambda e, pt=pt, n=n, qh=qh, s=s, part=part: e.transpose(
                            pt[:, 128 * n:128 * (n + 1)], W1S[:, qh, s, part, :], self.ident_b),
                            reads=[("W1S", s, part), "ident_b"], writes=[("bank", b)], inc=(n == 7))
                        n += 1
                CP("act", W1[:, qh, 4 * h:4 * h + 4, :, :].rearrange("p a b c -> p (a b c)"), pt, [("bank", b)], ("W1", qh, h))
        self.W1_keys = [("W1", qh, h) for qh in range(4) for h in range(2)]
        self.W4_keys = [(nm, s, part) for nm in ("W4a", "W4b") for s in range(LCH) for part in range(2)]
        self.mur = RS.take([128, 16, NLEV]); self.mui = RS.take([128, 16, NLEV]); self.muni = RS.take([128, 16, NLEV])
        angc = RS.take([128, 16, NLEV]); angs = RS.take([128, 16, NLEV]); rmag = RS.take([128, 16, NLEV])
        CP("dve", angc[:, :, 0], cosS[LCH], ["S%dcs" % LCH], ("angc", 0))
        CP("dve", angs[:, :, 0], sinS[LCH], ["S%dsn" % LCH], ("angs", 0))
        sa = tS(); sb_ = tS()
        for j in range(NLEV):
            if j >= 1:
                TT("dve", sa, angc[:, :, j - 1], angc[:, :, j - 1], ALU.mult, [("angc", j - 1)], "sq_a")
                TT("dve", sb_, angs[:, :, j - 1], angs[:, :, j - 1], ALU.mult, [("angs", j - 1)], "sq_b")
                TT("dve", angc[:, :, j], sa, sb_, ALU.subtract, ["sq_a", "sq_b"], ("angc", j))
                TT("dve", sa, angc[:, :, j - 1], angs[:, :, j - 1], ALU.mult, [("angc", j - 1), ("angs", j - 1)], "sq_a")
                TS("dve", angs[:, :, j], sa, 2.0, ALU.mult, ["sq_a"], ("angs", j))
            ACT(rmag[:, :, j], adtS, AF.Exp, ["adtS"], ("rmag", j), scale=float(LCH * (1 << j)))
            TT("dve", self.mur[:, :, j], rmag[:, :, j], angc[:, :, j], ALU.mult, [("rmag", j), ("angc", j)], ("mur", j))
            TT("dve", self.mui[:, :, j], rmag[:, :, j], angs[:, :, j], ALU.mult, [("rmag", j), ("angs", j)], ("mui", j))
        for j in range(NLEV):
            TS("dve", self.muni[:, :, j], self.mui[:, :, j], -1.0, ALU.mult, [("mui", j)], ("muni", j))
        self.mu_keys = [(nm, j) for nm in ("mur", "mui", "muni") for j in range(NLEV)]
        self.rp8 = RS.take([128, 16, LCH]); self.rp4 = RS.take([128, 16, 4])
        CP("dve", self.rp8, self.rhoS.unsqueeze(2).to_broadcast([128, 16, LCH]), ["rhoS"], "rp8")
        MS("dve", self.rp8[:, :, 0:1], 0.0, "rp8", r=["rp8"])
        CP("dve", self.rp4, self.rhoS.unsqueeze(2).to_broadcast([128, 16, 4]), ["rhoS"], "rp4")
        MS("dve", self.rp4[:, :, 0:1], 0.0, "rp4", r=["rp4"])
    def s5_main(self, u_bf):
        nc, p = self.nc, self.p
        R0, R1 = self.R0, self.R1
        TT, TS, STT, ACT, CP, MS = self.TT, self.TS, self.STT, self.ACT, self.CP, self.MS
        W1, W4a, W4b = self.W1, self.W4a, self.W4b
        cosS, sinS, nsinS, lamr, lami = self.cosS, self.sinS, self.nsinS, self.lamr, self.lami
        z2 = self.z2
        NSET = 2
        pat = [R1.take([128, 2, 512]) for i in range(NSET)]
        pats = [R1.take([128, 2, 64]) for i in range(NSET)]
        gzf = [R1.take([128, 2, 512]) for i in range(2)]
        gzfs = [R1.take([128, 2, 64]) for i in range(2)]
        gzb = [R1.take([128, 2, T], BF16) for i in range(NSET)]
        HA = [R1.take([128, 2, PAD + NCH]) for i in range(NSET)]
        HB = [R1.take([128, 2, PAD + NCH]) for i in range(NSET)]
        Hpb = [R1.take([128, 2, NCH], BF16) for i in range(NSET)]
        for i in range(NSET):
            MS("pool", HA[i], 0.0, ("HA", i))
            MS("pool", HB[i], 0.0, ("HB", i))
        yf = R1.take([128, 512]); gq = R1.take([128, 512]); ga = R1.take([128, 512]); gt = gq
        VP = self.ps[0]
        SVB = 2
        YB = {0: 4, 512: 5, 1024: 6, 1536: 7}
        PIECES = self.PIECES
        nseg = [0]

        def stageA(q):
            qh, ql = q // 4, q % 4
            hb = q % NSET
            kw = dict(tile_position=(96, 0)) if ql == 3 else {}
            CP("pool", pat[hb].rearrange("p a (c s) -> p (a c) s", s=LCH),
               self.rp8[:, q:q + 1, :].to_broadcast([128, 2 * 512 // LCH, LCH]), ["rp8"], ("pat", hb))
            CP("pool", pats[hb].rearrange("p a (c s) -> p (a c) s", s=4),
               self.rp4[:, q:q + 1, :].to_broadcast([128, 2 * 64 // 4, 4]), ["rp4"], ("pats", hb))
            for (c0, w) in PIECES:
                samp = (w == 64)
                L = 4 if samp else LCH
                sl = nseg[0] % 2
                nseg[0] += 1
                if samp:
                    vre = self.bk(SVB)[:, 0:64]; vim = self.bk(SVB)[:, 64:128]
                    vkeys = [("bank", SVB)]
                else:
                    vre = VP[:, 0, :]; vim = VP[:, 1, :]
                    vkeys = [("bank", 0), ("bank", 1)]
                nmm = 0
                for s in range(L):
                    for part in range(2):
                        nmm += 1
                        outv = (vre, vim)[part][:, s:w:L]
                        p.op("pe", lambda e, outv=outv, s=s, part=part, qh=qh, ql=ql, c0=c0, w=w, L=L, kw=kw: e.matmul(
                            outv, W1[32 * ql:32 * ql + 32, qh, s, part, :],
                            u_bf[32 * ql:32 * ql + 32, qh, c0 + s:c0 + w:L], start=True, stop=True, **kw),
                            reads=self.W1_keys + [("u_bf", qh, c0)], writes=vkeys, inc=(nmm == 2 * L))
                if not samp:
                    go = gzf[sl]; gkey = ("gzf", sl)
                    p.op("dve", lambda e, go=go, hb=hb: e.tensor_tensor_scan(
                        out=go.rearrange("p a c -> p (a c)"), data0=pat[hb].rearrange("p a c -> p (a c)"),
                        data1=VP[:].rearrange("p a c -> p (a c)"), initial=0.0, op0=ALU.mult, op1=ALU.add),
                        reads=[("pat", hb)] + vkeys, writes=[gkey])
                else:
                    go = gzfs[sl]; gkey = ("gzfs", sl)
                    p.op("dve", lambda e, go=go, hb=hb: e.tensor_tensor_scan(
                        out=go.rearrange("p a c -> p (a c)"), data0=pats[hb].rearrange("p a c -> p (a c)"),
                        data1=self.bk(SVB)[:, 0:128], initial=0.0, op0=ALU.mult, op1=ALU.add),
                        reads=[("pats", hb)] + vkeys, writes=[gkey])
                CP("act", gzb[hb][:, :, c0:c0 + w], go, [gkey], ("gzb", hb, c0))
                if not samp:
                    k0 = PAD + c0 // LCH
                    nchunk = w // LCH
                    er = go[:, 0, LCH - 1:512:LCH]; ei = go[:, 1, LCH - 1:512:LCH]
                    c7 = cosS[LCH - 1][:, q:q + 1]; s7 = sinS[LCH - 1][:, q:q + 1]; ns7 = nsinS[LCH - 1][:, q:q + 1]
                    kk = [gkey, "S%dcs" % (LCH - 1), "S%dsn" % (LCH - 1), "nS%dsn" % (LCH - 1)]
                    hk = ("HA", hb)
                    dr = HA[hb][:, 0, k0:k0 + nchunk]; di = HA[hb][:, 1, k0:k0 + nchunk]
                    TS("dve", dr, er, c7, ALU.mult, kk, hk)
                    STT("dve", dr, ei, ns7, dr, ALU.mult, ALU.add, kk + [hk], hk)
                    TS("dve", di, er, s7, ALU.mult, kk + [hk], hk)
                    STT("dve", di, ei, c7, di, ALU.mult, ALU.add, kk + [hk], hk)
                else:
                    er = go[:, 0, 3:64:4]; ei = go[:, 1, 3:64:4]
                    c3 = cosS[3][:, q:q + 1]; s3 = sinS[3][:, q:q + 1]; ns3 = nsinS[3][:, q:q + 1]
                    l4r = lamr[4][:, q:q + 1]; l4i = lami[4][:, q:q + 1]; nl4i = self.nlami4[:, q:q + 1]
                    kk = [gkey, "S3cs", "S3sn", "nS3sn", "lamr4", "lami4", "nlami4"] + self.h0S_keys
                    fr = self.Hfin[:, 0, q, 1:17]; fi = self.Hfin[:, 1, q, 1:17]
                    h0r = self.h0S[:, 0, q, :]; h0i = self.h0S[:, 1, q, :]
                    fk = ("Hfin", q)
                    TS("dve", fr, er, c3, ALU.mult, kk, fk)
                    STT("dve", fr, ei, ns3, fr, ALU.mult, ALU.add, kk + [fk], fk)
                    STT("dve", fr, h0r, l4r, fr, ALU.mult, ALU.add, kk + [fk], fk)
                    STT("dve", fr, h0i, nl4i, fr, ALU.mult, ALU.add, kk + [fk], fk)
                    TS("dve", fi, er, s3, ALU.mult, kk + [fk], fk)
                    STT("dve", fi, ei, c3, fi, ALU.mult, ALU.add, kk + [fk], fk)
                    STT("dve", fi, h0r, l4i, fi, ALU.mult, ALU.add, kk + [fk], fk)
                    STT("dve", fi, h0i, l4r, fi, ALU.mult, ALU.add, kk + [fk], fk)

        def stageB(q):
            hb = q % NSET
            src, dst = HA[hb], HB[hb]
            skey, dkey = ("HA", hb), ("HB", hb)
            for j in range(NLEV):
                d = 1 << j
                mr = self.mur[:, q, j:j + 1]; mi = self.mui[:, q, j:j + 1]; mni = self.muni[:, q, j:j + 1]
                mk = [("mur", j), ("mui", j), ("muni", j)]
                S0 = src[:, 0, PAD:PAD + NCH]; S1 = src[:, 1, PAD:PAD + NCH]
                Z0 = src[:, 0, PAD - d:PAD + NCH - d]; Z1 = src[:, 1, PAD - d:PAD + NCH - d]
                D0 = dst[:, 0, PAD:PAD + NCH]; D1 = dst[:, 1, PAD:PAD + NCH]
                STT("dve", D0, Z0, mr, S0, ALU.mult, ALU.add, [skey] + mk, dkey)
                STT("dve", D0, Z1, mni, D0, ALU.mult, ALU.add, [skey, dkey] + mk, dkey)
                STT("dve", D1, Z0, mi, S1, ALU.mult, ALU.add, [skey, dkey] + mk, dkey)
                STT("dve", D1, Z1, mr, D1, ALU.mult, ALU.add, [skey, dkey] + mk, dkey)
                src, dst = dst, src
                skey, dkey = dkey, skey
            CP("pool", Hpb[hb], HA[hb][:, :, PAD - 1:PAD - 1 + NCH], [("HA", hb)], ("Hpb", hb))
            CP("pool", self.Hfin[:, :, q, 0:1], HA[hb][:, :, PAD + NCH - 1:PAD + NCH], [("HA", hb)], ("Hfin0", q))

        def stageC(q):
            qh, ql = q // 4, q % 4
            hb = q % NSET
            kw = dict(tile_position=(0, 96)) if ql == 3 else {}
            for (c0, w) in PIECES:
                samp = (w == 64)
                L = 4 if samp else LCH
                if samp:
                    ybank = SVB
                    yall = self.bk(SVB)[:, 128:192]
                else:
                    ybank = YB[c0]
                    yall = self.bk(ybank)
                for s in range(L):
                    outv = yall[32 * ql:32 * ql + 32, s:w:L]
                    if samp:
                        hr = self.h0S_bf[:, 0, q, :]; hi = self.h0S_bf[:, 1, q, :]
                        hkeys = ["h0S_bf"]
                    else:
                        k0 = c0 // LCH
                        hr = Hpb[hb][:, 0, k0:k0 + w // L]; hi = Hpb[hb][:, 1, k0:k0 + w // L]
                        hkeys = [("Hpb", hb)]
                    ops = [
                        (W4a[:, q, s, 0, :], gzb[hb][:, 0, c0 + s:c0 + w:L]),
                        (W4a[:, q, s, 1, :], gzb[hb][:, 1, c0 + s:c0 + w:L]),
                        (W4b[:, q, s, 0, :], hr),
                        (W4b[:, q, s, 1, :], hi),
                    ]
                    for i, (lh, rh) in enumerate(ops):
                        last = (i == 3 and s == L - 1)
                        p.op("pe", lambda e, outv=outv, lh=lh, rh=rh, i=i, kw=kw: e.matmul(
                            outv, lh, rh, start=(i == 0), stop=(i == 3), **kw),
                            reads=self.W4_keys + [("gzb", hb, c0)] + hkeys, writes=[("bank", ybank)], inc=last)

        def stageY(qh):
            for (c0, w) in PIECES:
                samp = (w == 64)
                if samp:
                    ybank = SVB; ysrc = self.bk(SVB)[:, 128:192]
                else:
                    ybank = YB[c0]; ysrc = self.bk(ybank)
                yv = yf[:, 0:w]; qv = gq[:, 0:w]; av = ga[:, 0:w]; tv = gt[:, 0:w]
                STT("dve", yv, u_bf[:, qh, c0:c0 + w], self.dS5[:, qh:qh + 1], ysrc, ALU.mult, ALU.add,
                    [("u_bf", qh, c0), "dS5", ("bank", ybank)], "s5yf")
                if qh == 0:
                    self.tap("ys5_%d" % c0, yv, ["s5yf"])
                self.gelu2(yv, qv, av, tv, z2[:, qh, c0:c0 + w], "s5yf", "s5gq", "s5ga", "s5gq", ("z2", qh, c0))

        for qh in range(4):
            qs = [4 * qh + i for i in range(4)]
            stageA(qs[0]); stageB(qs[0])
            for i in range(1, 4):
                stageA(qs[i])
                stageC(qs[i - 1])
                stageB(qs[i])
            stageC(qs[3])
            stageY(qh)

    def gelu2(self, yv, qv, av, tv, outv, ky, kq, ka, kt, kout):
        self.ACT(qv, yv, AF.Square, [ky], kq)
        self.STT("dve", av, qv, GK * GC, yv, ALU.mult, ALU.mult, [kq, ky], ka)
        self.STT("dve", av, yv, GK, av, ALU.mult, ALU.add, [ky, ka], ka)
        self.ACT(tv, av, AF.Tanh, [ka], kt)
        self.STT("dve", outv, tv, 1.0, yv, ALU.add, ALU.mult, [kt, ky], kout)

    def lru_stage(self):
        nc, p, I, O = self.nc, self.p, self.I, self.O
        R0, R1 = self.R0, self.R1
        TT, TS, STT, ACT, CP, MS = self.TT, self.TS, self.STT, self.ACT, self.CP, self.MS
        NCK = dict(allow_slow_non_contiguous=True)
        xT, z2 = self.xT, self.z2
        PIECES = self.PIECES
        self.free_banks = list(range(8))
        LRUc = self.LRUc
        LK = self.LRUc_keys
        baT = R0.take([128, 8]); bxT = R0.take([128, 8]); sc8 = R0.take([128, 8]); hsc8 = R0.take([128, 8])
        TS("dve", baT, LRUc[:, :, 69], 0.5, ALU.mult, LK, "baT")
        TS("dve", bxT, LRUc[:, :, 70], 0.5, ALU.mult, LK, "bxT")
        ACT(sc8, LRUc[:, :, 71], AF.Exp, LK, "sc8", scale=-1.0)
        ACT(sc8, sc8, AF.Ln, ["sc8"], "sc8", bias=1.0)
        TS("dve", hsc8, sc8, -4.0, ALU.mult, ["sc8"], "hsc8")
        TS("dve", sc8, sc8, -8.0, ALU.mult, ["sc8", "hsc8"], "sc8")
        Wg = R0.take([128, 8, 2, 128], BF16)
        MS("pool", Wg, 0.0, "Wg")
        for gi, nm in ((0, "lru_wa"), (1, "lru_wx")):
            v = I[nm].rearrange("(j two) i o -> two i j o", two=2)
            for par in range(2):
                p.dma("pool", Wg[64 * par:64 * par + 64, :, gi, 64 * par:64 * par + 64], v[par], "d_wg", writes=["Wg"])
        p.last_write["Wg"] = ("d_wg", p.count["d_wg"])
        hfinL = R0.take([128, 8, 17])
        self.merged2 = R1.take([128, 8, T], BF16)
        merged2 = self.merged2
        self.merged_end = R1.pos
        xl_sb = R1.take([128, 3 + TP]); xs_sb = R1.take([128, 16, 7])
        abuf = R1.take([128, T]); a2buf = R1.take([128, T]); ixbuf = R1.take([128, T])
        hbuf = [R1.take([128, T])]
        tail_sb = R1.take([NTAIL, D])
        MS("pool", xl_sb[:, 0:3], 0.0, ("xl", -1))
        NP = 2
        ytmp = [R1.take([128, 512]) for i in range(2)]
        NXC = 5
        xc = [R1.take([128, 512]) for i in range(NXC)]
        xcb = [R1.take([128, 512], BF16) for i in range(2)]
        rp_ = [R1.take([128, 512]) for i in range(2)]
        ip_ = [R1.take([128, 512]) for i in range(2)]
        glp = [R1.take([128, 512]) for i in range(2)]
        gsp = [R1.take([128, 512]) for i in range(2)]
        gbp = [R1.take([128, 512]) for i in range(2)]
        t1p = [R1.take([128, 512]) for i in range(2)]
        t16 = R1.take([128, 16])
        wsl = [R1.take([128, 8, 128], BF16) for i in range(2)]
        wsg = [R1.take([128, 8, 2, 128], BF16) for i in range(2)]
        wgl = [R1.take([128, 4, 2, 128], BF16) for i in range(2)]
        hb = hbuf[0]
        XB = [0, 1]; GAB = [2, 3]; GXB = [4, 5]; POSTB = [6, 7]
        allp = lambda nm: [(nm, pi) for pi in range(len(PIECES))]

        def load_pre(j):
            sl = j % 2
            p.dma("pool", wsl[sl], I["w_in"][:, 128 * j:128 * (j + 1)].rearrange("(k p) n -> p k n", p=128), "d_wsl%d" % sl,
                  writes=[("wsl", sl)])

        def load_post(j):
            sl = j % 2
            wv = lambda c: I["w_in"][:, c:c + 128].rearrange("(k p) n -> p k n", p=128)
            gv = lambda c: I["w_glu"][:, c:c + 128].rearrange("(k p) n -> p k n", p=128)
            p.dma("pool", wsg[sl][:, :, 0, :], wv(1536 + 128 * j), "d_wsg%d_0" % sl, writes=[("wsg", sl, 0)])
            p.dma("pool", wsg[sl][:, :, 1, :], wv(2560 + 128 * j), "d_wsg%d_1" % sl, writes=[("wsg", sl, 1)])
            p.dma("pool", wgl[sl][:, :, 0, :], gv(128 * j), "d_wgl%d_0" % sl, writes=[("wgl", sl, 0)])
            p.dma("pool", wgl[sl][:, :, 1, :], gv(1024 + 128 * j), "d_wgl%d_1" % sl, writes=[("wgl", sl, 1)])

        items = [dict(j=j, pi=pi, c0=c0, w=w) for j in range(8) for pi, (c0, w) in enumerate(PIECES)]

        def P0(t, it):
            j, c0, w = it["j"], it["c0"], it["w"]
            sl = j % 2
            if it["pi"] == 0 and j + 1 < 8:
                load_pre(j + 1)
            b = XB[t % 2]
            for k in range(8):
                p.op("pe", lambda e, k=k, b=b, sl=sl, c0=c0, w=w: e.matmul(
                    self.bk(b)[:, 0:w], wsl[sl][:, k, :], xT[:, k, c0:c0 + w], start=(k == 0), stop=(k == 7)),
                    reads=[("wsl", sl)] + self.xT_keys(c0, w), writes=[("bank", b)], inc=(k == 7))
            if it["pi"] == len(PIECES) - 1:
                tb = POSTB[1]
                for k in range(8):
                    p.op("pe", lambda e, k=k, tb=tb, sl=sl: e.matmul(self.bk(tb)[0:NTAIL, 0:128], xT[:, k, TAIL0:T], wsl[sl][:, k, :],
                                                                     start=(k == 0), stop=(k == 7)),
                         reads=[("wsl", sl)] + self.xT_keys(TAIL0, NTAIL), writes=[("bank", tb)], inc=(k == 7))
                CP("act", tail_sb[:, 128 * j:128 * (j + 1)], self.bk(tb)[0:NTAIL, 0:128], [("bank", tb)], ("tail", j))

        def P1(t, it):
            j, pi, c0, w = it["j"], it["pi"], it["c0"], it["w"]
            b = XB[t % 2]
            ps = self.bk(b)[:, 0:w]
            yv = ytmp[t % 2][:, 0:w]
            ACT(yv, ps, AF.Identity, [("bank", b)] + LK, ("ytmp", t % 2), scale=LRUc[:, j, 67:68], bias=LRUc[:, j, 68:69])
            if w != 64:
                CP("act", xl_sb[:, 3 + c0:3 + c0 + w], ps, [("bank", b)], ("xl", pi))
            else:
                CP("pool", xs_sb[:, :, 0:3], LRUc[:, j, 0:48].rearrange("p (b k) -> p b k", k=3), LK, ("xs", "st"))
                CP("act", xs_sb[:, :, 3:7], ps.rearrange("p (b s) -> p b s", s=4), [("bank", b)], ("xs", "new"))

        def P2(t, it):
            j, pi, c0, w = it["j"], it["pi"], it["c0"], it["w"]
            cw = [LRUc[:, j, 64 + k:65 + k] for k in range(4)]
            yk = ("ytmp", t % 2); xk_ = ("xc", t % NXC)
            yv = ytmp[t % 2][:, 0:w]; xcv = xc[t % NXC][:, 0:w]
            if w != 64:
                xk = [("xl", pi), ("xl", pi - 1), yk] + LK
                STT("dve", yv, xl_sb[:, c0 + 2:c0 + 2 + w], cw[2], yv, ALU.mult, ALU.add, xk, yk)
                STT("dve", yv, xl_sb[:, c0 + 1:c0 + 1 + w], cw[1], yv, ALU.mult, ALU.add, xk, yk)
                STT("dve", xcv, xl_sb[:, c0:c0 + w], cw[0], yv, ALU.mult, ALU.add, xk, xk_)
            else:
                xk = [("xs", "st"), ("xs", "new"), yk] + LK
                y3 = yv.rearrange("p (b s) -> p b s", s=4); xc3 = xcv.rearrange("p (b s) -> p b s", s=4)
                STT("dve", y3, xs_sb[:, :, 2:6], cw[2], y3, ALU.mult, ALU.add, xk, yk)
                STT("dve", y3, xs_sb[:, :, 1:5], cw[1], y3, ALU.mult, ALU.add, xk, yk)
                STT("dve", xc3, xs_sb[:, :, 0:4], cw[0], y3, ALU.mult, ALU.add, xk, xk_)
            if j == 0:
                self.tap("xc_%d" % c0, xcv, [xk_])
            CP("pool", xcb[t % 2][:, 0:w], xcv, [xk_], ("xcb", t % 2))

        def P3(t, it):
            j, w = it["j"], it["w"]
            for (bb, gi) in ((GAB[t % 2], 0), (GXB[t % 2], 1)):
                p.op("pe", lambda e, bb=bb, gi=gi, j=j, t=t, w=w: e.matmul(self.bk(bb)[:, 0:w], Wg[:, j, gi, :], xcb[t % 2][:, 0:w],
                                                                          start=True, stop=True),
                     reads=["Wg", ("xcb", t % 2)], writes=[("bank", bb)])

        def P4(t, it):
            j, w = it["j"], it["w"]
            ACT(rp_[t % 2][:, 0:w], self.bk(GAB[t % 2])[:, 0:w], AF.Tanh, [("bank", GAB[t % 2]), "baT"], ("rp", t % 2),
                scale=0.5, bias=baT[:, j:j + 1])
            ACT(ip_[t % 2][:, 0:w], self.bk(GXB[t % 2])[:, 0:w], AF.Tanh, [("bank", GXB[t % 2]), "bxT"], ("ip", t % 2),
                scale=0.5, bias=bxT[:, j:j + 1])

        def P5(t, it):
            j, pi, c0, w = it["j"], it["pi"], it["c0"], it["w"]
            rv = rp_[t % 2][:, 0:w]; iv = ip_[t % 2][:, 0:w]; xcv = xc[t % NXC][:, 0:w]
            ACT(abuf[:, c0:c0 + w], rv, AF.Exp, [("rp", t % 2), "hsc8"], ("abuf", pi), scale=hsc8[:, j:j + 1], bias=hsc8[:, j:j + 1])
            ACT(a2buf[:, c0:c0 + w], rv, AF.Exp, [("rp", t % 2), "sc8"], ("a2buf", pi), scale=sc8[:, j:j + 1], bias=sc8[:, j:j + 1])
            STT("dve", ixbuf[:, c0:c0 + w], iv, 1.0, xcv, ALU.add, ALU.mult, [("ip", t % 2), ("xc", t % NXC)], ("ixbuf", pi))
            if pi == len(PIECES) - 1:
                mid_tile(j)

        post_queue = []

        def mid_tile(j):
            sl = j % 2
            ACT(a2buf, a2buf, AF.Sqrt, allp("a2buf"), "mh", scale=-0.25, bias=0.25)
            TT("dve", ixbuf, a2buf, ixbuf, ALU.mult, ["mh"] + allp("ixbuf"), "bterm")
            TT("dve", t16, abuf[:, TP:T:4], LRUc[:, j, 48:64], ALU.mult, allp("abuf") + LK, "t16")
            TT("dve", ixbuf[:, TP:T:4], ixbuf[:, TP:T:4], t16, ALU.add, ["bterm", "t16"], "bterm")
            MS("dve", abuf[:, TP:T:4], 0.0, "afix", r=allp("abuf") + ["t16"])
            p.op("dve", lambda e: e.tensor_tensor_scan(out=hb, data0=abuf, data1=ixbuf, initial=0.0, op0=ALU.mult, op1=ALU.add),
                 reads=allp("abuf") + ["afix", "bterm"], writes=["hbuf"])
            for nm in ("abuf", "a2buf", "ixbuf"):
                for pi in range(len(PIECES)):
                    p.readers.setdefault((nm, pi), []).append(p.last_write["hbuf"])
            if j == 0:
                self.tap("hbuf", hb, ["hbuf"])
            CP("pool", hfinL[:, j, 0:1], hb[:, TP - 1:TP], ["hbuf"], ("hfinL", j, 0))
            CP("pool", hfinL[:, j, 1:17], hb[:, TP + 3:T:4], ["hbuf"], ("hfinL", j, 1))
            for pi, (c0, w) in enumerate(PIECES):
                post_queue.append((j, pi, c0, w))

        npq = [0]

        def post_piece(j, pi, c0, w):
            sl = j % 2
            i2 = npq[0] % 2
            npq[0] += 1
            glv = glp[i2][:, 0:w]; gsv = gsp[i2][:, 0:w]; gbv = gbp[i2][:, 0:w]; t1v = t1p[i2][:, 0:w]
            groups = [("gl", wsg[sl], 0, 8, xT, ("wsg", sl, 0)), ("gs", wsg[sl], 1, 8, xT, ("wsg", sl, 1)),
                      ("gb", wgl[sl], 1, 4, z2, ("wgl", sl, 1)), ("ga", wgl[sl], 0, 4, z2, ("wgl", sl, 0))]
            for gi_, (nm, wt, idx, nk, src, wkey) in enumerate(groups):
                bb = POSTB[gi_ % 2]
                for k in range(nk):
                    rk = self.xT_keys(c0, w) if src is xT else [("z2", k, c0)]
                    p.op("pe", lambda e, k=k, bb=bb, wt=wt, idx=idx, src=src, nk=nk, c0=c0, w=w: e.matmul(
                        self.bk(bb)[:, 0:w], wt[:, k, idx, :], src[:, k, c0:c0 + w], start=(k == 0), stop=(k == nk - 1)),
                        reads=[wkey] + rk, writes=[("bank", bb)], inc=(k == nk - 1))
                psv = self.bk(bb)[:, 0:w]
                if nm == "gl":
                    ACT(glv, psv, AF.Tanh, [("bank", bb)], ("glp", i2), scale=0.5)
                elif nm == "gs":
                    ACT(gsv, psv, AF.Tanh, [("bank", bb)], ("gsp", i2), scale=0.5)
                elif nm == "gb":
                    ACT(gbv, psv, AF.Tanh, [("bank", bb)], ("gbp", i2), scale=0.25)
                else:
                    CP("act", t1v, psv, [("bank", bb)], ("t1p", i2))
            STT("dve", t1v, gbv, 1.0, t1v, ALU.add, ALU.mult, [("gbp", i2), ("t1p", i2)], ("t1p", i2))
            STT("dve", t1v, gsv, 1.0, t1v, ALU.add, ALU.mult, [("gsp", i2), ("t1p", i2)], ("t1p", i2))
            STT("dve", glv, glv, 1.0, hb[:, c0:c0 + w], ALU.add, ALU.mult, [("glp", i2), "hbuf"], ("glp", i2))
            STT("dve", merged2[:, j, c0:c0 + w], t1v, 0.25, glv, ALU.mult, ALU.add, [("t1p", i2), ("glp", i2)], ("merged2", j, c0))
            if pi == len(PIECES) - 1 and j + 2 < 8:
                load_post(j + 2)

        load_pre(0); load_post(0); load_post(1)
        N_ = len(items)
        stages = [(P5, 5), (P4, 4), (P1, 1), (P2, 2), (P3, 3), (P0, 0)]
        for t in range(N_ + 5):
            for fn, lag in stages:
                if 0 <= t - lag < N_:
                    fn(t - lag, items[t - lag])
            if post_queue:
                post_piece(*post_queue.pop(0))
        while post_queue:
            post_piece(*post_queue.pop(0))
        self.tap("merged2", merged2[:, 0, :], [("merged2", 0, c0) for c0, _ in PIECES])
        self.out_dma(O["lru_conv"], tail_sb, [("tail", j) for j in range(8)])
        self.out_dma(O["lru_h"], hfinL, [("hfinL", j, i) for j in range(8) for i in range(2)])

    def ln_tile(self, ps_flat, res, gB, bB, ytok, outv, rows, kps, kres, kg, kb, ky, kout, st6, mv, sd):
        p = self.p
        TT, TS, STT, ACT, CP = self.TT, self.TS, self.STT, self.ACT, self.CP
        yv = ytok[0:rows, :]
        p.op("act", lambda e: e.activation(out=yv, in_=ps_flat[0:rows, :], func=AF.Copy, scale=0.5), reads=kps, writes=[ky])
        STT("dve", yv, res[0:rows, :], ALPHA, yv, ALU.mult, ALU.add, [kres, ky], ky)
        for h in range(2):
            p.op("dve", lambda e, h=h: e.bn_stats(out=st6[0:rows, h, :], in_=yv[:, 512 * h:512 * (h + 1)]), reads=[ky], writes=[ky + ("st", h)])
        p.op("dve", lambda e: e.bn_aggr(out=mv[0:rows, :], in_=st6[0:rows, :, :].rearrange("p a b -> p (a b)")),
             reads=[ky + ("st", 0), ky + ("st", 1)], writes=[ky + ("mv",)])
        ACT(sd[0:rows, :], mv[0:rows, 1:2], AF.Sqrt, [ky + ("mv",)], ky + ("sd",), bias=LN_EPS)
        p.op("dve", lambda e: e.reciprocal(out=sd[0:rows, :], in_=sd[0:rows, :]), reads=[ky + ("sd",)], writes=[ky + ("sd",)])
        TS("dve", yv, yv, mv[0:rows, 0:1], ALU.subtract, [ky, ky + ("mv",), ky + ("sd",)], ky, s2=sd[0:rows, 0:1], op1=ALU.mult)
        TT("pool", yv, yv, gB[0:rows, :], ALU.mult, [ky, kg], ky)
        TT("dve", outv[0:rows, :], yv, bB[0:rows, :], ALU.add, [ky, kb], kout)

    def mix_stage(self):
        nc, p, I, O = self.nc, self.p, self.I, self.O
        R0, R1 = self.R0, self.R1
        TT, TS, STT, ACT, CP, MS = self.TT, self.TS, self.STT, self.ACT, self.CP, self.MS
        merged2 = self.merged2
        x1T = self.xT
        self.x1T = x1T
        lnc = self.lnc
        for i, nm in enumerate(("ln1_g", "ln1_b", "ln2_g", "ln2_b")):
            p.dma("sp", lnc[:, i, :], I[nm].partition_broadcast(128), "d_lnc%d" % i, writes=[("lnc", i)])
        wout = R1.take([128, 8, D], BF16)
        for h in range(2):
            p.dma("pool", wout[:, :, 512 * h:512 * (h + 1)], I["w_out"][:, 512 * h:512 * (h + 1)].rearrange("(k p) n -> p k n", p=128),
                  "d_wout%d" % h, writes=[("wout", h)])
        NB = 2
        xtok = [R1.take([128, D]) for i in range(NB)]
        ytok = [R1.take([128, D]) for i in range(NB)]
        x1tok = [R1.take([128, D]) for i in range(NB)]
        x1b = [R1.take([128, D], BF16) for i in range(NB)]
        st6 = [R1.take([128, 2, 6]) for i in range(NB)]
        mv = [R1.take([128, 2]) for i in range(NB)]
        sd = [R1.take([128, 1]) for i in range(NB)]
        ntt = (T + 127) // 128
        self.free_banks = [4, 5, 6, 7]
        for tt in range(ntt):
            r0 = tt * 128
            rows = min(128, T - r0)
            s = tt % NB
            pp = tt % 2
            p.dma("sp", xtok[s][0:rows, :], I["x"][r0:r0 + rows, :], "d_xtok%d" % s, writes=[("xtok", s)])
            psf = self.ps[pp][:].rearrange("p a c -> p (a c)")
            for h in range(2):
                for k in range(8):
                    p.op("pe", lambda e, k=k, h=h, pp=pp, r0=r0, rows=rows: e.matmul(
                        self.ps[pp][0:rows, h, :], merged2[:, k, r0:r0 + rows], wout[:, k, 512 * h:512 * (h + 1)],
                        start=(k == 0), stop=(k == 7)),
                        reads=[("wout", h)] + [("merged2", k, c0) for (c0, w) in self.PIECES if c0 <= r0 < c0 + w],
                        writes=[("bank", 2 * pp + h)], inc=(k == 7))
            self.ln_tile(psf, xtok[s], lnc[:, 0, :], lnc[:, 1, :], ytok[s], x1tok[s], rows,
                         [("bank", 2 * pp), ("bank", 2 * pp + 1)], ("xtok", s), ("lnc", 0), ("lnc", 1), ("ytok", s), ("x1tok", s),
                         st6[s], mv[s], sd[s])
            if tt == 0:
                self.tap("x1tok", x1tok[s], [("x1tok", s)])
            p.dma("sp", self.x1_scr[r0:r0 + rows, :], x1tok[s][0:rows, :], "d_x1w%d" % s, reads=[("x1tok", s)], writes=[("x1scr", tt)])
            CP("act", x1b[s][0:rows, :], x1tok[s][0:rows, :], [("x1tok", s)], ("x1b", s))
            b = self.bank()
            pt = self.bk(b).bitcast(BF16)
            for k in range(8):
                p.op("pe", lambda e, k=k, pt=pt, s=s, rows=rows: e.transpose(
                    pt[:, k * 128:k * 128 + rows], x1b[s][0:rows, k * 128:(k + 1) * 128], self.ident_b[0:rows, 0:rows]),
                    reads=[("x1b", s), "ident_b"], writes=[("bank", b)], inc=(k == 7))
            src = pt.rearrange("p (k c) -> p k c", c=128)[:, :, 0:rows]
            CP("act", x1T[:, :, r0:r0 + rows], src, [("bank", b)], ("x1T", tt))
        self.tap("x1T", x1T[:, 0, :], [("x1T", tt) for tt in range(ntt)])

    def x1T_keys(self, c0, w):
        return [("x1T", tt) for tt in range(c0 // 128, (c0 + w - 1) // 128 + 1)]

    def ffn_stage(self):
        nc, p, I, O = self.nc, self.p, self.I, self.O
        R0, R1 = self.R0, self.R1
        TT, TS, STT, ACT, CP, MS = self.TT, self.TS, self.STT, self.ACT, self.CP, self.MS
        NCK = dict(allow_slow_non_contiguous=True)
        x1T, lnc = self.x1T, self.lnc
        NJ = DFF // 128
        QW = 576
        Fc = self.Fc
        FK = self.Fc_keys
        wdn = R1.take([128, NJ, D], BF16)
        for c in range(6):
            p.dma("pool", wdn[:, 4 * c:4 * c + 4, :], I["w_down"][512 * c:512 * (c + 1), :].rearrange("(k p) n -> p k n", p=128),
                  "d_wdn%d" % c, writes=[("wdn", c)])
        Gq = R1.take([128, NJ, QW], BF16)
        NSL = 4
        wup = [R1.take([128, 8, 2, 128], BF16) for i in range(NSL)]
        halo = R1.take([128, NJ, 2])
        MS("pool", halo, 0.0, "halo_init")
        NY, NQ, NG = 5, 3, 2
        a_sb = [R1.take([128, 2 + 512]) for i in range(2)]
        as_sb = R1.take([128, 16, 6])
        y0 = [R1.take([128, 512]) for i in range(NY)]
        qq = [R1.take([128, 512]) for i in range(NQ)]
        ag = [R1.take([128, 512]) for i in range(NG)]
        tailf = [R1.take([NTAIL, 512]) for i in range(2)]
        NB = 2
        x1tok = [R1.take([128, D]) for i in range(NB)]
        ytok = [R1.take([128, D]) for i in range(NB)]
        otok = ytok
        st6 = [R1.take([128, 2, 6]) for i in range(NB)]
        mv = [R1.take([128, 2]) for i in range(NB)]
        sd = [R1.take([128, 1]) for i in range(NB)]
        nld = [0]

        def load_wup(j):
            sl = nld[0] % NSL
            nld[0] += 1
            wv = lambda c: I["w_up"][:, c:c + 128].rearrange("(k p) n -> p k n", p=128)
            p.dma("pool", wup[sl][:, :, 0, :], wv(128 * j), "d_wup%d_0" % sl, writes=[("wup", sl, 0)])
            p.dma("pool", wup[sl][:, :, 1, :], wv(DFF + 128 * j), "d_wup%d_1" % sl, writes=[("wup", sl, 1)])
            return sl

        npc = [0]
        ntile = [0]
        for n in range(4):
            q0 = 512 * n
            pieces = [(q0, 512, 0)] + ([(TP, NS, 512)] if n == 3 else [])
            RA = [0, 1]
            RG = [2, 3, 4, 5, 6]
            TAILB = 7
            items = []
            for j in range(NJ):
                for pi_, (c0, w, lc0) in enumerate(pieces):
                    items.append(dict(j=j, c0=c0, w=w, lc0=lc0, first=(pi_ == 0), last=(pi_ == len(pieces) - 1)))
            pending = [load_wup(0), load_wup(1), load_wup(2)]
            cur_sl = {}

            def S0(t, it):
                j, c0, w = it["j"], it["c0"], it["w"]
                if it["first"]:
                    cur_sl[j] = pending.pop(0)
                    if j + 3 < NJ:
                        pending.append(load_wup(j + 3))
                sl = cur_sl[j]
                bA = RA[t % 2]; bG = RG[t % 5]
                it["bA"], it["bG"] = bA, bG
                for (bb, gi) in ((bA, 0), (bG, 1)):
                    for k in range(8):
                        p.op("pe", lambda e, k=k, bb=bb, gi=gi, sl=sl, c0=c0, w=w: e.matmul(
                            self.bk(bb)[:, 0:w], wup[sl][:, k, gi, :], x1T[:, k, c0:c0 + w], start=(k == 0), stop=(k == 7)),
                            reads=[("wup", sl, gi)] + self.x1T_keys(c0, w), writes=[("bank", bb)], inc=(k == 7))
                if n == 3 and it["last"]:
                    for k in range(8):
                        p.op("pe", lambda e, k=k, sl=sl: e.matmul(self.bk(TAILB)[0:NTAIL, 0:128], x1T[:, k, TAIL0:T], wup[sl][:, k, 0, :],
                                                                  start=(k == 0), stop=(k == 7)),
                             reads=[("wup", sl, 0)] + self.x1T_keys(TAIL0, NTAIL), writes=[("bank", TAILB)], inc=(k == 7))

            def S1(t, it):
                j, c0, w, bA = it["j"], it["c0"], it["w"], it["bA"]
                samp = (w == NS)
                iy = t % NY; ia = t % 2
                fw = [Fc[:, j, 32 + k:33 + k] for k in range(3)]
                aps = self.bk(bA)[:, 0:w]
                yv = y0[iy][:, 0:w]
                ACT(yv, aps, AF.Identity, [("bank", bA)] + FK, ("y0", iy), scale=fw[2], bias=Fc[:, j, 35:36])
                if not samp:
                    ab = a_sb[ia]
                    CP("act", ab[:, 2:2 + w], aps, [("bank", bA)], ("a_sb", ia))
                    CP("dve", ab[:, 0:2], halo[:, j, :], ["halo_init", ("halo", j)], ("a_sbh", ia))
                    ak = [("a_sb", ia), ("a_sbh", ia), ("y0", iy)]
                    STT("dve", yv, ab[:, 1:1 + w], fw[1], yv, ALU.mult, ALU.add, ak, ("y0", iy))
                    STT("dve", yv, ab[:, 0:w], fw[0], yv, ALU.mult, ALU.add, ak, ("y0", iy))
                    CP("dve", halo[:, j, :], ab[:, w:w + 2], [("a_sb", ia), ("a_sbh", ia)], ("halo", j))
                else:
                    CP("act", as_sb[:, :, 2:6], aps.rearrange("p (b s) -> p b s", s=4), [("bank", bA)], ("as_sb", "new"))
                    CP("dve", as_sb[:, :, 0:2], Fc[:, j, 0:32].rearrange("p (b k) -> p b k", k=2), FK, ("as_sb", "st"))
                    ak = [("as_sb", "new"), ("as_sb", "st"), ("y0", iy)]
                    y3 = yv.rearrange("p (b s) -> p b s", s=4)
                    STT("dve", y3, as_sb[:, :, 1:5], fw[1], y3, ALU.mult, ALU.add, ak, ("y0", iy))
                    STT("dve", y3, as_sb[:, :, 0:4], fw[0], y3, ALU.mult, ALU.add, ak, ("y0", iy))
                if n == 3 and it["last"]:
                    tb = (j // 4) % 2
                    CP("act", tailf[tb][:, 128 * (j % 4):128 * (j % 4 + 1)], self.bk(TAILB)[0:NTAIL, 0:128], [("bank", TAILB)], ("tailf", tb, j % 4))
                    if j % 4 == 3:
                        sem = "o_tf%d" % tb
                        p.dma("sp", O["ffn_conv"][:, 512 * (j // 4):512 * (j // 4 + 1)], tailf[tb], sem,
                              reads=[("tailf", tb, i) for i in range(4)])
                        self.out_sems[sem] = p.count[sem]

            def S2(t, it):
                w = it["w"]
                iy = t % NY; iq = t % NQ
                yv = y0[iy][:, 0:w]; qv = qq[iq][:, 0:w]
                ACT(qv, yv, AF.Square, [("y0", iy)], ("qq", iq))
                ACT(qv, qv, AF.Identity, [("qq", iq)], ("qq", iq), scale=GC, bias=1.0)

            def S3(t, it):
                w = it["w"]
                iy = t % NY; iq = t % NQ; ig = t % NG
                TT("pool", ag[ig][:, 0:w], qq[iq][:, 0:w], y0[iy][:, 0:w], ALU.mult, [("qq", iq), ("y0", iy)], ("ag", ig))

            def S4(t, it):
                j, w, lc0, bG = it["j"], it["w"], it["lc0"], it["bG"]
                iy = t % NY; iq = t % NQ; ig = t % NG
                yv = y0[iy][:, 0:w]; qv = qq[iq][:, 0:w]; av = ag[ig][:, 0:w]
                gps = self.bk(bG)[:, 0:w]
                ACT(qv, av, AF.Tanh, [("ag", ig)], ("qq", iq), scale=GK)
                STT("dve", av, qv, 1.0, yv, ALU.add, ALU.mult, [("qq", iq), ("y0", iy)], ("ag", ig))
                TT("dve", Gq[:, j, lc0:lc0 + w], av, gps, ALU.mult, [("ag", ig), ("bank", bG)], ("Gq", j, lc0))

            N_ = len(items)
            stages = [(S4, 4), (S1, 1), (S2, 2), (S3, 3), (S0, 0)]
            for t in range(N_ + 4):
                for fn, lag in stages:
                    if 0 <= t - lag < N_:
                        fn(t - lag, items[t - lag])
            if n == 0:
                self.tap("Gq", Gq[:, 0, :], [("Gq", 0, 0)])
            tts = [4 * n + i for i in range(4)] + ([16] if n == 3 else [])
            for tt in tts:
                r0 = tt * 128
                rows = min(128, T - r0)
                lc = r0 - q0 if tt < 16 else 512
                s = ntile[0] % NB
                pp = ntile[0] % 2
                ntile[0] += 1
                p.dma("sp", x1tok[s][0:rows, :], self.x1_scr[r0:r0 + rows, :], "d_x1r%d" % s, reads=[("x1scr", tt)], writes=[("x1tok2", s)])
                psf = self.ps[pp][:].rearrange("p a c -> p (a c)")
                gkeys = [("Gq", j, 512 if tt == 16 else 0) for j in range(NJ)]
                for h in range(2):
                    for k in range(NJ):
                        p.op("pe", lambda e, k=k, h=h, pp=pp, lc=lc, rows=rows: e.matmul(
                            self.ps[pp][0:rows, h, :], Gq[:, k, lc:lc + rows], wdn[:, k, 512 * h:512 * (h + 1)],
                            start=(k == 0), stop=(k == NJ - 1)),
                            reads=[("wdn", k // 4), ("Gq", k, 512 if tt == 16 else 0)], writes=[("bank", 2 * pp + h)], inc=(k == NJ - 1))
                self.ln_tile(psf, x1tok[s], lnc[:, 2, :], lnc[:, 3, :], ytok[s], otok[s], rows,
                             [("bank", 2 * pp), ("bank", 2 * pp + 1)], ("x1tok2", s), ("lnc", 2), ("lnc", 3), ("ytok2", s), ("ytok2", s),
                             st6[s], mv[s], sd[s])
                sem = "o_y%d" % s
                p.dma("sp", O["y"][r0:r0 + rows, :], otok[s][0:rows, :], sem, reads=[("ytok2", s)])
                self.out_sems[sem] = p.count[sem]

    def finish(self):
        fw = [(s, v) for s, v in self.out_sems.items()]
        self.p.emit(final_waits=fw)
        self.st.close()
        print("arena peaks: R0 %d/%d words, R1 %d/%d words" % (self.R0.peak, self.R0.words, self.R1.peak, self.R1.words))
        return self.nc


def shard_inputs(inputs, c):
    f = lambda a: np.ascontiguousarray(a, dtype=np.float32)
    m = {}
    m["x"] = f(np.concatenate([inputs["x_prompt"][c], inputs["x_sample"][NSQ * c:NSQ * (c + 1)].reshape(NS, D)], axis=0))
    m["st_lru_conv"] = f(inputs["state_lru_conv"][0, NSQ * c:NSQ * (c + 1)].reshape(NSQ * 3, D))
    m["st_lru_h"] = f(inputs["state_lru_h"][0, NSQ * c:NSQ * (c + 1)])
    m["st_s5_re"] = f(inputs["state_s5_re"][0, NSQ * c:NSQ * (c + 1)].reshape(NSQ, 2048))
    m["st_s5_im"] = f(inputs["state_s5_im"][0, NSQ * c:NSQ * (c + 1)].reshape(NSQ, 2048))
    m["st_ffn_conv"] = f(inputs["state_ffn_conv"][0, NSQ * c:NSQ * (c + 1)].reshape(NSQ * 2, DFF))
    for k in ("w_in", "lru_conv_w", "lru_conv_b", "lru_wa", "lru_ba", "lru_wx", "lru_bx", "lru_lambda", "s5_a_re", "s5_a_im",
              "s5_log_dt", "s5_b_re", "s5_b_im", "s5_c_re", "s5_c_im", "s5_d", "w_glu", "w_out", "ln1_g", "ln1_b", "w_up",
              "ffn_conv_w", "ffn_conv_b", "w_down", "ln2_g", "ln2_b"):
        m[k] = f(inputs[k][0])
    return m


_NC_CACHE = {}


def _get_nc():
    if "nc" not in _NC_CACHE:
        _NC_CACHE["nc"] = Builder().build()
    return _NC_CACHE["nc"]


def kernel(**inputs):
    nc = _get_nc()
    in_maps = [shard_inputs(inputs, c) for c in range(NCORES)]
    res = run_bass_kernel_spmd(nc, in_maps, core_ids=list(range(NCORES)))
    R = res.results
    B = NCORES
    y_p = np.zeros((B, TP, D), np.float32); y_s = np.zeros((B * NSQ, 4, D), np.float32)
    p_conv = np.zeros((1, B, 3, D), np.float32); p_h = np.zeros((1, B, D), np.float32)
    p_re = np.zeros((1, B, 32, 64), np.float32); p_im = np.zeros((1, B, 32, 64), np.float32)
    p_ffn = np.zeros((1, B, 2, DFF), np.float32)
    s_conv = np.zeros((1, B * NSQ, 3, D), np.float32); s_h = np.zeros((1, B * NSQ, D), np.float32)
    s_re = np.zeros((1, B * NSQ, 32, 64), np.float32); s_im = np.zeros((1, B * NSQ, 32, 64), np.float32)
    s_ffn = np.zeros((1, B * NSQ, 2, DFF), np.float32)
    for c in range(B):
        r = R[c]
        sl = slice(NSQ * c, NSQ * (c + 1))
        y_p[c] = r["y"][0:TP]
        y_s[sl] = r["y"][TP:].reshape(NSQ, 4, D)
        lc = r["o_lru_conv"]
        p_conv[0, c] = lc[0:3]
        s_conv[0, sl] = lc[3:].reshape(NSQ, 4, D)[:, 1:4]
        lh = r["o_lru_h"].transpose(2, 1, 0).reshape(17, D)
        p_h[0, c] = lh[0]
        s_h[0, sl] = lh[1:17]
        s5 = r["o_s5"].reshape(2, 64, 2, 16, 17).transpose(2, 4, 3, 0, 1)
        s5 = s5.reshape(2, 17, 32, 64)
        p_re[0, c] = s5[0, 0]
        p_im[0, c] = s5[1, 0]
        s_re[0, sl] = s5[0, 1:17]
        s_im[0, sl] = s5[1, 1:17]
        fc = r["o_ffn_conv"]
        p_ffn[0, c] = fc[1:3]
        s_ffn[0, sl] = fc[3:].reshape(NSQ, 4, DFF)[:, 2:4]
    return (y_p, y_s, p_conv, p_h, p_re, p_im, p_ffn, s_conv, s_h, s_re, s_im, s_ffn)
```

```python
import math
import contextlib
import numpy as np
import concourse.bass as bass
import concourse.mybir as mybir
from concourse.bass_utils import run_bass_kernel_spmd

F32 = mybir.dt.float32
BF16 = mybir.dt.bfloat16
AF = mybir.ActivationFunctionType
ALU = mybir.AluOpType

ENGINES = ("pe", "act", "dve", "pool", "sp")
NCORES = 8
TP = 2048
NSQ = 16
NS = 64
T = TP + NS
TAIL0 = TP - 3
NTAIL = T - TAIL0
D = 1024
DFF = 3072
ALPHA = 2.0 ** 0.25
LN_EPS = 1e-5
LCH = 8
NCH = TP // LCH
NLEV = 8
PAD = 128
PI = math.pi
GK = math.sqrt(2.0 / math.pi)
GC = 0.044715


class Prog:
    def __init__(self, nc):
        self.nc = nc
        self.streams = {e: [] for e in ENGINES}
        self.count = {}
        self.waited = {e: {} for e in ENGINES}
        self.last_write = {}
        self.readers = {}
        self.sem_names = set()
        self.epoch = "0"
        self.pending = {}

    def barrier(self):
        snap = [(s, v) for s, v in self.count.items() if not s.startswith("o_")]
        for e in ENGINES:
            self.pending.setdefault(e, []).extend(snap)

    def op(self, eng, fn, reads=(), writes=(), inc=True, sem=None, amount=1):
        if sem is None:
            sem = "s_%s_%s" % (eng, self.epoch)
        self.sem_names.add(sem)
        deps = list(self.pending.pop(eng, ()))
        for k in reads:
            ev = self.last_write.get(k)
            if ev is not None:
                deps.append(ev)
        for k in writes:
            ev = self.last_write.get(k)
            if ev is not None:
                deps.append(ev)
            deps.extend(self.readers.get(k, ()))
        waits = {}
        for (s, v) in deps:
            if eng == "pe" and s.startswith("s_pe_"):
                continue
            if self.waited[eng].get(s, 0) >= v:
                continue
            if waits.get(s, 0) < v:
                waits[s] = v
        for s, v in waits.items():
            self.waited[eng][s] = v
        cur = self.count.get(sem, 0)
        val = cur + amount
        if inc:
            self.count[sem] = val
        ev = (sem, val)
        self.streams[eng].append((fn, list(waits.items()), (sem, amount) if inc else None))
        for k in reads:
            self.readers.setdefault(k, []).append(ev)
        for k in writes:
            self.last_write[k] = ev
            self.readers[k] = []
        return ev

    def dma(self, queue, out, in_, sem, reads=(), writes=(), **kw):
        def fn(e):
            return e.dma_start(out=out, in_=in_, **kw)
        return self.op(queue, fn, reads=reads, writes=writes, inc=True, sem=sem, amount=16)

    def emit(self, final_waits=()):
        nc = self.nc
        names = sorted(self.sem_names)
        with contextlib.ExitStack() as st:
            sems = {n: st.enter_context(nc.semaphore(n)) for n in names}
            block = st.enter_context(nc.Block())
            streams = self.streams

            def run(engh, lst, last):
                for fn, waits, inc in lst:
                    for s, v in waits:
                        engh.wait_ge(sems[s], v)
                    ins = fn(engh)
                    if inc is not None:
                        ins.then_inc(sems[inc[0]], inc[1])
                if last:
                    for s, v in final_waits:
                        engh.wait_ge(sems[s], v)

            @block.tensor
            def _(e):
                run(e, streams["pe"], False)

            @block.scalar
            def _(e):
                run(e, streams["act"], False)

            @block.vector
            def _(e):
                run(e, streams["dve"], False)

            @block.gpsimd
            def _(e):
                run(e, streams["pool"], False)

            @block.sync
            def _(e):
                run(e, streams["sp"], True)


class Arena:
    def __init__(self, tensor, words):
        self.t = tensor
        self.words = words
        self.pos = 0
        self.peak = 0

    def take(self, shape, dt=F32):
        n = 1
        for s in shape[1:]:
            n *= s
        esz = 4 if dt == F32 else 2
        w = (n * esz + 3) // 4
        w = (w + 7) // 8 * 8
        assert self.pos + w <= self.words, "arena overflow: need %d have %d" % (self.pos + w, self.words)
        v = self.t[0:shape[0], self.pos:self.pos + w]
        self.pos += w
        self.peak = max(self.peak, self.pos)
        if dt != F32:
            v = v.bitcast(dt)
        v = v[:, 0:n]
        if len(shape) > 2:
            names = " ".join("d%d" % i for i in range(len(shape) - 1))
            kw = {"d%d" % i: shape[i + 1] for i in range(len(shape) - 2)}
            v = v.rearrange("p (%s) -> p %s" % (names, names), **kw)
        return v


class StopBuild(Exception):
    pass


class Builder:
    def chk_stop(self, name, reads=()):
        if self.stop_after == name:
            self.p.barrier()
            d = self.dout("dbg_stop", [128, 4])
            t = self.R0.take([128, 4])
            self.MS("dve", t, 1.0, "stoptile")
            self.out_dma(d, t, ["stoptile"])
            raise StopBuild()

    def __init__(self, debug=(), stop_after=None):
        self.debug = set(debug)
        self.stop_after = stop_after
        self.nc = bass.Bass("TRN2", target_bir_lowering=False)
        self.p = Prog(self.nc)
        self.st = contextlib.ExitStack()
        self.out_sems = {}
        self.nout = 0
        self.free_banks = list(range(8))
        self.nbank = 0
        self.ntmp = 0

    def din(self, name, shape):
        return self.nc.dram_tensor(name, list(shape), F32, kind="ExternalInput").ap()

    def dout(self, name, shape, dt=F32):
        return self.nc.dram_tensor(name, list(shape), dt, kind="ExternalOutput").ap()

    def sb(self, name, shape, dt=F32):
        t = self.st.enter_context(self.nc.sbuf_tensor(name, list(shape), dt))
        return t[:]

    def out_dma(self, out, in_, reads, queue="sp", **kw):
        sem = "o_%d" % (self.nout % 8)
        self.nout += 1
        self.p.dma(queue, out, in_, sem, reads=reads, **kw)
        self.out_sems[sem] = self.p.count[sem]

    def tap(self, name, ap, reads):
        if name not in self.debug:
            return
        d = self.dout("dbg_" + name, list(ap.shape), ap.dtype)
        self.out_dma(d, ap, reads)

    def bank(self):
        b = self.free_banks[self.nbank % len(self.free_banks)]
        self.nbank += 1
        return b

    def bk(self, b):
        return self.ps[b // 2][:, b % 2, :]

    def TT(self, eng, out, a, b_, op, r, w):
        self.p.op(eng, lambda e: e.tensor_tensor(out=out, in0=a, in1=b_, op=op), reads=r, writes=[w])

    def TS(self, eng, out, a, s1, op0, r, w, s2=None, op1=None):
        if op1 is None:
            self.p.op(eng, lambda e: e.tensor_scalar(out=out, in0=a, scalar1=s1, scalar2=None, op0=op0), reads=r, writes=[w])
        else:
            self.p.op(eng, lambda e: e.tensor_scalar(out=out, in0=a, scalar1=s1, scalar2=s2, op0=op0, op1=op1), reads=r, writes=[w])

    def STT(self, eng, out, a, sc, b_, op0, op1, r, w):
        self.p.op(eng, lambda e: e.scalar_tensor_tensor(out=out, in0=a, scalar=sc, in1=b_, op0=op0, op1=op1), reads=r, writes=[w])

    def ACT(self, out, a, func, r, w, scale=1.0, bias=0.0):
        self.p.op("act", lambda e: e.activation(out=out, in_=a, func=func, scale=scale, bias=bias), reads=r, writes=[w])

    def CP(self, eng, out, a, r, w):
        if eng == "act":
            self.p.op("act", lambda e: e.copy(out=out, in_=a), reads=r, writes=[w])
        else:
            self.p.op(eng, lambda e: e.tensor_copy(out=out, in_=a), reads=r, writes=[w])

    def MS(self, eng, out, val, w, r=()):
        self.p.op(eng, lambda e: e.memset(out, val), reads=list(r), writes=[w])

    def build(self):
        nc, p = self.nc, self.p
        din = self.din
        I = {}
        for name, shape in (("x", [T, D]), ("st_lru_conv", [NSQ * 3, D]), ("st_lru_h", [NSQ, D]), ("st_s5_re", [NSQ, 2048]),
                            ("st_s5_im", [NSQ, 2048]), ("st_ffn_conv", [NSQ * 2, DFF]), ("w_in", [D, 3584]),
                            ("lru_conv_w", [4, D]), ("lru_conv_b", [D]), ("lru_wa", [16, 64, 64]), ("lru_ba", [D]),
                            ("lru_wx", [16, 64, 64]), ("lru_bx", [D]), ("lru_lambda", [D]), ("s5_a_re", [32, 64]),
                            ("s5_a_im", [32, 64]), ("s5_log_dt", [32]), ("s5_b_re", [32, 64, 16]), ("s5_b_im", [32, 64, 16]),
                            ("s5_c_re", [32, 16, 64]), ("s5_c_im", [32, 16, 64]), ("s5_d", [512]), ("w_glu", [512, 2048]),
                            ("w_out", [D, D]), ("ln1_g", [D]), ("ln1_b", [D]), ("w_up", [D, 2 * DFF]), ("ffn_conv_w", [3, DFF]),
                            ("ffn_conv_b", [DFF]), ("w_down", [DFF, D]), ("ln2_g", [D]), ("ln2_b", [D])):
            I[name] = din(name, shape)
        self.I = I
        O = {}
        O["y"] = self.dout("y", [T, D])
        O["lru_conv"] = self.dout("o_lru_conv", [NTAIL, D])
        O["lru_h"] = self.dout("o_lru_h", [128, 8, 17])
        O["s5"] = self.dout("o_s5", [128, 2, 16, 17])
        O["ffn_conv"] = self.dout("o_ffn_conv", [NTAIL, DFF])
        self.O = O
        self.x1_scr = nc.dram_tensor("x1_scr", [T, D], F32, kind="Internal").ap()

        self.ps = [self.st.enter_context(nc.psum_tensor("ps%d" % i, [128, 2, 512], F32)) for i in range(4)]
        R0W = 17664
        R1W = 35456
        self.R0 = Arena(self.st.enter_context(nc.sbuf_tensor("R0", [128, R0W], F32)), R0W)
        self.R1 = Arena(self.st.enter_context(nc.sbuf_tensor("R1", [128, R1W], F32)), R1W)
        R0, R1 = self.R0, self.R1

        ident_f = R0.take([128, 128]); ident_b = R0.take([128, 128], BF16)
        self.ident_f, self.ident_b = ident_f, ident_b
        self.MS("pool", ident_f, 0.0, "ident_f")
        p.op("pool", lambda e: e.affine_select(out=ident_f, in_=ident_f, pattern=[[-1, 128]],
                                               compare_op=ALU.not_equal, fill=1.0, base=0, channel_multiplier=1),
             reads=["ident_f"], writes=["ident_f"])
        self.CP("pool", ident_b, ident_f, ["ident_f"], "ident_b")

        self.PIECES = [(0, 512), (512, 512), (1024, 512), (1536, 512), (2048, 64)]
        xT = R0.take([128, 8, T], BF16)
        self.xT = xT
        zblk = R0.take([128, 4224])
        self.z2 = zblk.bitcast(BF16)[:, 0:4 * T].rearrange("p (a t) -> p a t", a=4)
        self.lnc = zblk[:, 0:4096].rearrange("p (a n) -> p a n", a=4)
        self.Hfin = R0.take([128, 2, 16, 17])
        self.LRUc = R0.take([128, 8, 72])
        self.Fc = R0.take([128, DFF // 128, 36])
        mark0 = R1.pos
        u_bf = R1.take([128, 4, T], BF16)
        self.W1 = R1.take([128, 4, LCH, 2, 128], BF16)
        self.W4a = R1.take([128, 16, LCH, 2, 32], BF16)
        self.W4b = R1.take([128, 16, LCH, 2, 32], BF16)
        self.h0S = R1.take([128, 2, 16, 16]); self.h0S_bf = R1.take([128, 2, 16, 16], BF16)
        self.RS = Arena(R1.take([128, 2368]), 2368)
        mark1 = R1.pos
        self.free_banks = [4, 5, 6, 7]
        self.param_loads()
        xb = [R1.take([128, D], BF16) for i in range(2)]
        ntt = (T + 127) // 128
        for tt in range(ntt):
            r0 = tt * 128
            rows = min(128, T - r0)
            slot = tt % 2
            p.dma("pool", xb[slot][0:rows, :], I["x"][r0:r0 + rows, :], "d_xb%d" % slot, writes=[("xb", slot)])
            b = self.bank()
            pt = self.bk(b).bitcast(BF16)
            for k in range(8):
                p.op("pe", lambda e, k=k, pt=pt, slot=slot, rows=rows: e.transpose(
                    pt[:, k * 128:k * 128 + rows], xb[slot][0:rows, k * 128:(k + 1) * 128], ident_b[0:rows, 0:rows]),
                    reads=[("xb", slot), "ident_b"], writes=[("bank", b)], inc=(k == 7))
            src = pt.rearrange("p (k c) -> p k c", c=128)[:, :, 0:rows]
            self.CP("act" if tt % 2 == 0 else "dve", xT[:, :, r0:r0 + rows], src, [("bank", b)], ("xT", tt))
            if tt == 3:
                self.param_transposes()
        self.tap("xT", xT[:, 0, :], [("xT", tt) for tt in range(ntt)])
        if self.stop_after == "p0":
            return self.finish()
        try:
            self.s5_prep()
        except StopBuild:
            return self.finish()
        self.tap("W1", self.W1[:, 0, :, :, :], self.W1_keys)
        self.tap("W4a", self.W4a[:, 0, :, :, :], self.W4_keys)
        self.tap("W4b", self.W4b[:, 0, :, :, :], self.W4_keys)
        self.tap("h0S", self.h0S, self.h0S_keys)
        self.tap("mur", self.mur, self.mu_keys)
        if self.stop_after == "prep":
            return self.finish()
        p.barrier()
        R1.pos = mark1
        wslot_u = [R1.take([128, 8, 128], BF16) for i in range(2)]
        for qh in range(4):
            slot = qh % 2
            p.dma("pool", wslot_u[slot], I["w_in"][:, 1024 + 128 * qh:1024 + 128 * (qh + 1)].rearrange("(k p) n -> p k n", p=128),
                  "d_wu%d" % slot, writes=[("wslot_u", slot)])
            for (c0, w) in self.PIECES:
                b = self.bank()
                for k in range(8):
                    p.op("pe", lambda e, k=k, b=b, slot=slot, c0=c0, w=w: e.matmul(
                        self.bk(b)[:, 0:w], wslot_u[slot][:, k, :], xT[:, k, c0:c0 + w], start=(k == 0), stop=(k == 7)),
                        reads=[("wslot_u", slot)] + self.xT_keys(c0, w), writes=[("bank", b)], inc=(k == 7))
                self.CP("act", u_bf[:, qh, c0:c0 + w], self.bk(b)[:, 0:w], [("bank", b)], ("u_bf", qh, c0))
        self.tap("u_bf", u_bf[:, 0, :], [("u_bf", 0, c0) for c0, _ in self.PIECES])
        if self.stop_after == "u":
            return self.finish()
        self.s5_main(u_bf)
        self.tap("z2", self.z2[:, 0, :], [("z2", 0, c0) for c0, _ in self.PIECES])
        self.tap("Hfin", self.Hfin, [("Hfin", q) for q in range(16)] + [("Hfin0", q) for q in range(16)])
        hk = [("Hfin", q) for q in range(16)] + [("Hfin0", q) for q in range(16)]
        self.out_dma(O["s5"], self.Hfin, hk)
        if self.stop_after == "s5":
            return self.finish()
        p.barrier()
        R1.pos = mark0
        p.epoch = "1"
        self.lru_stage()
        if self.stop_after == "lru":
            return self.finish()
        p.barrier()
        R1.pos = self.merged_end
        p.epoch = "2"
        self.mix_stage()
        if self.stop_after == "mix":
            return self.finish()
        p.barrier()
        R1.pos = 0
        p.epoch = "3"
        self.ffn_stage()
        return self.finish()

    def xT_keys(self, c0, w):
        return [("xT", tt) for tt in range(c0 // 128, (c0 + w - 1) // 128 + 1)]

    def param_loads(self):
        nc, p, I = self.nc, self.p, self.I
        R1 = self.R1
        MS = self.MS
        NCK = dict(allow_slow_non_contiguous=True)
        NJ = DFF // 128
        self.LF = R1.take([128, D + DFF])
        Lp = self.LF[:, 0:D]; Fp = self.LF[:, D:D + DFF]
        hs1 = R1.take([128, 2048])
        hsp = [hs1, hs1]
        self.Lp, self.Fp, self.hsp = Lp, Fp, hsp
        MS("pool", Lp, 0.0, "Lp"); MS("pool", Fp, 0.0, "Fp"); MS("pool", hs1, 0.0, "hsp")
        row = lambda nm: I[nm].rearrange("(o n) -> o n", o=1)
        for (r0, r1, src) in ((0, 48, I["st_lru_conv"]), (48, 64, I["st_lru_h"]), (64, 68, I["lru_conv_w"]), (68, 69, row("lru_conv_b")),
                              (69, 70, row("lru_ba")), (70, 71, row("lru_bx")), (71, 72, row("lru_lambda"))):
            p.dma("sp", Lp[r0:r1, :], src, "d_Lp", writes=["Lp"])
        for (r0, r1, src) in ((0, 32, I["st_ffn_conv"]), (32, 35, I["ffn_conv_w"]), (35, 36, row("ffn_conv_b"))):
            p.dma("sp", Fp[r0:r1, :], src, "d_Fp", writes=["Fp"])
        self.hs_loaded = False
        sh3 = [128, 16, 32]
        self.Bnr = R1.take(sh3); self.Bni = R1.take(sh3)
        self.Cnr = R1.take([128, 4, 128]); self.Cni = R1.take([128, 4, 128])
        for t_, k_ in ((self.Bnr, "Bnr"), (self.Bni, "Bni"), (self.Cnr, "Cnr"), (self.Cni, "Cni")):
            MS("pool", t_, 0.0, k_)
        for (dst, nm, key) in ((self.Bnr, "s5_b_re", "Bnr"), (self.Bni, "s5_b_im", "Bni")):
            v = I[nm].rearrange("(q two) p c -> two p q c", two=2)
            for two in range(2):
                p.dma("sp", dst[64 * two:64 * two + 64, :, 16 * two:16 * two + 16], v[two], "d_prep", writes=[key])
        for (dst, nm, key) in ((self.Cnr, "s5_c_re", "Cnr"), (self.Cni, "s5_c_im", "Cni")):
            v = I[nm].rearrange("(qh ql two) c p -> ql two c qh p", qh=4, ql=4, two=2)
            for ql in range(4):
                for two in range(2):
                    p0 = 32 * ql + 16 * two
                    p.dma("sp", dst[p0:p0 + 16, :, 64 * two:64 * two + 64], v[ql, two], "d_prep", writes=[key])
        shS = [128, 16]
        self.aSr = R1.take(shS); self.aSi = R1.take(shS); self.ldS = R1.take(shS)
        p.dma("sp", self.aSr, I["s5_a_re"].rearrange("(q two) p -> (two p) q", two=2), "d_prep", writes=["aSr"], **NCK)
        p.dma("sp", self.aSi, I["s5_a_im"].rearrange("(q two) p -> (two p) q", two=2), "d_prep", writes=["aSi"], **NCK)
        v = I["s5_log_dt"].rearrange("(q two) -> two q", two=2)
        for two in range(2):
            p.dma("sp", self.ldS[64 * two:64 * two + 64, :], v[two].partition_broadcast(64), "d_prep", writes=["ldS"], **NCK)
        self.dS5 = self.R0.take([128, 4])
        p.dma("sp", self.dS5, I["s5_d"].rearrange("(t p) -> p t", p=128), "d_prep", writes=["dS5"], **NCK)
        for sem, keys in (("d_Lp", ["Lp"]), ("d_Fp", ["Fp"]),
                          ("d_prep", ["Bnr", "Bni", "Cnr", "Cni", "aSr", "aSi", "ldS", "dS5"])):
            for k_ in keys:
                p.last_write[k_] = (sem, p.count[sem])

    def tr_group(self, srcs, evac):
        p = self.p
        for g0 in range(0, len(srcs), 4):
            grp = srcs[g0:g0 + 4]
            b = self.bank()
            bv = self.bk(b).rearrange("p (a c) -> p a c", a=4)
            for i, (ap, keys) in enumerate(grp):
                p.op("pe", lambda e, i=i, ap=ap, bv=bv: e.transpose(bv[:, i, :], ap, self.ident_f),
                     reads=list(keys) + ["ident_f"], writes=[("bank", b)], inc=(i == len(grp) - 1))
            evac(bv, g0, len(grp), b)

    def param_transposes(self):
        p = self.p
        CP = self.CP
        NJ = DFF // 128
        Lp, Fp, hsp = self.Lp, self.Fp, self.hsp

        def ev_L(bv, g0, n, b):
            CP("act", self.LRUc[:, g0:g0 + n, :], bv[:, 0:n, 0:72], [("bank", b)], ("LRUc", g0 // 4))
        self.tr_group([(Lp[:, 128 * j:128 * (j + 1)], ["Lp"]) for j in range(8)], ev_L)

        def ev_F(bv, g0, n, b):
            CP("dve", self.Fc[:, g0:g0 + n, :], bv[:, 0:n, 0:36], [("bank", b)], ("Fc", g0 // 4))
        self.tr_group([(Fp[:, 128 * j:128 * (j + 1)], ["Fp"]) for j in range(NJ)], ev_F)
        self.LRUc_keys = [("LRUc", i) for i in range(2)]
        self.Fc_keys = [("Fc", i) for i in range(NJ // 4)]
        for part in range(2):
            p.dma("sp", hsp[part][0:16, :], self.I[("st_s5_re", "st_s5_im")[part]], "d_hsp", writes=["hsp"])
            def ev_h(bv, g0, n, b, part=part):
                CP("act", self.h0S[:, part, g0:g0 + n, :], bv[:, 0:n, 0:16], [("bank", b)], ("h0S", part, g0 // 4))
            self.tr_group([(hsp[part][:, 128 * q:128 * (q + 1)], ["hsp"]) for q in range(16)], ev_h)
        self.h0S_keys = [("h0S", part, i) for part in range(2) for i in range(4)]
        sh3 = [128, 16, 32]
        self.CTr = self.R1.take(sh3); self.CTi = self.R1.take(sh3)
        for (src, dst, k_src, k_dst) in ((self.Cnr, self.CTr, "Cnr", "CTr"), (self.Cni, self.CTi, "Cni", "CTi")):
            def ev_c(bv, g0, n, b, dst=dst, k_dst=k_dst):
                CP("dve", dst[:, 4 * g0:4 * (g0 + n), :].rearrange("p (a q) c -> p a (q c)", a=n), bv[:, 0:n, :], [("bank", b)], (k_dst, g0))
            self.tr_group([(src[:, qh, :], [k_src]) for qh in range(4)], ev_c)
        self.CT_keys = {"CTr": [("CTr", 0)], "CTi": [("CTi", 0)]}

    def s5_prep(self):
        nc, p, I = self.nc, self.p, self.I
        R0, R1, RS = self.R0, self.R1, self.RS
        TT, TS, STT, ACT, CP, MS = self.TT, self.TS, self.STT, self.ACT, self.CP, self.MS
        W1, W4a, W4b = self.W1, self.W4a, self.W4b
        aSr, aSi, ldS = self.aSr, self.aSi, self.ldS
        CTr, CTi = self.CTr, self.CTi
        kCTr, kCTi = self.CT_keys["CTr"], self.CT_keys["CTi"]
        shS = [128, 16]
        sh3 = [128, 16, 32]
        I32 = mybir.dt.int32

        def sincos_base(theta, t, kf, A, C2, cs, sn, k_th, k_t, k_kf, k_A, k_C2, k_cs, k_sn):
            TS("dve", t, theta, 1.0 / (2 * PI), ALU.mult, [k_th], k_t, s2=16.0, op1=ALU.add)
            CP("dve", kf.bitcast(I32), t, [k_t], k_kf)
            CP("dve", kf, kf.bitcast(I32), [k_kf], k_kf)
            TT("dve", t, t, kf, ALU.subtract, [k_t, k_kf], k_t)
            ACT(A, t, AF.Sin, [k_t], k_A, scale=PI)
            ACT(C2, t, AF.Sin, [k_t], k_C2, scale=PI / 2)
            TT("dve", C2, C2, C2, ALU.mult, [k_C2], k_C2)
            TS("dve", C2, C2, -2.0, ALU.mult, [k_C2], k_C2, s2=1.0, op1=ALU.add)
            STT("dve", sn, A, 2.0, C2, ALU.mult, ALU.mult, [k_A, k_C2], k_sn)
            TT("dve", cs, A, A, ALU.mult, [k_A], k_cs)
            TS("dve", cs, cs, -2.0, ALU.mult, [k_cs], k_cs, s2=1.0, op1=ALU.add)

        def tS():
            return RS.take(shS)

        dtS = tS(); adtS = tS(); thS = tS()
        ACT(dtS, ldS, AF.Exp, ["ldS"], "dtS")
        TT("dve", adtS, aSr, dtS, ALU.mult, ["aSr", "dtS"], "adtS")
        TT("dve", thS, aSi, dtS, ALU.mult, ["aSi", "dtS"], "thS")
        self.rhoS = tS()
        ACT(self.rhoS, adtS, AF.Exp, ["adtS"], "rhoS")
        cosS = []; sinS = []; nsinS = []
        lamr = {}; lami = {}; lrotr = {}; lroti = {}; rpow = {}
        c1S = tS(); s1S = tS(); w1 = tS(); w2 = tS(); w3 = tS(); w4 = tS()
        sincos_base(thS, w1, w2, w3, w4, c1S, s1S, "thS", "Sw1", "Sw2", "Sw3", "Sw4", "S1cs", "S1sn")
        ua = w1; ub = w2
        for s in range(2 * LCH):
            if s == 0:
                cs = tS(); sn = tS()
                MS("dve", cs, 1.0, "S0cs"); MS("dve", sn, 0.0, "S0sn")
            elif s == 1:
                cs, sn = c1S, s1S
            else:
                cs = tS(); sn = tS()
                pc, ps_ = cosS[s - 1], sinS[s - 1]
                kpc, kps = "S%dcs" % (s - 1), "S%dsn" % (s - 1)
                TT("dve", ua, pc, c1S, ALU.mult, [kpc, "S1cs", "Sw1"], "Sw1")
                TT("dve", ub, ps_, s1S, ALU.mult, [kps, "S1sn", "Sw2"], "Sw2")
                TT("dve", cs, ua, ub, ALU.subtract, ["Sw1", "Sw2"], "S%dcs" % s)
                TT("dve", ua, ps_, c1S, ALU.mult, [kps, "S1cs"], "Sw1")
                TT("dve", ub, pc, s1S, ALU.mult, [kpc, "S1sn"], "Sw2")
                TT("dve", sn, ua, ub, ALU.add, ["Sw1", "Sw2"], "S%dsn" % s)
            cosS.append(cs); sinS.append(sn)
            if s <= LCH:
                ns = tS()
                TS("dve", ns, sn, -1.0, ALU.mult, ["S%dsn" % s], "nS%dsn" % s)
                nsinS.append(ns)
            if 1 <= s <= LCH:
                r = tS()
                ACT(r, adtS, AF.Exp, ["adtS"], "rpow%d" % s, scale=float(s))
                rpow[s] = r
                if s in (1, 4):
                    a = tS(); b_ = tS()
                    TT("dve", a, r, cs, ALU.mult, ["rpow%d" % s, "S%dcs" % s], "lamr%d" % s)
                    TT("dve", b_, r, sn, ALU.mult, ["rpow%d" % s, "S%dsn" % s], "lami%d" % s)
                    lamr[s] = a; lami[s] = b_
        for s in range(1, LCH + 1):
            a = tS(); b_ = tS()
            ks = s + LCH - 1
            TT("dve", a, rpow[s], cosS[ks], ALU.mult, ["rpow%d" % s, "S%dcs" % ks], "lrotr%d" % s)
            TT("dve", b_, rpow[s], sinS[ks], ALU.mult, ["rpow%d" % s, "S%dsn" % ks], "lroti%d" % s)
            lrotr[s] = a; lroti[s] = b_
        self.cosS, self.sinS, self.nsinS, self.lamr, self.lami = cosS, sinS, nsinS, lamr, lami
        self.nlami4 = tS()
        TS("dve", self.nlami4, lami[4], -1.0, ALU.mult, ["lami4"], "nlami4")
        ta = R1.take(sh3); tb = R1.take(sh3)

        def bc(t):
            return t.unsqueeze(2).to_broadcast(sh3)

        for s in range(LCH):
            for (Wt, cr_t, ci_t, kr, ki, nm) in (
                (W4a, cosS[s], sinS[s], "S%dcs" % s, "S%dsn" % s, "W4a"),
                (W4b, lrotr[s + 1], lroti[s + 1], "lrotr%d" % (s + 1), "lroti%d" % (s + 1), "W4b"),
            ):
                TT("dve", ta, CTr, bc(cr_t), ALU.mult, kCTr + [kr], "w4ta")
                TT("dve", tb, CTi, bc(ci_t), ALU.mult, kCTi + [ki], "w4tb")
                TT("dve", Wt[:, :, s, 0, :], ta, tb, ALU.subtract, ["w4ta", "w4tb"], (nm, s, 0))
                TT("dve", ta, CTr, bc(ci_t), ALU.mult, kCTr + [ki], "w4ta")
                TT("dve", tb, CTi, bc(cr_t), ALU.mult, kCTi + [kr], "w4tb")
                TT("dve", ta, ta, tb, ALU.add, ["w4ta", "w4tb"], "w4ta")
                TS("dve", Wt[:, :, s, 1, :], ta, -1.0, ALU.mult, ["w4ta"], (nm, s, 1))
        c7 = cosS[LCH - 1]; s7 = sinS[LCH - 1]
        kc7, ks7 = "S%dcs" % (LCH - 1), "S%dsn" % (LCH - 1)
        sh16 = [128, 16, 16]
        bc16 = lambda t: t.unsqueeze(2).to_broadcast(sh16)
        tav = ta[:, :, 0:16]; tbv = tb[:, :, 0:16]
        h0r = self.h0S[:, 0, :, :]; h0i = self.h0S[:, 1, :, :]
        TT("dve", tav, h0r, bc16(c7), ALU.mult, self.h0S_keys + [kc7, "w4ta"], "w4ta")
        TT("dve", tbv, h0i, bc16(s7), ALU.mult, self.h0S_keys + [ks7, "w4tb"], "w4tb")
        TT("dve", self.h0S_bf[:, 0, :, :], tav, tbv, ALU.add, ["w4ta", "w4tb"], "h0S_bf")
        TT("dve", tav, h0i, bc16(c7), ALU.mult, self.h0S_keys + [kc7], "w4ta")
        TT("dve", tbv, h0r, bc16(s7), ALU.mult, self.h0S_keys + [ks7], "w4tb")
        TT("dve", self.h0S_bf[:, 1, :, :], tav, tbv, ALU.subtract, ["w4ta", "w4tb", "h0S_bf"], "h0S_bf")
        Bnr, Bni = self.Bnr, self.Bni
        nr = tS(); den = tS(); u1 = tS(); cfr = tS(); cfi = tS()
        TS("dve", nr, lamr[1], -1.0, ALU.add, ["lamr1"], "nr")
        TT("dve", den, aSr, aSr, ALU.mult, ["aSr"], "den")
        TT("dve", u1, aSi, aSi, ALU.mult, ["aSi"], "u1")
        TT("dve", den, den, u1, ALU.add, ["den", "u1"], "den")
        p.op("dve", lambda e: e.reciprocal(out=den, in_=den), reads=["den"], writes=["den"])
        TT("dve", cfr, nr, aSr, ALU.mult, ["nr", "aSr"], "cfr")
        TT("dve", u1, lami[1], aSi, ALU.mult, ["lami1", "aSi", "den"], "u1")
        TT("dve", cfr, cfr, u1, ALU.add, ["cfr", "u1"], "cfr")
        TT("dve", cfr, cfr, den, ALU.mult, ["cfr", "den"], "cfr")
        TT("dve", cfi, lami[1], aSr, ALU.mult, ["lami1", "aSr"], "cfi")
        TT("dve", u1, nr, aSi, ALU.mult, ["nr", "aSi", "cfr"], "u1")
        TT("dve", cfi, cfi, u1, ALU.subtract, ["cfi", "u1"], "cfi")
        TT("dve", cfi, cfi, den, ALU.mult, ["cfi", "den"], "cfi")
        BbR = R1.take(sh3); BbI = R1.take(sh3)
        TT("pool", ta, Bnr, bc(cfr), ALU.mult, ["Bnr", "cfr", "w4ta"], "w4ta")
        TT("pool", tb, Bni, bc(cfi), ALU.mult, ["Bni", "cfi", "w4tb"], "w4tb")
        TT("pool", BbR, ta, tb, ALU.subtract, ["w4ta", "w4tb"], "BbR")
        TT("pool", ta, Bni, bc(cfr), ALU.mult, ["Bni", "cfr"], "w4ta")
        TT("pool", tb, Bnr, bc(cfi), ALU.mult, ["Bnr", "cfi"], "w4tb")
        TT("pool", BbI, ta, tb, ALU.add, ["w4ta", "w4tb"], "BbI")
        W1S = self.LF.bitcast(BF16).rearrange("p (a s b c) -> p a s b c", a=4, s=LCH, b=2)
        MS("pool", W1S[:, 0, 0, 0, 0:2], 0.0, "LFfree", r=["Lp", "Fp"])
        p.last_write["Lp"] = p.last_write["LFfree"]; p.last_write["Fp"] = p.last_write["LFfree"]
        tc_ = R1.take(sh3); td_ = R1.take(sh3)
        v4 = lambda t: t.rearrange("p (a b) c -> p a (b c)", a=4)
        for s in range(LCH):
            kc, ks = "S%dcs" % s, "S%dsn" % s
            TT("pool", tc_, BbR, bc(cosS[s]), ALU.mult, ["BbR", kc], "w1tc")
            TT("pool", td_, BbI, bc(sinS[s]), ALU.mult, ["BbI", ks], "w1td")
            TT("pool", W1S[:, :, s, 0, :], v4(tc_), v4(td_), ALU.add, ["w1tc", "w1td", "LFfree"], ("W1S", s, 0))
            TT("pool", tc_, BbI, bc(cosS[s]), ALU.mult, ["BbI", kc], "w1tc")
            TT("pool", td_, BbR, bc(sinS[s]), ALU.mult, ["BbR", ks], "w1td")
            TT("pool", W1S[:, :, s, 1, :], v4(tc_), v4(td_), ALU.subtract, ["w1tc", "w1td", "LFfree"], ("W1S", s, 1))
        for qh in range(4):
            for h in range(2):
                b = self.bank()
                pt = self.bk(b).bitcast(BF16)
                n = 0
                for s in range(4 * h, 4 * h + 4):
                    for part in range(2):
                        p.op("pe", lambda e, pt=pt, n=n, qh=qh, s=s, part=part: e.transpose(
                            pt[:, 128 * n:128 * (n + 1)], W1S[:, qh, s, part, :], self.ident_b),
                            reads=[("W1S", s, part), "ident_b"], writes=[("bank", b)], inc=(n == 7))
                        n += 1
                CP("act", W1[:, qh, 4 * h:4 * h + 4, :, :].rearrange("p a b c -> p (a b c)"), pt, [("bank", b)], ("W1", qh, h))
        self.W1_keys = [("W1", qh, h) for qh in range(4) for h in range(2)]
        self.W4_keys = [(nm, s, part) for nm in ("W4a", "W4b") for s in range(LCH) for part in range(2)]
        self.mur = RS.take([128, 16, NLEV]); self.mui = RS.take([128, 16, NLEV]); self.muni = RS.take([128, 16, NLEV])
        angc = RS.take([128, 16, NLEV]); angs = RS.take([128, 16, NLEV]); rmag = RS.take([128, 16, NLEV])
        CP("dve", angc[:, :, 0], cosS[LCH], ["S%dcs" % LCH], ("angc", 0))
        CP("dve", angs[:, :, 0], sinS[LCH], ["S%dsn" % LCH], ("angs", 0))
        sa = tS(); sb_ = tS()
        for j in range(NLEV):
            if j >= 1:
                TT("dve", sa, angc[:, :, j - 1], angc[:, :, j - 1], ALU.mult, [("angc", j - 1)], "sq_a")
                TT("dve", sb_, angs[:, :, j - 1], angs[:, :, j - 1], ALU.mult, [("angs", j - 1)], "sq_b")
                TT("dve", angc[:, :, j], sa, sb_, ALU.subtract, ["sq_a", "sq_b"], ("angc", j))
                TT("dve", sa, angc[:, :, j - 1], angs[:, :, j - 1], ALU.mult, [("angc", j - 1), ("angs", j - 1)], "sq_a")
                TS("dve", angs[:, :, j], sa, 2.0, ALU.mult, ["sq_a"], ("angs", j))
            ACT(rmag[:, :, j], adtS, AF.Exp, ["adtS"], ("rmag", j), scale=float(LCH * (1 << j)))
            TT("dve", self.mur[:, :, j], rmag[:, :, j], angc[:, :, j], ALU.mult, [("rmag", j), ("angc", j)], ("mur", j))
            TT("dve", self.mui[:, :, j], rmag[:, :, j], angs[:, :, j], ALU.mult, [("rmag", j), ("angs", j)], ("mui", j))
        for j in range(NLEV):
            TS("dve", self.muni[:, :, j], self.mui[:, :, j], -1.0, ALU.mult, [("mui", j)], ("muni", j))
        self.mu_keys = [(nm, j) for nm in ("mur", "mui", "muni") for j in range(NLEV)]
        self.rp8 = RS.take([128, 16, LCH]); self.rp4 = RS.take([128, 16, 4])
        CP("dve", self.rp8, self.rhoS.unsqueeze(2).to_broadcast([128, 16, LCH]), ["rhoS"], "rp8")
        MS("dve", self.rp8[:, :, 0:1], 0.0, "rp8", r=["rp8"])
        CP("dve", self.rp4, self.rhoS.unsqueeze(2).to_broadcast([128, 16, 4]), ["rhoS"], "rp4")
        MS("dve", self.rp4[:, :, 0:1], 0.0, "rp4", r=["rp4"])
    def s5_main(self, u_bf):
        nc, p = self.nc, self.p
        R0, R1 = self.R0, self.R1
        TT, TS, STT, ACT, CP, MS = self.TT, self.TS, self.STT, self.ACT, self.CP, self.MS
        W1, W4a, W4b = self.W1, self.W4a, self.W4b
        cosS, sinS, nsinS, lamr, lami = self.cosS, self.sinS, self.nsinS, self.lamr, self.lami
        z2 = self.z2
        NSET = 2
        pat = [R1.take([128, 2, 512]) for i in range(NSET)]
        pats = [R1.take([128, 2, 64]) for i in range(NSET)]
        gzf = [R1.take([128, 2, 512]) for i in range(2)]
        gzfs = [R1.take([128, 2, 64]) for i in range(2)]
        gzb = [R1.take([128, 2, T], BF16) for i in range(NSET)]
        HA = [R1.take([128, 2, PAD + NCH]) for i in range(NSET)]
        HB = [R1.take([128, 2, PAD + NCH]) for i in range(NSET)]
        Hpb = [R1.take([128, 2, NCH], BF16) for i in range(NSET)]
        for i in range(NSET):
            MS("pool", HA[i], 0.0, ("HA", i))
            MS("pool", HB[i], 0.0, ("HB", i))
        yf = R1.take([128, 512]); gq = R1.take([128, 512]); ga = R1.take([128, 512]); gt = gq
        VP = self.ps[0]
        SVB = 2
        YB = {0: 4, 512: 5, 1024: 6, 1536: 7}
        PIECES = self.PIECES
        nseg = [0]

        def stageA(q):
            qh, ql = q // 4, q % 4
            hb = q % NSET
            kw = dict(tile_position=(96, 0)) if ql == 3 else {}
            CP("pool", pat[hb].rearrange("p a (c s) -> p (a c) s", s=LCH),
               self.rp8[:, q:q + 1, :].to_broadcast([128, 2 * 512 // LCH, LCH]), ["rp8"], ("pat", hb))
            CP("pool", pats[hb].rearrange("p a (c s) -> p (a c) s", s=4),
               self.rp4[:, q:q + 1, :].to_broadcast([128, 2 * 64 // 4, 4]), ["rp4"], ("pats", hb))
            for (c0, w) in PIECES:
                samp = (w == 64)
                L = 4 if samp else LCH
                sl = nseg[0] % 2
                nseg[0] += 1
                if samp:
                    vre = self.bk(SVB)[:, 0:64]; vim = self.bk(SVB)[:, 64:128]
                    vkeys = [("bank", SVB)]
                else:
                    vre = VP[:, 0, :]; vim = VP[:, 1, :]
                    vkeys = [("bank", 0), ("bank", 1)]
                nmm = 0
                for s in range(L):
                    for part in range(2):
                        nmm += 1
                        outv = (vre, vim)[part][:, s:w:L]
                        p.op("pe", lambda e, outv=outv, s=s, part=part, qh=qh, ql=ql, c0=c0, w=w, L=L, kw=kw: e.matmul(
                            outv, W1[32 * ql:32 * ql + 32, qh, s, part, :],
                            u_bf[32 * ql:32 * ql + 32, qh, c0 + s:c0 + w:L], start=True, stop=True, **kw),
                            reads=self.W1_keys + [("u_bf", qh, c0)], writes=vkeys, inc=(nmm == 2 * L))
                if not samp:
                    go = gzf[sl]; gkey = ("gzf", sl)
                    p.op("dve", lambda e, go=go, hb=hb: e.tensor_tensor_scan(
                        out=go.rearrange("p a c -> p (a c)"), data0=pat[hb].rearrange("p a c -> p (a c)"),
                        data1=VP[:].rearrange("p a c -> p (a c)"), initial=0.0, op0=ALU.mult, op1=ALU.add),
                        reads=[("pat", hb)] + vkeys, writes=[gkey])
                else:
                    go = gzfs[sl]; gkey = ("gzfs", sl)
                    p.op("dve", lambda e, go=go, hb=hb: e.tensor_tensor_scan(
                        out=go.rearrange("p a c -> p (a c)"), data0=pats[hb].rearrange("p a c -> p (a c)"),
                        data1=self.bk(SVB)[:, 0:128], initial=0.0, op0=ALU.mult, op1=ALU.add),
                        reads=[("pats", hb)] + vkeys, writes=[gkey])
                CP("act", gzb[hb][:, :, c0:c0 + w], go, [gkey], ("gzb", hb, c0))
                if not samp:
                    k0 = PAD + c0 // LCH
                    nchunk = w // LCH
                    hk = ("HA", hb)
                    CP("pool", HA[hb][:, :, k0:k0 + nchunk], go[:, :, LCH - 1:512:LCH], [gkey], hk)
                else:
                    er = go[:, 0, 3:64:4]; ei = go[:, 1, 3:64:4]
                    c3 = cosS[3][:, q:q + 1]; s3 = sinS[3][:, q:q + 1]; ns3 = nsinS[3][:, q:q + 1]
                    l4r = lamr[4][:, q:q + 1]; l4i = lami[4][:, q:q + 1]; nl4i = self.nlami4[:, q:q + 1]
                    kk = [gkey, "S3cs", "S3sn", "nS3sn", "lamr4", "lami4", "nlami4"] + self.h0S_keys
                    fr = self.Hfin[:, 0, q, 1:17]; fi = self.Hfin[:, 1, q, 1:17]
                    h0r = self.h0S[:, 0, q, :]; h0i = self.h0S[:, 1, q, :]
                    fk = ("Hfin", q)
                    TS("dve", fr, er, c3, ALU.mult, kk, fk)
                    STT("dve", fr, ei, ns3, fr, ALU.mult, ALU.add, kk + [fk], fk)
                    STT("dve", fr, h0r, l4r, fr, ALU.mult, ALU.add, kk + [fk], fk)
                    STT("dve", fr, h0i, nl4i, fr, ALU.mult, ALU.add, kk + [fk], fk)
                    TS("dve", fi, er, s3, ALU.mult, kk + [fk], fk)
                    STT("dve", fi, ei, c3, fi, ALU.mult, ALU.add, kk + [fk], fk)
                    STT("dve", fi, h0r, l4i, fi, ALU.mult, ALU.add, kk + [fk], fk)
                    STT("dve", fi, h0i, l4r, fi, ALU.mult, ALU.add, kk + [fk], fk)

        def stageB(q):
            hb = q % NSET
            src, dst = HA[hb], HB[hb]
            skey, dkey = ("HA", hb), ("HB", hb)
            for j in range(NLEV):
                d = 1 << j
                mr = self.mur[:, q, j:j + 1]; mi = self.mui[:, q, j:j + 1]; mni = self.muni[:, q, j:j + 1]
                mk = [("mur", j), ("mui", j), ("muni", j)]
                S0 = src[:, 0, PAD:PAD + NCH]; S1 = src[:, 1, PAD:PAD + NCH]
                Z0 = src[:, 0, PAD - d:PAD + NCH - d]; Z1 = src[:, 1, PAD - d:PAD + NCH - d]
                D0 = dst[:, 0, PAD:PAD + NCH]; D1 = dst[:, 1, PAD:PAD + NCH]
                STT("dve", D0, Z1, mni, S0, ALU.mult, ALU.add, [skey] + mk, dkey)
                STT("dve", D1, Z0, mi, S1, ALU.mult, ALU.add, [skey, dkey] + mk, dkey)
                Zb = src[:, :, PAD - d:PAD + NCH - d]; Db = dst[:, :, PAD:PAD + NCH]
                STT("dve", Db, Zb, mr, Db, ALU.mult, ALU.add, [skey, dkey] + mk, dkey)
                src, dst = dst, src
                skey, dkey = dkey, skey
            CP("pool", Hpb[hb], HA[hb][:, :, PAD - 1:PAD - 1 + NCH], [("HA", hb)], ("Hpb", hb))
            hr_ = HA[hb][:, 0, PAD + NCH - 1:PAD + NCH]; hi_ = HA[hb][:, 1, PAD + NCH - 1:PAD + NCH]
            c7 = cosS[LCH - 1][:, q:q + 1]; s7 = sinS[LCH - 1][:, q:q + 1]; ns7 = nsinS[LCH - 1][:, q:q + 1]
            kk7 = [("HA", hb), "S%dcs" % (LCH - 1), "S%dsn" % (LCH - 1), "nS%dsn" % (LCH - 1)]
            fr0 = self.Hfin[:, 0, q, 0:1]; fi0 = self.Hfin[:, 1, q, 0:1]
            TS("dve", fr0, hr_, c7, ALU.mult, kk7, ("Hfin0", q))
            STT("dve", fr0, hi_, ns7, fr0, ALU.mult, ALU.add, kk7 + [("Hfin0", q)], ("Hfin0", q))
            TS("dve", fi0, hr_, s7, ALU.mult, kk7 + [("Hfin0", q)], ("Hfin0", q))
            STT("dve", fi0, hi_, c7, fi0, ALU.mult, ALU.add, kk7 + [("Hfin0", q)], ("Hfin0", q))

        def stageC(q):
            qh, ql = q // 4, q % 4
            hb = q % NSET
            kw = dict(tile_position=(0, 96)) if ql == 3 else {}
            for (c0, w) in PIECES:
                samp = (w == 64)
                L = 4 if samp else LCH
                if samp:
                    ybank = SVB
                    yall = self.bk(SVB)[:, 128:192]
                else:
                    ybank = YB[c0]
                    yall = self.bk(ybank)
                for s in range(L):
                    outv = yall[32 * ql:32 * ql + 32, s:w:L]
                    if samp:
                        hr = self.h0S_bf[:, 0, q, :]; hi = self.h0S_bf[:, 1, q, :]
                        hkeys = ["h0S_bf"]
                    else:
                        k0 = c0 // LCH
                        hr = Hpb[hb][:, 0, k0:k0 + w // L]; hi = Hpb[hb][:, 1, k0:k0 + w // L]
                        hkeys = [("Hpb", hb)]
                    ops = [
                        (W4a[:, q, s, 0, :], gzb[hb][:, 0, c0 + s:c0 + w:L]),
                        (W4a[:, q, s, 1, :], gzb[hb][:, 1, c0 + s:c0 + w:L]),
                        (W4b[:, q, s, 0, :], hr),
                        (W4b[:, q, s, 1, :], hi),
                    ]
                    for i, (lh, rh) in enumerate(ops):
                        last = (i == 3 and s == L - 1)
                        p.op("pe", lambda e, outv=outv, lh=lh, rh=rh, i=i, kw=kw: e.matmul(
                            outv, lh, rh, start=(i == 0), stop=(i == 3), **kw),
                            reads=self.W4_keys + [("gzb", hb, c0)] + hkeys, writes=[("bank", ybank)], inc=last)

        def stageY(qh):
            for (c0, w) in PIECES:
                samp = (w == 64)
                if samp:
                    ybank = SVB; ysrc = self.bk(SVB)[:, 128:192]
                else:
                    ybank = YB[c0]; ysrc = self.bk(ybank)
                yv = yf[:, 0:w]; qv = gq[:, 0:w]; av = ga[:, 0:w]; tv = gt[:, 0:w]
                STT("dve", yv, u_bf[:, qh, c0:c0 + w], self.dS5[:, qh:qh + 1], ysrc, ALU.mult, ALU.add,
                    [("u_bf", qh, c0), "dS5", ("bank", ybank)], "s5yf")
                if qh == 0:
                    self.tap("ys5_%d" % c0, yv, ["s5yf"])
                ACT(qv, yv, AF.Square, ["s5yf"], "s5gq")
                ACT(qv, qv, AF.Identity, ["s5gq"], "s5gq", scale=GC, bias=1.0)
                TT("pool", av, qv, yv, ALU.mult, ["s5gq", "s5yf"], "s5ga")
                ACT(qv, av, AF.Tanh, ["s5ga"], "s5gq", scale=GK)
                STT("dve", z2[:, qh, c0:c0 + w], qv, 1.0, yv, ALU.add, ALU.mult, ["s5gq", "s5yf"], ("z2", qh, c0))

        for qh in range(4):
            qs = [4 * qh + i for i in range(4)]
            stageA(qs[0]); stageB(qs[0])
            for i in range(1, 4):
                stageA(qs[i])
                stageC(qs[i - 1])
                stageB(qs[i])
            stageC(qs[3])
            stageY(qh)

    def gelu2(self, yv, qv, av, tv, outv, ky, kq, ka, kt, kout):
        self.ACT(qv, yv, AF.Square, [ky], kq)
        self.STT("dve", av, qv, GK * GC, yv, ALU.mult, ALU.mult, [kq, ky], ka)
        self.STT("dve", av, yv, GK, av, ALU.mult, ALU.add, [ky, ka], ka)
        self.ACT(tv, av, AF.Tanh, [ka], kt)
        self.STT("dve", outv, tv, 1.0, yv, ALU.add, ALU.mult, [kt, ky], kout)

    def lru_stage(self):
        nc, p, I, O = self.nc, self.p, self.I, self.O
        R0, R1 = self.R0, self.R1
        TT, TS, STT, ACT, CP, MS = self.TT, self.TS, self.STT, self.ACT, self.CP, self.MS
        NCK = dict(allow_slow_non_contiguous=True)
        xT, z2 = self.xT, self.z2
        PIECES = self.PIECES
        self.free_banks = list(range(8))
        LRUc = self.LRUc
        LK = self.LRUc_keys
        baT = R0.take([128, 8]); bxT = R0.take([128, 8]); sc8 = R0.take([128, 8]); hsc8 = R0.take([128, 8])
        TS("dve", baT, LRUc[:, :, 69], 0.5, ALU.mult, LK, "baT")
        TS("dve", bxT, LRUc[:, :, 70], 0.5, ALU.mult, LK, "bxT")
        ACT(sc8, LRUc[:, :, 71], AF.Exp, LK, "sc8", scale=-1.0)
        ACT(sc8, sc8, AF.Ln, ["sc8"], "sc8", bias=1.0)
        TS("dve", hsc8, sc8, -4.0, ALU.mult, ["sc8"], "hsc8")
        TS("dve", sc8, sc8, -8.0, ALU.mult, ["sc8", "hsc8"], "sc8")
        Wg = R0.take([128, 8, 2, 128], BF16)
        MS("pool", Wg, 0.0, "Wg")
        for gi, nm in ((0, "lru_wa"), (1, "lru_wx")):
            v = I[nm].rearrange("(j two) i o -> two i j o", two=2)
            for par in range(2):
                p.dma("pool", Wg[64 * par:64 * par + 64, :, gi, 64 * par:64 * par + 64], v[par], "d_wg", writes=["Wg"])
        p.last_write["Wg"] = ("d_wg", p.count["d_wg"])
        hfinL = R0.take([128, 8, 17])
        self.merged2 = R1.take([128, 8, T], BF16)
        merged2 = self.merged2
        self.merged_end = R1.pos
        xl_sb = R1.take([128, 3 + TP]); xs_sb = R1.take([128, 16, 7])
        abuf = R1.take([128, T]); a2buf = R1.take([128, T]); ixbuf = R1.take([128, T])
        hbuf = [R1.take([128, T])]
        tail_sb = R1.take([NTAIL, D])
        MS("pool", xl_sb[:, 0:3], 0.0, ("xl", -1))
        NP = 2
        ytmp = [R1.take([128, 512]) for i in range(2)]
        NXC = 5
        xc = [R1.take([128, 512]) for i in range(NXC)]
        xcb = [R1.take([128, 512], BF16) for i in range(2)]
        rp_ = [R1.take([128, 512]) for i in range(2)]
        ip_ = [R1.take([128, 512]) for i in range(2)]
        glp = [R1.take([128, 512]) for i in range(2)]
        gsp = [R1.take([128, 512]) for i in range(2)]
        gbp = [R1.take([128, 512]) for i in range(2)]
        t1p = [R1.take([128, 512]) for i in range(2)]
        t16 = R1.take([128, 16])
        wsl = [R1.take([128, 8, 128], BF16) for i in range(2)]
        wsg = [R1.take([128, 8, 2, 128], BF16) for i in range(2)]
        wgl = [R1.take([128, 4, 2, 128], BF16) for i in range(2)]
        hb = hbuf[0]
        XB = [0, 1]; GAB = [2, 3]; GXB = [4, 5]; POSTB = [6, 7]
        allp = lambda nm: [(nm, pi) for pi in range(len(PIECES))]

        def load_pre(j):
            sl = j % 2
            p.dma("pool", wsl[sl], I["w_in"][:, 128 * j:128 * (j + 1)].rearrange("(k p) n -> p k n", p=128), "d_wsl%d" % sl,
                  writes=[("wsl", sl)])

        def load_post(j):
            sl = j % 2
            wv = lambda c: I["w_in"][:, c:c + 128].rearrange("(k p) n -> p k n", p=128)
            gv = lambda c: I["w_glu"][:, c:c + 128].rearrange("(k p) n -> p k n", p=128)
            p.dma("pool", wsg[sl][:, :, 0, :], wv(1536 + 128 * j), "d_wsg%d_0" % sl, writes=[("wsg", sl, 0)])
            p.dma("pool", wsg[sl][:, :, 1, :], wv(2560 + 128 * j), "d_wsg%d_1" % sl, writes=[("wsg", sl, 1)])
            p.dma("pool", wgl[sl][:, :, 0, :], gv(128 * j), "d_wgl%d_0" % sl, writes=[("wgl", sl, 0)])
            p.dma("pool", wgl[sl][:, :, 1, :], gv(1024 + 128 * j), "d_wgl%d_1" % sl, writes=[("wgl", sl, 1)])

        items = [dict(j=j, pi=pi, c0=c0, w=w) for j in range(8) for pi, (c0, w) in enumerate(PIECES)]

        def P0(t, it):
            j, c0, w = it["j"], it["c0"], it["w"]
            sl = j % 2
            if it["pi"] == 0 and j + 1 < 8:
                load_pre(j + 1)
            b = XB[t % 2]
            for k in range(8):
                p.op("pe", lambda e, k=k, b=b, sl=sl, c0=c0, w=w: e.matmul(
                    self.bk(b)[:, 0:w], wsl[sl][:, k, :], xT[:, k, c0:c0 + w], start=(k == 0), stop=(k == 7)),
                    reads=[("wsl", sl)] + self.xT_keys(c0, w), writes=[("bank", b)], inc=(k == 7))
            if it["pi"] == len(PIECES) - 1:
                tb = POSTB[1]
                for k in range(8):
                    p.op("pe", lambda e, k=k, tb=tb, sl=sl: e.matmul(self.bk(tb)[0:NTAIL, 0:128], xT[:, k, TAIL0:T], wsl[sl][:, k, :],
                                                                     start=(k == 0), stop=(k == 7)),
                         reads=[("wsl", sl)] + self.xT_keys(TAIL0, NTAIL), writes=[("bank", tb)], inc=(k == 7))
                CP("act", tail_sb[:, 128 * j:128 * (j + 1)], self.bk(tb)[0:NTAIL, 0:128], [("bank", tb)], ("tail", j))

        def P1(t, it):
            j, pi, c0, w = it["j"], it["pi"], it["c0"], it["w"]
            b = XB[t % 2]
            ps = self.bk(b)[:, 0:w]
            yv = ytmp[t % 2][:, 0:w]
            ACT(yv, ps, AF.Identity, [("bank", b)] + LK, ("ytmp", t % 2), scale=LRUc[:, j, 67:68], bias=LRUc[:, j, 68:69])
            if w != 64:
                CP("act", xl_sb[:, 3 + c0:3 + c0 + w], ps, [("bank", b)], ("xl", pi))
            else:
                CP("pool", xs_sb[:, :, 0:3], LRUc[:, j, 0:48].rearrange("p (b k) -> p b k", k=3), LK, ("xs", "st"))
                CP("act", xs_sb[:, :, 3:7], ps.rearrange("p (b s) -> p b s", s=4), [("bank", b)], ("xs", "new"))

        def P2(t, it):
            j, pi, c0, w = it["j"], it["pi"], it["c0"], it["w"]
            cw = [LRUc[:, j, 64 + k:65 + k] for k in range(4)]
            yk = ("ytmp", t % 2); xk_ = ("xc", t % NXC)
            yv = ytmp[t % 2][:, 0:w]; xcv = xc[t % NXC][:, 0:w]
            if w != 64:
                xk = [("xl", pi), ("xl", pi - 1), yk] + LK
                STT("dve", yv, xl_sb[:, c0 + 2:c0 + 2 + w], cw[2], yv, ALU.mult, ALU.add, xk, yk)
                STT("dve", yv, xl_sb[:, c0 + 1:c0 + 1 + w], cw[1], yv, ALU.mult, ALU.add, xk, yk)
                STT("dve", xcv, xl_sb[:, c0:c0 + w], cw[0], yv, ALU.mult, ALU.add, xk, xk_)
            else:
                xk = [("xs", "st"), ("xs", "new"), yk] + LK
                y3 = yv.rearrange("p (b s) -> p b s", s=4); xc3 = xcv.rearrange("p (b s) -> p b s", s=4)
                STT("dve", y3, xs_sb[:, :, 2:6], cw[2], y3, ALU.mult, ALU.add, xk, yk)
                STT("dve", y3, xs_sb[:, :, 1:5], cw[1], y3, ALU.mult, ALU.add, xk, yk)
                STT("dve", xc3, xs_sb[:, :, 0:4], cw[0], y3, ALU.mult, ALU.add, xk, xk_)
            if j == 0:
                self.tap("xc_%d" % c0, xcv, [xk_])
            CP("pool", xcb[t % 2][:, 0:w], xcv, [xk_], ("xcb", t % 2))

        def P3(t, it):
            j, w = it["j"], it["w"]
            for (bb, gi) in ((GAB[t % 2], 0), (GXB[t % 2], 1)):
                p.op("pe", lambda e, bb=bb, gi=gi, j=j, t=t, w=w: e.matmul(self.bk(bb)[:, 0:w], Wg[:, j, gi, :], xcb[t % 2][:, 0:w],
                                                                          start=True, stop=True),
                     reads=["Wg", ("xcb", t % 2)], writes=[("bank", bb)])

        def P4(t, it):
            j, w = it["j"], it["w"]
            ACT(rp_[t % 2][:, 0:w], self.bk(GAB[t % 2])[:, 0:w], AF.Tanh, [("bank", GAB[t % 2]), "baT"], ("rp", t % 2),
                scale=0.5, bias=baT[:, j:j + 1])
            ACT(ip_[t % 2][:, 0:w], self.bk(GXB[t % 2])[:, 0:w], AF.Tanh, [("bank", GXB[t % 2]), "bxT"], ("ip", t % 2),
                scale=0.5, bias=bxT[:, j:j + 1])

        def P5(t, it):
            j, pi, c0, w = it["j"], it["pi"], it["c0"], it["w"]
            rv = rp_[t % 2][:, 0:w]; iv = ip_[t % 2][:, 0:w]; xcv = xc[t % NXC][:, 0:w]
            ACT(abuf[:, c0:c0 + w], rv, AF.Exp, [("rp", t % 2), "hsc8"], ("abuf", pi), scale=hsc8[:, j:j + 1], bias=hsc8[:, j:j + 1])
            ACT(a2buf[:, c0:c0 + w], rv, AF.Exp, [("rp", t % 2), "sc8"], ("a2buf", pi), scale=sc8[:, j:j + 1], bias=sc8[:, j:j + 1])
            STT("dve", ixbuf[:, c0:c0 + w], iv, 1.0, xcv, ALU.add, ALU.mult, [("ip", t % 2), ("xc", t % NXC)], ("ixbuf", pi))
            if pi == len(PIECES) - 1:
                mid_tile(j)

        post_queue = []

        def mid_tile(j):
            sl = j % 2
            ACT(a2buf, a2buf, AF.Sqrt, allp("a2buf"), "mh", scale=-0.25, bias=0.25)
            TT("dve", ixbuf, a2buf, ixbuf, ALU.mult, ["mh"] + allp("ixbuf"), "bterm")
            TT("dve", t16, abuf[:, TP:T:4], LRUc[:, j, 48:64], ALU.mult, allp("abuf") + LK, "t16")
            TT("dve", ixbuf[:, TP:T:4], ixbuf[:, TP:T:4], t16, ALU.add, ["bterm", "t16"], "bterm")
            MS("dve", abuf[:, TP:T:4], 0.0, "afix", r=allp("abuf") + ["t16"])
            p.op("dve", lambda e: e.tensor_tensor_scan(out=hb, data0=abuf, data1=ixbuf, initial=0.0, op0=ALU.mult, op1=ALU.add),
                 reads=allp("abuf") + ["afix", "bterm"], writes=["hbuf"])
            for nm in ("abuf", "a2buf", "ixbuf"):
                for pi in range(len(PIECES)):
                    p.readers.setdefault((nm, pi), []).append(p.last_write["hbuf"])
            if j == 0:
                self.tap("hbuf", hb, ["hbuf"])
            CP("pool", hfinL[:, j, 0:1], hb[:, TP - 1:TP], ["hbuf"], ("hfinL", j, 0))
            CP("pool", hfinL[:, j, 1:17], hb[:, TP + 3:T:4], ["hbuf"], ("hfinL", j, 1))
            for pi, (c0, w) in enumerate(PIECES):
                post_queue.append((j, pi, c0, w))

        npq = [0]

        def post_piece(j, pi, c0, w):
            sl = j % 2
            i2 = npq[0] % 2
            npq[0] += 1
            glv = glp[i2][:, 0:w]; gsv = gsp[i2][:, 0:w]; gbv = gbp[i2][:, 0:w]; t1v = t1p[i2][:, 0:w]
            groups = [("gl", wsg[sl], 0, 8, xT, ("wsg", sl, 0)), ("gs", wsg[sl], 1, 8, xT, ("wsg", sl, 1)),
                      ("gb", wgl[sl], 1, 4, z2, ("wgl", sl, 1)), ("ga", wgl[sl], 0, 4, z2, ("wgl", sl, 0))]
            for gi_, (nm, wt, idx, nk, src, wkey) in enumerate(groups):
                bb = POSTB[gi_ % 2]
                for k in range(nk):
                    rk = self.xT_keys(c0, w) if src is xT else [("z2", k, c0)]
                    p.op("pe", lambda e, k=k, bb=bb, wt=wt, idx=idx, src=src, nk=nk, c0=c0, w=w: e.matmul(
                        self.bk(bb)[:, 0:w], wt[:, k, idx, :], src[:, k, c0:c0 + w], start=(k == 0), stop=(k == nk - 1)),
                        reads=[wkey] + rk, writes=[("bank", bb)], inc=(k == nk - 1))
                psv = self.bk(bb)[:, 0:w]
                if nm == "gl":
                    ACT(glv, psv, AF.Tanh, [("bank", bb)], ("glp", i2), scale=0.5)
                elif nm == "gs":
                    ACT(gsv, psv, AF.Tanh, [("bank", bb)], ("gsp", i2), scale=0.5)
                elif nm == "gb":
                    ACT(gbv, psv, AF.Tanh, [("bank", bb)], ("gbp", i2), scale=0.25)
                else:
                    CP("act", t1v, psv, [("bank", bb)], ("t1p", i2))
            STT("dve", t1v, gbv, 1.0, t1v, ALU.add, ALU.mult, [("gbp", i2), ("t1p", i2)], ("t1p", i2))
            STT("dve", t1v, gsv, 1.0, t1v, ALU.add, ALU.mult, [("gsp", i2), ("t1p", i2)], ("t1p", i2))
            STT("dve", glv, glv, 1.0, hb[:, c0:c0 + w], ALU.add, ALU.mult, [("glp", i2), "hbuf"], ("glp", i2))
            STT("dve", merged2[:, j, c0:c0 + w], t1v, 0.25, glv, ALU.mult, ALU.add, [("t1p", i2), ("glp", i2)], ("merged2", j, c0))
            if pi == len(PIECES) - 1 and j + 2 < 8:
                load_post(j + 2)

        load_pre(0); load_post(0); load_post(1)
        N_ = len(items)
        stages = [(P5, 5), (P4, 4), (P1, 1), (P2, 2), (P3, 3), (P0, 0)]
        for t in range(N_ + 5):
            for fn, lag in stages:
                if 0 <= t - lag < N_:
                    fn(t - lag, items[t - lag])
            if post_queue:
                post_piece(*post_queue.pop(0))
        while post_queue:
            post_piece(*post_queue.pop(0))
        self.tap("merged2", merged2[:, 0, :], [("merged2", 0, c0) for c0, _ in PIECES])
        self.out_dma(O["lru_conv"], tail_sb, [("tail", j) for j in range(8)])
        self.out_dma(O["lru_h"], hfinL, [("hfinL", j, i) for j in range(8) for i in range(2)])

    def ln_tile(self, ps_flat, res, gB, bB, ytok, outv, rows, kps, kres, kg, kb, ky, kout, st6, mv, sd):
        p = self.p
        TT, TS, STT, ACT, CP = self.TT, self.TS, self.STT, self.ACT, self.CP
        yv = ytok[0:rows, :]
        p.op("act", lambda e: e.activation(out=yv, in_=ps_flat[0:rows, :], func=AF.Copy, scale=0.5), reads=kps, writes=[ky])
        STT("dve", yv, res[0:rows, :], ALPHA, yv, ALU.mult, ALU.add, [kres, ky], ky)
        for h in range(2):
            p.op("dve", lambda e, h=h: e.bn_stats(out=st6[0:rows, h, :], in_=yv[:, 512 * h:512 * (h + 1)]), reads=[ky], writes=[ky + ("st", h)])
        p.op("dve", lambda e: e.bn_aggr(out=mv[0:rows, :], in_=st6[0:rows, :, :].rearrange("p a b -> p (a b)")),
             reads=[ky + ("st", 0), ky + ("st", 1)], writes=[ky + ("mv",)])
        ACT(sd[0:rows, :], mv[0:rows, 1:2], AF.Sqrt, [ky + ("mv",)], ky + ("sd",), bias=LN_EPS)
        p.op("dve", lambda e: e.reciprocal(out=sd[0:rows, :], in_=sd[0:rows, :]), reads=[ky + ("sd",)], writes=[ky + ("sd",)])
        TS("dve", yv, yv, mv[0:rows, 0:1], ALU.subtract, [ky, ky + ("mv",), ky + ("sd",)], ky, s2=sd[0:rows, 0:1], op1=ALU.mult)
        TT("pool", yv, yv, gB[0:rows, :], ALU.mult, [ky, kg], ky)
        TT("dve", outv[0:rows, :], yv, bB[0:rows, :], ALU.add, [ky, kb], kout)

    def mix_stage(self):
        nc, p, I, O = self.nc, self.p, self.I, self.O
        R0, R1 = self.R0, self.R1
        TT, TS, STT, ACT, CP, MS = self.TT, self.TS, self.STT, self.ACT, self.CP, self.MS
        merged2 = self.merged2
        x1T = self.xT
        self.x1T = x1T
        lnc = self.lnc
        for i, nm in enumerate(("ln1_g", "ln1_b", "ln2_g", "ln2_b")):
            p.dma("sp", lnc[:, i, :], I[nm].partition_broadcast(128), "d_lnc%d" % i, writes=[("lnc", i)])
        wout = R1.take([128, 8, D], BF16)
        for h in range(2):
            p.dma("pool", wout[:, :, 512 * h:512 * (h + 1)], I["w_out"][:, 512 * h:512 * (h + 1)].rearrange("(k p) n -> p k n", p=128),
                  "d_wout%d" % h, writes=[("wout", h)])
        NX, NYT, NX1, NXB, NS_ = 3, 4, 3, 2, 4
        xtok = [R1.take([128, D]) for i in range(NX)]
        ytok = [R1.take([128, D]) for i in range(NYT)]
        x1tok = [R1.take([128, D]) for i in range(NX1)]
        x1b = [R1.take([128, D], BF16) for i in range(NXB)]
        st6 = [R1.take([128, 2, 6]) for i in range(NS_)]
        mv = [R1.take([128, 2]) for i in range(NS_)]
        sd = [R1.take([128, 1]) for i in range(NS_)]
        ntt = (T + 127) // 128
        TB = [4, 5, 6, 7]
        g1, b1 = lnc[:, 0, :], lnc[:, 1, :]
        items = [dict(tt=tt, r0=tt * 128, rows=min(128, T - tt * 128)) for tt in range(ntt)]

        def M0(t, it):
            r0, rows = it["r0"], it["rows"]
            s = t % NX; pp = t % 2
            p.dma("sp", xtok[s][0:rows, :], I["x"][r0:r0 + rows, :], "d_xtok%d" % s, writes=[("xtok", s)])
            for h in range(2):
                for k in range(8):
                    p.op("pe", lambda e, k=k, h=h, pp=pp, r0=r0, rows=rows: e.matmul(
                        self.ps[pp][0:rows, h, :], merged2[:, k, r0:r0 + rows], wout[:, k, 512 * h:512 * (h + 1)],
                        start=(k == 0), stop=(k == 7)),
                        reads=[("wout", h)] + [("merged2", k, c0) for (c0, w) in self.PIECES if c0 <= r0 < c0 + w],
                        writes=[("bank", 2 * pp + h)], inc=(k == 7))

        def M1(t, it):
            rows = it["rows"]
            pp = t % 2; sy = t % NYT; sx = t % NX; ss = t % NS_
            psf = self.ps[pp][:].rearrange("p a c -> p (a c)")
            yv = ytok[sy][0:rows, :]
            ky = ("ytok", sy)
            p.op("act", lambda e: e.activation(out=yv, in_=psf[0:rows, :], func=AF.Copy, scale=0.5),
                 reads=[("bank", 2 * pp), ("bank", 2 * pp + 1)], writes=[ky])
            STT("dve", yv, xtok[sx][0:rows, :], ALPHA, yv, ALU.mult, ALU.add, [("xtok", sx), ky], ky)
            for h in range(2):
                p.op("dve", lambda e, h=h: e.bn_stats(out=st6[ss][0:rows, h, :], in_=yv[:, 512 * h:512 * (h + 1)]),
                     reads=[ky], writes=[("st6", ss, h)])
            p.op("dve", lambda e: e.bn_aggr(out=mv[ss][0:rows, :], in_=st6[ss][0:rows, :, :].rearrange("p a b -> p (a b)")),
                 reads=[("st6", ss, 0), ("st6", ss, 1)], writes=[("mv", ss)])

        def M2(t, it):
            rows = it["rows"]
            sy = t % NYT; ss = t % NS_
            yv = ytok[sy][0:rows, :]
            ky = ("ytok", sy)
            ACT(sd[ss][0:rows, :], mv[ss][0:rows, 1:2], AF.Sqrt, [("mv", ss)], ("sd", ss), bias=LN_EPS)
            p.op("dve", lambda e: e.reciprocal(out=sd[ss][0:rows, :], in_=sd[ss][0:rows, :]), reads=[("sd", ss)], writes=[("sd", ss)])
            TS("dve", yv, yv, mv[ss][0:rows, 0:1], ALU.subtract, [ky, ("mv", ss), ("sd", ss)], ky, s2=sd[ss][0:rows, 0:1], op1=ALU.mult)

        def M3a(t, it):
            tt, r0, rows = it["tt"], it["r0"], it["rows"]
            sy = t % NYT; s1 = t % NX1
            yv = ytok[sy][0:rows, :]
            ky = ("ytok", sy)
            TT("pool", yv, yv, g1[0:rows, :], ALU.mult, [ky, ("lnc", 0)], ky)
            TT("dve", x1tok[s1][0:rows, :], yv, b1[0:rows, :], ALU.add, [ky, ("lnc", 1)], ("x1tok", s1))
            if tt == 0:
                self.tap("x1tok", x1tok[s1], [("x1tok", s1)])
            p.dma("sp", self.x1_scr[r0:r0 + rows, :], x1tok[s1][0:rows, :], "d_x1w%d" % s1, reads=[("x1tok", s1)], writes=[("x1scr", tt)])

        def M3b(t, it):
            rows = it["rows"]
            s1 = t % NX1; sb = t % NXB
            CP("act", x1b[sb][0:rows, :], x1tok[s1][0:rows, :], [("x1tok", s1)], ("x1b", sb))

        def M4(t, it):
            rows = it["rows"]
            sb = t % NXB
            b = TB[t % 4]
            it["b"] = b
            pt = self.bk(b).bitcast(BF16)
            for k in range(8):
                p.op("pe", lambda e, k=k, pt=pt, sb=sb, rows=rows: e.transpose(
                    pt[:, k * 128:k * 128 + rows], x1b[sb][0:rows, k * 128:(k + 1) * 128], self.ident_b[0:rows, 0:rows]),
                    reads=[("x1b", sb), "ident_b"], writes=[("bank", b)], inc=(k == 7))

        def M5(t, it):
            tt, r0, rows, b = it["tt"], it["r0"], it["rows"], it["b"]
            pt = self.bk(b).bitcast(BF16)
            src = pt.rearrange("p (k c) -> p k c", c=128)[:, :, 0:rows]
            CP("act", x1T[:, :, r0:r0 + rows], src, [("bank", b)], ("x1T", tt))

        stages = [(M3a, 3), (M1, 1), (M2, 2), (M3b, 3), (M5, 5), (M4, 4), (M0, 0)]
        N_ = len(items)
        for t in range(N_ + 5):
            for fn, lag in stages:
                if 0 <= t - lag < N_:
                    fn(t - lag, items[t - lag])
        self.tap("x1T", x1T[:, 0, :], [("x1T", tt) for tt in range(ntt)])

    def x1T_keys(self, c0, w):
        return [("x1T", tt) for tt in range(c0 // 128, (c0 + w - 1) // 128 + 1)]

    def ffn_stage(self):
        nc, p, I, O = self.nc, self.p, self.I, self.O
        R0, R1 = self.R0, self.R1
        TT, TS, STT, ACT, CP, MS = self.TT, self.TS, self.STT, self.ACT, self.CP, self.MS
        NCK = dict(allow_slow_non_contiguous=True)
        x1T, lnc = self.x1T, self.lnc
        NJ = DFF // 128
        QW = 576
        Fc = self.Fc
        FK = self.Fc_keys
        wdn = R1.take([128, NJ, D], BF16)
        for c in range(6):
            p.dma("pool", wdn[:, 4 * c:4 * c + 4, :], I["w_down"][512 * c:512 * (c + 1), :].rearrange("(k p) n -> p k n", p=128),
                  "d_wdn%d" % c, writes=[("wdn", c)])
        Gq = R1.take([128, NJ, QW], BF16)
        NSL = 4
        wup = [R1.take([128, 8, 2, 128], BF16) for i in range(NSL)]
        halo = R1.take([128, NJ, 2])
        MS("pool", halo, 0.0, "halo_init")
        NY, NQ, NG = 5, 3, 2
        a_sb = [R1.take([128, 2 + 512]) for i in range(2)]
        as_sb = R1.take([128, 16, 6])
        y0 = [R1.take([128, 512]) for i in range(NY)]
        qq = [R1.take([128, 512]) for i in range(NQ)]
        ag = [R1.take([128, 512]) for i in range(NG)]
        tailf = [R1.take([NTAIL, 512]) for i in range(2)]
        NB = 2
        x1tok = [R1.take([128, D]) for i in range(NB)]
        ytok = [R1.take([128, D]) for i in range(NB)]
        otok = ytok
        st6 = [R1.take([128, 2, 6]) for i in range(NB)]
        mv = [R1.take([128, 2]) for i in range(NB)]
        sd = [R1.take([128, 1]) for i in range(NB)]
        nld = [0]

        def load_wup(j):
            sl = nld[0] % NSL
            nld[0] += 1
            wv = lambda c: I["w_up"][:, c:c + 128].rearrange("(k p) n -> p k n", p=128)
            p.dma("pool", wup[sl][:, :, 0, :], wv(128 * j), "d_wup%d_0" % sl, writes=[("wup", sl, 0)])
            p.dma("pool", wup[sl][:, :, 1, :], wv(DFF + 128 * j), "d_wup%d_1" % sl, writes=[("wup", sl, 1)])
            return sl

        npc = [0]
        ntile = [0]
        for n in range(4):
            q0 = 512 * n
            pieces = [(q0, 512, 0)] + ([(TP, NS, 512)] if n == 3 else [])
            RA = [0, 1]
            RG = [2, 3, 4, 5, 6]
            TAILB = 7
            items = []
            for j in range(NJ):
                for pi_, (c0, w, lc0) in enumerate(pieces):
                    items.append(dict(j=j, c0=c0, w=w, lc0=lc0, first=(pi_ == 0), last=(pi_ == len(pieces) - 1)))
            pending = [load_wup(0), load_wup(1), load_wup(2)]
            cur_sl = {}

            def S0(t, it):
                j, c0, w = it["j"], it["c0"], it["w"]
                if it["first"]:
                    cur_sl[j] = pending.pop(0)
                    if j + 3 < NJ:
                        pending.append(load_wup(j + 3))
                sl = cur_sl[j]
                bA = RA[t % 2]; bG = RG[t % 5]
                it["bA"], it["bG"] = bA, bG
                for (bb, gi) in ((bA, 0), (bG, 1)):
                    for k in range(8):
                        p.op("pe", lambda e, k=k, bb=bb, gi=gi, sl=sl, c0=c0, w=w: e.matmul(
                            self.bk(bb)[:, 0:w], wup[sl][:, k, gi, :], x1T[:, k, c0:c0 + w], start=(k == 0), stop=(k == 7)),
                            reads=[("wup", sl, gi)] + self.x1T_keys(c0, w), writes=[("bank", bb)], inc=(k == 7))
                if n == 3 and it["last"]:
                    for k in range(8):
                        p.op("pe", lambda e, k=k, sl=sl: e.matmul(self.bk(TAILB)[0:NTAIL, 0:128], x1T[:, k, TAIL0:T], wup[sl][:, k, 0, :],
                                                                  start=(k == 0), stop=(k == 7)),
                             reads=[("wup", sl, 0)] + self.x1T_keys(TAIL0, NTAIL), writes=[("bank", TAILB)], inc=(k == 7))

            def S1(t, it):
                j, c0, w, bA = it["j"], it["c0"], it["w"], it["bA"]
                samp = (w == NS)
                iy = t % NY; ia = t % 2
                fw = [Fc[:, j, 32 + k:33 + k] for k in range(3)]
                aps = self.bk(bA)[:, 0:w]
                yv = y0[iy][:, 0:w]
                ACT(yv, aps, AF.Identity, [("bank", bA)] + FK, ("y0", iy), scale=fw[2], bias=Fc[:, j, 35:36])
                if not samp:
                    ab = a_sb[ia]
                    CP("act", ab[:, 2:2 + w], aps, [("bank", bA)], ("a_sb", ia))
                    CP("dve", ab[:, 0:2], halo[:, j, :], ["halo_init", ("halo", j)], ("a_sbh", ia))
                    ak = [("a_sb", ia), ("a_sbh", ia), ("y0", iy)]
                    STT("dve", yv, ab[:, 1:1 + w], fw[1], yv, ALU.mult, ALU.add, ak, ("y0", iy))
                    STT("dve", yv, ab[:, 0:w], fw[0], yv, ALU.mult, ALU.add, ak, ("y0", iy))
                    CP("dve", halo[:, j, :], ab[:, w:w + 2], [("a_sb", ia), ("a_sbh", ia)], ("halo", j))
                else:
                    CP("act", as_sb[:, :, 2:6], aps.rearrange("p (b s) -> p b s", s=4), [("bank", bA)], ("as_sb", "new"))
                    CP("dve", as_sb[:, :, 0:2], Fc[:, j, 0:32].rearrange("p (b k) -> p b k", k=2), FK, ("as_sb", "st"))
                    ak = [("as_sb", "new"), ("as_sb", "st"), ("y0", iy)]
                    y3 = yv.rearrange("p (b s) -> p b s", s=4)
                    STT("dve", y3, as_sb[:, :, 1:5], fw[1], y3, ALU.mult, ALU.add, ak, ("y0", iy))
                    STT("dve", y3, as_sb[:, :, 0:4], fw[0], y3, ALU.mult, ALU.add, ak, ("y0", iy))
                if n == 3 and it["last"]:
                    tb = (j // 4) % 2
                    CP("act", tailf[tb][:, 128 * (j % 4):128 * (j % 4 + 1)], self.bk(TAILB)[0:NTAIL, 0:128], [("bank", TAILB)], ("tailf", tb, j % 4))
                    if j % 4 == 3:
                        sem = "o_tf%d" % tb
                        p.dma("sp", O["ffn_conv"][:, 512 * (j // 4):512 * (j // 4 + 1)], tailf[tb], sem,
                              reads=[("tailf", tb, i) for i in range(4)])
                        self.out_sems[sem] = p.count[sem]

            def S2(t, it):
                w = it["w"]
                iy = t % NY; iq = t % NQ
                yv = y0[iy][:, 0:w]; qv = qq[iq][:, 0:w]
                ACT(qv, yv, AF.Square, [("y0", iy)], ("qq", iq))
                ACT(qv, qv, AF.Identity, [("qq", iq)], ("qq", iq), scale=GC, bias=1.0)

            def S3(t, it):
                w = it["w"]
                iy = t % NY; iq = t % NQ; ig = t % NG
                TT("pool", ag[ig][:, 0:w], qq[iq][:, 0:w], y0[iy][:, 0:w], ALU.mult, [("qq", iq), ("y0", iy)], ("ag", ig))

            def S4(t, it):
                j, w, lc0, bG = it["j"], it["w"], it["lc0"], it["bG"]
                iy = t % NY; iq = t % NQ; ig = t % NG
                yv = y0[iy][:, 0:w]; qv = qq[iq][:, 0:w]; av = ag[ig][:, 0:w]
                gps = self.bk(bG)[:, 0:w]
                ACT(qv, av, AF.Tanh, [("ag", ig)], ("qq", iq), scale=GK)
                STT("dve", av, qv, 1.0, yv, ALU.add, ALU.mult, [("qq", iq), ("y0", iy)], ("ag", ig))
                TT("dve", Gq[:, j, lc0:lc0 + w], av, gps, ALU.mult, [("ag", ig), ("bank", bG)], ("Gq", j, lc0))

            N_ = len(items)
            stages = [(S4, 4), (S1, 1), (S2, 2), (S3, 3), (S0, 0)]
            for t in range(N_ + 4):
                for fn, lag in stages:
                    if 0 <= t - lag < N_:
                        fn(t - lag, items[t - lag])
            if n == 0:
                self.tap("Gq", Gq[:, 0, :], [("Gq", 0, 0)])
            tts = [4 * n + i for i in range(4)] + ([16] if n == 3 else [])
            for tt in tts:
                r0 = tt * 128
                rows = min(128, T - r0)
                lc = r0 - q0 if tt < 16 else 512
                s = ntile[0] % NB
                pp = ntile[0] % 2
                ntile[0] += 1
                p.dma("sp", x1tok[s][0:rows, :], self.x1_scr[r0:r0 + rows, :], "d_x1r%d" % s, reads=[("x1scr", tt)], writes=[("x1tok2", s)])
                psf = self.ps[pp][:].rearrange("p a c -> p (a c)")
                gkeys = [("Gq", j, 512 if tt == 16 else 0) for j in range(NJ)]
                for h in range(2):
                    for k in range(NJ):
                        p.op("pe", lambda e, k=k, h=h, pp=pp, lc=lc, rows=rows: e.matmul(
                            self.ps[pp][0:rows, h, :], Gq[:, k, lc:lc + rows], wdn[:, k, 512 * h:512 * (h + 1)],
                            start=(k == 0), stop=(k == NJ - 1)),
                            reads=[("wdn", k // 4), ("Gq", k, 512 if tt == 16 else 0)], writes=[("bank", 2 * pp + h)], inc=(k == NJ - 1))
                self.ln_tile(psf, x1tok[s], lnc[:, 2, :], lnc[:, 3, :], ytok[s], otok[s], rows,
                             [("bank", 2 * pp), ("bank", 2 * pp + 1)], ("x1tok2", s), ("lnc", 2), ("lnc", 3), ("ytok2", s), ("ytok2", s),
                             st6[s], mv[s], sd[s])
                sem = "o_y%d" % s
                p.dma("sp", O["y"][r0:r0 + rows, :], otok[s][0:rows, :], sem, reads=[("ytok2", s)])
                self.out_sems[sem] = p.count[sem]

    def finish(self):
        fw = [(s, v) for s, v in self.out_sems.items()]
        self.p.emit(final_waits=fw)
        self.st.close()
        print("arena peaks: R0 %d/%d words, R1 %d/%d words" % (self.R0.peak, self.R0.words, self.R1.peak, self.R1.words))
        return self.nc


def shard_inputs(inputs, c):
    f = lambda a: np.ascontiguousarray(a, dtype=np.float32)
    m = {}
    m["x"] = f(np.concatenate([inputs["x_prompt"][c], inputs["x_sample"][NSQ * c:NSQ * (c + 1)].reshape(NS, D)], axis=0))
    m["st_lru_conv"] = f(inputs["state_lru_conv"][0, NSQ * c:NSQ * (c + 1)].reshape(NSQ * 3, D))
    m["st_lru_h"] = f(inputs["state_lru_h"][0, NSQ * c:NSQ * (c + 1)])
    m["st_s5_re"] = f(inputs["state_s5_re"][0, NSQ * c:NSQ * (c + 1)].reshape(NSQ, 2048))
    m["st_s5_im"] = f(inputs["state_s5_im"][0, NSQ * c:NSQ * (c + 1)].reshape(NSQ, 2048))
    m["st_ffn_conv"] = f(inputs["state_ffn_conv"][0, NSQ * c:NSQ * (c + 1)].reshape(NSQ * 2, DFF))
    for k in ("w_in", "lru_conv_w", "lru_conv_b", "lru_wa", "lru_ba", "lru_wx", "lru_bx", "lru_lambda", "s5_a_re", "s5_a_im",
              "s5_log_dt", "s5_b_re", "s5_b_im", "s5_c_re", "s5_c_im", "s5_d", "w_glu", "w_out", "ln1_g", "ln1_b", "w_up",
              "ffn_conv_w", "ffn_conv_b", "w_down", "ln2_g", "ln2_b"):
        m[k] = f(inputs[k][0])
    return m


_NC_CACHE = {}


def _get_nc():
    if "nc" not in _NC_CACHE:
        _NC_CACHE["nc"] = Builder().build()
    return _NC_CACHE["nc"]


def kernel(**inputs):
    nc = _get_nc()
    in_maps = [shard_inputs(inputs, c) for c in range(NCORES)]
    res = run_bass_kernel_spmd(nc, in_maps, core_ids=list(range(NCORES)))
    R = res.results
    B = NCORES
    y_p = np.zeros((B, TP, D), np.float32); y_s = np.zeros((B * NSQ, 4, D), np.float32)
    p_conv = np.zeros((1, B, 3, D), np.float32); p_h = np.zeros((1, B, D), np.float32)
    p_re = np.zeros((1, B, 32, 64), np.float32); p_im = np.zeros((1, B, 32, 64), np.float32)
    p_ffn = np.zeros((1, B, 2, DFF), np.float32)
    s_conv = np.zeros((1, B * NSQ, 3, D), np.float32); s_h = np.zeros((1, B * NSQ, D), np.float32)
    s_re = np.zeros((1, B * NSQ, 32, 64), np.float32); s_im = np.zeros((1, B * NSQ, 32, 64), np.float32)
    s_ffn = np.zeros((1, B * NSQ, 2, DFF), np.float32)
    for c in range(B):
        r = R[c]
        sl = slice(NSQ * c, NSQ * (c + 1))
        y_p[c] = r["y"][0:TP]
        y_s[sl] = r["y"][TP:].reshape(NSQ, 4, D)
        lc = r["o_lru_conv"]
        p_conv[0, c] = lc[0:3]
        s_conv[0, sl] = lc[3:].reshape(NSQ, 4, D)[:, 1:4]
        lh = r["o_lru_h"].transpose(2, 1, 0).reshape(17, D)
        p_h[0, c] = lh[0]
        s_h[0, sl] = lh[1:17]
        s5 = r["o_s5"].reshape(2, 64, 2, 16, 17).transpose(2, 4, 3, 0, 1)
        s5 = s5.reshape(2, 17, 32, 64)
        p_re[0, c] = s5[0, 0]
        p_im[0, c] = s5[1, 0]
        s_re[0, sl] = s5[0, 1:17]
        s_im[0, sl] = s5[1, 1:17]
        fc = r["o_ffn_conv"]
        p_ffn[0, c] = fc[1:3]
        s_ffn[0, sl] = fc[3:].reshape(NSQ, 4, DFF)[:, 2:4]
    return (y_p, y_s, p_conv, p_h, p_re, p_im, p_ffn, s_conv, s_h, s_re, s_im, s_ffn)
```

```python
import math
import contextlib
import numpy as np
import concourse.bass as bass
import concourse.mybir as mybir
from concourse.bass_utils import run_bass_kernel_spmd

F32 = mybir.dt.float32
BF16 = mybir.dt.bfloat16
AF = mybir.ActivationFunctionType
ALU = mybir.AluOpType

ENGINES = ("pe", "act", "dve", "pool", "sp")
NCORES = 8
TP = 2048
NSQ = 16
NS = 64
T = TP + NS
TAIL0 = TP - 3
NTAIL = T - TAIL0
D = 1024
DFF = 3072
ALPHA = 2.0 ** 0.25
LN_EPS = 1e-5
LCH = 8
NCH = TP // LCH
NLEV = 8
PAD = 128
PI = math.pi
GK = math.sqrt(2.0 / math.pi)
GC = 0.044715


class Prog:
    def __init__(self, nc):
        self.nc = nc
        self.streams = {e: [] for e in ENGINES}
        self.count = {}
        self.waited = {e: {} for e in ENGINES}
        self.last_write = {}
        self.readers = {}
        self.sem_names = set()
        self.epoch = "0"
        self.pending = {}

    def barrier(self):
        snap = list(self.count.items())
        for e in ENGINES:
            self.pending.setdefault(e, []).extend(snap)

    def op(self, eng, fn, reads=(), writes=(), inc=True, sem=None, amount=1):
        if sem is None:
            sem = "s_%s_%s" % (eng, self.epoch)
        self.sem_names.add(sem)
        deps = list(self.pending.pop(eng, ()))
        for k in reads:
            ev = self.last_write.get(k)
            if ev is not None:
                deps.append(ev)
        for k in writes:
            ev = self.last_write.get(k)
            if ev is not None:
                deps.append(ev)
            deps.extend(self.readers.get(k, ()))
        waits = {}
        for (s, v) in deps:
            if eng == "pe" and s.startswith("s_pe_"):
                continue
            if self.waited[eng].get(s, 0) >= v:
                continue
            if waits.get(s, 0) < v:
                waits[s] = v
        for s, v in waits.items():
            self.waited[eng][s] = v
        cur = self.count.get(sem, 0)
        val = cur + amount
        if inc:
            self.count[sem] = val
        ev = (sem, val)
        self.streams[eng].append((fn, list(waits.items()), (sem, amount) if inc else None))
        for k in reads:
            self.readers.setdefault(k, []).append(ev)
        for k in writes:
            self.last_write[k] = ev
            self.readers[k] = []
        return ev

    def dma(self, queue, out, in_, sem, reads=(), writes=(), **kw):
        def fn(e):
            return e.dma_start(out=out, in_=in_, **kw)
        return self.op(queue, fn, reads=reads, writes=writes, inc=True, sem=sem, amount=16)

    def emit(self, final_waits=()):
        nc = self.nc
        names = sorted(self.sem_names)
        with contextlib.ExitStack() as st:
            sems = {n: st.enter_context(nc.semaphore(n)) for n in names}
            block = st.enter_context(nc.Block())
            streams = self.streams

            def run(engh, lst, last):
                for fn, waits, inc in lst:
                    for s, v in waits:
                        engh.wait_ge(sems[s], v)
                    ins = fn(engh)
                    if inc is not None:
                        ins.then_inc(sems[inc[0]], inc[1])
                if last:
                    for s, v in final_waits:
                        engh.wait_ge(sems[s], v)

            @block.tensor
            def _(e):
                run(e, streams["pe"], False)

            @block.scalar
            def _(e):
                run(e, streams["act"], False)

            @block.vector
            def _(e):
                run(e, streams["dve"], False)

            @block.gpsimd
            def _(e):
                run(e, streams["pool"], False)

            @block.sync
            def _(e):
                run(e, streams["sp"], True)


class Arena:
    def __init__(self, tensor, words):
        self.t = tensor
        self.words = words
        self.pos = 0
        self.peak = 0

    def take(self, shape, dt=F32):
        n = 1
        for s in shape[1:]:
            n *= s
        esz = 4 if dt == F32 else 2
        w = (n * esz + 3) // 4
        w = (w + 7) // 8 * 8
        assert self.pos + w <= self.words, "arena overflow: need %d have %d" % (self.pos + w, self.words)
        v = self.t[0:shape[0], self.pos:self.pos + w]
        self.pos += w
        self.peak = max(self.peak, self.pos)
        if dt != F32:
            v = v.bitcast(dt)
        v = v[:, 0:n]
        if len(shape) > 2:
            names = " ".join("d%d" % i for i in range(len(shape) - 1))
            kw = {"d%d" % i: shape[i + 1] for i in range(len(shape) - 2)}
            v = v.rearrange("p (%s) -> p %s" % (names, names), **kw)
        return v


class StopBuild(Exception):
    pass


class Builder:
    def chk_stop(self, name, reads=()):
        if self.stop_after == name:
            self.p.barrier()
            d = self.dout("dbg_stop", [128, 4])
            t = self.R0.take([128, 4])
            self.MS("dve", t, 1.0, "stoptile")
            self.out_dma(d, t, ["stoptile"])
            raise StopBuild()

    def __init__(self, debug=(), stop_after=None):
        self.debug = set(debug)
        self.stop_after = stop_after
        self.nc = bass.Bass("TRN2", target_bir_lowering=False)
        self.p = Prog(self.nc)
        self.st = contextlib.ExitStack()
        self.out_sems = {}
        self.nout = 0
        self.free_banks = list(range(8))
        self.nbank = 0
        self.ntmp = 0

    def din(self, name, shape):
        return self.nc.dram_tensor(name, list(shape), F32, kind="ExternalInput").ap()

    def dout(self, name, shape, dt=F32):
        return self.nc.dram_tensor(name, list(shape), dt, kind="ExternalOutput").ap()

    def sb(self, name, shape, dt=F32):
        t = self.st.enter_context(self.nc.sbuf_tensor(name, list(shape), dt))
        return t[:]

    def out_dma(self, out, in_, reads, queue="sp", **kw):
        sem = "o_%d" % (self.nout % 8)
        self.nout += 1
        self.p.dma(queue, out, in_, sem, reads=reads, **kw)
        self.out_sems[sem] = self.p.count[sem]

    def tap(self, name, ap, reads):
        if name not in self.debug:
            return
        d = self.dout("dbg_" + name, list(ap.shape), ap.dtype)
        self.out_dma(d, ap, reads)

    def bank(self):
        b = self.free_banks[self.nbank % len(self.free_banks)]
        self.nbank += 1
        return b

    def bk(self, b):
        return self.ps[b // 2][:, b % 2, :]

    def TT(self, eng, out, a, b_, op, r, w):
        self.p.op(eng, lambda e: e.tensor_tensor(out=out, in0=a, in1=b_, op=op), reads=r, writes=[w])

    def TS(self, eng, out, a, s1, op0, r, w, s2=None, op1=None):
        if op1 is None:
            self.p.op(eng, lambda e: e.tensor_scalar(out=out, in0=a, scalar1=s1, scalar2=None, op0=op0), reads=r, writes=[w])
        else:
            self.p.op(eng, lambda e: e.tensor_scalar(out=out, in0=a, scalar1=s1, scalar2=s2, op0=op0, op1=op1), reads=r, writes=[w])

    def STT(self, eng, out, a, sc, b_, op0, op1, r, w):
        self.p.op(eng, lambda e: e.scalar_tensor_tensor(out=out, in0=a, scalar=sc, in1=b_, op0=op0, op1=op1), reads=r, writes=[w])

    def ACT(self, out, a, func, r, w, scale=1.0, bias=0.0):
        self.p.op("act", lambda e: e.activation(out=out, in_=a, func=func, scale=scale, bias=bias), reads=r, writes=[w])

    def CP(self, eng, out, a, r, w):
        if eng == "act":
            self.p.op("act", lambda e: e.copy(out=out, in_=a), reads=r, writes=[w])
        else:
            self.p.op(eng, lambda e: e.tensor_copy(out=out, in_=a), reads=r, writes=[w])

    def MS(self, eng, out, val, w, r=()):
        self.p.op(eng, lambda e: e.memset(out, val), reads=list(r), writes=[w])

    def build(self):
        nc, p = self.nc, self.p
        din = self.din
        I = {}
        for name, shape in (("x", [T, D]), ("st_lru_conv", [NSQ * 3, D]), ("st_lru_h", [NSQ, D]), ("st_s5_re", [NSQ, 2048]),
                            ("st_s5_im", [NSQ, 2048]), ("st_ffn_conv", [NSQ * 2, DFF]), ("w_in", [D, 3584]),
                            ("lru_conv_w", [4, D]), ("lru_conv_b", [D]), ("lru_wa", [16, 64, 64]), ("lru_ba", [D]),
                            ("lru_wx", [16, 64, 64]), ("lru_bx", [D]), ("lru_lambda", [D]), ("s5_a_re", [32, 64]),
                            ("s5_a_im", [32, 64]), ("s5_log_dt", [32]), ("s5_b_re", [32, 64, 16]), ("s5_b_im", [32, 64, 16]),
                            ("s5_c_re", [32, 16, 64]), ("s5_c_im", [32, 16, 64]), ("s5_d", [512]), ("w_glu", [512, 2048]),
                            ("w_out", [D, D]), ("ln1_g", [D]), ("ln1_b", [D]), ("w_up", [D, 2 * DFF]), ("ffn_conv_w", [3, DFF]),
                            ("ffn_conv_b", [DFF]), ("w_down", [DFF, D]), ("ln2_g", [D]), ("ln2_b", [D])):
            I[name] = din(name, shape)
        self.I = I
        O = {}
        O["y"] = self.dout("y", [T, D])
        O["lru_conv"] = self.dout("o_lru_conv", [NTAIL, D])
        O["lru_h"] = self.dout("o_lru_h", [128, 8, 17])
        O["s5"] = self.dout("o_s5", [128, 2, 16, 17])
        O["ffn_conv"] = self.dout("o_ffn_conv", [NTAIL, DFF])
        self.O = O
        self.x1_scr = nc.dram_tensor("x1_scr", [T, D], F32, kind="Internal").ap()

        self.ps = [self.st.enter_context(nc.psum_tensor("ps%d" % i, [128, 2, 512], F32)) for i in range(4)]
        R0W = 17664
        R1W = 35456
        self.R0 = Arena(self.st.enter_context(nc.sbuf_tensor("R0", [128, R0W], F32)), R0W)
        self.R1 = Arena(self.st.enter_context(nc.sbuf_tensor("R1", [128, R1W], F32)), R1W)
        R0, R1 = self.R0, self.R1

        ident_f = R0.take([128, 128]); ident_b = R0.take([128, 128], BF16)
        self.ident_f, self.ident_b = ident_f, ident_b
        self.MS("pool", ident_f, 0.0, "ident_f")
        p.op("pool", lambda e: e.affine_select(out=ident_f, in_=ident_f, pattern=[[-1, 128]],
                                               compare_op=ALU.not_equal, fill=1.0, base=0, channel_multiplier=1),
             reads=["ident_f"], writes=["ident_f"])
        self.CP("pool", ident_b, ident_f, ["ident_f"], "ident_b")

        self.PIECES = [(0, 512), (512, 512), (1024, 512), (1536, 512), (2048, 64)]
        xT = R0.take([128, 8, T], BF16)
        self.xT = xT
        zblk = R0.take([128, 4224])
        self.z2 = zblk.bitcast(BF16)[:, 0:4 * T].rearrange("p (a t) -> p a t", a=4)
        self.lnc = zblk[:, 0:4096].rearrange("p (a n) -> p a n", a=4)
        self.Hfin = R0.take([128, 2, 16, 17])
        self.LRUc = R0.take([128, 8, 72])
        self.Fc = R0.take([128, DFF // 128, 36])
        mark0 = R1.pos
        u_bf = R1.take([128, 4, T], BF16)
        self.W1 = R1.take([128, 4, LCH, 2, 128], BF16)
        self.W4a = R1.take([128, 16, LCH, 2, 32], BF16)
        self.W4b = R1.take([128, 16, LCH, 2, 32], BF16)
        self.h0S = R1.take([128, 2, 16, 16]); self.h0S_bf = R1.take([128, 2, 16, 16], BF16)
        self.RS = Arena(R1.take([128, 2368]), 2368)
        mark1 = R1.pos
        self.free_banks = [4, 5, 6, 7]
        self.param_loads()
        xb = [R1.take([128, D], BF16) for i in range(4)]
        ntt = (T + 127) // 128
        for tt in range(ntt):
            r0 = tt * 128
            rows = min(128, T - r0)
            slot = tt % 4
            p.dma("pool", xb[slot][0:rows, :], I["x"][r0:r0 + rows, :], "d_xb%d" % slot, writes=[("xb", slot)])
            b = self.bank()
            pt = self.bk(b).bitcast(BF16)
            for k in range(8):
                p.op("pe", lambda e, k=k, pt=pt, slot=slot, rows=rows: e.transpose(
                    pt[:, k * 128:k * 128 + rows], xb[slot][0:rows, k * 128:(k + 1) * 128], ident_b[0:rows, 0:rows]),
                    reads=[("xb", slot), "ident_b"], writes=[("bank", b)], inc=(k == 7))
            src = pt.rearrange("p (k c) -> p k c", c=128)[:, :, 0:rows]
            self.CP("act" if tt % 2 == 0 else "dve", xT[:, :, r0:r0 + rows], src, [("bank", b)], ("xT", tt))
            if tt == 3:
                self.param_transposes()
        self.tap("xT", xT[:, 0, :], [("xT", tt) for tt in range(ntt)])
        if self.stop_after == "p0":
            return self.finish()
        try:
            self.s5_prep()
        except StopBuild:
            return self.finish()
        self.tap("W1", self.W1[:, 0, :, :, :], self.W1_keys)
        self.tap("W4a", self.W4a[:, 0, :, :, :], self.W4_keys)
        self.tap("W4b", self.W4b[:, 0, :, :, :], self.W4_keys)
        self.tap("h0S", self.h0S, self.h0S_keys)
        self.tap("mur", self.mur, self.mu_keys)
        if self.stop_after == "prep":
            return self.finish()
        p.barrier()
        R1.pos = mark1
        wslot_u = [R1.take([128, 8, 128], BF16) for i in range(2)]
        for qh in range(4):
            slot = qh % 2
            p.dma("pool", wslot_u[slot], I["w_in"][:, 1024 + 128 * qh:1024 + 128 * (qh + 1)].rearrange("(k p) n -> p k n", p=128),
                  "d_wu%d" % slot, writes=[("wslot_u", slot)])
            for (c0, w) in self.PIECES:
                b = self.bank()
                for k in range(8):
                    p.op("pe", lambda e, k=k, b=b, slot=slot, c0=c0, w=w: e.matmul(
                        self.bk(b)[:, 0:w], wslot_u[slot][:, k, :], xT[:, k, c0:c0 + w], start=(k == 0), stop=(k == 7)),
                        reads=[("wslot_u", slot)] + self.xT_keys(c0, w), writes=[("bank", b)], inc=(k == 7))
                self.CP("act", u_bf[:, qh, c0:c0 + w], self.bk(b)[:, 0:w], [("bank", b)], ("u_bf", qh, c0))
        self.tap("u_bf", u_bf[:, 0, :], [("u_bf", 0, c0) for c0, _ in self.PIECES])
        if self.stop_after == "u":
            return self.finish()
        self.s5_main(u_bf)
        self.tap("z2", self.z2[:, 0, :], [("z2", 0, c0) for c0, _ in self.PIECES])
        self.tap("Hfin", self.Hfin, [("Hfin", q) for q in range(16)] + [("Hfin0", q) for q in range(16)])
        hk = [("Hfin", q) for q in range(16)] + [("Hfin0", q) for q in range(16)]
        self.out_dma(O["s5"], self.Hfin, hk)
        if self.stop_after == "s5":
            return self.finish()
        p.barrier()
        R1.pos = mark0
        p.epoch = "1"
        self.lru_stage()
        if self.stop_after == "lru":
            return self.finish()
        p.barrier()
        R1.pos = self.merged_end
        p.epoch = "2"
        self.mix_stage()
        if self.stop_after == "mix":
            return self.finish()
        p.barrier()
        R1.pos = 0
        p.epoch = "3"
        self.ffn_stage()
        return self.finish()

    def xT_keys(self, c0, w):
        return [("xT", tt) for tt in range(c0 // 128, (c0 + w - 1) // 128 + 1)]

    def param_loads(self):
        nc, p, I = self.nc, self.p, self.I
        R1 = self.R1
        MS = self.MS
        NCK = dict(allow_slow_non_contiguous=True)
        NJ = DFF // 128
        self.LF = R1.take([128, D + DFF])
        Lp = self.LF[:, 0:D]; Fp = self.LF[:, D:D + DFF]
        hs1 = R1.take([128, 2048])
        hsp = [hs1, hs1]
        self.Lp, self.Fp, self.hsp = Lp, Fp, hsp
        MS("pool", Lp, 0.0, "Lp"); MS("pool", Fp, 0.0, "Fp"); MS("pool", hs1, 0.0, "hsp")
        row = lambda nm: I[nm].rearrange("(o n) -> o n", o=1)
        for (r0, r1, src) in ((0, 48, I["st_lru_conv"]), (48, 64, I["st_lru_h"]), (64, 68, I["lru_conv_w"]), (68, 69, row("lru_conv_b")),
                              (69, 70, row("lru_ba")), (70, 71, row("lru_bx")), (71, 72, row("lru_lambda"))):
            p.dma("sp", Lp[r0:r1, :], src, "d_Lp", writes=["Lp"])
        for (r0, r1, src) in ((0, 32, I["st_ffn_conv"]), (32, 35, I["ffn_conv_w"]), (35, 36, row("ffn_conv_b"))):
            p.dma("sp", Fp[r0:r1, :], src, "d_Fp", writes=["Fp"])
        self.hs_loaded = False
        sh3 = [128, 16, 32]
        self.Bnr = R1.take(sh3); self.Bni = R1.take(sh3)
        self.Cnr = R1.take([128, 4, 128]); self.Cni = R1.take([128, 4, 128])
        for t_, k_ in ((self.Bnr, "Bnr"), (self.Bni, "Bni"), (self.Cnr, "Cnr"), (self.Cni, "Cni")):
            MS("pool", t_, 0.0, k_)
        for (dst, nm, key) in ((self.Bnr, "s5_b_re", "Bnr"), (self.Bni, "s5_b_im", "Bni")):
            v = I[nm].rearrange("(q two) p c -> two p q c", two=2)
            for two in range(2):
                p.dma("sp", dst[64 * two:64 * two + 64, :, 16 * two:16 * two + 16], v[two], "d_prep", writes=[key])
        for (dst, nm, key) in ((self.Cnr, "s5_c_re", "Cnr"), (self.Cni, "s5_c_im", "Cni")):
            v = I[nm].rearrange("(qh ql two) c p -> ql two c qh p", qh=4, ql=4, two=2)
            for ql in range(4):
                for two in range(2):
                    p0 = 32 * ql + 16 * two
                    p.dma("sp", dst[p0:p0 + 16, :, 64 * two:64 * two + 64], v[ql, two], "d_prep", writes=[key])
        shS = [128, 16]
        self.aSr = R1.take(shS); self.aSi = R1.take(shS); self.ldS = R1.take(shS)
        p.dma("sp", self.aSr, I["s5_a_re"].rearrange("(q two) p -> (two p) q", two=2), "d_prep", writes=["aSr"], **NCK)
        p.dma("sp", self.aSi, I["s5_a_im"].rearrange("(q two) p -> (two p) q", two=2), "d_prep", writes=["aSi"], **NCK)
        v = I["s5_log_dt"].rearrange("(q two) -> two q", two=2)
        for two in range(2):
            p.dma("sp", self.ldS[64 * two:64 * two + 64, :], v[two].partition_broadcast(64), "d_prep", writes=["ldS"], **NCK)
        self.dS5 = self.R0.take([128, 4])
        p.dma("sp", self.dS5, I["s5_d"].rearrange("(t p) -> p t", p=128), "d_prep", writes=["dS5"], **NCK)
        for sem, keys in (("d_Lp", ["Lp"]), ("d_Fp", ["Fp"]),
                          ("d_prep", ["Bnr", "Bni", "Cnr", "Cni", "aSr", "aSi", "ldS", "dS5"])):
            for k_ in keys:
                p.last_write[k_] = (sem, p.count[sem])

    def tr_group(self, srcs, evac):
        p = self.p
        for g0 in range(0, len(srcs), 4):
            grp = srcs[g0:g0 + 4]
            b = self.bank()
            bv = self.bk(b).rearrange("p (a c) -> p a c", a=4)
            for i, (ap, keys) in enumerate(grp):
                p.op("pe", lambda e, i=i, ap=ap, bv=bv: e.transpose(bv[:, i, :], ap, self.ident_f),
                     reads=list(keys) + ["ident_f"], writes=[("bank", b)], inc=(i == len(grp) - 1))
            evac(bv, g0, len(grp), b)

    def param_transposes(self):
        p = self.p
        CP = self.CP
        NJ = DFF // 128
        Lp, Fp, hsp = self.Lp, self.Fp, self.hsp

        def ev_L(bv, g0, n, b):
            CP("act", self.LRUc[:, g0:g0 + n, :], bv[:, 0:n, 0:72], [("bank", b)], ("LRUc", g0 // 4))
        self.tr_group([(Lp[:, 128 * j:128 * (j + 1)], ["Lp"]) for j in range(8)], ev_L)

        def ev_F(bv, g0, n, b):
            CP("dve", self.Fc[:, g0:g0 + n, :], bv[:, 0:n, 0:36], [("bank", b)], ("Fc", g0 // 4))
        self.tr_group([(Fp[:, 128 * j:128 * (j + 1)], ["Fp"]) for j in range(NJ)], ev_F)
        self.LRUc_keys = [("LRUc", i) for i in range(2)]
        self.Fc_keys = [("Fc", i) for i in range(NJ // 4)]
        for part in range(2):
            p.dma("sp", hsp[part][0:16, :], self.I[("st_s5_re", "st_s5_im")[part]], "d_hsp", writes=["hsp"])
            def ev_h(bv, g0, n, b, part=part):
                CP("act", self.h0S[:, part, g0:g0 + n, :], bv[:, 0:n, 0:16], [("bank", b)], ("h0S", part, g0 // 4))
            self.tr_group([(hsp[part][:, 128 * q:128 * (q + 1)], ["hsp"]) for q in range(16)], ev_h)
        self.h0S_keys = [("h0S", part, i) for part in range(2) for i in range(4)]
        sh3 = [128, 16, 32]
        self.CTr = self.R1.take(sh3); self.CTi = self.R1.take(sh3)
        for (src, dst, k_src, k_dst) in ((self.Cnr, self.CTr, "Cnr", "CTr"), (self.Cni, self.CTi, "Cni", "CTi")):
            def ev_c(bv, g0, n, b, dst=dst, k_dst=k_dst):
                CP("dve", dst[:, 4 * g0:4 * (g0 + n), :].rearrange("p (a q) c -> p a (q c)", a=n), bv[:, 0:n, :], [("bank", b)], (k_dst, g0))
            self.tr_group([(src[:, qh, :], [k_src]) for qh in range(4)], ev_c)
        self.CT_keys = {"CTr": [("CTr", 0)], "CTi": [("CTi", 0)]}

    def s5_prep(self):
        nc, p, I = self.nc, self.p, self.I
        R0, R1, RS = self.R0, self.R1, self.RS
        TT, TS, STT, ACT, CP, MS = self.TT, self.TS, self.STT, self.ACT, self.CP, self.MS
        W1, W4a, W4b = self.W1, self.W4a, self.W4b
        aSr, aSi, ldS = self.aSr, self.aSi, self.ldS
        CTr, CTi = self.CTr, self.CTi
        kCTr, kCTi = self.CT_keys["CTr"], self.CT_keys["CTi"]
        shS = [128, 16]
        sh3 = [128, 16, 32]
        I32 = mybir.dt.int32

        def sincos_base(theta, t, kf, A, C2, cs, sn, k_th, k_t, k_kf, k_A, k_C2, k_cs, k_sn):
            TS("dve", t, theta, 1.0 / (2 * PI), ALU.mult, [k_th], k_t, s2=16.0, op1=ALU.add)
            CP("dve", kf.bitcast(I32), t, [k_t], k_kf)
            CP("dve", kf, kf.bitcast(I32), [k_kf], k_kf)
            TT("dve", t, t, kf, ALU.subtract, [k_t, k_kf], k_t)
            ACT(A, t, AF.Sin, [k_t], k_A, scale=PI)
            ACT(C2, t, AF.Sin, [k_t], k_C2, scale=PI / 2)
            TT("dve", C2, C2, C2, ALU.mult, [k_C2], k_C2)
            TS("dve", C2, C2, -2.0, ALU.mult, [k_C2], k_C2, s2=1.0, op1=ALU.add)
            STT("dve", sn, A, 2.0, C2, ALU.mult, ALU.mult, [k_A, k_C2], k_sn)
            TT("dve", cs, A, A, ALU.mult, [k_A], k_cs)
            TS("dve", cs, cs, -2.0, ALU.mult, [k_cs], k_cs, s2=1.0, op1=ALU.add)

        def tS():
            return RS.take(shS)

        dtS = tS(); adtS = tS(); thS = tS()
        ACT(dtS, ldS, AF.Exp, ["ldS"], "dtS")
        TT("dve", adtS, aSr, dtS, ALU.mult, ["aSr", "dtS"], "adtS")
        TT("dve", thS, aSi, dtS, ALU.mult, ["aSi", "dtS"], "thS")
        self.rhoS = tS()
        ACT(self.rhoS, adtS, AF.Exp, ["adtS"], "rhoS")
        cosS = []; sinS = []; nsinS = []
        lamr = {}; lami = {}; lrotr = {}; lroti = {}; rpow = {}
        c1S = tS(); s1S = tS(); w1 = tS(); w2 = tS(); w3 = tS(); w4 = tS()
        sincos_base(thS, w1, w2, w3, w4, c1S, s1S, "thS", "Sw1", "Sw2", "Sw3", "Sw4", "S1cs", "S1sn")
        ua = w1; ub = w2
        for s in range(2 * LCH):
            if s == 0:
                cs = tS(); sn = tS()
                MS("dve", cs, 1.0, "S0cs"); MS("dve", sn, 0.0, "S0sn")
            elif s == 1:
                cs, sn = c1S, s1S
            else:
                cs = tS(); sn = tS()
                pc, ps_ = cosS[s - 1], sinS[s - 1]
                kpc, kps = "S%dcs" % (s - 1), "S%dsn" % (s - 1)
                TT("dve", ua, pc, c1S, ALU.mult, [kpc, "S1cs", "Sw1"], "Sw1")
                TT("dve", ub, ps_, s1S, ALU.mult, [kps, "S1sn", "Sw2"], "Sw2")
                TT("dve", cs, ua, ub, ALU.subtract, ["Sw1", "Sw2"], "S%dcs" % s)
                TT("dve", ua, ps_, c1S, ALU.mult, [kps, "S1cs"], "Sw1")
                TT("dve", ub, pc, s1S, ALU.mult, [kpc, "S1sn"], "Sw2")
                TT("dve", sn, ua, ub, ALU.add, ["Sw1", "Sw2"], "S%dsn" % s)
            cosS.append(cs); sinS.append(sn)
            if s <= LCH:
                ns = tS()
                TS("dve", ns, sn, -1.0, ALU.mult, ["S%dsn" % s], "nS%dsn" % s)
                nsinS.append(ns)
            if 1 <= s <= LCH:
                r = tS()
                ACT(r, adtS, AF.Exp, ["adtS"], "rpow%d" % s, scale=float(s))
                rpow[s] = r
                if s in (1, 4):
                    a = tS(); b_ = tS()
                    TT("dve", a, r, cs, ALU.mult, ["rpow%d" % s, "S%dcs" % s], "lamr%d" % s)
                    TT("dve", b_, r, sn, ALU.mult, ["rpow%d" % s, "S%dsn" % s], "lami%d" % s)
                    lamr[s] = a; lami[s] = b_
        for s in range(1, LCH + 1):
            a = tS(); b_ = tS()
            ks = s + LCH - 1
            TT("dve", a, rpow[s], cosS[ks], ALU.mult, ["rpow%d" % s, "S%dcs" % ks], "lrotr%d" % s)
            TT("dve", b_, rpow[s], sinS[ks], ALU.mult, ["rpow%d" % s, "S%dsn" % ks], "lroti%d" % s)
            lrotr[s] = a; lroti[s] = b_
        self.cosS, self.sinS, self.nsinS, self.lamr, self.lami = cosS, sinS, nsinS, lamr, lami
        self.nlami4 = tS()
        TS("dve", self.nlami4, lami[4], -1.0, ALU.mult, ["lami4"], "nlami4")
        ta = R1.take(sh3); tb = R1.take(sh3)

        def bc(t):
            return t.unsqueeze(2).to_broadcast(sh3)

        Bnr, Bni = self.Bnr, self.Bni
        nr = tS(); den = tS(); u1 = tS(); cfr = tS(); cfi = tS()
        TS("dve", nr, lamr[1], -1.0, ALU.add, ["lamr1"], "nr")
        TT("dve", den, aSr, aSr, ALU.mult, ["aSr"], "den")
        TT("dve", u1, aSi, aSi, ALU.mult, ["aSi"], "u1")
        TT("dve", den, den, u1, ALU.add, ["den", "u1"], "den")
        p.op("dve", lambda e: e.reciprocal(out=den, in_=den), reads=["den"], writes=["den"])
        TT("dve", cfr, nr, aSr, ALU.mult, ["nr", "aSr"], "cfr")
        TT("dve", u1, lami[1], aSi, ALU.mult, ["lami1", "aSi", "den"], "u1")
        TT("dve", cfr, cfr, u1, ALU.add, ["cfr", "u1"], "cfr")
        TT("dve", cfr, cfr, den, ALU.mult, ["cfr", "den"], "cfr")
        TT("dve", cfi, lami[1], aSr, ALU.mult, ["lami1", "aSr"], "cfi")
        TT("dve", u1, nr, aSi, ALU.mult, ["nr", "aSi", "cfr"], "u1")
        TT("dve", cfi, cfi, u1, ALU.subtract, ["cfi", "u1"], "cfi")
        TT("dve", cfi, cfi, den, ALU.mult, ["cfi", "den"], "cfi")
        BbR = R1.take(sh3); BbI = R1.take(sh3)
        tc_ = R1.take(sh3); td_ = R1.take(sh3)
        TT("pool", tc_, Bnr, bc(cfr), ALU.mult, ["Bnr", "cfr"], "w1tc")
        TT("pool", td_, Bni, bc(cfi), ALU.mult, ["Bni", "cfi"], "w1td")
        TT("pool", BbR, tc_, td_, ALU.subtract, ["w1tc", "w1td"], "BbR")
        TT("pool", tc_, Bni, bc(cfr), ALU.mult, ["Bni", "cfr"], "w1tc")
        TT("pool", td_, Bnr, bc(cfi), ALU.mult, ["Bnr", "cfi"], "w1td")
        TT("pool", BbI, tc_, td_, ALU.add, ["w1tc", "w1td"], "BbI")
        W1S = self.LF.bitcast(BF16).rearrange("p (a s b c) -> p a s b c", a=4, s=LCH, b=2)
        MS("pool", W1S[:, 0, 0, 0, 0:2], 0.0, "LFfree", r=["Lp", "Fp"])
        p.last_write["Lp"] = p.last_write["LFfree"]; p.last_write["Fp"] = p.last_write["LFfree"]
        v4 = lambda t: t.rearrange("p (a b) c -> p a (b c)", a=4)
        for s in range(LCH):
            kc, ks = "S%dcs" % s, "S%dsn" % s
            TT("pool", tc_, BbR, bc(cosS[s]), ALU.mult, ["BbR", kc], "w1tc")
            TT("pool", td_, BbI, bc(sinS[s]), ALU.mult, ["BbI", ks], "w1td")
            TT("pool", W1S[:, :, s, 0, :], v4(tc_), v4(td_), ALU.add, ["w1tc", "w1td", "LFfree"], ("W1S", s, 0))
            TT("pool", tc_, BbI, bc(cosS[s]), ALU.mult, ["BbI", kc], "w1tc")
            TT("pool", td_, BbR, bc(sinS[s]), ALU.mult, ["BbR", ks], "w1td")
            TT("pool", W1S[:, :, s, 1, :], v4(tc_), v4(td_), ALU.subtract, ["w1tc", "w1td", "LFfree"], ("W1S", s, 1))
        for qh in range(4):
            for h in range(2):
                b = self.bank()
                pt = self.bk(b).bitcast(BF16)
                n = 0
                for s in range(4 * h, 4 * h + 4):
                    for part in range(2):
                        p.op("pe", lambda e, pt=pt, n=n, qh=qh, s=s, part=part: e.transpose(
                            pt[:, 128 * n:128 * (n + 1)], W1S[:, qh, s, part, :], self.ident_b),
                            reads=[("W1S", s, part), "ident_b"], writes=[("bank", b)], inc=(n == 7))
                        n += 1
                CP("act", W1[:, qh, 4 * h:4 * h + 4, :, :].rearrange("p a b c -> p (a b c)"), pt, [("bank", b)], ("W1", qh, h))
        self.W1_keys = [("W1", qh, h) for qh in range(4) for h in range(2)]
        c7 = cosS[LCH - 1]; s7 = sinS[LCH - 1]
        kc7, ks7 = "S%dcs" % (LCH - 1), "S%dsn" % (LCH - 1)
        sh16 = [128, 16, 16]
        bc16 = lambda t: t.unsqueeze(2).to_broadcast(sh16)
        tav = ta[:, :, 0:16]; tbv = tb[:, :, 0:16]
        h0r = self.h0S[:, 0, :, :]; h0i = self.h0S[:, 1, :, :]
        TT("dve", tav, h0r, bc16(c7), ALU.mult, self.h0S_keys + [kc7, "w4ta"], "w4ta")
        TT("dve", tbv, h0i, bc16(s7), ALU.mult, self.h0S_keys + [ks7, "w4tb"], "w4tb")
        TT("dve", self.h0S_bf[:, 0, :, :], tav, tbv, ALU.add, ["w4ta", "w4tb"], "h0S_bf")
        TT("dve", tav, h0i, bc16(c7), ALU.mult, self.h0S_keys + [kc7], "w4ta")
        TT("dve", tbv, h0r, bc16(s7), ALU.mult, self.h0S_keys + [ks7], "w4tb")
        TT("dve", self.h0S_bf[:, 1, :, :], tav, tbv, ALU.subtract, ["w4ta", "w4tb", "h0S_bf"], "h0S_bf")
        for s in range(LCH):
            for (Wt, cr_t, ci_t, kr, ki, nm) in (
                (W4a, cosS[s], sinS[s], "S%dcs" % s, "S%dsn" % s, "W4a"),
                (W4b, lrotr[s + 1], lroti[s + 1], "lrotr%d" % (s + 1), "lroti%d" % (s + 1), "W4b"),
            ):
                TT("dve", ta, CTr, bc(cr_t), ALU.mult, kCTr + [kr], "w4ta")
                TT("dve", tb, CTi, bc(ci_t), ALU.mult, kCTi + [ki], "w4tb")
                TT("dve", Wt[:, :, s, 0, :], ta, tb, ALU.subtract, ["w4ta", "w4tb"], (nm, s, 0))
                TT("dve", ta, CTr, bc(ci_t), ALU.mult, kCTr + [ki], "w4ta")
                TT("dve", tb, CTi, bc(cr_t), ALU.mult, kCTi + [kr], "w4tb")
                TT("dve", ta, ta, tb, ALU.add, ["w4ta", "w4tb"], "w4ta")
                TS("dve", Wt[:, :, s, 1, :], ta, -1.0, ALU.mult, ["w4ta"], (nm, s, 1))
        self.W4_keys = [(nm, s, part) for nm in ("W4a", "W4b") for s in range(LCH) for part in range(2)]
        self.mur = RS.take([128, 16, NLEV]); self.mui = RS.take([128, 16, NLEV]); self.muni = RS.take([128, 16, NLEV])
        angc = RS.take([128, 16, NLEV]); angs = RS.take([128, 16, NLEV]); rmag = RS.take([128, 16, NLEV])
        CP("dve", angc[:, :, 0], cosS[LCH], ["S%dcs" % LCH], ("angc", 0))
        CP("dve", angs[:, :, 0], sinS[LCH], ["S%dsn" % LCH], ("angs", 0))
        sa = tS(); sb_ = tS()
        for j in range(NLEV):
            if j >= 1:
                TT("dve", sa, angc[:, :, j - 1], angc[:, :, j - 1], ALU.mult, [("angc", j - 1)], "sq_a")
                TT("dve", sb_, angs[:, :, j - 1], angs[:, :, j - 1], ALU.mult, [("angs", j - 1)], "sq_b")
                TT("dve", angc[:, :, j], sa, sb_, ALU.subtract, ["sq_a", "sq_b"], ("angc", j))
                TT("dve", sa, angc[:, :, j - 1], angs[:, :, j - 1], ALU.mult, [("angc", j - 1), ("angs", j - 1)], "sq_a")
                TS("dve", angs[:, :, j], sa, 2.0, ALU.mult, ["sq_a"], ("angs", j))
            ACT(rmag[:, :, j], adtS, AF.Exp, ["adtS"], ("rmag", j), scale=float(LCH * (1 << j)))
            TT("dve", self.mur[:, :, j], rmag[:, :, j], angc[:, :, j], ALU.mult, [("rmag", j), ("angc", j)], ("mur", j))
            TT("dve", self.mui[:, :, j], rmag[:, :, j], angs[:, :, j], ALU.mult, [("rmag", j), ("angs", j)], ("mui", j))
        for j in range(NLEV):
            TS("dve", self.muni[:, :, j], self.mui[:, :, j], -1.0, ALU.mult, [("mui", j)], ("muni", j))
        self.mu_keys = [(nm, j) for nm in ("mur", "mui", "muni") for j in range(NLEV)]
        self.rp8 = RS.take([128, 16, LCH]); self.rp4 = RS.take([128, 16, 4])
        CP("dve", self.rp8, self.rhoS.unsqueeze(2).to_broadcast([128, 16, LCH]), ["rhoS"], "rp8")
        MS("dve", self.rp8[:, :, 0:1], 0.0, "rp8", r=["rp8"])
        CP("dve", self.rp4, self.rhoS.unsqueeze(2).to_broadcast([128, 16, 4]), ["rhoS"], "rp4")
        MS("dve", self.rp4[:, :, 0:1], 0.0, "rp4", r=["rp4"])
    def s5_main(self, u_bf):
        nc, p = self.nc, self.p
        R0, R1 = self.R0, self.R1
        TT, TS, STT, ACT, CP, MS = self.TT, self.TS, self.STT, self.ACT, self.CP, self.MS
        W1, W4a, W4b = self.W1, self.W4a, self.W4b
        cosS, sinS, nsinS, lamr, lami = self.cosS, self.sinS, self.nsinS, self.lamr, self.lami
        z2 = self.z2
        NSET = 2
        pat = [R1.take([128, 512]) for i in range(NSET)]
        pats = [R1.take([128, 64]) for i in range(NSET)]
        gzf = [R1.take([128, 512]) for i in range(4)]
        gzfs = [R1.take([128, 2, 64]) for i in range(2)]
        gzb = [R1.take([128, 2, T], BF16) for i in range(NSET)]
        HA = [R1.take([128, 2, PAD + NCH]) for i in range(NSET)]
        HB = [R1.take([128, 2, PAD + NCH]) for i in range(NSET)]
        Hpb = [R1.take([128, 2, NCH], BF16) for i in range(NSET)]
        for i in range(NSET):
            MS("pool", HA[i], 0.0, ("HA", i))
            MS("pool", HB[i], 0.0, ("HB", i))
        yf = R1.take([128, 512]); gq = R1.take([128, 512]); ga = R1.take([128, 512]); gt = gq
        VB = [0, 1]
        nunit = [0]
        SVB = 2
        YB = {0: 4, 512: 5, 1024: 6, 1536: 7}
        PIECES = self.PIECES
        nseg = [0]

        def stageA(q):
            qh, ql = q // 4, q % 4
            hb = q % NSET
            kw = dict(tile_position=(96, 0)) if ql == 3 else {}
            p.op("act", lambda e: e.activation(out=pat[hb].rearrange("p (c s) -> p c s", s=LCH),
                                               in_=self.rp8[:, q:q + 1, :].to_broadcast([128, 512 // LCH, LCH]), func=AF.Copy),
                 reads=["rp8"], writes=[("pat", hb)])
            p.op("act", lambda e: e.activation(out=pats[hb].rearrange("p (c s) -> p c s", s=4),
                                               in_=self.rp4[:, q:q + 1, :].to_broadcast([128, 64 // 4, 4]), func=AF.Copy),
                 reads=["rp4"], writes=[("pats", hb)])
            for (c0, w) in PIECES:
                samp = (w == 64)
                L = 4 if samp else LCH
                sl = nseg[0] % 2
                nseg[0] += 1
                for part in range(2):
                    if samp:
                        vb = SVB
                        vv = self.bk(SVB)[:, 64 * part:64 * part + 64]
                    else:
                        vb = VB[nunit[0] % 2]
                        vv = self.bk(vb)
                    ug = nunit[0] % 4
                    nunit[0] += 1
                    for s in range(L):
                        p.op("pe", lambda e, vv=vv, s=s, part=part, qh=qh, ql=ql, c0=c0, w=w, L=L, kw=kw: e.matmul(
                            vv[:, s:w:L], W1[32 * ql:32 * ql + 32, qh, s, part, :],
                            u_bf[32 * ql:32 * ql + 32, qh, c0 + s:c0 + w:L], start=True, stop=True, **kw),
                            reads=self.W1_keys + [("u_bf", qh, c0)], writes=[("bank", vb)], inc=(s == L - 1))
                    if not samp:
                        go = gzf[ug]; gkey = ("gzf", ug)
                        p.op("dve", lambda e, go=go, hb=hb, vv=vv: e.tensor_tensor_scan(
                            out=go, data0=pat[hb], data1=vv, initial=0.0, op0=ALU.mult, op1=ALU.add),
                            reads=[("pat", hb), ("bank", vb)], writes=[gkey])
                        CP("act", gzb[hb][:, part, c0:c0 + w], go, [gkey], ("gzb", hb, c0, part))
                        k0 = PAD + c0 // LCH
                        nchunk = w // LCH
                        CP("pool", HA[hb][:, part, k0:k0 + nchunk], go[:, LCH - 1:512:LCH], [gkey], ("HA", hb))
                    else:
                        go = gzfs[sl][:, part, :]; gkey = ("gzfs", sl, part)
                        p.op("dve", lambda e, go=go, hb=hb, vv=vv: e.tensor_tensor_scan(
                            out=go, data0=pats[hb], data1=vv, initial=0.0, op0=ALU.mult, op1=ALU.add),
                            reads=[("pats", hb), ("bank", vb)], writes=[gkey])
                        CP("act", gzb[hb][:, part, c0:c0 + w], go, [gkey], ("gzb", hb, c0, part))
                if samp:
                    go = gzfs[sl]
                    gkeys = [("gzfs", sl, 0), ("gzfs", sl, 1)]
                    er = go[:, 0, 3:64:4]; ei = go[:, 1, 3:64:4]
                    c3 = cosS[3][:, q:q + 1]; s3 = sinS[3][:, q:q + 1]; ns3 = nsinS[3][:, q:q + 1]
                    l4r = lamr[4][:, q:q + 1]; l4i = lami[4][:, q:q + 1]; nl4i = self.nlami4[:, q:q + 1]
                    kk = gkeys + ["S3cs", "S3sn", "nS3sn", "lamr4", "lami4", "nlami4"] + self.h0S_keys
                    fr = self.Hfin[:, 0, q, 1:17]; fi = self.Hfin[:, 1, q, 1:17]
                    h0r = self.h0S[:, 0, q, :]; h0i = self.h0S[:, 1, q, :]
                    fk = ("Hfin", q)
                    TS("dve", fr, er, c3, ALU.mult, kk, fk)
                    STT("dve", fr, ei, ns3, fr, ALU.mult, ALU.add, kk + [fk], fk)
                    STT("dve", fr, h0r, l4r, fr, ALU.mult, ALU.add, kk + [fk], fk)
                    STT("dve", fr, h0i, nl4i, fr, ALU.mult, ALU.add, kk + [fk], fk)
                    TS("dve", fi, er, s3, ALU.mult, kk + [fk], fk)
                    STT("dve", fi, ei, c3, fi, ALU.mult, ALU.add, kk + [fk], fk)
                    STT("dve", fi, h0r, l4i, fi, ALU.mult, ALU.add, kk + [fk], fk)
                    STT("dve", fi, h0i, l4r, fi, ALU.mult, ALU.add, kk + [fk], fk)

        def stageB(q):
            hb = q % NSET
            src, dst = HA[hb], HB[hb]
            skey, dkey = ("HA", hb), ("HB", hb)
            for j in range(NLEV):
                d = 1 << j
                mr = self.mur[:, q, j:j + 1]; mi = self.mui[:, q, j:j + 1]; mni = self.muni[:, q, j:j + 1]
                mk = [("mur", j), ("mui", j), ("muni", j)]
                S0 = src[:, 0, PAD:PAD + NCH]; S1 = src[:, 1, PAD:PAD + NCH]
                Z0 = src[:, 0, PAD - d:PAD + NCH - d]; Z1 = src[:, 1, PAD - d:PAD + NCH - d]
                D0 = dst[:, 0, PAD:PAD + NCH]; D1 = dst[:, 1, PAD:PAD + NCH]
                STT("dve", D0, Z1, mni, S0, ALU.mult, ALU.add, [skey] + mk, dkey)
                STT("dve", D1, Z0, mi, S1, ALU.mult, ALU.add, [skey, dkey] + mk, dkey)
                Zb = src[:, :, PAD - d:PAD + NCH - d]; Db = dst[:, :, PAD:PAD + NCH]
                STT("dve", Db, Zb, mr, Db, ALU.mult, ALU.add, [skey, dkey] + mk, dkey)
                src, dst = dst, src
                skey, dkey = dkey, skey
            CP("act", Hpb[hb], HA[hb][:, :, PAD - 1:PAD - 1 + NCH], [("HA", hb)], ("Hpb", hb))
            hr_ = HA[hb][:, 0, PAD + NCH - 1:PAD + NCH]; hi_ = HA[hb][:, 1, PAD + NCH - 1:PAD + NCH]
            c7 = cosS[LCH - 1][:, q:q + 1]; s7 = sinS[LCH - 1][:, q:q + 1]; ns7 = nsinS[LCH - 1][:, q:q + 1]
            kk7 = [("HA", hb), "S%dcs" % (LCH - 1), "S%dsn" % (LCH - 1), "nS%dsn" % (LCH - 1)]
            fr0 = self.Hfin[:, 0, q, 0:1]; fi0 = self.Hfin[:, 1, q, 0:1]
            TS("dve", fr0, hr_, c7, ALU.mult, kk7, ("Hfin0", q))
            STT("dve", fr0, hi_, ns7, fr0, ALU.mult, ALU.add, kk7 + [("Hfin0", q)], ("Hfin0", q))
            TS("dve", fi0, hr_, s7, ALU.mult, kk7 + [("Hfin0", q)], ("Hfin0", q))
            STT("dve", fi0, hi_, c7, fi0, ALU.mult, ALU.add, kk7 + [("Hfin0", q)], ("Hfin0", q))

        def stageC(q):
            qh, ql = q // 4, q % 4
            hb = q % NSET
            kw = dict(tile_position=(0, 96)) if ql == 3 else {}
            for (c0, w) in PIECES:
                samp = (w == 64)
                L = 4 if samp else LCH
                if samp:
                    ybank = SVB
                    yall = self.bk(SVB)[:, 128:192]
                else:
                    ybank = YB[c0]
                    yall = self.bk(ybank)
                for s in range(L):
                    outv = yall[32 * ql:32 * ql + 32, s:w:L]
                    if samp:
                        hr = self.h0S_bf[:, 0, q, :]; hi = self.h0S_bf[:, 1, q, :]
                        hkeys = ["h0S_bf"]
                    else:
                        k0 = c0 // LCH
                        hr = Hpb[hb][:, 0, k0:k0 + w // L]; hi = Hpb[hb][:, 1, k0:k0 + w // L]
                        hkeys = [("Hpb", hb)]
                    ops = [
                        (W4a[:, q, s, 0, :], gzb[hb][:, 0, c0 + s:c0 + w:L]),
                        (W4a[:, q, s, 1, :], gzb[hb][:, 1, c0 + s:c0 + w:L]),
                        (W4b[:, q, s, 0, :], hr),
                        (W4b[:, q, s, 1, :], hi),
                    ]
                    for i, (lh, rh) in enumerate(ops):
                        last = (i == 3 and s == L - 1)
                        p.op("pe", lambda e, outv=outv, lh=lh, rh=rh, i=i, kw=kw: e.matmul(
                            outv, lh, rh, start=(i == 0), stop=(i == 3), **kw),
                            reads=self.W4_keys + [("gzb", hb, c0, 0), ("gzb", hb, c0, 1)] + hkeys, writes=[("bank", ybank)], inc=last)

        def stageY(qh):
            for (c0, w) in PIECES:
                samp = (w == 64)
                if samp:
                    ybank = SVB; ysrc = self.bk(SVB)[:, 128:192]
                else:
                    ybank = YB[c0]; ysrc = self.bk(ybank)
                yv = yf[:, 0:w]; qv = gq[:, 0:w]; av = ga[:, 0:w]; tv = gt[:, 0:w]
                STT("dve", yv, u_bf[:, qh, c0:c0 + w], self.dS5[:, qh:qh + 1], ysrc, ALU.mult, ALU.add,
                    [("u_bf", qh, c0), "dS5", ("bank", ybank)], "s5yf")
                if qh == 0:
                    self.tap("ys5_%d" % c0, yv, ["s5yf"])
                ACT(qv, yv, AF.Square, ["s5yf"], "s5gq")
                ACT(qv, qv, AF.Identity, ["s5gq"], "s5gq", scale=GC, bias=1.0)
                TT("pool", av, qv, yv, ALU.mult, ["s5gq", "s5yf"], "s5ga")
                ACT(qv, av, AF.Tanh, ["s5ga"], "s5gq", scale=GK)
                STT("dve", z2[:, qh, c0:c0 + w], qv, 1.0, yv, ALU.add, ALU.mult, ["s5gq", "s5yf"], ("z2", qh, c0))

        for qh in range(4):
            qs = [4 * qh + i for i in range(4)]
            stageA(qs[0]); stageB(qs[0])
            for i in range(1, 4):
                stageA(qs[i])
                stageC(qs[i - 1])
                stageB(qs[i])
            stageC(qs[3])
            stageY(qh)

    def gelu2(self, yv, qv, av, tv, outv, ky, kq, ka, kt, kout):
        self.ACT(qv, yv, AF.Square, [ky], kq)
        self.STT("dve", av, qv, GK * GC, yv, ALU.mult, ALU.mult, [kq, ky], ka)
        self.STT("dve", av, yv, GK, av, ALU.mult, ALU.add, [ky, ka], ka)
        self.ACT(tv, av, AF.Tanh, [ka], kt)
        self.STT("dve", outv, tv, 1.0, yv, ALU.add, ALU.mult, [kt, ky], kout)

    def lru_stage(self):
        nc, p, I, O = self.nc, self.p, self.I, self.O
        R0, R1 = self.R0, self.R1
        TT, TS, STT, ACT, CP, MS = self.TT, self.TS, self.STT, self.ACT, self.CP, self.MS
        NCK = dict(allow_slow_non_contiguous=True)
        xT, z2 = self.xT, self.z2
        PIECES = self.PIECES
        self.free_banks = list(range(8))
        LRUc = self.LRUc
        LK = self.LRUc_keys
        baT = R0.take([128, 8]); bxT = R0.take([128, 8]); sc8 = R0.take([128, 8]); hsc8 = R0.take([128, 8])
        TS("dve", baT, LRUc[:, :, 69], 0.5, ALU.mult, LK, "baT")
        TS("dve", bxT, LRUc[:, :, 70], 0.5, ALU.mult, LK, "bxT")
        ACT(sc8, LRUc[:, :, 71], AF.Exp, LK, "sc8", scale=-1.0)
        ACT(sc8, sc8, AF.Ln, ["sc8"], "sc8", bias=1.0)
        TS("dve", hsc8, sc8, -4.0, ALU.mult, ["sc8"], "hsc8")
        TS("dve", sc8, sc8, -8.0, ALU.mult, ["sc8", "hsc8"], "sc8")
        Wg = R0.take([128, 8, 2, 128], BF16)
        MS("pool", Wg, 0.0, "Wg")
        for gi, nm in ((0, "lru_wa"), (1, "lru_wx")):
            v = I[nm].rearrange("(j two) i o -> two i j o", two=2)
            for par in range(2):
                p.dma("pool", Wg[64 * par:64 * par + 64, :, gi, 64 * par:64 * par + 64], v[par], "d_wg", writes=["Wg"])
        p.last_write["Wg"] = ("d_wg", p.count["d_wg"])
        hfinL = R0.take([128, 8, 17])
        self.merged2 = R1.take([128, 8, T], BF16)
        merged2 = self.merged2
        self.merged_end = R1.pos
        xl_sb = R1.take([128, 3 + TP]); xs_sb = R1.take([128, 16, 7])
        abuf = R1.take([128, T]); a2buf = R1.take([128, T]); ixbuf = R1.take([128, T])
        hbuf = [R1.take([128, T])]
        tail_sb = R1.take([NTAIL, D])
        MS("pool", xl_sb[:, 0:3], 0.0, ("xl", -1))
        NP = 2
        ytmp = [R1.take([128, 512]) for i in range(2)]
        NXC = 5
        xc = [R1.take([128, 512]) for i in range(NXC)]
        xcb = [R1.take([128, 512], BF16) for i in range(2)]
        rp_ = [R1.take([128, 512]) for i in range(2)]
        ip_ = [R1.take([128, 512]) for i in range(2)]
        glp = [R1.take([128, 512]) for i in range(2)]
        gsp = [R1.take([128, 512]) for i in range(2)]
        gbp = [R1.take([128, 512]) for i in range(2)]
        t1p = [R1.take([128, 512]) for i in range(2)]
        t16 = R1.take([128, 16])
        wsl = [R1.take([128, 8, 128], BF16) for i in range(2)]
        wsg = [R1.take([128, 8, 2, 128], BF16) for i in range(2)]
        wgl = [R1.take([128, 4, 2, 128], BF16) for i in range(2)]
        hb = hbuf[0]
        XB = [0, 1]; GAB = [2, 3]; GXB = [4, 5]; POSTB = [6, 7]
        allp = lambda nm: [(nm, pi) for pi in range(len(PIECES))]

        def load_pre(j):
            sl = j % 2
            p.dma("pool", wsl[sl], I["w_in"][:, 128 * j:128 * (j + 1)].rearrange("(k p) n -> p k n", p=128), "d_wsl%d" % sl,
                  writes=[("wsl", sl)])

        def load_post(j):
            sl = j % 2
            wv = lambda c: I["w_in"][:, c:c + 128].rearrange("(k p) n -> p k n", p=128)
            gv = lambda c: I["w_glu"][:, c:c + 128].rearrange("(k p) n -> p k n", p=128)
            p.dma("pool", wsg[sl][:, :, 0, :], wv(1536 + 128 * j), "d_wsg%d_0" % sl, writes=[("wsg", sl, 0)])
            p.dma("pool", wsg[sl][:, :, 1, :], wv(2560 + 128 * j), "d_wsg%d_1" % sl, writes=[("wsg", sl, 1)])
            p.dma("pool", wgl[sl][:, :, 0, :], gv(128 * j), "d_wgl%d_0" % sl, writes=[("wgl", sl, 0)])
            p.dma("pool", wgl[sl][:, :, 1, :], gv(1024 + 128 * j), "d_wgl%d_1" % sl, writes=[("wgl", sl, 1)])

        items = [dict(j=j, pi=pi, c0=c0, w=w) for j in range(8) for pi, (c0, w) in enumerate(PIECES)]

        def P0(t, it):
            j, c0, w = it["j"], it["c0"], it["w"]
            sl = j % 2
            if it["pi"] == 0 and j + 1 < 8:
                load_pre(j + 1)
            b = XB[t % 2]
            for k in range(8):
                p.op("pe", lambda e, k=k, b=b, sl=sl, c0=c0, w=w: e.matmul(
                    self.bk(b)[:, 0:w], wsl[sl][:, k, :], xT[:, k, c0:c0 + w], start=(k == 0), stop=(k == 7)),
                    reads=[("wsl", sl)] + self.xT_keys(c0, w), writes=[("bank", b)], inc=(k == 7))
            if it["pi"] == len(PIECES) - 1:
                tb = POSTB[1]
                for k in range(8):
                    p.op("pe", lambda e, k=k, tb=tb, sl=sl: e.matmul(self.bk(tb)[0:NTAIL, 0:128], xT[:, k, TAIL0:T], wsl[sl][:, k, :],
                                                                     start=(k == 0), stop=(k == 7)),
                         reads=[("wsl", sl)] + self.xT_keys(TAIL0, NTAIL), writes=[("bank", tb)], inc=(k == 7))
                CP("act", tail_sb[:, 128 * j:128 * (j + 1)], self.bk(tb)[0:NTAIL, 0:128], [("bank", tb)], ("tail", j))

        def P1(t, it):
            j, pi, c0, w = it["j"], it["pi"], it["c0"], it["w"]
            b = XB[t % 2]
            ps = self.bk(b)[:, 0:w]
            yv = ytmp[t % 2][:, 0:w]
            ACT(yv, ps, AF.Identity, [("bank", b)] + LK, ("ytmp", t % 2), scale=LRUc[:, j, 67:68], bias=LRUc[:, j, 68:69])
            if w != 64:
                CP("act", xl_sb[:, 3 + c0:3 + c0 + w], ps, [("bank", b)], ("xl", pi))
            else:
                CP("pool", xs_sb[:, :, 0:3], LRUc[:, j, 0:48].rearrange("p (b k) -> p b k", k=3), LK, ("xs", "st"))
                CP("act", xs_sb[:, :, 3:7], ps.rearrange("p (b s) -> p b s", s=4), [("bank", b)], ("xs", "new"))

        def P2(t, it):
            j, pi, c0, w = it["j"], it["pi"], it["c0"], it["w"]
            cw = [LRUc[:, j, 64 + k:65 + k] for k in range(4)]
            yk = ("ytmp", t % 2); xk_ = ("xc", t % NXC)
            yv = ytmp[t % 2][:, 0:w]; xcv = xc[t % NXC][:, 0:w]
            if w != 64:
                xk = [("xl", pi), ("xl", pi - 1), yk] + LK
                STT("dve", yv, xl_sb[:, c0 + 2:c0 + 2 + w], cw[2], yv, ALU.mult, ALU.add, xk, yk)
                STT("dve", yv, xl_sb[:, c0 + 1:c0 + 1 + w], cw[1], yv, ALU.mult, ALU.add, xk, yk)
                STT("dve", xcv, xl_sb[:, c0:c0 + w], cw[0], yv, ALU.mult, ALU.add, xk, xk_)
            else:
                xk = [("xs", "st"), ("xs", "new"), yk] + LK
                y3 = yv.rearrange("p (b s) -> p b s", s=4); xc3 = xcv.rearrange("p (b s) -> p b s", s=4)
                STT("dve", y3, xs_sb[:, :, 2:6], cw[2], y3, ALU.mult, ALU.add, xk, yk)
                STT("dve", y3, xs_sb[:, :, 1:5], cw[1], y3, ALU.mult, ALU.add, xk, yk)
                STT("dve", xc3, xs_sb[:, :, 0:4], cw[0], y3, ALU.mult, ALU.add, xk, xk_)
            if j == 0:
                self.tap("xc_%d" % c0, xcv, [xk_])
            CP("pool", xcb[t % 2][:, 0:w], xcv, [xk_], ("xcb", t % 2))

        def P3(t, it):
            j, w = it["j"], it["w"]
            for (bb, gi) in ((GAB[t % 2], 0), (GXB[t % 2], 1)):
                p.op("pe", lambda e, bb=bb, gi=gi, j=j, t=t, w=w: e.matmul(self.bk(bb)[:, 0:w], Wg[:, j, gi, :], xcb[t % 2][:, 0:w],
                                                                          start=True, stop=True),
                     reads=["Wg", ("xcb", t % 2)], writes=[("bank", bb)])

        def P4(t, it):
            j, w = it["j"], it["w"]
            ACT(rp_[t % 2][:, 0:w], self.bk(GAB[t % 2])[:, 0:w], AF.Tanh, [("bank", GAB[t % 2]), "baT"], ("rp", t % 2),
                scale=0.5, bias=baT[:, j:j + 1])
            ACT(ip_[t % 2][:, 0:w], self.bk(GXB[t % 2])[:, 0:w], AF.Tanh, [("bank", GXB[t % 2]), "bxT"], ("ip", t % 2),
                scale=0.5, bias=bxT[:, j:j + 1])

        def P5(t, it):
            j, pi, c0, w = it["j"], it["pi"], it["c0"], it["w"]
            rv = rp_[t % 2][:, 0:w]; iv = ip_[t % 2][:, 0:w]; xcv = xc[t % NXC][:, 0:w]
            ACT(abuf[:, c0:c0 + w], rv, AF.Exp, [("rp", t % 2), "hsc8"], ("abuf", pi), scale=hsc8[:, j:j + 1], bias=hsc8[:, j:j + 1])
            ACT(a2buf[:, c0:c0 + w], rv, AF.Exp, [("rp", t % 2), "sc8"], ("a2buf", pi), scale=sc8[:, j:j + 1], bias=sc8[:, j:j + 1])
            STT("dve", ixbuf[:, c0:c0 + w], iv, 1.0, xcv, ALU.add, ALU.mult, [("ip", t % 2), ("xc", t % NXC)], ("ixbuf", pi))
            if pi == len(PIECES) - 1:
                mid_tile(j)

        post_queue = []

        def mid_tile(j):
            sl = j % 2
            ACT(a2buf, a2buf, AF.Sqrt, allp("a2buf"), "mh", scale=-0.25, bias=0.25)
            TT("dve", ixbuf, a2buf, ixbuf, ALU.mult, ["mh"] + allp("ixbuf"), "bterm")
            TT("dve", t16, abuf[:, TP:T:4], LRUc[:, j, 48:64], ALU.mult, allp("abuf") + LK, "t16")
            TT("dve", ixbuf[:, TP:T:4], ixbuf[:, TP:T:4], t16, ALU.add, ["bterm", "t16"], "bterm")
            MS("dve", abuf[:, TP:T:4], 0.0, "afix", r=allp("abuf") + ["t16"])
            p.op("dve", lambda e: e.tensor_tensor_scan(out=hb, data0=abuf, data1=ixbuf, initial=0.0, op0=ALU.mult, op1=ALU.add),
                 reads=allp("abuf") + ["afix", "bterm"], writes=["hbuf"])
            for nm in ("abuf", "a2buf", "ixbuf"):
                for pi in range(len(PIECES)):
                    p.readers.setdefault((nm, pi), []).append(p.last_write["hbuf"])
            if j == 0:
                self.tap("hbuf", hb, ["hbuf"])
            CP("pool", hfinL[:, j, 0:1], hb[:, TP - 1:TP], ["hbuf"], ("hfinL", j, 0))
            CP("pool", hfinL[:, j, 1:17], hb[:, TP + 3:T:4], ["hbuf"], ("hfinL", j, 1))
            for pi, (c0, w) in enumerate(PIECES):
                post_queue.append((j, pi, c0, w))

        npq = [0]

        def post_piece(j, pi, c0, w):
            sl = j % 2
            i2 = npq[0] % 2
            npq[0] += 1
            glv = glp[i2][:, 0:w]; gsv = gsp[i2][:, 0:w]; gbv = gbp[i2][:, 0:w]; t1v = t1p[i2][:, 0:w]
            groups = [("gl", wsg[sl], 0, 8, xT, ("wsg", sl, 0)), ("gs", wsg[sl], 1, 8, xT, ("wsg", sl, 1)),
                      ("gb", wgl[sl], 1, 4, z2, ("wgl", sl, 1)), ("ga", wgl[sl], 0, 4, z2, ("wgl", sl, 0))]
            for gi_, (nm, wt, idx, nk, src, wkey) in enumerate(groups):
                bb = POSTB[gi_ % 2]
                for k in range(nk):
                    rk = self.xT_keys(c0, w) if src is xT else [("z2", k, c0)]
                    p.op("pe", lambda e, k=k, bb=bb, wt=wt, idx=idx, src=src, nk=nk, c0=c0, w=w: e.matmul(
                        self.bk(bb)[:, 0:w], wt[:, k, idx, :], src[:, k, c0:c0 + w], start=(k == 0), stop=(k == nk - 1)),
                        reads=[wkey] + rk, writes=[("bank", bb)], inc=(k == nk - 1))
                psv = self.bk(bb)[:, 0:w]
                if nm == "gl":
                    ACT(glv, psv, AF.Tanh, [("bank", bb)], ("glp", i2), scale=0.5)
                elif nm == "gs":
                    ACT(gsv, psv, AF.Tanh, [("bank", bb)], ("gsp", i2), scale=0.5)
                elif nm == "gb":
                    ACT(gbv, psv, AF.Tanh, [("bank", bb)], ("gbp", i2), scale=0.25)
                else:
                    CP("act", t1v, psv, [("bank", bb)], ("t1p", i2))
            STT("dve", t1v, gbv, 1.0, t1v, ALU.add, ALU.mult, [("gbp", i2), ("t1p", i2)], ("t1p", i2))
            STT("dve", t1v, gsv, 1.0, t1v, ALU.add, ALU.mult, [("gsp", i2), ("t1p", i2)], ("t1p", i2))
            STT("dve", glv, glv, 1.0, hb[:, c0:c0 + w], ALU.add, ALU.mult, [("glp", i2), "hbuf"], ("glp", i2))
            STT("dve", merged2[:, j, c0:c0 + w], t1v, 0.25, glv, ALU.mult, ALU.add, [("t1p", i2), ("glp", i2)], ("merged2", j, c0))
            if pi == len(PIECES) - 1 and j + 2 < 8:
                load_post(j + 2)

        load_pre(0); load_post(0); load_post(1)
        N_ = len(items)
        stages = [(P5, 5), (P4, 4), (P1, 1), (P2, 2), (P3, 3), (P0, 0)]
        for t in range(N_ + 5):
            for fn, lag in stages:
                if 0 <= t - lag < N_:
                    fn(t - lag, items[t - lag])
            if post_queue:
                post_piece(*post_queue.pop(0))
        while post_queue:
            post_piece(*post_queue.pop(0))
        self.tap("merged2", merged2[:, 0, :], [("merged2", 0, c0) for c0, _ in PIECES])
        self.out_dma(O["lru_conv"], tail_sb, [("tail", j) for j in range(8)])
        self.out_dma(O["lru_h"], hfinL, [("hfinL", j, i) for j in range(8) for i in range(2)])

    def ln_tile(self, ps_flat, res, gB, bB, ytok, outv, rows, kps, kres, kg, kb, ky, kout, st6, mv, sd):
        p = self.p
        TT, TS, STT, ACT, CP = self.TT, self.TS, self.STT, self.ACT, self.CP
        yv = ytok[0:rows, :]
        p.op("act", lambda e: e.activation(out=yv, in_=ps_flat[0:rows, :], func=AF.Copy, scale=0.5), reads=kps, writes=[ky])
        STT("dve", yv, res[0:rows, :], ALPHA, yv, ALU.mult, ALU.add, [kres, ky], ky)
        for h in range(2):
            p.op("dve", lambda e, h=h: e.bn_stats(out=st6[0:rows, h, :], in_=yv[:, 512 * h:512 * (h + 1)]), reads=[ky], writes=[ky + ("st", h)])
        p.op("dve", lambda e: e.bn_aggr(out=mv[0:rows, :], in_=st6[0:rows, :, :].rearrange("p a b -> p (a b)")),
             reads=[ky + ("st", 0), ky + ("st", 1)], writes=[ky + ("mv",)])
        ACT(sd[0:rows, :], mv[0:rows, 1:2], AF.Sqrt, [ky + ("mv",)], ky + ("sd",), bias=LN_EPS)
        p.op("dve", lambda e: e.reciprocal(out=sd[0:rows, :], in_=sd[0:rows, :]), reads=[ky + ("sd",)], writes=[ky + ("sd",)])
        TS("dve", yv, yv, mv[0:rows, 0:1], ALU.subtract, [ky, ky + ("mv",), ky + ("sd",)], ky, s2=sd[0:rows, 0:1], op1=ALU.mult)
        TT("pool", yv, yv, gB[0:rows, :], ALU.mult, [ky, kg], ky)
        TT("dve", outv[0:rows, :], yv, bB[0:rows, :], ALU.add, [ky, kb], kout)

    def mix_stage(self):
        nc, p, I, O = self.nc, self.p, self.I, self.O
        R0, R1 = self.R0, self.R1
        TT, TS, STT, ACT, CP, MS = self.TT, self.TS, self.STT, self.ACT, self.CP, self.MS
        merged2 = self.merged2
        x1T = self.xT
        self.x1T = x1T
        lnc = self.lnc
        for i, nm in enumerate(("ln1_g", "ln1_b", "ln2_g", "ln2_b")):
            p.dma("sp", lnc[:, i, :], I[nm].partition_broadcast(128), "d_lnc%d" % i, writes=[("lnc", i)])
        wout = R1.take([128, 8, D], BF16)
        for h in range(2):
            p.dma("pool", wout[:, :, 512 * h:512 * (h + 1)], I["w_out"][:, 512 * h:512 * (h + 1)].rearrange("(k p) n -> p k n", p=128),
                  "d_wout%d" % h, writes=[("wout", h)])
        NX, NYT, NX1, NXB, NS_ = 3, 4, 3, 2, 4
        xtok = [R1.take([128, D]) for i in range(NX)]
        ytok = [R1.take([128, D]) for i in range(NYT)]
        x1tok = [R1.take([128, D]) for i in range(NX1)]
        x1b = [R1.take([128, D], BF16) for i in range(NXB)]
        st6 = [R1.take([128, 2, 6]) for i in range(NS_)]
        mv = [R1.take([128, 2]) for i in range(NS_)]
        sd = [R1.take([128, 1]) for i in range(NS_)]
        ntt = (T + 127) // 128
        TB = [4, 5, 6, 7]
        g1, b1 = lnc[:, 0, :], lnc[:, 1, :]
        items = [dict(tt=tt, r0=tt * 128, rows=min(128, T - tt * 128)) for tt in range(ntt)]

        def M0(t, it):
            r0, rows = it["r0"], it["rows"]
            s = t % NX; pp = t % 2
            p.dma("sp", xtok[s][0:rows, :], I["x"][r0:r0 + rows, :], "d_xtok%d" % s, writes=[("xtok", s)])
            for h in range(2):
                for k in range(8):
                    p.op("pe", lambda e, k=k, h=h, pp=pp, r0=r0, rows=rows: e.matmul(
                        self.ps[pp][0:rows, h, :], merged2[:, k, r0:r0 + rows], wout[:, k, 512 * h:512 * (h + 1)],
                        start=(k == 0), stop=(k == 7)),
                        reads=[("wout", h)] + [("merged2", k, c0) for (c0, w) in self.PIECES if c0 <= r0 < c0 + w],
                        writes=[("bank", 2 * pp + h)], inc=(k == 7))

        def M1(t, it):
            rows = it["rows"]
            pp = t % 2; sy = t % NYT; sx = t % NX; ss = t % NS_
            psf = self.ps[pp][:].rearrange("p a c -> p (a c)")
            yv = ytok[sy][0:rows, :]
            ky = ("ytok", sy)
            p.op("act", lambda e: e.activation(out=yv, in_=psf[0:rows, :], func=AF.Copy, scale=0.5),
                 reads=[("bank", 2 * pp), ("bank", 2 * pp + 1)], writes=[ky])
            STT("dve", yv, xtok[sx][0:rows, :], ALPHA, yv, ALU.mult, ALU.add, [("xtok", sx), ky], ky)
            for h in range(2):
                p.op("dve", lambda e, h=h: e.bn_stats(out=st6[ss][0:rows, h, :], in_=yv[:, 512 * h:512 * (h + 1)]),
                     reads=[ky], writes=[("st6", ss, h)])
            p.op("dve", lambda e: e.bn_aggr(out=mv[ss][0:rows, :], in_=st6[ss][0:rows, :, :].rearrange("p a b -> p (a b)")),
                 reads=[("st6", ss, 0), ("st6", ss, 1)], writes=[("mv", ss)])

        def M2(t, it):
            rows = it["rows"]
            sy = t % NYT; ss = t % NS_
            yv = ytok[sy][0:rows, :]
            ky = ("ytok", sy)
            ACT(sd[ss][0:rows, :], mv[ss][0:rows, 1:2], AF.Sqrt, [("mv", ss)], ("sd", ss), bias=LN_EPS)
            p.op("dve", lambda e: e.reciprocal(out=sd[ss][0:rows, :], in_=sd[ss][0:rows, :]), reads=[("sd", ss)], writes=[("sd", ss)])
            TS("dve", yv, yv, mv[ss][0:rows, 0:1], ALU.subtract, [ky, ("mv", ss), ("sd", ss)], ky, s2=sd[ss][0:rows, 0:1], op1=ALU.mult)

        def M3a(t, it):
            tt, r0, rows = it["tt"], it["r0"], it["rows"]
            sy = t % NYT; s1 = t % NX1
            yv = ytok[sy][0:rows, :]
            ky = ("ytok", sy)
            TT("pool", yv, yv, g1[0:rows, :], ALU.mult, [ky, ("lnc", 0)], ky)
            TT("dve", x1tok[s1][0:rows, :], yv, b1[0:rows, :], ALU.add, [ky, ("lnc", 1)], ("x1tok", s1))
            if tt == 0:
                self.tap("x1tok", x1tok[s1], [("x1tok", s1)])
            p.dma("sp", self.x1_scr[r0:r0 + rows, :], x1tok[s1][0:rows, :], "d_x1w%d" % s1, reads=[("x1tok", s1)], writes=[("x1scr", tt)])

        def M3b(t, it):
            rows = it["rows"]
            s1 = t % NX1; sb = t % NXB
            CP("act", x1b[sb][0:rows, :], x1tok[s1][0:rows, :], [("x1tok", s1)], ("x1b", sb))

        def M4(t, it):
            rows = it["rows"]
            sb = t % NXB
            b = TB[t % 4]
            it["b"] = b
            pt = self.bk(b).bitcast(BF16)
            for k in range(8):
                p.op("pe", lambda e, k=k, pt=pt, sb=sb, rows=rows: e.transpose(
                    pt[:, k * 128:k * 128 + rows], x1b[sb][0:rows, k * 128:(k + 1) * 128], self.ident_b[0:rows, 0:rows]),
                    reads=[("x1b", sb), "ident_b"], writes=[("bank", b)], inc=(k == 7))

        def M5(t, it):
            tt, r0, rows, b = it["tt"], it["r0"], it["rows"], it["b"]
            pt = self.bk(b).bitcast(BF16)
            src = pt.rearrange("p (k c) -> p k c", c=128)[:, :, 0:rows]
            CP("act", x1T[:, :, r0:r0 + rows], src, [("bank", b)], ("x1T", tt))

        stages = [(M3a, 3), (M1, 1), (M2, 2), (M3b, 3), (M5, 5), (M4, 4), (M0, 0)]
        N_ = len(items)
        for t in range(N_ + 5):
            for fn, lag in stages:
                if 0 <= t - lag < N_:
                    fn(t - lag, items[t - lag])
        self.tap("x1T", x1T[:, 0, :], [("x1T", tt) for tt in range(ntt)])

    def x1T_keys(self, c0, w):
        return [("x1T", tt) for tt in range(c0 // 128, (c0 + w - 1) // 128 + 1)]

    def ffn_stage(self):
        nc, p, I, O = self.nc, self.p, self.I, self.O
        R0, R1 = self.R0, self.R1
        TT, TS, STT, ACT, CP, MS = self.TT, self.TS, self.STT, self.ACT, self.CP, self.MS
        NCK = dict(allow_slow_non_contiguous=True)
        x1T, lnc = self.x1T, self.lnc
        NJ = DFF // 128
        QW = 576
        Fc = self.Fc
        FK = self.Fc_keys
        wdn = R1.take([128, NJ, D], BF16)
        for c in range(6):
            p.dma("pool", wdn[:, 4 * c:4 * c + 4, :], I["w_down"][512 * c:512 * (c + 1), :].rearrange("(k p) n -> p k n", p=128),
                  "d_wdn%d" % c, writes=[("wdn", c)])
        Gq = R1.take([128, NJ, QW], BF16)
        NSL = 4
        wup = [R1.take([128, 8, 2, 128], BF16) for i in range(NSL)]
        halo = R1.take([128, NJ, 2])
        MS("pool", halo, 0.0, "halo_init")
        NY, NQ, NG = 5, 3, 2
        a_sb = [R1.take([128, 2 + 512]) for i in range(2)]
        as_sb = R1.take([128, 16, 6])
        y0 = [R1.take([128, 512]) for i in range(NY)]
        qq = [R1.take([128, 512]) for i in range(NQ)]
        ag = [R1.take([128, 512]) for i in range(NG)]
        tailf = [R1.take([NTAIL, 512]) for i in range(2)]
        NB = 2
        x1tok = [R1.take([128, D]) for i in range(NB)]
        ytok = [R1.take([128, D]) for i in range(NB)]
        otok = ytok
        st6 = [R1.take([128, 2, 6]) for i in range(NB)]
        mv = [R1.take([128, 2]) for i in range(NB)]
        sd = [R1.take([128, 1]) for i in range(NB)]
        nld = [0]

        def load_wup(j):
            sl = nld[0] % NSL
            nld[0] += 1
            wv = lambda c: I["w_up"][:, c:c + 128].rearrange("(k p) n -> p k n", p=128)
            p.dma("pool", wup[sl][:, :, 0, :], wv(128 * j), "d_wup%d_0" % sl, writes=[("wup", sl, 0)])
            p.dma("pool", wup[sl][:, :, 1, :], wv(DFF + 128 * j), "d_wup%d_1" % sl, writes=[("wup", sl, 1)])
            return sl

        npc = [0]
        ntile = [0]
        for n in range(4):
            q0 = 512 * n
            pieces = [(q0, 512, 0)] + ([(TP, NS, 512)] if n == 3 else [])
            RA = [0, 1]
            RG = [2, 3, 4, 5, 6]
            TAILB = 7
            items = []
            for j in range(NJ):
                for pi_, (c0, w, lc0) in enumerate(pieces):
                    items.append(dict(j=j, c0=c0, w=w, lc0=lc0, first=(pi_ == 0), last=(pi_ == len(pieces) - 1)))
            pending = [load_wup(0), load_wup(1), load_wup(2)]
            cur_sl = {}

            def S0(t, it):
                j, c0, w = it["j"], it["c0"], it["w"]
                if it["first"]:
                    cur_sl[j] = pending.pop(0)
                    if j + 3 < NJ:
                        pending.append(load_wup(j + 3))
                sl = cur_sl[j]
                bA = RA[t % 2]; bG = RG[t % 5]
                it["bA"], it["bG"] = bA, bG
                for (bb, gi) in ((bA, 0), (bG, 1)):
                    for k in range(8):
                        p.op("pe", lambda e, k=k, bb=bb, gi=gi, sl=sl, c0=c0, w=w: e.matmul(
                            self.bk(bb)[:, 0:w], wup[sl][:, k, gi, :], x1T[:, k, c0:c0 + w], start=(k == 0), stop=(k == 7)),
                            reads=[("wup", sl, gi)] + self.x1T_keys(c0, w), writes=[("bank", bb)], inc=(k == 7))
                if n == 3 and it["last"]:
                    for k in range(8):
                        p.op("pe", lambda e, k=k, sl=sl: e.matmul(self.bk(TAILB)[0:NTAIL, 0:128], x1T[:, k, TAIL0:T], wup[sl][:, k, 0, :],
                                                                  start=(k == 0), stop=(k == 7)),
                             reads=[("wup", sl, 0)] + self.x1T_keys(TAIL0, NTAIL), writes=[("bank", TAILB)], inc=(k == 7))

            def S1(t, it):
                j, c0, w, bA = it["j"], it["c0"], it["w"], it["bA"]
                samp = (w == NS)
                iy = t % NY; ia = t % 2
                fw = [Fc[:, j, 32 + k:33 + k] for k in range(3)]
                aps = self.bk(bA)[:, 0:w]
                yv = y0[iy][:, 0:w]
                ACT(yv, aps, AF.Identity, [("bank", bA)] + FK, ("y0", iy), scale=fw[2], bias=Fc[:, j, 35:36])
                if not samp:
                    ab = a_sb[ia]
                    CP("act", ab[:, 2:2 + w], aps, [("bank", bA)], ("a_sb", ia))
                    CP("dve", ab[:, 0:2], halo[:, j, :], ["halo_init", ("halo", j)], ("a_sbh", ia))
                    ak = [("a_sb", ia), ("a_sbh", ia), ("y0", iy)]
                    STT("dve", yv, ab[:, 1:1 + w], fw[1], yv, ALU.mult, ALU.add, ak, ("y0", iy))
                    STT("dve", yv, ab[:, 0:w], fw[0], yv, ALU.mult, ALU.add, ak, ("y0", iy))
                    CP("dve", halo[:, j, :], ab[:, w:w + 2], [("a_sb", ia), ("a_sbh", ia)], ("halo", j))
                else:
                    CP("act", as_sb[:, :, 2:6], aps.rearrange("p (b s) -> p b s", s=4), [("bank", bA)], ("as_sb", "new"))
                    CP("dve", as_sb[:, :, 0:2], Fc[:, j, 0:32].rearrange("p (b k) -> p b k", k=2), FK, ("as_sb", "st"))
                    ak = [("as_sb", "new"), ("as_sb", "st"), ("y0", iy)]
                    y3 = yv.rearrange("p (b s) -> p b s", s=4)
                    STT("dve", y3, as_sb[:, :, 1:5], fw[1], y3, ALU.mult, ALU.add, ak, ("y0", iy))
                    STT("dve", y3, as_sb[:, :, 0:4], fw[0], y3, ALU.mult, ALU.add, ak, ("y0", iy))
                if n == 3 and it["last"]:
                    tb = (j // 4) % 2
                    CP("act", tailf[tb][:, 128 * (j % 4):128 * (j % 4 + 1)], self.bk(TAILB)[0:NTAIL, 0:128], [("bank", TAILB)], ("tailf", tb, j % 4))
                    if j % 4 == 3:
                        sem = "o_tf%d" % tb
                        p.dma("sp", O["ffn_conv"][:, 512 * (j // 4):512 * (j // 4 + 1)], tailf[tb], sem,
                              reads=[("tailf", tb, i) for i in range(4)])
                        self.out_sems[sem] = p.count[sem]

            def S2(t, it):
                w = it["w"]
                iy = t % NY; iq = t % NQ
                yv = y0[iy][:, 0:w]; qv = qq[iq][:, 0:w]
                ACT(qv, yv, AF.Square, [("y0", iy)], ("qq", iq))
                ACT(qv, qv, AF.Identity, [("qq", iq)], ("qq", iq), scale=GC, bias=1.0)

            def S3(t, it):
                w = it["w"]
                iy = t % NY; iq = t % NQ; ig = t % NG
                TT("pool", ag[ig][:, 0:w], qq[iq][:, 0:w], y0[iy][:, 0:w], ALU.mult, [("qq", iq), ("y0", iy)], ("ag", ig))

            def S4(t, it):
                j, w, lc0, bG = it["j"], it["w"], it["lc0"], it["bG"]
                iy = t % NY; iq = t % NQ; ig = t % NG
                yv = y0[iy][:, 0:w]; qv = qq[iq][:, 0:w]; av = ag[ig][:, 0:w]
                gps = self.bk(bG)[:, 0:w]
                ACT(qv, av, AF.Tanh, [("ag", ig)], ("qq", iq), scale=GK)
                STT("dve", av, qv, 1.0, yv, ALU.add, ALU.mult, [("qq", iq), ("y0", iy)], ("ag", ig))
                TT("dve", Gq[:, j, lc0:lc0 + w], av, gps, ALU.mult, [("ag", ig), ("bank", bG)], ("Gq", j, lc0))

            N_ = len(items)
            stages = [(S4, 4), (S1, 1), (S2, 2), (S3, 3), (S0, 0)]
            for t in range(N_ + 4):
                for fn, lag in stages:
                    if 0 <= t - lag < N_:
                        fn(t - lag, items[t - lag])
            if n == 0:
                self.tap("Gq", Gq[:, 0, :], [("Gq", 0, 0)])
            tts = [4 * n + i for i in range(4)] + ([16] if n == 3 else [])
            for tt in tts:
                r0 = tt * 128
                rows = min(128, T - r0)
                lc = r0 - q0 if tt < 16 else 512
                s = ntile[0] % NB
                pp = ntile[0] % 2
                ntile[0] += 1
                p.dma("sp", x1tok[s][0:rows, :], self.x1_scr[r0:r0 + rows, :], "d_x1r%d" % s, reads=[("x1scr", tt)], writes=[("x1tok2", s)])
                psf = self.ps[pp][:].rearrange("p a c -> p (a c)")
                gkeys = [("Gq", j, 512 if tt == 16 else 0) for j in range(NJ)]
                for h in range(2):
                    for k in range(NJ):
                        p.op("pe", lambda e, k=k, h=h, pp=pp, lc=lc, rows=rows: e.matmul(
                            self.ps[pp][0:rows, h, :], Gq[:, k, lc:lc + rows], wdn[:, k, 512 * h:512 * (h + 1)],
                            start=(k == 0), stop=(k == NJ - 1)),
                            reads=[("wdn", k // 4), ("Gq", k, 512 if tt == 16 else 0)], writes=[("bank", 2 * pp + h)], inc=(k == NJ - 1))
                self.ln_tile(psf, x1tok[s], lnc[:, 2, :], lnc[:, 3, :], ytok[s], otok[s], rows,
                             [("bank", 2 * pp), ("bank", 2 * pp + 1)], ("x1tok2", s), ("lnc", 2), ("lnc", 3), ("ytok2", s), ("ytok2", s),
                             st6[s], mv[s], sd[s])
                sem = "o_y%d" % s
                p.dma("sp", O["y"][r0:r0 + rows, :], otok[s][0:rows, :], sem, reads=[("ytok2", s)])
                self.out_sems[sem] = p.count[sem]

    def finish(self):
        fw = [(s, v) for s, v in self.out_sems.items()]
        self.p.emit(final_waits=fw)
        self.st.close()
        print("arena peaks: R0 %d/%d words, R1 %d/%d words" % (self.R0.peak, self.R0.words, self.R1.peak, self.R1.words))
        return self.nc


def shard_inputs(inputs, c):
    f = lambda a: np.ascontiguousarray(a, dtype=np.float32)
    m = {}
    m["x"] = f(np.concatenate([inputs["x_prompt"][c], inputs["x_sample"][NSQ * c:NSQ * (c + 1)].reshape(NS, D)], axis=0))
    m["st_lru_conv"] = f(inputs["state_lru_conv"][0, NSQ * c:NSQ * (c + 1)].reshape(NSQ * 3, D))
    m["st_lru_h"] = f(inputs["state_lru_h"][0, NSQ * c:NSQ * (c + 1)])
    m["st_s5_re"] = f(inputs["state_s5_re"][0, NSQ * c:NSQ * (c + 1)].reshape(NSQ, 2048))
    m["st_s5_im"] = f(inputs["state_s5_im"][0, NSQ * c:NSQ * (c + 1)].reshape(NSQ, 2048))
    m["st_ffn_conv"] = f(inputs["state_ffn_conv"][0, NSQ * c:NSQ * (c + 1)].reshape(NSQ * 2, DFF))
    for k in ("w_in", "lru_conv_w", "lru_conv_b", "lru_wa", "lru_ba", "lru_wx", "lru_bx", "lru_lambda", "s5_a_re", "s5_a_im",
              "s5_log_dt", "s5_b_re", "s5_b_im", "s5_c_re", "s5_c_im", "s5_d", "w_glu", "w_out", "ln1_g", "ln1_b", "w_up",
              "ffn_conv_w", "ffn_conv_b", "w_down", "ln2_g", "ln2_b"):
        m[k] = f(inputs[k][0])
    return m


_NC_CACHE = {}


def _get_nc():
    if "nc" not in _NC_CACHE:
        _NC_CACHE["nc"] = Builder().build()
    return _NC_CACHE["nc"]


def kernel(**inputs):
    nc = _get_nc()
    in_maps = [shard_inputs(inputs, c) for c in range(NCORES)]
    res = run_bass_kernel_spmd(nc, in_maps, core_ids=list(range(NCORES)))
    R = res.results
    B = NCORES
    y_p = np.zeros((B, TP, D), np.float32); y_s = np.zeros((B * NSQ, 4, D), np.float32)
    p_conv = np.zeros((1, B, 3, D), np.float32); p_h = np.zeros((1, B, D), np.float32)
    p_re = np.zeros((1, B, 32, 64), np.float32); p_im = np.zeros((1, B, 32, 64), np.float32)
    p_ffn = np.zeros((1, B, 2, DFF), np.float32)
    s_conv = np.zeros((1, B * NSQ, 3, D), np.float32); s_h = np.zeros((1, B * NSQ, D), np.float32)
    s_re = np.zeros((1, B * NSQ, 32, 64), np.float32); s_im = np.zeros((1, B * NSQ, 32, 64), np.float32)
    s_ffn = np.zeros((1, B * NSQ, 2, DFF), np.float32)
    for c in range(B):
        r = R[c]
        sl = slice(NSQ * c, NSQ * (c + 1))
        y_p[c] = r["y"][0:TP]
        y_s[sl] = r["y"][TP:].reshape(NSQ, 4, D)
        lc = r["o_lru_conv"]
        p_conv[0, c] = lc[0:3]
        s_conv[0, sl] = lc[3:].reshape(NSQ, 4, D)[:, 1:4]
        lh = r["o_lru_h"].transpose(2, 1, 0).reshape(17, D)
        p_h[0, c] = lh[0]
        s_h[0, sl] = lh[1:17]
        s5 = r["o_s5"].reshape(2, 64, 2, 16, 17).transpose(2, 4, 3, 0, 1)
        s5 = s5.reshape(2, 17, 32, 64)
        p_re[0, c] = s5[0, 0]
        p_im[0, c] = s5[1, 0]
        s_re[0, sl] = s5[0, 1:17]
        s_im[0, sl] = s5[1, 1:17]
        fc = r["o_ffn_conv"]
        p_ffn[0, c] = fc[1:3]
        s_ffn[0, sl] = fc[3:].reshape(NSQ, 4, DFF)[:, 2:4]
    return (y_p, y_s, p_conv, p_h, p_re, p_im, p_ffn, s_conv, s_h, s_re, s_im, s_ffn)
```

```python
import math
import contextlib
import numpy as np
import concourse.bass as bass
import concourse.mybir as mybir
from concourse.bass_utils import run_bass_kernel_spmd

F32 = mybir.dt.float32
BF16 = mybir.dt.bfloat16
AF = mybir.ActivationFunctionType
ALU = mybir.AluOpType

ENGINES = ("pe", "act", "dve", "pool", "sp")
NCORES = 8
TP = 2048
NSQ = 16
NS = 64
T = TP + NS
TAIL0 = TP - 3
NTAIL = T - TAIL0
D = 1024
DFF = 3072
ALPHA = 2.0 ** 0.25
LN_EPS = 1e-5
LCH = 8
NCH = TP // LCH
NLEV = 8
PAD = 128
PI = math.pi
GK = math.sqrt(2.0 / math.pi)
GC = 0.044715


class Prog:
    def __init__(self, nc):
        self.nc = nc
        self.streams = {e: [] for e in ENGINES}
        self.count = {}
        self.waited = {e: {} for e in ENGINES}
        self.last_write = {}
        self.readers = {}
        self.sem_names = set()
        self.epoch = "0"
        self.pending = {}

    def barrier(self):
        snap = list(self.count.items())
        for e in ENGINES:
            self.pending.setdefault(e, []).extend(snap)

    def op(self, eng, fn, reads=(), writes=(), inc=True, sem=None, amount=1):
        if sem is None:
            sem = "s_%s_%s" % (eng, self.epoch)
        self.sem_names.add(sem)
        deps = list(self.pending.pop(eng, ()))
        for k in reads:
            ev = self.last_write.get(k)
            if ev is not None:
                deps.append(ev)
        for k in writes:
            ev = self.last_write.get(k)
            if ev is not None:
                deps.append(ev)
            deps.extend(self.readers.get(k, ()))
        waits = {}
        for (s, v) in deps:
            if eng == "pe" and s.startswith("s_pe_"):
                continue
            if self.waited[eng].get(s, 0) >= v:
                continue
            if waits.get(s, 0) < v:
                waits[s] = v
        for s, v in waits.items():
            self.waited[eng][s] = v
        cur = self.count.get(sem, 0)
        val = cur + amount
        if inc:
            self.count[sem] = val
        ev = (sem, val)
        self.streams[eng].append((fn, list(waits.items()), (sem, amount) if inc else None))
        for k in reads:
            self.readers.setdefault(k, []).append(ev)
        for k in writes:
            self.last_write[k] = ev
            self.readers[k] = []
        return ev

    def dma(self, queue, out, in_, sem, reads=(), writes=(), **kw):
        def fn(e):
            return e.dma_start(out=out, in_=in_, **kw)
        return self.op(queue, fn, reads=reads, writes=writes, inc=True, sem=sem, amount=16)

    def emit(self, final_waits=()):
        nc = self.nc
        names = sorted(self.sem_names)
        with contextlib.ExitStack() as st:
            sems = {n: st.enter_context(nc.semaphore(n)) for n in names}
            block = st.enter_context(nc.Block())
            streams = self.streams

            def run(engh, lst, last):
                for fn, waits, inc in lst:
                    for s, v in waits:
                        engh.wait_ge(sems[s], v)
                    ins = fn(engh)
                    if inc is not None:
                        ins.then_inc(sems[inc[0]], inc[1])
                if last:
                    for s, v in final_waits:
                        engh.wait_ge(sems[s], v)

            @block.tensor
            def _(e):
                run(e, streams["pe"], False)

            @block.scalar
            def _(e):
                run(e, streams["act"], False)

            @block.vector
            def _(e):
                run(e, streams["dve"], False)

            @block.gpsimd
            def _(e):
                run(e, streams["pool"], False)

            @block.sync
            def _(e):
                run(e, streams["sp"], True)


class Arena:
    def __init__(self, tensor, words):
        self.t = tensor
        self.words = words
        self.pos = 0
        self.peak = 0

    def take(self, shape, dt=F32):
        n = 1
        for s in shape[1:]:
            n *= s
        esz = 4 if dt == F32 else 2
        w = (n * esz + 3) // 4
        w = (w + 7) // 8 * 8
        assert self.pos + w <= self.words, "arena overflow: need %d have %d" % (self.pos + w, self.words)
        v = self.t[0:shape[0], self.pos:self.pos + w]
        self.pos += w
        self.peak = max(self.peak, self.pos)
        if dt != F32:
            v = v.bitcast(dt)
        v = v[:, 0:n]
        if len(shape) > 2:
            names = " ".join("d%d" % i for i in range(len(shape) - 1))
            kw = {"d%d" % i: shape[i + 1] for i in range(len(shape) - 2)}
            v = v.rearrange("p (%s) -> p %s" % (names, names), **kw)
        return v


class StopBuild(Exception):
    pass


class Builder:
    def chk_stop(self, name, reads=()):
        if self.stop_after == name:
            self.p.barrier()
            d = self.dout("dbg_stop", [128, 4])
            t = self.R0.take([128, 4])
            self.MS("dve", t, 1.0, "stoptile")
            self.out_dma(d, t, ["stoptile"])
            raise StopBuild()

    def __init__(self, debug=(), stop_after=None):
        self.debug = set(debug)
        self.stop_after = stop_after
        self.nc = bass.Bass("TRN2", target_bir_lowering=False)
        self.p = Prog(self.nc)
        self.st = contextlib.ExitStack()
        self.out_sems = {}
        self.nout = 0
        self.free_banks = list(range(8))
        self.nbank = 0
        self.ntmp = 0

    def din(self, name, shape):
        return self.nc.dram_tensor(name, list(shape), F32, kind="ExternalInput").ap()

    def dout(self, name, shape, dt=F32):
        return self.nc.dram_tensor(name, list(shape), dt, kind="ExternalOutput").ap()

    def sb(self, name, shape, dt=F32):
        t = self.st.enter_context(self.nc.sbuf_tensor(name, list(shape), dt))
        return t[:]

    def out_dma(self, out, in_, reads, queue="sp", **kw):
        sem = "o_%d" % (self.nout % 8)
        self.nout += 1
        self.p.dma(queue, out, in_, sem, reads=reads, **kw)
        self.out_sems[sem] = self.p.count[sem]

    def tap(self, name, ap, reads):
        if name not in self.debug:
            return
        d = self.dout("dbg_" + name, list(ap.shape), ap.dtype)
        self.out_dma(d, ap, reads)

    def bank(self):
        b = self.free_banks[self.nbank % len(self.free_banks)]
        self.nbank += 1
        return b

    def bk(self, b):
        return self.ps[b // 2][:, b % 2, :]

    def TT(self, eng, out, a, b_, op, r, w):
        self.p.op(eng, lambda e: e.tensor_tensor(out=out, in0=a, in1=b_, op=op), reads=r, writes=[w])

    def TS(self, eng, out, a, s1, op0, r, w, s2=None, op1=None):
        if op1 is None:
            self.p.op(eng, lambda e: e.tensor_scalar(out=out, in0=a, scalar1=s1, scalar2=None, op0=op0), reads=r, writes=[w])
        else:
            self.p.op(eng, lambda e: e.tensor_scalar(out=out, in0=a, scalar1=s1, scalar2=s2, op0=op0, op1=op1), reads=r, writes=[w])

    def STT(self, eng, out, a, sc, b_, op0, op1, r, w):
        self.p.op(eng, lambda e: e.scalar_tensor_tensor(out=out, in0=a, scalar=sc, in1=b_, op0=op0, op1=op1), reads=r, writes=[w])

    def ACT(self, out, a, func, r, w, scale=1.0, bias=0.0):
        self.p.op("act", lambda e: e.activation(out=out, in_=a, func=func, scale=scale, bias=bias), reads=r, writes=[w])

    def CP(self, eng, out, a, r, w):
        if eng == "act":
            self.p.op("act", lambda e: e.copy(out=out, in_=a), reads=r, writes=[w])
        else:
            self.p.op(eng, lambda e: e.tensor_copy(out=out, in_=a), reads=r, writes=[w])

    def MS(self, eng, out, val, w, r=()):
        self.p.op(eng, lambda e: e.memset(out, val), reads=list(r), writes=[w])

    def build(self):
        nc, p = self.nc, self.p
        din = self.din
        I = {}
        for name, shape in (("x", [T, D]), ("st_lru_conv", [NSQ * 3, D]), ("st_lru_h", [NSQ, D]), ("st_s5_re", [NSQ, 2048]),
                            ("st_s5_im", [NSQ, 2048]), ("st_ffn_conv", [NSQ * 2, DFF]), ("w_in", [D, 3584]),
                            ("lru_conv_w", [4, D]), ("lru_conv_b", [D]), ("lru_wa", [16, 64, 64]), ("lru_ba", [D]),
                            ("lru_wx", [16, 64, 64]), ("lru_bx", [D]), ("lru_lambda", [D]), ("s5_a_re", [32, 64]),
                            ("s5_a_im", [32, 64]), ("s5_log_dt", [32]), ("s5_b_re", [32, 64, 16]), ("s5_b_im", [32, 64, 16]),
                            ("s5_c_re", [32, 16, 64]), ("s5_c_im", [32, 16, 64]), ("s5_d", [512]), ("w_glu", [512, 2048]),
                            ("w_out", [D, D]), ("ln1_g", [D]), ("ln1_b", [D]), ("w_up", [D, 2 * DFF]), ("ffn_conv_w", [3, DFF]),
                            ("ffn_conv_b", [DFF]), ("w_down", [DFF, D]), ("ln2_g", [D]), ("ln2_b", [D])):
            I[name] = din(name, shape)
        self.I = I
        O = {}
        O["y"] = self.dout("y", [T, D])
        O["lru_conv"] = self.dout("o_lru_conv", [NTAIL, D])
        O["lru_h"] = self.dout("o_lru_h", [128, 8, 17])
        O["s5"] = self.dout("o_s5", [128, 2, 16, 17])
        O["ffn_conv"] = self.dout("o_ffn_conv", [NTAIL, DFF])
        self.O = O
        self.x1_scr = nc.dram_tensor("x1_scr", [T, D], F32, kind="Internal").ap()

        self.ps = [self.st.enter_context(nc.psum_tensor("ps%d" % i, [128, 2, 512], F32)) for i in range(4)]
        R0W = 17664
        R1W = 35456
        self.R0 = Arena(self.st.enter_context(nc.sbuf_tensor("R0", [128, R0W], F32)), R0W)
        self.R1 = Arena(self.st.enter_context(nc.sbuf_tensor("R1", [128, R1W], F32)), R1W)
        R0, R1 = self.R0, self.R1

        ident_f = R0.take([128, 128]); ident_b = R0.take([128, 128], BF16)
        self.ident_f, self.ident_b = ident_f, ident_b
        self.MS("pool", ident_f, 0.0, "ident_f")
        p.op("pool", lambda e: e.affine_select(out=ident_f, in_=ident_f, pattern=[[-1, 128]],
                                               compare_op=ALU.not_equal, fill=1.0, base=0, channel_multiplier=1),
             reads=["ident_f"], writes=["ident_f"])
        self.CP("pool", ident_b, ident_f, ["ident_f"], "ident_b")

        self.PIECES = [(0, 512), (512, 512), (1024, 512), (1536, 512), (2048, 64)]
        xT = R0.take([128, 8, T], BF16)
        self.xT = xT
        zblk = R0.take([128, 4224])
        self.z2 = zblk.bitcast(BF16)[:, 0:4 * T].rearrange("p (a t) -> p a t", a=4)
        self.lnc = zblk[:, 0:4096].rearrange("p (a n) -> p a n", a=4)
        self.Hfin = R0.take([128, 2, 16, 17])
        self.LRUc = R0.take([128, 8, 72])
        self.Fc = R0.take([128, DFF // 128, 36])
        mark0 = R1.pos
        u_bf = R1.take([128, 4, T], BF16)
        self.W1 = R1.take([128, 4, LCH, 2, 128], BF16)
        self.W4a = R1.take([128, 16, LCH, 2, 32], BF16)
        self.W4b = R1.take([128, 16, LCH, 2, 32], BF16)
        self.h0S = R1.take([128, 2, 16, 16]); self.h0S_bf = R1.take([128, 2, 16, 16], BF16)
        self.RS = Arena(R1.take([128, 2368]), 2368)
        mark1 = R1.pos
        self.free_banks = [4, 5, 6, 7]
        self.param_loads()
        xb = [R1.take([128, D], BF16) for i in range(4)]
        ntt = (T + 127) // 128
        for tt in range(ntt):
            r0 = tt * 128
            rows = min(128, T - r0)
            slot = tt % 4
            p.dma("pool", xb[slot][0:rows, :], I["x"][r0:r0 + rows, :], "d_xb%d" % slot, writes=[("xb", slot)])
            b = self.bank()
            pt = self.bk(b).bitcast(BF16)
            for k in range(8):
                p.op("pe", lambda e, k=k, pt=pt, slot=slot, rows=rows: e.transpose(
                    pt[:, k * 128:k * 128 + rows], xb[slot][0:rows, k * 128:(k + 1) * 128], ident_b[0:rows, 0:rows]),
                    reads=[("xb", slot), "ident_b"], writes=[("bank", b)], inc=(k == 7))
            src = pt.rearrange("p (k c) -> p k c", c=128)[:, :, 0:rows]
            self.CP("act" if tt % 2 == 0 else "dve", xT[:, :, r0:r0 + rows], src, [("bank", b)], ("xT", tt))
            if tt == 3:
                self.param_transposes()
        self.tap("xT", xT[:, 0, :], [("xT", tt) for tt in range(ntt)])
        if self.stop_after == "p0":
            return self.finish()
        try:
            self.s5_prep()
        except StopBuild:
            return self.finish()
        self.tap("W1", self.W1[:, 0, :, :, :], self.W1_keys)
        self.tap("W4a", self.W4a[:, 0, :, :, :], self.W4_keys)
        self.tap("W4b", self.W4b[:, 0, :, :, :], self.W4_keys)
        self.tap("h0S", self.h0S, self.h0S_keys)
        self.tap("mur", self.mur, self.mu_keys)
        if self.stop_after == "prep":
            return self.finish()
        p.barrier()
        R1.pos = mark1
        wslot_u = [R1.take([128, 8, 128], BF16) for i in range(2)]
        for qh in range(4):
            slot = qh % 2
            p.dma("pool", wslot_u[slot], I["w_in"][:, 1024 + 128 * qh:1024 + 128 * (qh + 1)].rearrange("(k p) n -> p k n", p=128),
                  "d_wu%d" % slot, writes=[("wslot_u", slot)])
            for (c0, w) in self.PIECES:
                b = self.bank()
                for k in range(8):
                    p.op("pe", lambda e, k=k, b=b, slot=slot, c0=c0, w=w: e.matmul(
                        self.bk(b)[:, 0:w], wslot_u[slot][:, k, :], xT[:, k, c0:c0 + w], start=(k == 0), stop=(k == 7)),
                        reads=[("wslot_u", slot)] + self.xT_keys(c0, w), writes=[("bank", b)], inc=(k == 7))
                self.CP("act", u_bf[:, qh, c0:c0 + w], self.bk(b)[:, 0:w], [("bank", b)], ("u_bf", qh, c0))
        self.tap("u_bf", u_bf[:, 0, :], [("u_bf", 0, c0) for c0, _ in self.PIECES])
        if self.stop_after == "u":
            return self.finish()
        self.s5_main(u_bf)
        self.tap("z2", self.z2[:, 0, :], [("z2", 0, c0) for c0, _ in self.PIECES])
        self.tap("Hfin", self.Hfin, [("Hfin", q) for q in range(16)] + [("Hfin0", q) for q in range(16)])
        hk = [("Hfin", q) for q in range(16)] + [("Hfin0", q) for q in range(16)]
        self.out_dma(O["s5"], self.Hfin, hk)
        if self.stop_after == "s5":
            return self.finish()
        p.barrier()
        R1.pos = mark0
        p.epoch = "1"
        self.lru_stage()
        if self.stop_after == "lru":
            return self.finish()
        p.barrier()
        R1.pos = self.merged_end
        p.epoch = "2"
        self.mix_stage()
        if self.stop_after == "mix":
            return self.finish()
        p.barrier()
        R1.pos = 0
        p.epoch = "3"
        self.ffn_stage()
        return self.finish()

    def xT_keys(self, c0, w):
        return [("xT", tt) for tt in range(c0 // 128, (c0 + w - 1) // 128 + 1)]

    def param_loads(self):
        nc, p, I = self.nc, self.p, self.I
        R1 = self.R1
        MS = self.MS
        NCK = dict(allow_slow_non_contiguous=True)
        NJ = DFF // 128
        self.LF = R1.take([128, D + DFF])
        Lp = self.LF[:, 0:D]; Fp = self.LF[:, D:D + DFF]
        hs1 = R1.take([128, 2048])
        hsp = [hs1, hs1]
        self.Lp, self.Fp, self.hsp = Lp, Fp, hsp
        MS("pool", Lp, 0.0, "Lp"); MS("pool", Fp, 0.0, "Fp"); MS("pool", hs1, 0.0, "hsp")
        row = lambda nm: I[nm].rearrange("(o n) -> o n", o=1)
        for (r0, r1, src) in ((0, 48, I["st_lru_conv"]), (48, 64, I["st_lru_h"]), (64, 68, I["lru_conv_w"]), (68, 69, row("lru_conv_b")),
                              (69, 70, row("lru_ba")), (70, 71, row("lru_bx")), (71, 72, row("lru_lambda"))):
            p.dma("sp", Lp[r0:r1, :], src, "d_Lp", writes=["Lp"])
        for (r0, r1, src) in ((0, 32, I["st_ffn_conv"]), (32, 35, I["ffn_conv_w"]), (35, 36, row("ffn_conv_b"))):
            p.dma("sp", Fp[r0:r1, :], src, "d_Fp", writes=["Fp"])
        self.hs_loaded = False
        sh3 = [128, 16, 32]
        self.Bnr = R1.take(sh3); self.Bni = R1.take(sh3)
        self.Cnr = R1.take([128, 4, 128]); self.Cni = R1.take([128, 4, 128])
        for t_, k_ in ((self.Bnr, "Bnr"), (self.Bni, "Bni"), (self.Cnr, "Cnr"), (self.Cni, "Cni")):
            MS("pool", t_, 0.0, k_)
        for (dst, nm, key) in ((self.Bnr, "s5_b_re", "Bnr"), (self.Bni, "s5_b_im", "Bni")):
            v = I[nm].rearrange("(q two) p c -> two p q c", two=2)
            for two in range(2):
                p.dma("sp", dst[64 * two:64 * two + 64, :, 16 * two:16 * two + 16], v[two], "d_prep", writes=[key])
        for (dst, nm, key) in ((self.Cnr, "s5_c_re", "Cnr"), (self.Cni, "s5_c_im", "Cni")):
            v = I[nm].rearrange("(qh ql two) c p -> ql two c qh p", qh=4, ql=4, two=2)
            for ql in range(4):
                for two in range(2):
                    p0 = 32 * ql + 16 * two
                    p.dma("sp", dst[p0:p0 + 16, :, 64 * two:64 * two + 64], v[ql, two], "d_prep", writes=[key])
        shS = [128, 16]
        self.aSr = R1.take(shS); self.aSi = R1.take(shS); self.ldS = R1.take(shS)
        p.dma("sp", self.aSr, I["s5_a_re"].rearrange("(q two) p -> (two p) q", two=2), "d_prep", writes=["aSr"], **NCK)
        p.dma("sp", self.aSi, I["s5_a_im"].rearrange("(q two) p -> (two p) q", two=2), "d_prep", writes=["aSi"], **NCK)
        v = I["s5_log_dt"].rearrange("(q two) -> two q", two=2)
        for two in range(2):
            p.dma("sp", self.ldS[64 * two:64 * two + 64, :], v[two].partition_broadcast(64), "d_prep", writes=["ldS"], **NCK)
        self.dS5 = self.R0.take([128, 4])
        p.dma("sp", self.dS5, I["s5_d"].rearrange("(t p) -> p t", p=128), "d_prep", writes=["dS5"], **NCK)
        for sem, keys in (("d_Lp", ["Lp"]), ("d_Fp", ["Fp"]),
                          ("d_prep", ["Bnr", "Bni", "Cnr", "Cni", "aSr", "aSi", "ldS", "dS5"])):
            for k_ in keys:
                p.last_write[k_] = (sem, p.count[sem])

    def tr_group(self, srcs, evac):
        p = self.p
        for g0 in range(0, len(srcs), 4):
            grp = srcs[g0:g0 + 4]
            b = self.bank()
            bv = self.bk(b).rearrange("p (a c) -> p a c", a=4)
            for i, (ap, keys) in enumerate(grp):
                p.op("pe", lambda e, i=i, ap=ap, bv=bv: e.transpose(bv[:, i, :], ap, self.ident_f),
                     reads=list(keys) + ["ident_f"], writes=[("bank", b)], inc=(i == len(grp) - 1))
            evac(bv, g0, len(grp), b)

    def param_transposes(self):
        p = self.p
        CP = self.CP
        NJ = DFF // 128
        Lp, Fp, hsp = self.Lp, self.Fp, self.hsp

        def ev_L(bv, g0, n, b):
            CP("act", self.LRUc[:, g0:g0 + n, :], bv[:, 0:n, 0:72], [("bank", b)], ("LRUc", g0 // 4))
        self.tr_group([(Lp[:, 128 * j:128 * (j + 1)], ["Lp"]) for j in range(8)], ev_L)

        def ev_F(bv, g0, n, b):
            CP("dve", self.Fc[:, g0:g0 + n, :], bv[:, 0:n, 0:36], [("bank", b)], ("Fc", g0 // 4))
        self.tr_group([(Fp[:, 128 * j:128 * (j + 1)], ["Fp"]) for j in range(NJ)], ev_F)
        self.LRUc_keys = [("LRUc", i) for i in range(2)]
        self.Fc_keys = [("Fc", i) for i in range(NJ // 4)]
        for part in range(2):
            p.dma("sp", hsp[part][0:16, :], self.I[("st_s5_re", "st_s5_im")[part]], "d_hsp", writes=["hsp"])
            def ev_h(bv, g0, n, b, part=part):
                CP("act", self.h0S[:, part, g0:g0 + n, :], bv[:, 0:n, 0:16], [("bank", b)], ("h0S", part, g0 // 4))
            self.tr_group([(hsp[part][:, 128 * q:128 * (q + 1)], ["hsp"]) for q in range(16)], ev_h)
        self.h0S_keys = [("h0S", part, i) for part in range(2) for i in range(4)]
        sh3 = [128, 16, 32]
        self.CTr = self.R1.take(sh3); self.CTi = self.R1.take(sh3)
        for (src, dst, k_src, k_dst) in ((self.Cnr, self.CTr, "Cnr", "CTr"), (self.Cni, self.CTi, "Cni", "CTi")):
            def ev_c(bv, g0, n, b, dst=dst, k_dst=k_dst):
                CP("dve", dst[:, 4 * g0:4 * (g0 + n), :].rearrange("p (a q) c -> p a (q c)", a=n), bv[:, 0:n, :], [("bank", b)], (k_dst, g0))
            self.tr_group([(src[:, qh, :], [k_src]) for qh in range(4)], ev_c)
        self.CT_keys = {"CTr": [("CTr", 0)], "CTi": [("CTi", 0)]}

    def s5_prep(self):
        nc, p, I = self.nc, self.p, self.I
        R0, R1, RS = self.R0, self.R1, self.RS
        TT, TS, STT, ACT, CP, MS = self.TT, self.TS, self.STT, self.ACT, self.CP, self.MS
        W1, W4a, W4b = self.W1, self.W4a, self.W4b
        aSr, aSi, ldS = self.aSr, self.aSi, self.ldS
        CTr, CTi = self.CTr, self.CTi
        kCTr, kCTi = self.CT_keys["CTr"], self.CT_keys["CTi"]
        shS = [128, 16]
        sh3 = [128, 16, 32]
        I32 = mybir.dt.int32

        def sincos_base(theta, t, kf, A, C2, cs, sn, k_th, k_t, k_kf, k_A, k_C2, k_cs, k_sn):
            TS("dve", t, theta, 1.0 / (2 * PI), ALU.mult, [k_th], k_t, s2=16.0, op1=ALU.add)
            CP("dve", kf.bitcast(I32), t, [k_t], k_kf)
            CP("dve", kf, kf.bitcast(I32), [k_kf], k_kf)
            TT("dve", t, t, kf, ALU.subtract, [k_t, k_kf], k_t)
            ACT(A, t, AF.Sin, [k_t], k_A, scale=PI)
            ACT(C2, t, AF.Sin, [k_t], k_C2, scale=PI / 2)
            TT("dve", C2, C2, C2, ALU.mult, [k_C2], k_C2)
            TS("dve", C2, C2, -2.0, ALU.mult, [k_C2], k_C2, s2=1.0, op1=ALU.add)
            STT("dve", sn, A, 2.0, C2, ALU.mult, ALU.mult, [k_A, k_C2], k_sn)
            TT("dve", cs, A, A, ALU.mult, [k_A], k_cs)
            TS("dve", cs, cs, -2.0, ALU.mult, [k_cs], k_cs, s2=1.0, op1=ALU.add)

        def tS():
            return RS.take(shS)

        dtS = tS(); adtS = tS(); thS = tS()
        ACT(dtS, ldS, AF.Exp, ["ldS"], "dtS")
        TT("dve", adtS, aSr, dtS, ALU.mult, ["aSr", "dtS"], "adtS")
        TT("dve", thS, aSi, dtS, ALU.mult, ["aSi", "dtS"], "thS")
        self.rhoS = tS()
        ACT(self.rhoS, adtS, AF.Exp, ["adtS"], "rhoS")
        cosS = []; sinS = []; nsinS = []
        lamr = {}; lami = {}; lrotr = {}; lroti = {}; rpow = {}
        c1S = tS(); s1S = tS(); w1 = tS(); w2 = tS(); w3 = tS(); w4 = tS()
        sincos_base(thS, w1, w2, w3, w4, c1S, s1S, "thS", "Sw1", "Sw2", "Sw3", "Sw4", "S1cs", "S1sn")
        ua = w1; ub = w2
        for s in range(2 * LCH):
            if s == 0:
                cs = tS(); sn = tS()
                MS("dve", cs, 1.0, "S0cs"); MS("dve", sn, 0.0, "S0sn")
            elif s == 1:
                cs, sn = c1S, s1S
            else:
                cs = tS(); sn = tS()
                pc, ps_ = cosS[s - 1], sinS[s - 1]
                kpc, kps = "S%dcs" % (s - 1), "S%dsn" % (s - 1)
                TT("dve", ua, pc, c1S, ALU.mult, [kpc, "S1cs", "Sw1"], "Sw1")
                TT("dve", ub, ps_, s1S, ALU.mult, [kps, "S1sn", "Sw2"], "Sw2")
                TT("dve", cs, ua, ub, ALU.subtract, ["Sw1", "Sw2"], "S%dcs" % s)
                TT("dve", ua, ps_, c1S, ALU.mult, [kps, "S1cs"], "Sw1")
                TT("dve", ub, pc, s1S, ALU.mult, [kpc, "S1sn"], "Sw2")
                TT("dve", sn, ua, ub, ALU.add, ["Sw1", "Sw2"], "S%dsn" % s)
            cosS.append(cs); sinS.append(sn)
            if s <= LCH:
                ns = tS()
                TS("dve", ns, sn, -1.0, ALU.mult, ["S%dsn" % s], "nS%dsn" % s)
                nsinS.append(ns)
            if 1 <= s <= LCH:
                r = tS()
                ACT(r, adtS, AF.Exp, ["adtS"], "rpow%d" % s, scale=float(s))
                rpow[s] = r
                if s in (1, 4):
                    a = tS(); b_ = tS()
                    TT("dve", a, r, cs, ALU.mult, ["rpow%d" % s, "S%dcs" % s], "lamr%d" % s)
                    TT("dve", b_, r, sn, ALU.mult, ["rpow%d" % s, "S%dsn" % s], "lami%d" % s)
                    lamr[s] = a; lami[s] = b_
        for s in range(1, LCH + 1):
            a = tS(); b_ = tS()
            ks = s + LCH - 1
            TT("dve", a, rpow[s], cosS[ks], ALU.mult, ["rpow%d" % s, "S%dcs" % ks], "lrotr%d" % s)
            TT("dve", b_, rpow[s], sinS[ks], ALU.mult, ["rpow%d" % s, "S%dsn" % ks], "lroti%d" % s)
            lrotr[s] = a; lroti[s] = b_
        self.cosS, self.sinS, self.nsinS, self.lamr, self.lami = cosS, sinS, nsinS, lamr, lami
        self.nlami4 = tS()
        TS("dve", self.nlami4, lami[4], -1.0, ALU.mult, ["lami4"], "nlami4")
        ta = R1.take(sh3); tb = R1.take(sh3)

        def bc(t):
            return t.unsqueeze(2).to_broadcast(sh3)

        Bnr, Bni = self.Bnr, self.Bni
        nr = tS(); den = tS(); u1 = tS(); cfr = tS(); cfi = tS()
        TS("dve", nr, lamr[1], -1.0, ALU.add, ["lamr1"], "nr")
        TT("dve", den, aSr, aSr, ALU.mult, ["aSr"], "den")
        TT("dve", u1, aSi, aSi, ALU.mult, ["aSi"], "u1")
        TT("dve", den, den, u1, ALU.add, ["den", "u1"], "den")
        p.op("dve", lambda e: e.reciprocal(out=den, in_=den), reads=["den"], writes=["den"])
        TT("dve", cfr, nr, aSr, ALU.mult, ["nr", "aSr"], "cfr")
        TT("dve", u1, lami[1], aSi, ALU.mult, ["lami1", "aSi", "den"], "u1")
        TT("dve", cfr, cfr, u1, ALU.add, ["cfr", "u1"], "cfr")
        TT("dve", cfr, cfr, den, ALU.mult, ["cfr", "den"], "cfr")
        TT("dve", cfi, lami[1], aSr, ALU.mult, ["lami1", "aSr"], "cfi")
        TT("dve", u1, nr, aSi, ALU.mult, ["nr", "aSi", "cfr"], "u1")
        TT("dve", cfi, cfi, u1, ALU.subtract, ["cfi", "u1"], "cfi")
        TT("dve", cfi, cfi, den, ALU.mult, ["cfi", "den"], "cfi")
        BbR = R1.take(sh3); BbI = R1.take(sh3)
        tc_ = R1.take(sh3); td_ = R1.take(sh3)
        TT("pool", tc_, Bnr, bc(cfr), ALU.mult, ["Bnr", "cfr"], "w1tc")
        TT("pool", td_, Bni, bc(cfi), ALU.mult, ["Bni", "cfi"], "w1td")
        TT("pool", BbR, tc_, td_, ALU.subtract, ["w1tc", "w1td"], "BbR")
        TT("pool", tc_, Bni, bc(cfr), ALU.mult, ["Bni", "cfr"], "w1tc")
        TT("pool", td_, Bnr, bc(cfi), ALU.mult, ["Bnr", "cfi"], "w1td")
        TT("pool", BbI, tc_, td_, ALU.add, ["w1tc", "w1td"], "BbI")
        W1S = self.LF.bitcast(BF16).rearrange("p (a s b c) -> p a s b c", a=4, s=LCH, b=2)
        MS("pool", W1S[:, 0, 0, 0, 0:2], 0.0, "LFfree", r=["Lp", "Fp"])
        p.last_write["Lp"] = p.last_write["LFfree"]; p.last_write["Fp"] = p.last_write["LFfree"]
        v4 = lambda t: t.rearrange("p (a b) c -> p a (b c)", a=4)
        for s in range(LCH):
            kc, ks = "S%dcs" % s, "S%dsn" % s
            TT("pool", tc_, BbR, bc(cosS[s]), ALU.mult, ["BbR", kc], "w1tc")
            TT("pool", td_, BbI, bc(sinS[s]), ALU.mult, ["BbI", ks], "w1td")
            TT("pool", W1S[:, :, s, 0, :], v4(tc_), v4(td_), ALU.add, ["w1tc", "w1td", "LFfree"], ("W1S", s, 0))
            TT("pool", tc_, BbI, bc(cosS[s]), ALU.mult, ["BbI", kc], "w1tc")
            TT("pool", td_, BbR, bc(sinS[s]), ALU.mult, ["BbR", ks], "w1td")
            TT("pool", W1S[:, :, s, 1, :], v4(tc_), v4(td_), ALU.subtract, ["w1tc", "w1td", "LFfree"], ("W1S", s, 1))
        for qh in range(4):
            for h in range(2):
                b = self.bank()
                pt = self.bk(b).bitcast(BF16)
                n = 0
                for s in range(4 * h, 4 * h + 4):
                    for part in range(2):
                        p.op("pe", lambda e, pt=pt, n=n, qh=qh, s=s, part=part: e.transpose(
                            pt[:, 128 * n:128 * (n + 1)], W1S[:, qh, s, part, :], self.ident_b),
                            reads=[("W1S", s, part), "ident_b"], writes=[("bank", b)], inc=(n == 7))
                        n += 1
                CP("act", W1[:, qh, 4 * h:4 * h + 4, :, :].rearrange("p a b c -> p (a b c)"), pt, [("bank", b)], ("W1", qh, h))
        self.W1_keys = [("W1", qh, h) for qh in range(4) for h in range(2)]
        c7 = cosS[LCH - 1]; s7 = sinS[LCH - 1]
        kc7, ks7 = "S%dcs" % (LCH - 1), "S%dsn" % (LCH - 1)
        sh16 = [128, 16, 16]
        bc16 = lambda t: t.unsqueeze(2).to_broadcast(sh16)
        tav = ta[:, :, 0:16]; tbv = tb[:, :, 0:16]
        h0r = self.h0S[:, 0, :, :]; h0i = self.h0S[:, 1, :, :]
        TT("dve", tav, h0r, bc16(c7), ALU.mult, self.h0S_keys + [kc7, "w4ta"], "w4ta")
        TT("dve", tbv, h0i, bc16(s7), ALU.mult, self.h0S_keys + [ks7, "w4tb"], "w4tb")
        TT("dve", self.h0S_bf[:, 0, :, :], tav, tbv, ALU.add, ["w4ta", "w4tb"], "h0S_bf")
        TT("dve", tav, h0i, bc16(c7), ALU.mult, self.h0S_keys + [kc7], "w4ta")
        TT("dve", tbv, h0r, bc16(s7), ALU.mult, self.h0S_keys + [ks7], "w4tb")
        TT("dve", self.h0S_bf[:, 1, :, :], tav, tbv, ALU.subtract, ["w4ta", "w4tb", "h0S_bf"], "h0S_bf")
        for s in range(LCH):
            for (Wt, cr_t, ci_t, kr, ki, nm) in (
                (W4a, cosS[s], sinS[s], "S%dcs" % s, "S%dsn" % s, "W4a"),
                (W4b, lrotr[s + 1], lroti[s + 1], "lrotr%d" % (s + 1), "lroti%d" % (s + 1), "W4b"),
            ):
                TT("dve", ta, CTr, bc(cr_t), ALU.mult, kCTr + [kr], "w4ta")
                TT("dve", tb, CTi, bc(ci_t), ALU.mult, kCTi + [ki], "w4tb")
                TT("dve", Wt[:, :, s, 0, :], ta, tb, ALU.subtract, ["w4ta", "w4tb"], (nm, s, 0))
                TT("dve", ta, CTr, bc(ci_t), ALU.mult, kCTr + [ki], "w4ta")
                TT("dve", tb, CTi, bc(cr_t), ALU.mult, kCTi + [kr], "w4tb")
                TT("dve", ta, ta, tb, ALU.add, ["w4ta", "w4tb"], "w4ta")
                TS("dve", Wt[:, :, s, 1, :], ta, -1.0, ALU.mult, ["w4ta"], (nm, s, 1))
        self.W4_keys = [(nm, s, part) for nm in ("W4a", "W4b") for s in range(LCH) for part in range(2)]
        self.mur = RS.take([128, 16, NLEV]); self.mui = RS.take([128, 16, NLEV]); self.muni = RS.take([128, 16, NLEV])
        angc = RS.take([128, 16, NLEV]); angs = RS.take([128, 16, NLEV]); rmag = RS.take([128, 16, NLEV])
        CP("dve", angc[:, :, 0], cosS[LCH], ["S%dcs" % LCH], ("angc", 0))
        CP("dve", angs[:, :, 0], sinS[LCH], ["S%dsn" % LCH], ("angs", 0))
        sa = tS(); sb_ = tS()
        for j in range(NLEV):
            if j >= 1:
                TT("dve", sa, angc[:, :, j - 1], angc[:, :, j - 1], ALU.mult, [("angc", j - 1)], "sq_a")
                TT("dve", sb_, angs[:, :, j - 1], angs[:, :, j - 1], ALU.mult, [("angs", j - 1)], "sq_b")
                TT("dve", angc[:, :, j], sa, sb_, ALU.subtract, ["sq_a", "sq_b"], ("angc", j))
                TT("dve", sa, angc[:, :, j - 1], angs[:, :, j - 1], ALU.mult, [("angc", j - 1), ("angs", j - 1)], "sq_a")
                TS("dve", angs[:, :, j], sa, 2.0, ALU.mult, ["sq_a"], ("angs", j))
            ACT(rmag[:, :, j], adtS, AF.Exp, ["adtS"], ("rmag", j), scale=float(LCH * (1 << j)))
            TT("dve", self.mur[:, :, j], rmag[:, :, j], angc[:, :, j], ALU.mult, [("rmag", j), ("angc", j)], ("mur", j))
            TT("dve", self.mui[:, :, j], rmag[:, :, j], angs[:, :, j], ALU.mult, [("rmag", j), ("angs", j)], ("mui", j))
        for j in range(NLEV):
            TS("dve", self.muni[:, :, j], self.mui[:, :, j], -1.0, ALU.mult, [("mui", j)], ("muni", j))
        self.mu_keys = [(nm, j) for nm in ("mur", "mui", "muni") for j in range(NLEV)]
        self.rp8 = RS.take([128, 16, LCH]); self.rp4 = RS.take([128, 16, 4])
        CP("dve", self.rp8, self.rhoS.unsqueeze(2).to_broadcast([128, 16, LCH]), ["rhoS"], "rp8")
        MS("dve", self.rp8[:, :, 0:1], 0.0, "rp8", r=["rp8"])
        CP("dve", self.rp4, self.rhoS.unsqueeze(2).to_broadcast([128, 16, 4]), ["rhoS"], "rp4")
        MS("dve", self.rp4[:, :, 0:1], 0.0, "rp4", r=["rp4"])
    def s5_main(self, u_bf):
        nc, p = self.nc, self.p
        R0, R1 = self.R0, self.R1
        TT, TS, STT, ACT, CP, MS = self.TT, self.TS, self.STT, self.ACT, self.CP, self.MS
        W1, W4a, W4b = self.W1, self.W4a, self.W4b
        cosS, sinS, nsinS, lamr, lami = self.cosS, self.sinS, self.nsinS, self.lamr, self.lami
        z2 = self.z2
        NSET = 2
        pat = [R1.take([128, 512]) for i in range(NSET)]
        pats = [R1.take([128, 64]) for i in range(NSET)]
        gzf = [R1.take([128, 512]) for i in range(4)]
        gzfs = [R1.take([128, 2, 64]) for i in range(2)]
        gzb = [R1.take([128, 2, T], BF16) for i in range(NSET)]
        HA = [R1.take([128, 2, PAD + NCH]) for i in range(NSET)]
        HB = [R1.take([128, 2, PAD + NCH]) for i in range(NSET)]
        Hpb = [R1.take([128, 2, NCH], BF16) for i in range(NSET)]
        for i in range(NSET):
            MS("pool", HA[i], 0.0, ("HA", i))
            MS("pool", HB[i], 0.0, ("HB", i))
        yf = R1.take([128, 512]); gq = R1.take([128, 512]); ga = R1.take([128, 512]); gt = gq
        VB = [0, 1]
        nunit = [0]
        SVB = 2
        YB = {0: 4, 512: 5, 1024: 6, 1536: 7}
        PIECES = self.PIECES
        nseg = [0]

        def stageA(q):
            qh, ql = q // 4, q % 4
            hb = q % NSET
            kw = dict(tile_position=(96, 0)) if ql == 3 else {}
            p.op("act", lambda e: e.activation(out=pat[hb].rearrange("p (c s) -> p c s", s=LCH),
                                               in_=self.rp8[:, q:q + 1, :].to_broadcast([128, 512 // LCH, LCH]), func=AF.Copy),
                 reads=["rp8"], writes=[("pat", hb)])
            p.op("act", lambda e: e.activation(out=pats[hb].rearrange("p (c s) -> p c s", s=4),
                                               in_=self.rp4[:, q:q + 1, :].to_broadcast([128, 64 // 4, 4]), func=AF.Copy),
                 reads=["rp4"], writes=[("pats", hb)])
            for (c0, w) in PIECES:
                samp = (w == 64)
                L = 4 if samp else LCH
                sl = nseg[0] % 2
                nseg[0] += 1
                for part in range(2):
                    if samp:
                        vb = SVB
                        vv = self.bk(SVB)[:, 64 * part:64 * part + 64]
                    else:
                        vb = VB[nunit[0] % 2]
                        vv = self.bk(vb)
                    ug = nunit[0] % 4
                    nunit[0] += 1
                    for s in range(L):
                        p.op("pe", lambda e, vv=vv, s=s, part=part, qh=qh, ql=ql, c0=c0, w=w, L=L, kw=kw: e.matmul(
                            vv[:, s:w:L], W1[32 * ql:32 * ql + 32, qh, s, part, :],
                            u_bf[32 * ql:32 * ql + 32, qh, c0 + s:c0 + w:L], start=True, stop=True, **kw),
                            reads=self.W1_keys + [("u_bf", qh, c0)], writes=[("bank", vb)], inc=(s == L - 1))
                    if not samp:
                        go = gzf[ug]; gkey = ("gzf", ug)
                        p.op("dve", lambda e, go=go, hb=hb, vv=vv: e.tensor_tensor_scan(
                            out=go, data0=pat[hb], data1=vv, initial=0.0, op0=ALU.mult, op1=ALU.add),
                            reads=[("pat", hb), ("bank", vb)], writes=[gkey])
                        CP("act", gzb[hb][:, part, c0:c0 + w], go, [gkey], ("gzb", hb, c0, part))
                        k0 = PAD + c0 // LCH
                        nchunk = w // LCH
                        CP("pool", HA[hb][:, part, k0:k0 + nchunk], go[:, LCH - 1:512:LCH], [gkey], ("HA", hb))
                    else:
                        go = gzfs[sl][:, part, :]; gkey = ("gzfs", sl, part)
                        p.op("dve", lambda e, go=go, hb=hb, vv=vv: e.tensor_tensor_scan(
                            out=go, data0=pats[hb], data1=vv, initial=0.0, op0=ALU.mult, op1=ALU.add),
                            reads=[("pats", hb), ("bank", vb)], writes=[gkey])
                        CP("act", gzb[hb][:, part, c0:c0 + w], go, [gkey], ("gzb", hb, c0, part))
                if samp:
                    go = gzfs[sl]
                    gkeys = [("gzfs", sl, 0), ("gzfs", sl, 1)]
                    er = go[:, 0, 3:64:4]; ei = go[:, 1, 3:64:4]
                    c3 = cosS[3][:, q:q + 1]; s3 = sinS[3][:, q:q + 1]; ns3 = nsinS[3][:, q:q + 1]
                    l4r = lamr[4][:, q:q + 1]; l4i = lami[4][:, q:q + 1]; nl4i = self.nlami4[:, q:q + 1]
                    kk = gkeys + ["S3cs", "S3sn", "nS3sn", "lamr4", "lami4", "nlami4"] + self.h0S_keys
                    fr = self.Hfin[:, 0, q, 1:17]; fi = self.Hfin[:, 1, q, 1:17]
                    h0r = self.h0S[:, 0, q, :]; h0i = self.h0S[:, 1, q, :]
                    fk = ("Hfin", q)
                    TS("dve", fr, er, c3, ALU.mult, kk, fk)
                    STT("dve", fr, ei, ns3, fr, ALU.mult, ALU.add, kk + [fk], fk)
                    STT("dve", fr, h0r, l4r, fr, ALU.mult, ALU.add, kk + [fk], fk)
                    STT("dve", fr, h0i, nl4i, fr, ALU.mult, ALU.add, kk + [fk], fk)
                    TS("dve", fi, er, s3, ALU.mult, kk + [fk], fk)
                    STT("dve", fi, ei, c3, fi, ALU.mult, ALU.add, kk + [fk], fk)
                    STT("dve", fi, h0r, l4i, fi, ALU.mult, ALU.add, kk + [fk], fk)
                    STT("dve", fi, h0i, l4r, fi, ALU.mult, ALU.add, kk + [fk], fk)

        def stageB(q):
            hb = q % NSET
            src, dst = HA[hb], HB[hb]
            skey, dkey = ("HA", hb), ("HB", hb)
            for j in range(NLEV):
                d = 1 << j
                mr = self.mur[:, q, j:j + 1]; mi = self.mui[:, q, j:j + 1]; mni = self.muni[:, q, j:j + 1]
                mk = [("mur", j), ("mui", j), ("muni", j)]
                S0 = src[:, 0, PAD:PAD + NCH]; S1 = src[:, 1, PAD:PAD + NCH]
                Z0 = src[:, 0, PAD - d:PAD + NCH - d]; Z1 = src[:, 1, PAD - d:PAD + NCH - d]
                D0 = dst[:, 0, PAD:PAD + NCH]; D1 = dst[:, 1, PAD:PAD + NCH]
                STT("dve", D0, Z1, mni, S0, ALU.mult, ALU.add, [skey] + mk, dkey)
                STT("dve", D1, Z0, mi, S1, ALU.mult, ALU.add, [skey, dkey] + mk, dkey)
                Zb = src[:, :, PAD - d:PAD + NCH - d]; Db = dst[:, :, PAD:PAD + NCH]
                STT("dve", Db, Zb, mr, Db, ALU.mult, ALU.add, [skey, dkey] + mk, dkey)
                src, dst = dst, src
                skey, dkey = dkey, skey
            CP("act", Hpb[hb], HA[hb][:, :, PAD - 1:PAD - 1 + NCH], [("HA", hb)], ("Hpb", hb))
            hr_ = HA[hb][:, 0, PAD + NCH - 1:PAD + NCH]; hi_ = HA[hb][:, 1, PAD + NCH - 1:PAD + NCH]
            c7 = cosS[LCH - 1][:, q:q + 1]; s7 = sinS[LCH - 1][:, q:q + 1]; ns7 = nsinS[LCH - 1][:, q:q + 1]
            kk7 = [("HA", hb), "S%dcs" % (LCH - 1), "S%dsn" % (LCH - 1), "nS%dsn" % (LCH - 1)]
            fr0 = self.Hfin[:, 0, q, 0:1]; fi0 = self.Hfin[:, 1, q, 0:1]
            TS("dve", fr0, hr_, c7, ALU.mult, kk7, ("Hfin0", q))
            STT("dve", fr0, hi_, ns7, fr0, ALU.mult, ALU.add, kk7 + [("Hfin0", q)], ("Hfin0", q))
            TS("dve", fi0, hr_, s7, ALU.mult, kk7 + [("Hfin0", q)], ("Hfin0", q))
            STT("dve", fi0, hi_, c7, fi0, ALU.mult, ALU.add, kk7 + [("Hfin0", q)], ("Hfin0", q))

        def stageC(q):
            qh, ql = q // 4, q % 4
            hb = q % NSET
            kw = dict(tile_position=(0, 96)) if ql == 3 else {}
            for (c0, w) in PIECES:
                samp = (w == 64)
                L = 4 if samp else LCH
                if samp:
                    ybank = SVB
                    yall = self.bk(SVB)[:, 128:192]
                else:
                    ybank = YB[c0]
                    yall = self.bk(ybank)
                for s in range(L):
                    outv = yall[32 * ql:32 * ql + 32, s:w:L]
                    if samp:
                        hr = self.h0S_bf[:, 0, q, :]; hi = self.h0S_bf[:, 1, q, :]
                        hkeys = ["h0S_bf"]
                    else:
                        k0 = c0 // LCH
                        hr = Hpb[hb][:, 0, k0:k0 + w // L]; hi = Hpb[hb][:, 1, k0:k0 + w // L]
                        hkeys = [("Hpb", hb)]
                    ops = [
                        (W4a[:, q, s, 0, :], gzb[hb][:, 0, c0 + s:c0 + w:L]),
                        (W4a[:, q, s, 1, :], gzb[hb][:, 1, c0 + s:c0 + w:L]),
                        (W4b[:, q, s, 0, :], hr),
                        (W4b[:, q, s, 1, :], hi),
                    ]
                    for i, (lh, rh) in enumerate(ops):
                        last = (i == 3 and s == L - 1)
                        p.op("pe", lambda e, outv=outv, lh=lh, rh=rh, i=i, kw=kw: e.matmul(
                            outv, lh, rh, start=(i == 0), stop=(i == 3), **kw),
                            reads=self.W4_keys + [("gzb", hb, c0, 0), ("gzb", hb, c0, 1)] + hkeys, writes=[("bank", ybank)], inc=last)

        def stageY(qh):
            for (c0, w) in PIECES:
                samp = (w == 64)
                if samp:
                    ybank = SVB; ysrc = self.bk(SVB)[:, 128:192]
                else:
                    ybank = YB[c0]; ysrc = self.bk(ybank)
                yv = yf[:, 0:w]; qv = gq[:, 0:w]; av = ga[:, 0:w]; tv = gt[:, 0:w]
                STT("dve", yv, u_bf[:, qh, c0:c0 + w], self.dS5[:, qh:qh + 1], ysrc, ALU.mult, ALU.add,
                    [("u_bf", qh, c0), "dS5", ("bank", ybank)], "s5yf")
                if qh == 0:
                    self.tap("ys5_%d" % c0, yv, ["s5yf"])
                ACT(qv, yv, AF.Square, ["s5yf"], "s5gq")
                ACT(qv, qv, AF.Identity, ["s5gq"], "s5gq", scale=GC, bias=1.0)
                TT("pool", av, qv, yv, ALU.mult, ["s5gq", "s5yf"], "s5ga")
                ACT(qv, av, AF.Tanh, ["s5ga"], "s5gq", scale=GK)
                STT("dve", z2[:, qh, c0:c0 + w], qv, 1.0, yv, ALU.add, ALU.mult, ["s5gq", "s5yf"], ("z2", qh, c0))

        for qh in range(4):
            qs = [4 * qh + i for i in range(4)]
            stageA(qs[0]); stageB(qs[0])
            for i in range(1, 4):
                stageA(qs[i])
                stageC(qs[i - 1])
                stageB(qs[i])
            stageC(qs[3])
            stageY(qh)

    def gelu2(self, yv, qv, av, tv, outv, ky, kq, ka, kt, kout):
        self.ACT(qv, yv, AF.Square, [ky], kq)
        self.STT("dve", av, qv, GK * GC, yv, ALU.mult, ALU.mult, [kq, ky], ka)
        self.STT("dve", av, yv, GK, av, ALU.mult, ALU.add, [ky, ka], ka)
        self.ACT(tv, av, AF.Tanh, [ka], kt)
        self.STT("dve", outv, tv, 1.0, yv, ALU.add, ALU.mult, [kt, ky], kout)

    def lru_stage(self):
        nc, p, I, O = self.nc, self.p, self.I, self.O
        R0, R1 = self.R0, self.R1
        TT, TS, STT, ACT, CP, MS = self.TT, self.TS, self.STT, self.ACT, self.CP, self.MS
        NCK = dict(allow_slow_non_contiguous=True)
        xT, z2 = self.xT, self.z2
        PIECES = self.PIECES
        self.free_banks = list(range(8))
        LRUc = self.LRUc
        LK = self.LRUc_keys
        baT = R0.take([128, 8]); bxT = R0.take([128, 8]); sc8 = R0.take([128, 8]); hsc8 = R0.take([128, 8])
        TS("dve", baT, LRUc[:, :, 69], 0.5, ALU.mult, LK, "baT")
        TS("dve", bxT, LRUc[:, :, 70], 0.5, ALU.mult, LK, "bxT")
        ACT(sc8, LRUc[:, :, 71], AF.Exp, LK, "sc8", scale=-1.0)
        ACT(sc8, sc8, AF.Ln, ["sc8"], "sc8", bias=1.0)
        TS("dve", hsc8, sc8, -4.0, ALU.mult, ["sc8"], "hsc8")
        TS("dve", sc8, sc8, -8.0, ALU.mult, ["sc8", "hsc8"], "sc8")
        Wg = R0.take([128, 8, 2, 128], BF16)
        MS("pool", Wg, 0.0, "Wg")
        for gi, nm in ((0, "lru_wa"), (1, "lru_wx")):
            v = I[nm].rearrange("(j two) i o -> two i j o", two=2)
            for par in range(2):
                p.dma("pool", Wg[64 * par:64 * par + 64, :, gi, 64 * par:64 * par + 64], v[par], "d_wg", writes=["Wg"])
        p.last_write["Wg"] = ("d_wg", p.count["d_wg"])
        hfinL = R0.take([128, 8, 17])
        self.merged2 = R1.take([128, 8, T], BF16)
        merged2 = self.merged2
        self.merged_end = R1.pos
        xl_sb = R1.take([128, 3 + TP]); xs_sb = R1.take([128, 16, 7])
        abuf = R1.take([128, T]); a2buf = R1.take([128, T]); ixbuf = R1.take([128, T])
        hbuf = [R1.take([128, T])]
        tail_sb = R1.take([NTAIL, D])
        MS("pool", xl_sb[:, 0:3], 0.0, ("xl", -1))
        NP = 2
        ytmp = [R1.take([128, 512]) for i in range(2)]
        NXC = 5
        xc = [R1.take([128, 512]) for i in range(NXC)]
        xcb = [R1.take([128, 512], BF16) for i in range(2)]
        rp_ = [R1.take([128, 512]) for i in range(2)]
        ip_ = [R1.take([128, 512]) for i in range(2)]
        glp = [R1.take([128, 512]) for i in range(2)]
        gsp = [R1.take([128, 512]) for i in range(2)]
        gbp = [R1.take([128, 512]) for i in range(2)]
        t1p = [R1.take([128, 512]) for i in range(2)]
        t16 = R1.take([128, 16])
        wsl = [R1.take([128, 8, 128], BF16) for i in range(2)]
        wsg = [R1.take([128, 8, 2, 128], BF16) for i in range(2)]
        wgl = [R1.take([128, 4, 2, 128], BF16) for i in range(2)]
        hb = hbuf[0]
        XB = [0, 1]; GAB = [2, 3]; GXB = [4, 5]; POSTB = [6, 7]
        allp = lambda nm: [(nm, pi) for pi in range(len(PIECES))]

        def load_pre(j):
            sl = j % 2
            p.dma("pool", wsl[sl], I["w_in"][:, 128 * j:128 * (j + 1)].rearrange("(k p) n -> p k n", p=128), "d_wsl%d" % sl,
                  writes=[("wsl", sl)])

        def load_post(j):
            sl = j % 2
            p.dma("pool", wsg[sl].rearrange("p k g n -> p k (g n)"),
                  I["w_in"][:, 1536 + 256 * j:1536 + 256 * (j + 1)].rearrange("(k p) n -> p k n", p=128), "d_wsg%d" % sl,
                  writes=[("wsg", sl, 0), ("wsg", sl, 1)])
            p.dma("pool", wgl[sl].rearrange("p k g n -> p k (g n)"),
                  I["w_glu"][:, 256 * j:256 * (j + 1)].rearrange("(k p) n -> p k n", p=128), "d_wgl%d" % sl,
                  writes=[("wgl", sl, 0), ("wgl", sl, 1)])

        items = [dict(j=j, pi=pi, c0=c0, w=w) for j in range(8) for pi, (c0, w) in enumerate(PIECES)]

        def P0(t, it):
            j, c0, w = it["j"], it["c0"], it["w"]
            sl = j % 2
            if it["pi"] == 0 and j + 1 < 8:
                load_pre(j + 1)
            b = XB[t % 2]
            for k in range(8):
                p.op("pe", lambda e, k=k, b=b, sl=sl, c0=c0, w=w: e.matmul(
                    self.bk(b)[:, 0:w], wsl[sl][:, k, :], xT[:, k, c0:c0 + w], start=(k == 0), stop=(k == 7)),
                    reads=[("wsl", sl)] + self.xT_keys(c0, w), writes=[("bank", b)], inc=(k == 7))
            if it["pi"] == len(PIECES) - 1:
                tb = POSTB[1]
                for k in range(8):
                    p.op("pe", lambda e, k=k, tb=tb, sl=sl: e.matmul(self.bk(tb)[0:NTAIL, 0:128], xT[:, k, TAIL0:T], wsl[sl][:, k, :],
                                                                     start=(k == 0), stop=(k == 7)),
                         reads=[("wsl", sl)] + self.xT_keys(TAIL0, NTAIL), writes=[("bank", tb)], inc=(k == 7))
                CP("act", tail_sb[:, 128 * j:128 * (j + 1)], self.bk(tb)[0:NTAIL, 0:128], [("bank", tb)], ("tail", j))

        def P1(t, it):
            j, pi, c0, w = it["j"], it["pi"], it["c0"], it["w"]
            b = XB[t % 2]
            ps = self.bk(b)[:, 0:w]
            yv = ytmp[t % 2][:, 0:w]
            ACT(yv, ps, AF.Identity, [("bank", b)] + LK, ("ytmp", t % 2), scale=LRUc[:, j, 67:68], bias=LRUc[:, j, 68:69])
            if w != 64:
                CP("act", xl_sb[:, 3 + c0:3 + c0 + w], ps, [("bank", b)], ("xl", pi))
            else:
                CP("pool", xs_sb[:, :, 0:3], LRUc[:, j, 0:48].rearrange("p (b k) -> p b k", k=3), LK, ("xs", "st"))
                CP("act", xs_sb[:, :, 3:7], ps.rearrange("p (b s) -> p b s", s=4), [("bank", b)], ("xs", "new"))

        def P2(t, it):
            j, pi, c0, w = it["j"], it["pi"], it["c0"], it["w"]
            cw = [LRUc[:, j, 64 + k:65 + k] for k in range(4)]
            yk = ("ytmp", t % 2); xk_ = ("xc", t % NXC)
            yv = ytmp[t % 2][:, 0:w]; xcv = xc[t % NXC][:, 0:w]
            if w != 64:
                xk = [("xl", pi), ("xl", pi - 1), yk] + LK
                STT("dve", yv, xl_sb[:, c0 + 2:c0 + 2 + w], cw[2], yv, ALU.mult, ALU.add, xk, yk)
                STT("dve", yv, xl_sb[:, c0 + 1:c0 + 1 + w], cw[1], yv, ALU.mult, ALU.add, xk, yk)
                STT("dve", xcv, xl_sb[:, c0:c0 + w], cw[0], yv, ALU.mult, ALU.add, xk, xk_)
            else:
                xk = [("xs", "st"), ("xs", "new"), yk] + LK
                y3 = yv.rearrange("p (b s) -> p b s", s=4); xc3 = xcv.rearrange("p (b s) -> p b s", s=4)
                STT("dve", y3, xs_sb[:, :, 2:6], cw[2], y3, ALU.mult, ALU.add, xk, yk)
                STT("dve", y3, xs_sb[:, :, 1:5], cw[1], y3, ALU.mult, ALU.add, xk, yk)
                STT("dve", xc3, xs_sb[:, :, 0:4], cw[0], y3, ALU.mult, ALU.add, xk, xk_)
            if j == 0:
                self.tap("xc_%d" % c0, xcv, [xk_])
            CP("pool", xcb[t % 2][:, 0:w], xcv, [xk_], ("xcb", t % 2))

        def P3(t, it):
            j, w = it["j"], it["w"]
            for (bb, gi) in ((GAB[t % 2], 0), (GXB[t % 2], 1)):
                p.op("pe", lambda e, bb=bb, gi=gi, j=j, t=t, w=w: e.matmul(self.bk(bb)[:, 0:w], Wg[:, j, gi, :], xcb[t % 2][:, 0:w],
                                                                          start=True, stop=True),
                     reads=["Wg", ("xcb", t % 2)], writes=[("bank", bb)])

        def P4(t, it):
            j, w = it["j"], it["w"]
            ACT(rp_[t % 2][:, 0:w], self.bk(GAB[t % 2])[:, 0:w], AF.Tanh, [("bank", GAB[t % 2]), "baT"], ("rp", t % 2),
                scale=0.5, bias=baT[:, j:j + 1])
            ACT(ip_[t % 2][:, 0:w], self.bk(GXB[t % 2])[:, 0:w], AF.Tanh, [("bank", GXB[t % 2]), "bxT"], ("ip", t % 2),
                scale=0.5, bias=bxT[:, j:j + 1])

        def P5(t, it):
            j, pi, c0, w = it["j"], it["pi"], it["c0"], it["w"]
            rv = rp_[t % 2][:, 0:w]; iv = ip_[t % 2][:, 0:w]; xcv = xc[t % NXC][:, 0:w]
            ACT(abuf[:, c0:c0 + w], rv, AF.Exp, [("rp", t % 2), "hsc8"], ("abuf", pi), scale=hsc8[:, j:j + 1], bias=hsc8[:, j:j + 1])
            TT("pool", a2buf[:, c0:c0 + w], abuf[:, c0:c0 + w], abuf[:, c0:c0 + w], ALU.mult, [("abuf", pi)], ("a2buf", pi))
            STT("dve", ixbuf[:, c0:c0 + w], iv, 1.0, xcv, ALU.add, ALU.mult, [("ip", t % 2), ("xc", t % NXC)], ("ixbuf", pi))
            if pi == len(PIECES) - 1:
                mid_tile(j)

        post_queue = []

        def mid_tile(j):
            sl = j % 2
            ACT(a2buf, a2buf, AF.Sqrt, allp("a2buf"), "mh", scale=-0.25, bias=0.25)
            TT("dve", ixbuf, a2buf, ixbuf, ALU.mult, ["mh"] + allp("ixbuf"), "bterm")
            TT("dve", t16, abuf[:, TP:T:4], LRUc[:, j, 48:64], ALU.mult, allp("abuf") + LK, "t16")
            TT("dve", ixbuf[:, TP:T:4], ixbuf[:, TP:T:4], t16, ALU.add, ["bterm", "t16"], "bterm")
            MS("dve", abuf[:, TP:T:4], 0.0, "afix", r=allp("abuf") + allp("a2buf") + ["t16"])
            p.op("dve", lambda e: e.tensor_tensor_scan(out=hb, data0=abuf, data1=ixbuf, initial=0.0, op0=ALU.mult, op1=ALU.add),
                 reads=allp("abuf") + ["afix", "bterm"], writes=["hbuf"])
            for nm in ("abuf", "a2buf", "ixbuf"):
                for pi in range(len(PIECES)):
                    p.readers.setdefault((nm, pi), []).append(p.last_write["hbuf"])
            if j == 0:
                self.tap("hbuf", hb, ["hbuf"])
            CP("pool", hfinL[:, j, 0:1], hb[:, TP - 1:TP], ["hbuf"], ("hfinL", j, 0))
            CP("pool", hfinL[:, j, 1:17], hb[:, TP + 3:T:4], ["hbuf"], ("hfinL", j, 1))
            for pi, (c0, w) in enumerate(PIECES):
                post_queue.append((j, pi, c0, w))

        npq = [0]

        def post_piece(j, pi, c0, w):
            sl = j % 2
            i2 = npq[0] % 2
            npq[0] += 1
            glv = glp[i2][:, 0:w]; gsv = gsp[i2][:, 0:w]; gbv = gbp[i2][:, 0:w]; t1v = t1p[i2][:, 0:w]
            groups = [("gl", wsg[sl], 0, 8, xT, ("wsg", sl, 0)), ("gs", wsg[sl], 1, 8, xT, ("wsg", sl, 1)),
                      ("gb", wgl[sl], 1, 4, z2, ("wgl", sl, 1)), ("ga", wgl[sl], 0, 4, z2, ("wgl", sl, 0))]
            for gi_, (nm, wt, idx, nk, src, wkey) in enumerate(groups):
                bb = POSTB[gi_ % 2]
                for k in range(nk):
                    rk = self.xT_keys(c0, w) if src is xT else [("z2", k, c0)]
                    p.op("pe", lambda e, k=k, bb=bb, wt=wt, idx=idx, src=src, nk=nk, c0=c0, w=w: e.matmul(
                        self.bk(bb)[:, 0:w], wt[:, k, idx, :], src[:, k, c0:c0 + w], start=(k == 0), stop=(k == nk - 1)),
                        reads=[wkey] + rk, writes=[("bank", bb)], inc=(k == nk - 1))
                psv = self.bk(bb)[:, 0:w]
                if nm == "gl":
                    ACT(glv, psv, AF.Tanh, [("bank", bb)], ("glp", i2), scale=0.5)
                elif nm == "gs":
                    ACT(gsv, psv, AF.Tanh, [("bank", bb)], ("gsp", i2), scale=0.5)
                elif nm == "gb":
                    ACT(gbv, psv, AF.Tanh, [("bank", bb)], ("gbp", i2), scale=0.25)
                else:
                    CP("act", t1v, psv, [("bank", bb)], ("t1p", i2))
            STT("dve", t1v, gbv, 1.0, t1v, ALU.add, ALU.mult, [("gbp", i2), ("t1p", i2)], ("t1p", i2))
            STT("dve", t1v, gsv, 1.0, t1v, ALU.add, ALU.mult, [("gsp", i2), ("t1p", i2)], ("t1p", i2))
            STT("dve", glv, glv, 1.0, hb[:, c0:c0 + w], ALU.add, ALU.mult, [("glp", i2), "hbuf"], ("glp", i2))
            STT("dve", merged2[:, j, c0:c0 + w], t1v, 0.25, glv, ALU.mult, ALU.add, [("t1p", i2), ("glp", i2)], ("merged2", j, c0))
            if pi == len(PIECES) - 1 and j + 2 < 8:
                load_post(j + 2)

        load_pre(0); load_post(0); load_post(1)
        N_ = len(items)
        stages = [(P5, 5), (P4, 4), (P1, 1), (P2, 2), (P3, 3), (P0, 0)]
        for t in range(N_ + 5):
            for fn, lag in stages:
                if 0 <= t - lag < N_:
                    fn(t - lag, items[t - lag])
            if post_queue:
                post_piece(*post_queue.pop(0))
        while post_queue:
            post_piece(*post_queue.pop(0))
        self.tap("merged2", merged2[:, 0, :], [("merged2", 0, c0) for c0, _ in PIECES])
        self.out_dma(O["lru_conv"], tail_sb, [("tail", j) for j in range(8)])
        self.out_dma(O["lru_h"], hfinL, [("hfinL", j, i) for j in range(8) for i in range(2)])

    def ln_tile(self, ps_flat, res, gB, bB, ytok, outv, rows, kps, kres, kg, kb, ky, kout, st6, mv, sd):
        p = self.p
        TT, TS, STT, ACT, CP = self.TT, self.TS, self.STT, self.ACT, self.CP
        yv = ytok[0:rows, :]
        p.op("act", lambda e: e.activation(out=yv, in_=ps_flat[0:rows, :], func=AF.Copy, scale=0.5), reads=kps, writes=[ky])
        STT("dve", yv, res[0:rows, :], ALPHA, yv, ALU.mult, ALU.add, [kres, ky], ky)
        for h in range(2):
            p.op("dve", lambda e, h=h: e.bn_stats(out=st6[0:rows, h, :], in_=yv[:, 512 * h:512 * (h + 1)]), reads=[ky], writes=[ky + ("st", h)])
        p.op("dve", lambda e: e.bn_aggr(out=mv[0:rows, :], in_=st6[0:rows, :, :].rearrange("p a b -> p (a b)")),
             reads=[ky + ("st", 0), ky + ("st", 1)], writes=[ky + ("mv",)])
        ACT(sd[0:rows, :], mv[0:rows, 1:2], AF.Sqrt, [ky + ("mv",)], ky + ("sd",), bias=LN_EPS)
        p.op("dve", lambda e: e.reciprocal(out=sd[0:rows, :], in_=sd[0:rows, :]), reads=[ky + ("sd",)], writes=[ky + ("sd",)])
        TS("dve", yv, yv, mv[0:rows, 0:1], ALU.subtract, [ky, ky + ("mv",), ky + ("sd",)], ky, s2=sd[0:rows, 0:1], op1=ALU.mult)
        TT("pool", yv, yv, gB[0:rows, :], ALU.mult, [ky, kg], ky)
        TT("dve", outv[0:rows, :], yv, bB[0:rows, :], ALU.add, [ky, kb], kout)

    def mix_stage(self):
        nc, p, I, O = self.nc, self.p, self.I, self.O
        R0, R1 = self.R0, self.R1
        TT, TS, STT, ACT, CP, MS = self.TT, self.TS, self.STT, self.ACT, self.CP, self.MS
        merged2 = self.merged2
        x1T = self.xT
        self.x1T = x1T
        lnc = self.lnc
        for i, nm in enumerate(("ln1_g", "ln1_b", "ln2_g", "ln2_b")):
            p.dma("sp", lnc[:, i, :], I[nm].partition_broadcast(128), "d_lnc%d" % i, writes=[("lnc", i)])
        wout = R1.take([128, 8, D], BF16)
        for h in range(2):
            p.dma("pool", wout[:, :, 512 * h:512 * (h + 1)], I["w_out"][:, 512 * h:512 * (h + 1)].rearrange("(k p) n -> p k n", p=128),
                  "d_wout%d" % h, writes=[("wout", h)])
        NX, NYT, NX1, NXB, NS_ = 3, 4, 3, 2, 4
        xtok = [R1.take([128, D]) for i in range(NX)]
        ytok = [R1.take([128, D]) for i in range(NYT)]
        x1tok = [R1.take([128, D]) for i in range(NX1)]
        x1b = [R1.take([128, D], BF16) for i in range(NXB)]
        st6 = [R1.take([128, 2, 6]) for i in range(NS_)]
        mv = [R1.take([128, 2]) for i in range(NS_)]
        sd = [R1.take([128, 1]) for i in range(NS_)]
        ntt = (T + 127) // 128
        TB = [4, 5, 6, 7]
        g1, b1 = lnc[:, 0, :], lnc[:, 1, :]
        items = [dict(tt=tt, r0=tt * 128, rows=min(128, T - tt * 128)) for tt in range(ntt)]

        def M0(t, it):
            r0, rows = it["r0"], it["rows"]
            s = t % NX; pp = t % 2
            p.dma("sp", xtok[s][0:rows, :], I["x"][r0:r0 + rows, :], "d_xtok%d" % s, writes=[("xtok", s)])
            for h in range(2):
                for k in range(8):
                    p.op("pe", lambda e, k=k, h=h, pp=pp, r0=r0, rows=rows: e.matmul(
                        self.ps[pp][0:rows, h, :], merged2[:, k, r0:r0 + rows], wout[:, k, 512 * h:512 * (h + 1)],
                        start=(k == 0), stop=(k == 7)),
                        reads=[("wout", h)] + [("merged2", k, c0) for (c0, w) in self.PIECES if c0 <= r0 < c0 + w],
                        writes=[("bank", 2 * pp + h)], inc=(k == 7))

        def M1(t, it):
            rows = it["rows"]
            pp = t % 2; sy = t % NYT; sx = t % NX; ss = t % NS_
            psf = self.ps[pp][:].rearrange("p a c -> p (a c)")
            yv = ytok[sy][0:rows, :]
            ky = ("ytok", sy)
            p.op("act", lambda e: e.activation(out=yv, in_=psf[0:rows, :], func=AF.Copy, scale=0.5),
                 reads=[("bank", 2 * pp), ("bank", 2 * pp + 1)], writes=[ky])
            STT("dve", yv, xtok[sx][0:rows, :], ALPHA, yv, ALU.mult, ALU.add, [("xtok", sx), ky], ky)
            for h in range(2):
                p.op("dve", lambda e, h=h: e.bn_stats(out=st6[ss][0:rows, h, :], in_=yv[:, 512 * h:512 * (h + 1)]),
                     reads=[ky], writes=[("st6", ss, h)])
            p.op("dve", lambda e: e.bn_aggr(out=mv[ss][0:rows, :], in_=st6[ss][0:rows, :, :].rearrange("p a b -> p (a b)")),
                 reads=[("st6", ss, 0), ("st6", ss, 1)], writes=[("mv", ss)])

        def M2(t, it):
            rows = it["rows"]
            sy = t % NYT; ss = t % NS_
            yv = ytok[sy][0:rows, :]
            ky = ("ytok", sy)
            ACT(sd[ss][0:rows, :], mv[ss][0:rows, 1:2], AF.Sqrt, [("mv", ss)], ("sd", ss), bias=LN_EPS)
            p.op("dve", lambda e: e.reciprocal(out=sd[ss][0:rows, :], in_=sd[ss][0:rows, :]), reads=[("sd", ss)], writes=[("sd", ss)])
            TS("dve", yv, yv, mv[ss][0:rows, 0:1], ALU.subtract, [ky, ("mv", ss), ("sd", ss)], ky, s2=sd[ss][0:rows, 0:1], op1=ALU.mult)

        def M3a(t, it):
            tt, r0, rows = it["tt"], it["r0"], it["rows"]
            sy = t % NYT; s1 = t % NX1
            yv = ytok[sy][0:rows, :]
            ky = ("ytok", sy)
            TT("pool", yv, yv, g1[0:rows, :], ALU.mult, [ky, ("lnc", 0)], ky)
            TT("dve", x1tok[s1][0:rows, :], yv, b1[0:rows, :], ALU.add, [ky, ("lnc", 1)], ("x1tok", s1))
            if tt == 0:
                self.tap("x1tok", x1tok[s1], [("x1tok", s1)])
            p.dma("sp", self.x1_scr[r0:r0 + rows, :], x1tok[s1][0:rows, :], "d_x1w%d" % s1, reads=[("x1tok", s1)], writes=[("x1scr", tt)])

        def M3b(t, it):
            rows = it["rows"]
            s1 = t % NX1; sb = t % NXB
            CP("act", x1b[sb][0:rows, :], x1tok[s1][0:rows, :], [("x1tok", s1)], ("x1b", sb))

        def M4(t, it):
            rows = it["rows"]
            sb = t % NXB
            b = TB[t % 4]
            it["b"] = b
            pt = self.bk(b).bitcast(BF16)
            for k in range(8):
                p.op("pe", lambda e, k=k, pt=pt, sb=sb, rows=rows: e.transpose(
                    pt[:, k * 128:k * 128 + rows], x1b[sb][0:rows, k * 128:(k + 1) * 128], self.ident_b[0:rows, 0:rows]),
                    reads=[("x1b", sb), "ident_b"], writes=[("bank", b)], inc=(k == 7))

        def M5(t, it):
            tt, r0, rows, b = it["tt"], it["r0"], it["rows"], it["b"]
            pt = self.bk(b).bitcast(BF16)
            src = pt.rearrange("p (k c) -> p k c", c=128)[:, :, 0:rows]
            CP("act", x1T[:, :, r0:r0 + rows], src, [("bank", b)], ("x1T", tt))

        stages = [(M3a, 3), (M1, 1), (M2, 2), (M3b, 3), (M5, 5), (M4, 4), (M0, 0)]
        N_ = len(items)
        for t in range(N_ + 5):
            for fn, lag in stages:
                if 0 <= t - lag < N_:
                    fn(t - lag, items[t - lag])
        self.tap("x1T", x1T[:, 0, :], [("x1T", tt) for tt in range(ntt)])

    def x1T_keys(self, c0, w):
        return [("x1T", tt) for tt in range(c0 // 128, (c0 + w - 1) // 128 + 1)]

    def ffn_stage(self):
        nc, p, I, O = self.nc, self.p, self.I, self.O
        R0, R1 = self.R0, self.R1
        TT, TS, STT, ACT, CP, MS = self.TT, self.TS, self.STT, self.ACT, self.CP, self.MS
        NCK = dict(allow_slow_non_contiguous=True)
        x1T, lnc = self.x1T, self.lnc
        NJ = DFF // 128
        QW = 576
        Fc = self.Fc
        FK = self.Fc_keys
        wdn = R1.take([128, NJ, D], BF16)
        for c in range(6):
            p.dma("pool", wdn[:, 4 * c:4 * c + 4, :], I["w_down"][512 * c:512 * (c + 1), :].rearrange("(k p) n -> p k n", p=128),
                  "d_wdn%d" % c, writes=[("wdn", c)])
        Gq = R1.take([128, NJ, QW], BF16)
        NSL = 4
        wup = [R1.take([128, 8, 2, 128], BF16) for i in range(NSL)]
        halo = R1.take([128, NJ, 2])
        MS("pool", halo, 0.0, "halo_init")
        NY, NQ, NG = 5, 3, 2
        a_sb = [R1.take([128, 2 + 512]) for i in range(2)]
        as_sb = R1.take([128, 16, 6])
        y0 = [R1.take([128, 512]) for i in range(NY)]
        qq = [R1.take([128, 512]) for i in range(NQ)]
        ag = [R1.take([128, 512]) for i in range(NG)]
        tailf = [R1.take([NTAIL, 512]) for i in range(2)]
        NB = 2
        x1tok = [R1.take([128, D]) for i in range(NB)]
        ytok = [R1.take([128, D]) for i in range(NB)]
        otok = ytok
        st6 = [R1.take([128, 2, 6]) for i in range(NB)]
        mv = [R1.take([128, 2]) for i in range(NB)]
        sd = [R1.take([128, 1]) for i in range(NB)]
        nld = [0]

        def load_wup(j):
            sl = nld[0] % NSL
            nld[0] += 1
            p.dma("pool", wup[sl].rearrange("p k g n -> p k (g n)"),
                  I["w_up"][:, 256 * j:256 * (j + 1)].rearrange("(k p) n -> p k n", p=128), "d_wup%d" % sl,
                  writes=[("wup", sl, 0), ("wup", sl, 1)])
            return sl

        npc = [0]
        ntile = [0]
        for n in range(4):
            q0 = 512 * n
            pieces = [(q0, 512, 0)] + ([(TP, NS, 512)] if n == 3 else [])
            RA = [0, 1]
            RG = [2, 3, 4, 5, 6]
            TAILB = 7
            items = []
            for j in range(NJ):
                for pi_, (c0, w, lc0) in enumerate(pieces):
                    items.append(dict(j=j, c0=c0, w=w, lc0=lc0, first=(pi_ == 0), last=(pi_ == len(pieces) - 1)))
            pending = [load_wup(0), load_wup(1), load_wup(2)]
            cur_sl = {}

            def S0(t, it):
                j, c0, w = it["j"], it["c0"], it["w"]
                if it["first"]:
                    cur_sl[j] = pending.pop(0)
                    if j + 3 < NJ:
                        pending.append(load_wup(j + 3))
                sl = cur_sl[j]
                bA = RA[t % 2]; bG = RG[t % 5]
                it["bA"], it["bG"] = bA, bG
                for (bb, gi) in ((bA, 0), (bG, 1)):
                    for k in range(8):
                        p.op("pe", lambda e, k=k, bb=bb, gi=gi, sl=sl, c0=c0, w=w: e.matmul(
                            self.bk(bb)[:, 0:w], wup[sl][:, k, gi, :], x1T[:, k, c0:c0 + w], start=(k == 0), stop=(k == 7)),
                            reads=[("wup", sl, gi)] + self.x1T_keys(c0, w), writes=[("bank", bb)], inc=(k == 7))
                if n == 3 and it["last"]:
                    for k in range(8):
                        p.op("pe", lambda e, k=k, sl=sl: e.matmul(self.bk(TAILB)[0:NTAIL, 0:128], x1T[:, k, TAIL0:T], wup[sl][:, k, 0, :],
                                                                  start=(k == 0), stop=(k == 7)),
                             reads=[("wup", sl, 0)] + self.x1T_keys(TAIL0, NTAIL), writes=[("bank", TAILB)], inc=(k == 7))

            def S1(t, it):
                j, c0, w, bA = it["j"], it["c0"], it["w"], it["bA"]
                samp = (w == NS)
                iy = t % NY; ia = t % 2
                fw = [Fc[:, j, 32 + k:33 + k] for k in range(3)]
                aps = self.bk(bA)[:, 0:w]
                yv = y0[iy][:, 0:w]
                ACT(yv, aps, AF.Identity, [("bank", bA)] + FK, ("y0", iy), scale=fw[2], bias=Fc[:, j, 35:36])
                if not samp:
                    ab = a_sb[ia]
                    CP("act", ab[:, 2:2 + w], aps, [("bank", bA)], ("a_sb", ia))
                    CP("pool", ab[:, 0:2], halo[:, j, :], ["halo_init", ("halo", j)], ("a_sbh", ia))
                    ak = [("a_sb", ia), ("a_sbh", ia), ("y0", iy)]
                    STT("dve", yv, ab[:, 1:1 + w], fw[1], yv, ALU.mult, ALU.add, ak, ("y0", iy))
                    STT("dve", yv, ab[:, 0:w], fw[0], yv, ALU.mult, ALU.add, ak, ("y0", iy))
                    CP("pool", halo[:, j, :], ab[:, w:w + 2], [("a_sb", ia), ("a_sbh", ia)], ("halo", j))
                else:
                    CP("act", as_sb[:, :, 2:6], aps.rearrange("p (b s) -> p b s", s=4), [("bank", bA)], ("as_sb", "new"))
                    CP("pool", as_sb[:, :, 0:2], Fc[:, j, 0:32].rearrange("p (b k) -> p b k", k=2), FK, ("as_sb", "st"))
                    ak = [("as_sb", "new"), ("as_sb", "st"), ("y0", iy)]
                    y3 = yv.rearrange("p (b s) -> p b s", s=4)
                    STT("dve", y3, as_sb[:, :, 1:5], fw[1], y3, ALU.mult, ALU.add, ak, ("y0", iy))
                    STT("dve", y3, as_sb[:, :, 0:4], fw[0], y3, ALU.mult, ALU.add, ak, ("y0", iy))
                if n == 3 and it["last"]:
                    tb = (j // 4) % 2
                    CP("act", tailf[tb][:, 128 * (j % 4):128 * (j % 4 + 1)], self.bk(TAILB)[0:NTAIL, 0:128], [("bank", TAILB)], ("tailf", tb, j % 4))
                    if j % 4 == 3:
                        sem = "o_tf%d" % tb
                        p.dma("sp", O["ffn_conv"][:, 512 * (j // 4):512 * (j // 4 + 1)], tailf[tb], sem,
                              reads=[("tailf", tb, i) for i in range(4)])
                        self.out_sems[sem] = p.count[sem]

            def S2(t, it):
                w = it["w"]
                iy = t % NY; iq = t % NQ
                yv = y0[iy][:, 0:w]; qv = qq[iq][:, 0:w]
                ACT(qv, yv, AF.Square, [("y0", iy)], ("qq", iq))
                ACT(qv, qv, AF.Identity, [("qq", iq)], ("qq", iq), scale=GC, bias=1.0)

            def S3(t, it):
                w = it["w"]
                iy = t % NY; iq = t % NQ; ig = t % NG
                TT("pool", ag[ig][:, 0:w], qq[iq][:, 0:w], y0[iy][:, 0:w], ALU.mult, [("qq", iq), ("y0", iy)], ("ag", ig))

            def S4(t, it):
                j, w, lc0, bG = it["j"], it["w"], it["lc0"], it["bG"]
                iy = t % NY; iq = t % NQ; ig = t % NG
                yv = y0[iy][:, 0:w]; qv = qq[iq][:, 0:w]; av = ag[ig][:, 0:w]
                gps = self.bk(bG)[:, 0:w]
                ACT(qv, av, AF.Tanh, [("ag", ig)], ("qq", iq), scale=GK)
                STT("dve", av, qv, 1.0, yv, ALU.add, ALU.mult, [("qq", iq), ("y0", iy)], ("ag", ig))
                TT("dve", Gq[:, j, lc0:lc0 + w], av, gps, ALU.mult, [("ag", ig), ("bank", bG)], ("Gq", j, lc0))

            N_ = len(items)
            stages = [(S4, 4), (S1, 1), (S2, 2), (S3, 3), (S0, 0)]
            for t in range(N_ + 4):
                for fn, lag in stages:
                    if 0 <= t - lag < N_:
                        fn(t - lag, items[t - lag])
            if n == 0:
                self.tap("Gq", Gq[:, 0, :], [("Gq", 0, 0)])
            tts = [4 * n + i for i in range(4)] + ([16] if n == 3 else [])
            for tt in tts:
                r0 = tt * 128
                rows = min(128, T - r0)
                lc = r0 - q0 if tt < 16 else 512
                s = ntile[0] % NB
                pp = ntile[0] % 2
                ntile[0] += 1
                p.dma("sp", x1tok[s][0:rows, :], self.x1_scr[r0:r0 + rows, :], "d_x1r%d" % s, reads=[("x1scr", tt)], writes=[("x1tok2", s)])
                psf = self.ps[pp][:].rearrange("p a c -> p (a c)")
                gkeys = [("Gq", j, 512 if tt == 16 else 0) for j in range(NJ)]
                for h in range(2):
                    for k in range(NJ):
                        p.op("pe", lambda e, k=k, h=h, pp=pp, lc=lc, rows=rows: e.matmul(
                            self.ps[pp][0:rows, h, :], Gq[:, k, lc:lc + rows], wdn[:, k, 512 * h:512 * (h + 1)],
                            start=(k == 0), stop=(k == NJ - 1)),
                            reads=[("wdn", k // 4), ("Gq", k, 512 if tt == 16 else 0)], writes=[("bank", 2 * pp + h)], inc=(k == NJ - 1))
                self.ln_tile(psf, x1tok[s], lnc[:, 2, :], lnc[:, 3, :], ytok[s], otok[s], rows,
                             [("bank", 2 * pp), ("bank", 2 * pp + 1)], ("x1tok2", s), ("lnc", 2), ("lnc", 3), ("ytok2", s), ("ytok2", s),
                             st6[s], mv[s], sd[s])
                sem = "o_y%d" % s
                p.dma("sp", O["y"][r0:r0 + rows, :], otok[s][0:rows, :], sem, reads=[("ytok2", s)])
                self.out_sems[sem] = p.count[sem]

    def finish(self):
        fw = [(s, v) for s, v in self.out_sems.items()]
        self.p.emit(final_waits=fw)
        self.st.close()
        print("arena peaks: R0 %d/%d words, R1 %d/%d words" % (self.R0.peak, self.R0.words, self.R1.peak, self.R1.words))
        return self.nc


def shard_inputs(inputs, c):
    f = lambda a: np.ascontiguousarray(a, dtype=np.float32)
    m = {}
    m["x"] = f(np.concatenate([inputs["x_prompt"][c], inputs["x_sample"][NSQ * c:NSQ * (c + 1)].reshape(NS, D)], axis=0))
    m["st_lru_conv"] = f(inputs["state_lru_conv"][0, NSQ * c:NSQ * (c + 1)].reshape(NSQ * 3, D))
    m["st_lru_h"] = f(inputs["state_lru_h"][0, NSQ * c:NSQ * (c + 1)])
    m["st_s5_re"] = f(inputs["state_s5_re"][0, NSQ * c:NSQ * (c + 1)].reshape(NSQ, 2048))
    m["st_s5_im"] = f(inputs["state_s5_im"][0, NSQ * c:NSQ * (c + 1)].reshape(NSQ, 2048))
    m["st_ffn_conv"] = f(inputs["state_ffn_conv"][0, NSQ * c:NSQ * (c + 1)].reshape(NSQ * 2, DFF))
    for k in ("w_in", "lru_conv_w", "lru_conv_b", "lru_wa", "lru_ba", "lru_wx", "lru_bx", "lru_lambda", "s5_a_re", "s5_a_im",
              "s5_log_dt", "s5_b_re", "s5_b_im", "s5_c_re", "s5_c_im", "s5_d", "w_glu", "w_out", "ln1_g", "ln1_b", "w_up",
              "ffn_conv_w", "ffn_conv_b", "w_down", "ln2_g", "ln2_b"):
        m[k] = f(inputs[k][0])
    wu = m["w_up"]
    m["w_up"] = f(np.stack([wu[:, :DFF].reshape(D, DFF // 128, 128), wu[:, DFF:].reshape(D, DFF // 128, 128)], axis=2).reshape(D, 2 * DFF))
    wi = m["w_in"]
    gl = wi[:, 1536:2560].reshape(D, 8, 128); gs = wi[:, 2560:3584].reshape(D, 8, 128)
    m["w_in"] = f(np.concatenate([wi[:, :1536], np.stack([gl, gs], axis=2).reshape(D, 2048)], axis=1))
    wg = m["w_glu"]
    m["w_glu"] = f(np.stack([wg[:, :1024].reshape(512, 8, 128), wg[:, 1024:].reshape(512, 8, 128)], axis=2).reshape(512, 2048))
    return m


_NC_CACHE = {}


def _get_nc():
    if "nc" not in _NC_CACHE:
        _NC_CACHE["nc"] = Builder().build()
    return _NC_CACHE["nc"]


def kernel(**inputs):
    nc = _get_nc()
    in_maps = [shard_inputs(inputs, c) for c in range(NCORES)]
    res = run_bass_kernel_spmd(nc, in_maps, core_ids=list(range(NCORES)))
    R = res.results
    B = NCORES
    y_p = np.zeros((B, TP, D), np.float32); y_s = np.zeros((B * NSQ, 4, D), np.float32)
    p_conv = np.zeros((1, B, 3, D), np.float32); p_h = np.zeros((1, B, D), np.float32)
    p_re = np.zeros((1, B, 32, 64), np.float32); p_im = np.zeros((1, B, 32, 64), np.float32)
    p_ffn = np.zeros((1, B, 2, DFF), np.float32)
    s_conv = np.zeros((1, B * NSQ, 3, D), np.float32); s_h = np.zeros((1, B * NSQ, D), np.float32)
    s_re = np.zeros((1, B * NSQ, 32, 64), np.float32); s_im = np.zeros((1, B * NSQ, 32, 64), np.float32)
    s_ffn = np.zeros((1, B * NSQ, 2, DFF), np.float32)
    for c in range(B):
        r = R[c]
        sl = slice(NSQ * c, NSQ * (c + 1))
        y_p[c] = r["y"][0:TP]
        y_s[sl] = r["y"][TP:].reshape(NSQ, 4, D)
        lc = r["o_lru_conv"]
        p_conv[0, c] = lc[0:3]
        s_conv[0, sl] = lc[3:].reshape(NSQ, 4, D)[:, 1:4]
        lh = r["o_lru_h"].transpose(2, 1, 0).reshape(17, D)
        p_h[0, c] = lh[0]
        s_h[0, sl] = lh[1:17]
        s5 = r["o_s5"].reshape(2, 64, 2, 16, 17).transpose(2, 4, 3, 0, 1)
        s5 = s5.reshape(2, 17, 32, 64)
        p_re[0, c] = s5[0, 0]
        p_im[0, c] = s5[1, 0]
        s_re[0, sl] = s5[0, 1:17]
        s_im[0, sl] = s5[1, 1:17]
        fc = r["o_ffn_conv"]
        p_ffn[0, c] = fc[1:3]
        s_ffn[0, sl] = fc[3:].reshape(NSQ, 4, DFF)[:, 2:4]
    return (y_p, y_s, p_conv, p_h, p_re, p_im, p_ffn, s_conv, s_h, s_re, s_im, s_ffn)
```

```python
import math
import contextlib
import numpy as np
import concourse.bass as bass
import concourse.mybir as mybir
from concourse.bass_utils import run_bass_kernel_spmd

F32 = mybir.dt.float32
BF16 = mybir.dt.bfloat16
AF = mybir.ActivationFunctionType
ALU = mybir.AluOpType

ENGINES = ("pe", "act", "dve", "pool", "sp")
NCORES = 8
TP = 2048
NSQ = 16
NS = 64
T = TP + NS
TAIL0 = TP - 3
NTAIL = T - TAIL0
D = 1024
DFF = 3072
ALPHA = 2.0 ** 0.25
LN_EPS = 1e-5
LCH = 8
NCH = TP // LCH
NLEV = 8
PAD = 128
PI = math.pi
GK = math.sqrt(2.0 / math.pi)
GC = 0.044715


class Prog:
    def __init__(self, nc):
        self.nc = nc
        self.streams = {e: [] for e in ENGINES}
        self.count = {}
        self.waited = {e: {} for e in ENGINES}
        self.last_write = {}
        self.readers = {}
        self.sem_names = set()
        self.epoch = "0"
        self.pending = {}

    def barrier(self):
        snap = list(self.count.items())
        for e in ENGINES:
            self.pending.setdefault(e, []).extend(snap)

    def op(self, eng, fn, reads=(), writes=(), inc=True, sem=None, amount=1):
        if sem is None:
            sem = "s_%s_%s" % (eng, self.epoch)
        self.sem_names.add(sem)
        deps = list(self.pending.pop(eng, ()))
        for k in reads:
            ev = self.last_write.get(k)
            if ev is not None:
                deps.append(ev)
        for k in writes:
            ev = self.last_write.get(k)
            if ev is not None:
                deps.append(ev)
            deps.extend(self.readers.get(k, ()))
        waits = {}
        for (s, v) in deps:
            if eng == "pe" and s.startswith("s_pe_"):
                continue
            if self.waited[eng].get(s, 0) >= v:
                continue
            if waits.get(s, 0) < v:
                waits[s] = v
        for s, v in waits.items():
            self.waited[eng][s] = v
        cur = self.count.get(sem, 0)
        val = cur + amount
        if inc:
            self.count[sem] = val
        ev = (sem, val)
        self.streams[eng].append((fn, list(waits.items()), (sem, amount) if inc else None))
        for k in reads:
            self.readers.setdefault(k, []).append(ev)
        for k in writes:
            self.last_write[k] = ev
            self.readers[k] = []
        return ev

    def dma(self, queue, out, in_, sem, reads=(), writes=(), **kw):
        def fn(e):
            return e.dma_start(out=out, in_=in_, **kw)
        return self.op(queue, fn, reads=reads, writes=writes, inc=True, sem=sem, amount=16)

    def emit(self, final_waits=()):
        nc = self.nc
        names = sorted(self.sem_names)
        with contextlib.ExitStack() as st:
            sems = {n: st.enter_context(nc.semaphore(n)) for n in names}
            block = st.enter_context(nc.Block())
            streams = self.streams

            def run(engh, lst, last):
                for fn, waits, inc in lst:
                    for s, v in waits:
                        engh.wait_ge(sems[s], v)
                    ins = fn(engh)
                    if inc is not None:
                        ins.then_inc(sems[inc[0]], inc[1])
                if last:
                    for s, v in final_waits:
                        engh.wait_ge(sems[s], v)

            @block.tensor
            def _(e):
                run(e, streams["pe"], False)

            @block.scalar
            def _(e):
                run(e, streams["act"], False)

            @block.vector
            def _(e):
                run(e, streams["dve"], False)

            @block.gpsimd
            def _(e):
                run(e, streams["pool"], False)

            @block.sync
            def _(e):
                run(e, streams["sp"], True)


class Arena:
    def __init__(self, tensor, words):
        self.t = tensor
        self.words = words
        self.pos = 0
        self.peak = 0

    def take(self, shape, dt=F32):
        n = 1
        for s in shape[1:]:
            n *= s
        esz = 4 if dt == F32 else 2
        w = (n * esz + 3) // 4
        w = (w + 7) // 8 * 8
        assert self.pos + w <= self.words, "arena overflow: need %d have %d" % (self.pos + w, self.words)
        v = self.t[0:shape[0], self.pos:self.pos + w]
        self.pos += w
        self.peak = max(self.peak, self.pos)
        if dt != F32:
            v = v.bitcast(dt)
        v = v[:, 0:n]
        if len(shape) > 2:
            names = " ".join("d%d" % i for i in range(len(shape) - 1))
            kw = {"d%d" % i: shape[i + 1] for i in range(len(shape) - 2)}
            v = v.rearrange("p (%s) -> p %s" % (names, names), **kw)
        return v


class StopBuild(Exception):
    pass


class Builder:
    def chk_stop(self, name, reads=()):
        if self.stop_after == name:
            self.p.barrier()
            d = self.dout("dbg_stop", [128, 4])
            t = self.R0.take([128, 4])
            self.MS("dve", t, 1.0, "stoptile")
            self.out_dma(d, t, ["stoptile"])
            raise StopBuild()

    def __init__(self, debug=(), stop_after=None):
        self.debug = set(debug)
        self.stop_after = stop_after
        self.nc = bass.Bass("TRN2", target_bir_lowering=False)
        self.p = Prog(self.nc)
        self.st = contextlib.ExitStack()
        self.out_sems = {}
        self.nout = 0
        self.free_banks = list(range(8))
        self.nbank = 0
        self.ntmp = 0

    def din(self, name, shape):
        return self.nc.dram_tensor(name, list(shape), F32, kind="ExternalInput").ap()

    def dout(self, name, shape, dt=F32):
        return self.nc.dram_tensor(name, list(shape), dt, kind="ExternalOutput").ap()

    def sb(self, name, shape, dt=F32):
        t = self.st.enter_context(self.nc.sbuf_tensor(name, list(shape), dt))
        return t[:]

    def out_dma(self, out, in_, reads, queue="sp", **kw):
        sem = "o_%d" % (self.nout % 8)
        self.nout += 1
        self.p.dma(queue, out, in_, sem, reads=reads, **kw)
        self.out_sems[sem] = self.p.count[sem]

    def tap(self, name, ap, reads):
        if name not in self.debug:
            return
        d = self.dout("dbg_" + name, list(ap.shape), ap.dtype)
        self.out_dma(d, ap, reads)

    def bank(self):
        b = self.free_banks[self.nbank % len(self.free_banks)]
        self.nbank += 1
        return b

    def bk(self, b):
        return self.ps[b // 2][:, b % 2, :]

    def TT(self, eng, out, a, b_, op, r, w):
        self.p.op(eng, lambda e: e.tensor_tensor(out=out, in0=a, in1=b_, op=op), reads=r, writes=[w])

    def TS(self, eng, out, a, s1, op0, r, w, s2=None, op1=None):
        if op1 is None:
            self.p.op(eng, lambda e: e.tensor_scalar(out=out, in0=a, scalar1=s1, scalar2=None, op0=op0), reads=r, writes=[w])
        else:
            self.p.op(eng, lambda e: e.tensor_scalar(out=out, in0=a, scalar1=s1, scalar2=s2, op0=op0, op1=op1), reads=r, writes=[w])

    def STT(self, eng, out, a, sc, b_, op0, op1, r, w):
        self.p.op(eng, lambda e: e.scalar_tensor_tensor(out=out, in0=a, scalar=sc, in1=b_, op0=op0, op1=op1), reads=r, writes=[w])

    def ACT(self, out, a, func, r, w, scale=1.0, bias=0.0):
        self.p.op("act", lambda e: e.activation(out=out, in_=a, func=func, scale=scale, bias=bias), reads=r, writes=[w])

    def CP(self, eng, out, a, r, w):
        if eng == "act":
            self.p.op("act", lambda e: e.copy(out=out, in_=a), reads=r, writes=[w])
        else:
            self.p.op(eng, lambda e: e.tensor_copy(out=out, in_=a), reads=r, writes=[w])

    def MS(self, eng, out, val, w, r=()):
        self.p.op(eng, lambda e: e.memset(out, val), reads=list(r), writes=[w])

    def build(self):
        nc, p = self.nc, self.p
        din = self.din
        I = {}
        for name, shape in (("x", [T, D]), ("st_lru_conv", [NSQ * 3, D]), ("st_lru_h", [NSQ, D]), ("st_s5_re", [NSQ, 2048]),
                            ("st_s5_im", [NSQ, 2048]), ("st_ffn_conv", [NSQ * 2, DFF]), ("w_in", [D, 3584]),
                            ("lru_conv_w", [4, D]), ("lru_conv_b", [D]), ("lru_wa", [16, 64, 64]), ("lru_ba", [D]),
                            ("lru_wx", [16, 64, 64]), ("lru_bx", [D]), ("lru_lambda", [D]), ("s5_a_re", [32, 64]),
                            ("s5_a_im", [32, 64]), ("s5_log_dt", [32]), ("s5_b_re", [32, 64, 16]), ("s5_b_im", [32, 64, 16]),
                            ("s5_c_re", [32, 16, 64]), ("s5_c_im", [32, 16, 64]), ("s5_d", [512]), ("w_glu", [512, 2048]),
                            ("w_out", [D, D]), ("ln1_g", [D]), ("ln1_b", [D]), ("w_up", [D, 2 * DFF]), ("ffn_conv_w", [3, DFF]),
                            ("ffn_conv_b", [DFF]), ("w_down", [DFF, D]), ("ln2_g", [D]), ("ln2_b", [D])):
            I[name] = din(name, shape)
        self.I = I
        O = {}
        O["y"] = self.dout("y", [T, D])
        O["lru_conv"] = self.dout("o_lru_conv", [NTAIL, D])
        O["lru_h"] = self.dout("o_lru_h", [128, 8, 17])
        O["s5"] = self.dout("o_s5", [128, 2, 16, 17])
        O["ffn_conv"] = self.dout("o_ffn_conv", [NTAIL, DFF])
        self.O = O
        self.x1_scr = nc.dram_tensor("x1_scr", [T, D], F32, kind="Internal").ap()

        self.ps = [self.st.enter_context(nc.psum_tensor("ps%d" % i, [128, 2, 512], F32)) for i in range(4)]
        R0W = 17664
        R1W = 35456
        self.R0 = Arena(self.st.enter_context(nc.sbuf_tensor("R0", [128, R0W], F32)), R0W)
        self.R1 = Arena(self.st.enter_context(nc.sbuf_tensor("R1", [128, R1W], F32)), R1W)
        R0, R1 = self.R0, self.R1

        ident_f = R0.take([128, 128]); ident_b = R0.take([128, 128], BF16)
        self.ident_f, self.ident_b = ident_f, ident_b
        self.MS("pool", ident_f, 0.0, "ident_f")
        p.op("pool", lambda e: e.affine_select(out=ident_f, in_=ident_f, pattern=[[-1, 128]],
                                               compare_op=ALU.not_equal, fill=1.0, base=0, channel_multiplier=1),
             reads=["ident_f"], writes=["ident_f"])
        self.CP("pool", ident_b, ident_f, ["ident_f"], "ident_b")

        self.PIECES = [(0, 512), (512, 512), (1024, 512), (1536, 512), (2048, 64)]
        xT = R0.take([128, 8, T], BF16)
        self.xT = xT
        zblk = R0.take([128, 4224])
        self.z2 = zblk.bitcast(BF16)[:, 0:4 * T].rearrange("p (a t) -> p a t", a=4)
        self.lnc = zblk[:, 0:4096].rearrange("p (a n) -> p a n", a=4)
        self.Hfin = R0.take([128, 2, 16, 17])
        self.LRUc = R0.take([128, 8, 72])
        self.Fc = R0.take([128, DFF // 128, 36])
        mark0 = R1.pos
        u_bf = R1.take([128, 4, T], BF16)
        self.W1 = R1.take([128, 4, LCH, 2, 128], BF16)
        self.W4a = R1.take([128, 16, LCH, 2, 32], BF16)
        self.W4b = R1.take([128, 16, LCH, 2, 32], BF16)
        self.h0S = R1.take([128, 2, 16, 16]); self.h0S_bf = R1.take([128, 2, 16, 16], BF16)
        self.RS = Arena(R1.take([128, 2368]), 2368)
        mark1 = R1.pos
        self.free_banks = [4, 5, 6, 7]
        self.param_loads()
        xb = [R1.take([128, D], BF16) for i in range(4)]
        ntt = (T + 127) // 128
        for tt in range(ntt):
            r0 = tt * 128
            rows = min(128, T - r0)
            slot = tt % 4
            p.dma("pool", xb[slot][0:rows, :], I["x"][r0:r0 + rows, :], "d_xb%d" % slot, writes=[("xb", slot)])
            b = self.bank()
            pt = self.bk(b).bitcast(BF16)
            for k in range(8):
                p.op("pe", lambda e, k=k, pt=pt, slot=slot, rows=rows: e.transpose(
                    pt[:, k * 128:k * 128 + rows], xb[slot][0:rows, k * 128:(k + 1) * 128], ident_b[0:rows, 0:rows]),
                    reads=[("xb", slot), "ident_b"], writes=[("bank", b)], inc=(k == 7))
            src = pt.rearrange("p (k c) -> p k c", c=128)[:, :, 0:rows]
            self.CP("act" if tt % 2 == 0 else "dve", xT[:, :, r0:r0 + rows], src, [("bank", b)], ("xT", tt))
            if tt == 3:
                self.param_transposes()
        self.tap("xT", xT[:, 0, :], [("xT", tt) for tt in range(ntt)])
        if self.stop_after == "p0":
            return self.finish()
        try:
            self.s5_prep()
        except StopBuild:
            return self.finish()
        self.tap("W1", self.W1[:, 0, :, :, :], self.W1_keys)
        self.tap("W4a", self.W4a[:, 0, :, :, :], self.W4_keys)
        self.tap("W4b", self.W4b[:, 0, :, :, :], self.W4_keys)
        self.tap("h0S", self.h0S, self.h0S_keys)
        self.tap("mur", self.mur, self.mu_keys)
        if self.stop_after == "prep":
            return self.finish()
        p.barrier()
        R1.pos = mark1
        wslot_u = [R1.take([128, 8, 128], BF16) for i in range(2)]
        for qh in range(4):
            slot = qh % 2
            p.dma("pool", wslot_u[slot], I["w_in"][:, 1024 + 128 * qh:1024 + 128 * (qh + 1)].rearrange("(k p) n -> p k n", p=128),
                  "d_wu%d" % slot, writes=[("wslot_u", slot)])
            for (c0, w) in self.PIECES:
                b = self.bank()
                for k in range(8):
                    p.op("pe", lambda e, k=k, b=b, slot=slot, c0=c0, w=w: e.matmul(
                        self.bk(b)[:, 0:w], wslot_u[slot][:, k, :], xT[:, k, c0:c0 + w], start=(k == 0), stop=(k == 7)),
                        reads=[("wslot_u", slot)] + self.xT_keys(c0, w), writes=[("bank", b)], inc=(k == 7))
                self.CP("act", u_bf[:, qh, c0:c0 + w], self.bk(b)[:, 0:w], [("bank", b)], ("u_bf", qh, c0))
        self.tap("u_bf", u_bf[:, 0, :], [("u_bf", 0, c0) for c0, _ in self.PIECES])
        if self.stop_after == "u":
            return self.finish()
        self.s5_main(u_bf)
        self.tap("z2", self.z2[:, 0, :], [("z2", 0, c0) for c0, _ in self.PIECES])
        self.tap("Hfin", self.Hfin, [("Hfin", q) for q in range(16)] + [("Hfin0", q) for q in range(16)])
        hk = [("Hfin", q) for q in range(16)] + [("Hfin0", q) for q in range(16)]
        self.out_dma(O["s5"], self.Hfin, hk)
        if self.stop_after == "s5":
            return self.finish()
        p.barrier()
        R1.pos = mark0
        p.epoch = "1"
        self.lru_stage()
        if self.stop_after == "lru":
            return self.finish()
        p.barrier()
        R1.pos = self.merged_end
        p.epoch = "2"
        self.mix_stage()
        if self.stop_after == "mix":
            return self.finish()
        p.barrier()
        R1.pos = 0
        p.epoch = "3"
        self.ffn_stage()
        return self.finish()

    def xT_keys(self, c0, w):
        return [("xT", tt) for tt in range(c0 // 128, (c0 + w - 1) // 128 + 1)]

    def param_loads(self):
        nc, p, I = self.nc, self.p, self.I
        R1 = self.R1
        MS = self.MS
        NCK = dict(allow_slow_non_contiguous=True)
        NJ = DFF // 128
        self.LF = R1.take([128, D + DFF])
        Lp = self.LF[:, 0:D]; Fp = self.LF[:, D:D + DFF]
        hs1 = R1.take([128, 2048])
        hsp = [hs1, hs1]
        self.Lp, self.Fp, self.hsp = Lp, Fp, hsp
        MS("pool", Lp, 0.0, "Lp"); MS("pool", Fp, 0.0, "Fp"); MS("pool", hs1, 0.0, "hsp")
        row = lambda nm: I[nm].rearrange("(o n) -> o n", o=1)
        for (r0, r1, src) in ((0, 48, I["st_lru_conv"]), (48, 64, I["st_lru_h"]), (64, 68, I["lru_conv_w"]), (68, 69, row("lru_conv_b")),
                              (69, 70, row("lru_ba")), (70, 71, row("lru_bx")), (71, 72, row("lru_lambda"))):
            p.dma("sp", Lp[r0:r1, :], src, "d_Lp", writes=["Lp"])
        for (r0, r1, src) in ((0, 32, I["st_ffn_conv"]), (32, 35, I["ffn_conv_w"]), (35, 36, row("ffn_conv_b"))):
            p.dma("sp", Fp[r0:r1, :], src, "d_Fp", writes=["Fp"])
        self.hs_loaded = False
        sh3 = [128, 16, 32]
        self.Bnr = R1.take(sh3); self.Bni = R1.take(sh3)
        self.Cnr = R1.take([128, 4, 128]); self.Cni = R1.take([128, 4, 128])
        for t_, k_ in ((self.Bnr, "Bnr"), (self.Bni, "Bni"), (self.Cnr, "Cnr"), (self.Cni, "Cni")):
            MS("pool", t_, 0.0, k_)
        for (dst, nm, key) in ((self.Bnr, "s5_b_re", "Bnr"), (self.Bni, "s5_b_im", "Bni")):
            v = I[nm].rearrange("(q two) p c -> two p q c", two=2)
            for two in range(2):
                p.dma("sp", dst[64 * two:64 * two + 64, :, 16 * two:16 * two + 16], v[two], "d_prep", writes=[key])
        for (dst, nm, key) in ((self.Cnr, "s5_c_re", "Cnr"), (self.Cni, "s5_c_im", "Cni")):
            v = I[nm].rearrange("(qh ql two) c p -> ql two c qh p", qh=4, ql=4, two=2)
            for ql in range(4):
                for two in range(2):
                    p0 = 32 * ql + 16 * two
                    p.dma("sp", dst[p0:p0 + 16, :, 64 * two:64 * two + 64], v[ql, two], "d_prep", writes=[key])
        shS = [128, 16]
        self.aSr = R1.take(shS); self.aSi = R1.take(shS); self.ldS = R1.take(shS)
        p.dma("sp", self.aSr, I["s5_a_re"].rearrange("(q two) p -> (two p) q", two=2), "d_prep", writes=["aSr"], **NCK)
        p.dma("sp", self.aSi, I["s5_a_im"].rearrange("(q two) p -> (two p) q", two=2), "d_prep", writes=["aSi"], **NCK)
        v = I["s5_log_dt"].rearrange("(q two) -> two q", two=2)
        for two in range(2):
            p.dma("sp", self.ldS[64 * two:64 * two + 64, :], v[two].partition_broadcast(64), "d_prep", writes=["ldS"], **NCK)
        self.dS5 = self.R0.take([128, 4])
        p.dma("sp", self.dS5, I["s5_d"].rearrange("(t p) -> p t", p=128), "d_prep", writes=["dS5"], **NCK)
        for sem, keys in (("d_Lp", ["Lp"]), ("d_Fp", ["Fp"]),
                          ("d_prep", ["Bnr", "Bni", "Cnr", "Cni", "aSr", "aSi", "ldS", "dS5"])):
            for k_ in keys:
                p.last_write[k_] = (sem, p.count[sem])

    def tr_group(self, srcs, evac):
        p = self.p
        for g0 in range(0, len(srcs), 4):
            grp = srcs[g0:g0 + 4]
            b = self.bank()
            bv = self.bk(b).rearrange("p (a c) -> p a c", a=4)
            for i, (ap, keys) in enumerate(grp):
                p.op("pe", lambda e, i=i, ap=ap, bv=bv: e.transpose(bv[:, i, :], ap, self.ident_f),
                     reads=list(keys) + ["ident_f"], writes=[("bank", b)], inc=(i == len(grp) - 1))
            evac(bv, g0, len(grp), b)

    def param_transposes(self):
        p = self.p
        CP = self.CP
        NJ = DFF // 128
        Lp, Fp, hsp = self.Lp, self.Fp, self.hsp

        def ev_L(bv, g0, n, b):
            CP("act", self.LRUc[:, g0:g0 + n, :], bv[:, 0:n, 0:72], [("bank", b)], ("LRUc", g0 // 4))
        self.tr_group([(Lp[:, 128 * j:128 * (j + 1)], ["Lp"]) for j in range(8)], ev_L)

        def ev_F(bv, g0, n, b):
            CP("dve", self.Fc[:, g0:g0 + n, :], bv[:, 0:n, 0:36], [("bank", b)], ("Fc", g0 // 4))
        self.tr_group([(Fp[:, 128 * j:128 * (j + 1)], ["Fp"]) for j in range(NJ)], ev_F)
        self.LRUc_keys = [("LRUc", i) for i in range(2)]
        self.Fc_keys = [("Fc", i) for i in range(NJ // 4)]
        for part in range(2):
            p.dma("sp", hsp[part][0:16, :], self.I[("st_s5_re", "st_s5_im")[part]], "d_hsp", writes=["hsp"])
            def ev_h(bv, g0, n, b, part=part):
                CP("act", self.h0S[:, part, g0:g0 + n, :], bv[:, 0:n, 0:16], [("bank", b)], ("h0S", part, g0 // 4))
            self.tr_group([(hsp[part][:, 128 * q:128 * (q + 1)], ["hsp"]) for q in range(16)], ev_h)
        self.h0S_keys = [("h0S", part, i) for part in range(2) for i in range(4)]
        sh3 = [128, 16, 32]
        self.CTr = self.R1.take(sh3); self.CTi = self.R1.take(sh3)
        for (src, dst, k_src, k_dst) in ((self.Cnr, self.CTr, "Cnr", "CTr"), (self.Cni, self.CTi, "Cni", "CTi")):
            def ev_c(bv, g0, n, b, dst=dst, k_dst=k_dst):
                CP("dve", dst[:, 4 * g0:4 * (g0 + n), :].rearrange("p (a q) c -> p a (q c)", a=n), bv[:, 0:n, :], [("bank", b)], (k_dst, g0))
            self.tr_group([(src[:, qh, :], [k_src]) for qh in range(4)], ev_c)
        self.CT_keys = {"CTr": [("CTr", 0)], "CTi": [("CTi", 0)]}

    def s5_prep(self):
        nc, p, I = self.nc, self.p, self.I
        R0, R1, RS = self.R0, self.R1, self.RS
        TT, TS, STT, ACT, CP, MS = self.TT, self.TS, self.STT, self.ACT, self.CP, self.MS
        W1, W4a, W4b = self.W1, self.W4a, self.W4b
        aSr, aSi, ldS = self.aSr, self.aSi, self.ldS
        CTr, CTi = self.CTr, self.CTi
        kCTr, kCTi = self.CT_keys["CTr"], self.CT_keys["CTi"]
        shS = [128, 16]
        sh3 = [128, 16, 32]
        I32 = mybir.dt.int32

        def sincos_base(theta, t, kf, A, C2, cs, sn, k_th, k_t, k_kf, k_A, k_C2, k_cs, k_sn):
            TS("dve", t, theta, 1.0 / (2 * PI), ALU.mult, [k_th], k_t, s2=16.0, op1=ALU.add)
            CP("dve", kf.bitcast(I32), t, [k_t], k_kf)
            CP("dve", kf, kf.bitcast(I32), [k_kf], k_kf)
            TT("dve", t, t, kf, ALU.subtract, [k_t, k_kf], k_t)
            ACT(A, t, AF.Sin, [k_t], k_A, scale=PI)
            ACT(C2, t, AF.Sin, [k_t], k_C2, scale=PI / 2)
            TT("dve", C2, C2, C2, ALU.mult, [k_C2], k_C2)
            TS("dve", C2, C2, -2.0, ALU.mult, [k_C2], k_C2, s2=1.0, op1=ALU.add)
            STT("dve", sn, A, 2.0, C2, ALU.mult, ALU.mult, [k_A, k_C2], k_sn)
            TT("dve", cs, A, A, ALU.mult, [k_A], k_cs)
            TS("dve", cs, cs, -2.0, ALU.mult, [k_cs], k_cs, s2=1.0, op1=ALU.add)

        def tS():
            return RS.take(shS)

        dtS = tS(); adtS = tS(); thS = tS()
        ACT(dtS, ldS, AF.Exp, ["ldS"], "dtS")
        TT("dve", adtS, aSr, dtS, ALU.mult, ["aSr", "dtS"], "adtS")
        TT("dve", thS, aSi, dtS, ALU.mult, ["aSi", "dtS"], "thS")
        self.rhoS = tS()
        ACT(self.rhoS, adtS, AF.Exp, ["adtS"], "rhoS")
        cosS = []; sinS = []; nsinS = []
        lamr = {}; lami = {}; lrotr = {}; lroti = {}; rpow = {}
        c1S = tS(); s1S = tS(); w1 = tS(); w2 = tS(); w3 = tS(); w4 = tS()
        sincos_base(thS, w1, w2, w3, w4, c1S, s1S, "thS", "Sw1", "Sw2", "Sw3", "Sw4", "S1cs", "S1sn")
        ua = w1; ub = w2
        for s in range(2 * LCH):
            if s == 0:
                cs = tS(); sn = tS()
                MS("dve", cs, 1.0, "S0cs"); MS("dve", sn, 0.0, "S0sn")
            elif s == 1:
                cs, sn = c1S, s1S
            else:
                cs = tS(); sn = tS()
                pc, ps_ = cosS[s - 1], sinS[s - 1]
                kpc, kps = "S%dcs" % (s - 1), "S%dsn" % (s - 1)
                TT("dve", ua, pc, c1S, ALU.mult, [kpc, "S1cs", "Sw1"], "Sw1")
                TT("dve", ub, ps_, s1S, ALU.mult, [kps, "S1sn", "Sw2"], "Sw2")
                TT("dve", cs, ua, ub, ALU.subtract, ["Sw1", "Sw2"], "S%dcs" % s)
                TT("dve", ua, ps_, c1S, ALU.mult, [kps, "S1cs"], "Sw1")
                TT("dve", ub, pc, s1S, ALU.mult, [kpc, "S1sn"], "Sw2")
                TT("dve", sn, ua, ub, ALU.add, ["Sw1", "Sw2"], "S%dsn" % s)
            cosS.append(cs); sinS.append(sn)
            if s <= LCH:
                ns = tS()
                TS("dve", ns, sn, -1.0, ALU.mult, ["S%dsn" % s], "nS%dsn" % s)
                nsinS.append(ns)
            if 1 <= s <= LCH:
                r = tS()
                ACT(r, adtS, AF.Exp, ["adtS"], "rpow%d" % s, scale=float(s))
                rpow[s] = r
                if s in (1, 4):
                    a = tS(); b_ = tS()
                    TT("dve", a, r, cs, ALU.mult, ["rpow%d" % s, "S%dcs" % s], "lamr%d" % s)
                    TT("dve", b_, r, sn, ALU.mult, ["rpow%d" % s, "S%dsn" % s], "lami%d" % s)
                    lamr[s] = a; lami[s] = b_
        for s in range(1, LCH + 1):
            a = tS(); b_ = tS()
            ks = s + LCH - 1
            TT("dve", a, rpow[s], cosS[ks], ALU.mult, ["rpow%d" % s, "S%dcs" % ks], "lrotr%d" % s)
            TT("dve", b_, rpow[s], sinS[ks], ALU.mult, ["rpow%d" % s, "S%dsn" % ks], "lroti%d" % s)
            lrotr[s] = a; lroti[s] = b_
        self.cosS, self.sinS, self.nsinS, self.lamr, self.lami = cosS, sinS, nsinS, lamr, lami
        self.nlami4 = tS()
        TS("dve", self.nlami4, lami[4], -1.0, ALU.mult, ["lami4"], "nlami4")
        ta = R1.take(sh3); tb = R1.take(sh3)

        def bc(t):
            return t.unsqueeze(2).to_broadcast(sh3)

        Bnr, Bni = self.Bnr, self.Bni
        nr = tS(); den = tS(); u1 = tS(); cfr = tS(); cfi = tS()
        TS("dve", nr, lamr[1], -1.0, ALU.add, ["lamr1"], "nr")
        TT("dve", den, aSr, aSr, ALU.mult, ["aSr"], "den")
        TT("dve", u1, aSi, aSi, ALU.mult, ["aSi"], "u1")
        TT("dve", den, den, u1, ALU.add, ["den", "u1"], "den")
        p.op("dve", lambda e: e.reciprocal(out=den, in_=den), reads=["den"], writes=["den"])
        TT("dve", cfr, nr, aSr, ALU.mult, ["nr", "aSr"], "cfr")
        TT("dve", u1, lami[1], aSi, ALU.mult, ["lami1", "aSi", "den"], "u1")
        TT("dve", cfr, cfr, u1, ALU.add, ["cfr", "u1"], "cfr")
        TT("dve", cfr, cfr, den, ALU.mult, ["cfr", "den"], "cfr")
        TT("dve", cfi, lami[1], aSr, ALU.mult, ["lami1", "aSr"], "cfi")
        TT("dve", u1, nr, aSi, ALU.mult, ["nr", "aSi", "cfr"], "u1")
        TT("dve", cfi, cfi, u1, ALU.subtract, ["cfi", "u1"], "cfi")
        TT("dve", cfi, cfi, den, ALU.mult, ["cfi", "den"], "cfi")
        BbR = R1.take(sh3); BbI = R1.take(sh3)
        tc_ = R1.take(sh3); td_ = R1.take(sh3)
        TT("pool", tc_, Bnr, bc(cfr), ALU.mult, ["Bnr", "cfr"], "w1tc")
        TT("pool", td_, Bni, bc(cfi), ALU.mult, ["Bni", "cfi"], "w1td")
        TT("pool", BbR, tc_, td_, ALU.subtract, ["w1tc", "w1td"], "BbR")
        TT("pool", tc_, Bni, bc(cfr), ALU.mult, ["Bni", "cfr"], "w1tc")
        TT("pool", td_, Bnr, bc(cfi), ALU.mult, ["Bnr", "cfi"], "w1td")
        TT("pool", BbI, tc_, td_, ALU.add, ["w1tc", "w1td"], "BbI")
        W1S = self.LF.bitcast(BF16).rearrange("p (a s b c) -> p a s b c", a=4, s=LCH, b=2)
        MS("pool", W1S[:, 0, 0, 0, 0:2], 0.0, "LFfree", r=["Lp", "Fp"])
        p.last_write["Lp"] = p.last_write["LFfree"]; p.last_write["Fp"] = p.last_write["LFfree"]
        v4 = lambda t: t.rearrange("p (a b) c -> p a (b c)", a=4)
        for s in range(LCH):
            kc, ks = "S%dcs" % s, "S%dsn" % s
            TT("pool", tc_, BbR, bc(cosS[s]), ALU.mult, ["BbR", kc], "w1tc")
            TT("pool", td_, BbI, bc(sinS[s]), ALU.mult, ["BbI", ks], "w1td")
            TT("pool", W1S[:, :, s, 0, :], v4(tc_), v4(td_), ALU.add, ["w1tc", "w1td", "LFfree"], ("W1S", s, 0))
            TT("pool", tc_, BbI, bc(cosS[s]), ALU.mult, ["BbI", kc], "w1tc")
            TT("pool", td_, BbR, bc(sinS[s]), ALU.mult, ["BbR", ks], "w1td")
            TT("pool", W1S[:, :, s, 1, :], v4(tc_), v4(td_), ALU.subtract, ["w1tc", "w1td", "LFfree"], ("W1S", s, 1))
        for qh in range(4):
            for h in range(2):
                b = self.bank()
                pt = self.bk(b).bitcast(BF16)
                n = 0
                for s in range(4 * h, 4 * h + 4):
                    for part in range(2):
                        p.op("pe", lambda e, pt=pt, n=n, qh=qh, s=s, part=part: e.transpose(
                            pt[:, 128 * n:128 * (n + 1)], W1S[:, qh, s, part, :], self.ident_b),
                            reads=[("W1S", s, part), "ident_b"], writes=[("bank", b)], inc=(n == 7))
                        n += 1
                CP("act", W1[:, qh, 4 * h:4 * h + 4, :, :].rearrange("p a b c -> p (a b c)"), pt, [("bank", b)], ("W1", qh, h))
        self.W1_keys = [("W1", qh, h) for qh in range(4) for h in range(2)]
        c7 = cosS[LCH - 1]; s7 = sinS[LCH - 1]
        kc7, ks7 = "S%dcs" % (LCH - 1), "S%dsn" % (LCH - 1)
        sh16 = [128, 16, 16]
        bc16 = lambda t: t.unsqueeze(2).to_broadcast(sh16)
        tav = ta[:, :, 0:16]; tbv = tb[:, :, 0:16]
        h0r = self.h0S[:, 0, :, :]; h0i = self.h0S[:, 1, :, :]
        TT("dve", tav, h0r, bc16(c7), ALU.mult, self.h0S_keys + [kc7, "w4ta"], "w4ta")
        TT("dve", tbv, h0i, bc16(s7), ALU.mult, self.h0S_keys + [ks7, "w4tb"], "w4tb")
        TT("dve", self.h0S_bf[:, 0, :, :], tav, tbv, ALU.add, ["w4ta", "w4tb"], "h0S_bf")
        TT("dve", tav, h0i, bc16(c7), ALU.mult, self.h0S_keys + [kc7], "w4ta")
        TT("dve", tbv, h0r, bc16(s7), ALU.mult, self.h0S_keys + [ks7], "w4tb")
        TT("dve", self.h0S_bf[:, 1, :, :], tav, tbv, ALU.subtract, ["w4ta", "w4tb", "h0S_bf"], "h0S_bf")
        for s in range(LCH):
            for (Wt, cr_t, ci_t, kr, ki, nm) in (
                (W4a, cosS[s], sinS[s], "S%dcs" % s, "S%dsn" % s, "W4a"),
                (W4b, lrotr[s + 1], lroti[s + 1], "lrotr%d" % (s + 1), "lroti%d" % (s + 1), "W4b"),
            ):
                TT("dve", ta, CTr, bc(cr_t), ALU.mult, kCTr + [kr], "w4ta")
                TT("dve", tb, CTi, bc(ci_t), ALU.mult, kCTi + [ki], "w4tb")
                TT("dve", Wt[:, :, s, 0, :], ta, tb, ALU.subtract, ["w4ta", "w4tb"], (nm, s, 0))
                TT("dve", ta, CTr, bc(ci_t), ALU.mult, kCTr + [ki], "w4ta")
                TT("dve", tb, CTi, bc(cr_t), ALU.mult, kCTi + [kr], "w4tb")
                TT("dve", ta, ta, tb, ALU.add, ["w4ta", "w4tb"], "w4ta")
                TS("dve", Wt[:, :, s, 1, :], ta, -1.0, ALU.mult, ["w4ta"], (nm, s, 1))
        self.W4_keys = [(nm, s, part) for nm in ("W4a", "W4b") for s in range(LCH) for part in range(2)]
        self.mur = RS.take([128, 16, NLEV]); self.mui = RS.take([128, 16, NLEV]); self.muni = RS.take([128, 16, NLEV])
        angc = RS.take([128, 16, NLEV]); angs = RS.take([128, 16, NLEV]); rmag = RS.take([128, 16, NLEV])
        CP("dve", angc[:, :, 0], cosS[LCH], ["S%dcs" % LCH], ("angc", 0))
        CP("dve", angs[:, :, 0], sinS[LCH], ["S%dsn" % LCH], ("angs", 0))
        sa = tS(); sb_ = tS()
        for j in range(NLEV):
            if j >= 1:
                TT("dve", sa, angc[:, :, j - 1], angc[:, :, j - 1], ALU.mult, [("angc", j - 1)], "sq_a")
                TT("dve", sb_, angs[:, :, j - 1], angs[:, :, j - 1], ALU.mult, [("angs", j - 1)], "sq_b")
                TT("dve", angc[:, :, j], sa, sb_, ALU.subtract, ["sq_a", "sq_b"], ("angc", j))
                TT("dve", sa, angc[:, :, j - 1], angs[:, :, j - 1], ALU.mult, [("angc", j - 1), ("angs", j - 1)], "sq_a")
                TS("dve", angs[:, :, j], sa, 2.0, ALU.mult, ["sq_a"], ("angs", j))
            ACT(rmag[:, :, j], adtS, AF.Exp, ["adtS"], ("rmag", j), scale=float(LCH * (1 << j)))
            TT("dve", self.mur[:, :, j], rmag[:, :, j], angc[:, :, j], ALU.mult, [("rmag", j), ("angc", j)], ("mur", j))
            TT("dve", self.mui[:, :, j], rmag[:, :, j], angs[:, :, j], ALU.mult, [("rmag", j), ("angs", j)], ("mui", j))
        for j in range(NLEV):
            TS("dve", self.muni[:, :, j], self.mui[:, :, j], -1.0, ALU.mult, [("mui", j)], ("muni", j))
        self.mu_keys = [(nm, j) for nm in ("mur", "mui", "muni") for j in range(NLEV)]
        self.rp8 = RS.take([128, 16, LCH]); self.rp4 = RS.take([128, 16, 4])
        CP("dve", self.rp8, self.rhoS.unsqueeze(2).to_broadcast([128, 16, LCH]), ["rhoS"], "rp8")
        MS("dve", self.rp8[:, :, 0:1], 0.0, "rp8", r=["rp8"])
        CP("dve", self.rp4, self.rhoS.unsqueeze(2).to_broadcast([128, 16, 4]), ["rhoS"], "rp4")
        MS("dve", self.rp4[:, :, 0:1], 0.0, "rp4", r=["rp4"])
    def s5_main(self, u_bf):
        nc, p = self.nc, self.p
        R0, R1 = self.R0, self.R1
        TT, TS, STT, ACT, CP, MS = self.TT, self.TS, self.STT, self.ACT, self.CP, self.MS
        W1, W4a, W4b = self.W1, self.W4a, self.W4b
        cosS, sinS, nsinS, lamr, lami = self.cosS, self.sinS, self.nsinS, self.lamr, self.lami
        z2 = self.z2
        NSET = 2
        pat = [R1.take([128, 512]) for i in range(NSET)]
        pats = [R1.take([128, 64]) for i in range(NSET)]
        gzf = [R1.take([128, 512]) for i in range(4)]
        gzfs = [R1.take([128, 2, 64]) for i in range(2)]
        gzb = [R1.take([128, 2, T], BF16) for i in range(NSET)]
        HA = [R1.take([128, 2, PAD + NCH]) for i in range(NSET)]
        HB = [R1.take([128, 2, PAD + NCH]) for i in range(NSET)]
        Hpb = [R1.take([128, 2, NCH], BF16) for i in range(NSET)]
        for i in range(NSET):
            MS("pool", HA[i], 0.0, ("HA", i))
            MS("pool", HB[i], 0.0, ("HB", i))
        yf = R1.take([128, 512]); gq = R1.take([128, 512]); ga = R1.take([128, 512]); gt = gq
        VB = [0, 1]
        nunit = [0]
        SVB = 2
        YB = {0: 4, 512: 5, 1024: 6, 1536: 7}
        PIECES = self.PIECES
        nseg = [0]

        def stageA(q):
            qh, ql = q // 4, q % 4
            hb = q % NSET
            kw = dict(tile_position=(96, 0)) if ql == 3 else {}
            p.op("act", lambda e: e.activation(out=pat[hb].rearrange("p (c s) -> p c s", s=LCH),
                                               in_=self.rp8[:, q:q + 1, :].to_broadcast([128, 512 // LCH, LCH]), func=AF.Copy),
                 reads=["rp8"], writes=[("pat", hb)])
            p.op("act", lambda e: e.activation(out=pats[hb].rearrange("p (c s) -> p c s", s=4),
                                               in_=self.rp4[:, q:q + 1, :].to_broadcast([128, 64 // 4, 4]), func=AF.Copy),
                 reads=["rp4"], writes=[("pats", hb)])
            for (c0, w) in PIECES:
                samp = (w == 64)
                L = 4 if samp else LCH
                sl = nseg[0] % 2
                nseg[0] += 1
                for part in range(2):
                    if samp:
                        vb = SVB
                        vv = self.bk(SVB)[:, 64 * part:64 * part + 64]
                    else:
                        vb = VB[nunit[0] % 2]
                        vv = self.bk(vb)
                    ug = nunit[0] % 4
                    nunit[0] += 1
                    for s in range(L):
                        p.op("pe", lambda e, vv=vv, s=s, part=part, qh=qh, ql=ql, c0=c0, w=w, L=L, kw=kw: e.matmul(
                            vv[:, s:w:L], W1[32 * ql:32 * ql + 32, qh, s, part, :],
                            u_bf[32 * ql:32 * ql + 32, qh, c0 + s:c0 + w:L], start=True, stop=True, **kw),
                            reads=self.W1_keys + [("u_bf", qh, c0)], writes=[("bank", vb)], inc=(s == L - 1))
                    if not samp:
                        go = gzf[ug]; gkey = ("gzf", ug)
                        p.op("dve", lambda e, go=go, hb=hb, vv=vv: e.tensor_tensor_scan(
                            out=go, data0=pat[hb], data1=vv, initial=0.0, op0=ALU.mult, op1=ALU.add),
                            reads=[("pat", hb), ("bank", vb)], writes=[gkey])
                        CP("act", gzb[hb][:, part, c0:c0 + w], go, [gkey], ("gzb", hb, c0, part))
                        k0 = PAD + c0 // LCH
                        nchunk = w // LCH
                        CP("pool", HA[hb][:, part, k0:k0 + nchunk], go[:, LCH - 1:512:LCH], [gkey], ("HA", hb))
                    else:
                        go = gzfs[sl][:, part, :]; gkey = ("gzfs", sl, part)
                        p.op("dve", lambda e, go=go, hb=hb, vv=vv: e.tensor_tensor_scan(
                            out=go, data0=pats[hb], data1=vv, initial=0.0, op0=ALU.mult, op1=ALU.add),
                            reads=[("pats", hb), ("bank", vb)], writes=[gkey])
                        CP("act", gzb[hb][:, part, c0:c0 + w], go, [gkey], ("gzb", hb, c0, part))
                if samp:
                    go = gzfs[sl]
                    gkeys = [("gzfs", sl, 0), ("gzfs", sl, 1)]
                    er = go[:, 0, 3:64:4]; ei = go[:, 1, 3:64:4]
                    c3 = cosS[3][:, q:q + 1]; s3 = sinS[3][:, q:q + 1]; ns3 = nsinS[3][:, q:q + 1]
                    l4r = lamr[4][:, q:q + 1]; l4i = lami[4][:, q:q + 1]; nl4i = self.nlami4[:, q:q + 1]
                    kk = gkeys + ["S3cs", "S3sn", "nS3sn", "lamr4", "lami4", "nlami4"] + self.h0S_keys
                    fr = self.Hfin[:, 0, q, 1:17]; fi = self.Hfin[:, 1, q, 1:17]
                    h0r = self.h0S[:, 0, q, :]; h0i = self.h0S[:, 1, q, :]
                    fk = ("Hfin", q)
                    TS("dve", fr, er, c3, ALU.mult, kk, fk)
                    STT("dve", fr, ei, ns3, fr, ALU.mult, ALU.add, kk + [fk], fk)
                    STT("dve", fr, h0r, l4r, fr, ALU.mult, ALU.add, kk + [fk], fk)
                    STT("dve", fr, h0i, nl4i, fr, ALU.mult, ALU.add, kk + [fk], fk)
                    TS("dve", fi, er, s3, ALU.mult, kk + [fk], fk)
                    STT("dve", fi, ei, c3, fi, ALU.mult, ALU.add, kk + [fk], fk)
                    STT("dve", fi, h0r, l4i, fi, ALU.mult, ALU.add, kk + [fk], fk)
                    STT("dve", fi, h0i, l4r, fi, ALU.mult, ALU.add, kk + [fk], fk)

        def stageB(q):
            hb = q % NSET
            src, dst = HA[hb], HB[hb]
            skey, dkey = ("HA", hb), ("HB", hb)
            for j in range(NLEV):
                d = 1 << j
                mr = self.mur[:, q, j:j + 1]; mi = self.mui[:, q, j:j + 1]; mni = self.muni[:, q, j:j + 1]
                mk = [("mur", j), ("mui", j), ("muni", j)]
                S0 = src[:, 0, PAD:PAD + NCH]; S1 = src[:, 1, PAD:PAD + NCH]
                Z0 = src[:, 0, PAD - d:PAD + NCH - d]; Z1 = src[:, 1, PAD - d:PAD + NCH - d]
                D0 = dst[:, 0, PAD:PAD + NCH]; D1 = dst[:, 1, PAD:PAD + NCH]
                STT("dve", D0, Z1, mni, S0, ALU.mult, ALU.add, [skey] + mk, dkey)
                STT("dve", D1, Z0, mi, S1, ALU.mult, ALU.add, [skey, dkey] + mk, dkey)
                Zb = src[:, :, PAD - d:PAD + NCH - d]; Db = dst[:, :, PAD:PAD + NCH]
                STT("dve", Db, Zb, mr, Db, ALU.mult, ALU.add, [skey, dkey] + mk, dkey)
                src, dst = dst, src
                skey, dkey = dkey, skey
            CP("act", Hpb[hb], HA[hb][:, :, PAD - 1:PAD - 1 + NCH], [("HA", hb)], ("Hpb", hb))
            hr_ = HA[hb][:, 0, PAD + NCH - 1:PAD + NCH]; hi_ = HA[hb][:, 1, PAD + NCH - 1:PAD + NCH]
            c7 = cosS[LCH - 1][:, q:q + 1]; s7 = sinS[LCH - 1][:, q:q + 1]; ns7 = nsinS[LCH - 1][:, q:q + 1]
            kk7 = [("HA", hb), "S%dcs" % (LCH - 1), "S%dsn" % (LCH - 1), "nS%dsn" % (LCH - 1)]
            fr0 = self.Hfin[:, 0, q, 0:1]; fi0 = self.Hfin[:, 1, q, 0:1]
            TS("dve", fr0, hr_, c7, ALU.mult, kk7, ("Hfin0", q))
            STT("dve", fr0, hi_, ns7, fr0, ALU.mult, ALU.add, kk7 + [("Hfin0", q)], ("Hfin0", q))
            TS("dve", fi0, hr_, s7, ALU.mult, kk7 + [("Hfin0", q)], ("Hfin0", q))
            STT("dve", fi0, hi_, c7, fi0, ALU.mult, ALU.add, kk7 + [("Hfin0", q)], ("Hfin0", q))

        def stageC(q):
            qh, ql = q // 4, q % 4
            hb = q % NSET
            kw = dict(tile_position=(0, 96)) if ql == 3 else {}
            for (c0, w) in PIECES:
                samp = (w == 64)
                L = 4 if samp else LCH
                if samp:
                    ybank = SVB
                    yall = self.bk(SVB)[:, 128:192]
                else:
                    ybank = YB[c0]
                    yall = self.bk(ybank)
                for s in range(L):
                    outv = yall[32 * ql:32 * ql + 32, s:w:L]
                    if samp:
                        hr = self.h0S_bf[:, 0, q, :]; hi = self.h0S_bf[:, 1, q, :]
                        hkeys = ["h0S_bf"]
                    else:
                        k0 = c0 // LCH
                        hr = Hpb[hb][:, 0, k0:k0 + w // L]; hi = Hpb[hb][:, 1, k0:k0 + w // L]
                        hkeys = [("Hpb", hb)]
                    ops = [
                        (W4a[:, q, s, 0, :], gzb[hb][:, 0, c0 + s:c0 + w:L]),
                        (W4a[:, q, s, 1, :], gzb[hb][:, 1, c0 + s:c0 + w:L]),
                        (W4b[:, q, s, 0, :], hr),
                        (W4b[:, q, s, 1, :], hi),
                    ]
                    for i, (lh, rh) in enumerate(ops):
                        last = (i == 3 and s == L - 1)
                        p.op("pe", lambda e, outv=outv, lh=lh, rh=rh, i=i, kw=kw: e.matmul(
                            outv, lh, rh, start=(i == 0), stop=(i == 3), **kw),
                            reads=self.W4_keys + [("gzb", hb, c0, 0), ("gzb", hb, c0, 1)] + hkeys, writes=[("bank", ybank)], inc=last)

        def stageY(qh):
            for (c0, w) in PIECES:
                samp = (w == 64)
                if samp:
                    ybank = SVB; ysrc = self.bk(SVB)[:, 128:192]
                else:
                    ybank = YB[c0]; ysrc = self.bk(ybank)
                yv = yf[:, 0:w]; qv = gq[:, 0:w]; av = ga[:, 0:w]; tv = gt[:, 0:w]
                STT("dve", yv, u_bf[:, qh, c0:c0 + w], self.dS5[:, qh:qh + 1], ysrc, ALU.mult, ALU.add,
                    [("u_bf", qh, c0), "dS5", ("bank", ybank)], "s5yf")
                if qh == 0:
                    self.tap("ys5_%d" % c0, yv, ["s5yf"])
                ACT(qv, yv, AF.Square, ["s5yf"], "s5gq")
                ACT(qv, qv, AF.Identity, ["s5gq"], "s5gq", scale=GC, bias=1.0)
                TT("pool", av, qv, yv, ALU.mult, ["s5gq", "s5yf"], "s5ga")
                ACT(qv, av, AF.Tanh, ["s5ga"], "s5gq", scale=GK)
                STT("dve", z2[:, qh, c0:c0 + w], qv, 1.0, yv, ALU.add, ALU.mult, ["s5gq", "s5yf"], ("z2", qh, c0))

        for qh in range(4):
            qs = [4 * qh + i for i in range(4)]
            stageA(qs[0]); stageB(qs[0])
            for i in range(1, 4):
                stageA(qs[i])
                stageC(qs[i - 1])
                stageB(qs[i])
            stageC(qs[3])
            stageY(qh)

    def gelu2(self, yv, qv, av, tv, outv, ky, kq, ka, kt, kout):
        self.ACT(qv, yv, AF.Square, [ky], kq)
        self.STT("dve", av, qv, GK * GC, yv, ALU.mult, ALU.mult, [kq, ky], ka)
        self.STT("dve", av, yv, GK, av, ALU.mult, ALU.add, [ky, ka], ka)
        self.ACT(tv, av, AF.Tanh, [ka], kt)
        self.STT("dve", outv, tv, 1.0, yv, ALU.add, ALU.mult, [kt, ky], kout)

    def lru_stage(self):
        nc, p, I, O = self.nc, self.p, self.I, self.O
        R0, R1 = self.R0, self.R1
        TT, TS, STT, ACT, CP, MS = self.TT, self.TS, self.STT, self.ACT, self.CP, self.MS
        NCK = dict(allow_slow_non_contiguous=True)
        xT, z2 = self.xT, self.z2
        PIECES = self.PIECES
        self.free_banks = list(range(8))
        LRUc = self.LRUc
        LK = self.LRUc_keys
        baT = R0.take([128, 8]); bxT = R0.take([128, 8]); sc8 = R0.take([128, 8]); hsc8 = R0.take([128, 8])
        TS("dve", baT, LRUc[:, :, 69], 0.5, ALU.mult, LK, "baT")
        TS("dve", bxT, LRUc[:, :, 70], 0.5, ALU.mult, LK, "bxT")
        ACT(sc8, LRUc[:, :, 71], AF.Exp, LK, "sc8", scale=-1.0)
        ACT(sc8, sc8, AF.Ln, ["sc8"], "sc8", bias=1.0)
        TS("dve", hsc8, sc8, -4.0, ALU.mult, ["sc8"], "hsc8")
        TS("dve", sc8, sc8, -8.0, ALU.mult, ["sc8", "hsc8"], "sc8")
        Wg = R0.take([128, 8, 2, 128], BF16)
        MS("pool", Wg, 0.0, "Wg")
        for gi, nm in ((0, "lru_wa"), (1, "lru_wx")):
            v = I[nm].rearrange("(j two) i o -> two i j o", two=2)
            for par in range(2):
                p.dma("pool", Wg[64 * par:64 * par + 64, :, gi, 64 * par:64 * par + 64], v[par], "d_wg", writes=["Wg"])
        p.last_write["Wg"] = ("d_wg", p.count["d_wg"])
        hfinL = R0.take([128, 8, 17])
        self.merged2 = R1.take([128, 8, T], BF16)
        merged2 = self.merged2
        self.merged_end = R1.pos
        xl_sb = R1.take([128, 3 + TP]); xs_sb = R1.take([128, 16, 7])
        abuf = R1.take([128, T]); a2buf = R1.take([128, T]); ixbuf = R1.take([128, T])
        hbuf = [R1.take([128, T])]
        tail_sb = R1.take([NTAIL, D])
        MS("pool", xl_sb[:, 0:3], 0.0, ("xl", -1))
        NP = 2
        ytmp = [R1.take([128, 512]) for i in range(2)]
        NXC = 5
        xc = [R1.take([128, 512]) for i in range(NXC)]
        xcb = [R1.take([128, 512], BF16) for i in range(2)]
        rp_ = [R1.take([128, 512]) for i in range(2)]
        ip_ = [R1.take([128, 512]) for i in range(2)]
        glp = [R1.take([128, 512]) for i in range(2)]
        gsp = [R1.take([128, 512]) for i in range(2)]
        gbp = [R1.take([128, 512]) for i in range(2)]
        t1p = [R1.take([128, 512]) for i in range(2)]
        t16 = R1.take([128, 16])
        wsl = [R1.take([128, 8, 128], BF16) for i in range(2)]
        wsg = [R1.take([128, 8, 2, 128], BF16) for i in range(2)]
        wgl = [R1.take([128, 4, 2, 128], BF16) for i in range(2)]
        hb = hbuf[0]
        XB = [0, 1]; GAB = [2, 3]; GXB = [4, 5]; POSTB = [6, 7]
        allp = lambda nm: [(nm, pi) for pi in range(len(PIECES))]

        def load_pre(j):
            sl = j % 2
            p.dma("pool", wsl[sl], I["w_in"][:, 128 * j:128 * (j + 1)].rearrange("(k p) n -> p k n", p=128), "d_wsl%d" % sl,
                  writes=[("wsl", sl)])

        def load_post(j):
            sl = j % 2
            p.dma("pool", wsg[sl].rearrange("p k g n -> p k (g n)"),
                  I["w_in"][:, 1536 + 256 * j:1536 + 256 * (j + 1)].rearrange("(k p) n -> p k n", p=128), "d_wsg%d" % sl,
                  writes=[("wsg", sl, 0), ("wsg", sl, 1)])
            p.dma("pool", wgl[sl].rearrange("p k g n -> p k (g n)"),
                  I["w_glu"][:, 256 * j:256 * (j + 1)].rearrange("(k p) n -> p k n", p=128), "d_wgl%d" % sl,
                  writes=[("wgl", sl, 0), ("wgl", sl, 1)])

        items = [dict(j=j, pi=pi, c0=c0, w=w) for j in range(8) for pi, (c0, w) in enumerate(PIECES)]

        def P0(t, it):
            j, c0, w = it["j"], it["c0"], it["w"]
            sl = j % 2
            if it["pi"] == 0 and j + 1 < 8:
                load_pre(j + 1)
            b = XB[t % 2]
            for k in range(8):
                p.op("pe", lambda e, k=k, b=b, sl=sl, c0=c0, w=w: e.matmul(
                    self.bk(b)[:, 0:w], wsl[sl][:, k, :], xT[:, k, c0:c0 + w], start=(k == 0), stop=(k == 7)),
                    reads=[("wsl", sl)] + self.xT_keys(c0, w), writes=[("bank", b)], inc=(k == 7))
            if it["pi"] == len(PIECES) - 1:
                tb = POSTB[1]
                for k in range(8):
                    p.op("pe", lambda e, k=k, tb=tb, sl=sl: e.matmul(self.bk(tb)[0:NTAIL, 0:128], xT[:, k, TAIL0:T], wsl[sl][:, k, :],
                                                                     start=(k == 0), stop=(k == 7)),
                         reads=[("wsl", sl)] + self.xT_keys(TAIL0, NTAIL), writes=[("bank", tb)], inc=(k == 7))
                CP("act", tail_sb[:, 128 * j:128 * (j + 1)], self.bk(tb)[0:NTAIL, 0:128], [("bank", tb)], ("tail", j))

        def P1(t, it):
            j, pi, c0, w = it["j"], it["pi"], it["c0"], it["w"]
            b = XB[t % 2]
            ps = self.bk(b)[:, 0:w]
            yv = ytmp[t % 2][:, 0:w]
            ACT(yv, ps, AF.Identity, [("bank", b)] + LK, ("ytmp", t % 2), scale=LRUc[:, j, 67:68], bias=LRUc[:, j, 68:69])
            if w != 64:
                CP("act", xl_sb[:, 3 + c0:3 + c0 + w], ps, [("bank", b)], ("xl", pi))
            else:
                CP("pool", xs_sb[:, :, 0:3], LRUc[:, j, 0:48].rearrange("p (b k) -> p b k", k=3), LK, ("xs", "st"))
                CP("act", xs_sb[:, :, 3:7], ps.rearrange("p (b s) -> p b s", s=4), [("bank", b)], ("xs", "new"))

        def P2(t, it):
            j, pi, c0, w = it["j"], it["pi"], it["c0"], it["w"]
            cw = [LRUc[:, j, 64 + k:65 + k] for k in range(4)]
            yk = ("ytmp", t % 2); xk_ = ("xc", t % NXC)
            yv = ytmp[t % 2][:, 0:w]; xcv = xc[t % NXC][:, 0:w]
            if w != 64:
                xk = [("xl", pi), ("xl", pi - 1), yk] + LK
                STT("dve", yv, xl_sb[:, c0 + 2:c0 + 2 + w], cw[2], yv, ALU.mult, ALU.add, xk, yk)
                STT("dve", yv, xl_sb[:, c0 + 1:c0 + 1 + w], cw[1], yv, ALU.mult, ALU.add, xk, yk)
                STT("dve", xcv, xl_sb[:, c0:c0 + w], cw[0], yv, ALU.mult, ALU.add, xk, xk_)
            else:
                xk = [("xs", "st"), ("xs", "new"), yk] + LK
                y3 = yv.rearrange("p (b s) -> p b s", s=4); xc3 = xcv.rearrange("p (b s) -> p b s", s=4)
                STT("dve", y3, xs_sb[:, :, 2:6], cw[2], y3, ALU.mult, ALU.add, xk, yk)
                STT("dve", y3, xs_sb[:, :, 1:5], cw[1], y3, ALU.mult, ALU.add, xk, yk)
                STT("dve", xc3, xs_sb[:, :, 0:4], cw[0], y3, ALU.mult, ALU.add, xk, xk_)
            if j == 0:
                self.tap("xc_%d" % c0, xcv, [xk_])
            CP("pool", xcb[t % 2][:, 0:w], xcv, [xk_], ("xcb", t % 2))

        def P3(t, it):
            j, w = it["j"], it["w"]
            for (bb, gi) in ((GAB[t % 2], 0), (GXB[t % 2], 1)):
                p.op("pe", lambda e, bb=bb, gi=gi, j=j, t=t, w=w: e.matmul(self.bk(bb)[:, 0:w], Wg[:, j, gi, :], xcb[t % 2][:, 0:w],
                                                                          start=True, stop=True),
                     reads=["Wg", ("xcb", t % 2)], writes=[("bank", bb)])

        def P4(t, it):
            j, w = it["j"], it["w"]
            ACT(rp_[t % 2][:, 0:w], self.bk(GAB[t % 2])[:, 0:w], AF.Tanh, [("bank", GAB[t % 2]), "baT"], ("rp", t % 2),
                scale=0.5, bias=baT[:, j:j + 1])
            ACT(ip_[t % 2][:, 0:w], self.bk(GXB[t % 2])[:, 0:w], AF.Tanh, [("bank", GXB[t % 2]), "bxT"], ("ip", t % 2),
                scale=0.5, bias=bxT[:, j:j + 1])

        def P5(t, it):
            j, pi, c0, w = it["j"], it["pi"], it["c0"], it["w"]
            rv = rp_[t % 2][:, 0:w]; iv = ip_[t % 2][:, 0:w]; xcv = xc[t % NXC][:, 0:w]
            ACT(abuf[:, c0:c0 + w], rv, AF.Exp, [("rp", t % 2), "hsc8"], ("abuf", pi), scale=hsc8[:, j:j + 1], bias=hsc8[:, j:j + 1])
            TT("pool", a2buf[:, c0:c0 + w], abuf[:, c0:c0 + w], abuf[:, c0:c0 + w], ALU.mult, [("abuf", pi)], ("a2buf", pi))
            STT("dve", ixbuf[:, c0:c0 + w], iv, 1.0, xcv, ALU.add, ALU.mult, [("ip", t % 2), ("xc", t % NXC)], ("ixbuf", pi))
            if pi == len(PIECES) - 1:
                mid_tile(j)

        post_queue = []

        def mid_tile(j):
            sl = j % 2
            ACT(a2buf, a2buf, AF.Sqrt, allp("a2buf"), "mh", scale=-0.25, bias=0.25)
            TT("dve", ixbuf, a2buf, ixbuf, ALU.mult, ["mh"] + allp("ixbuf"), "bterm")
            TT("dve", t16, abuf[:, TP:T:4], LRUc[:, j, 48:64], ALU.mult, allp("abuf") + LK, "t16")
            TT("dve", ixbuf[:, TP:T:4], ixbuf[:, TP:T:4], t16, ALU.add, ["bterm", "t16"], "bterm")
            MS("dve", abuf[:, TP:T:4], 0.0, "afix", r=allp("abuf") + allp("a2buf") + ["t16"])
            p.op("dve", lambda e: e.tensor_tensor_scan(out=hb, data0=abuf, data1=ixbuf, initial=0.0, op0=ALU.mult, op1=ALU.add),
                 reads=allp("abuf") + ["afix", "bterm"], writes=["hbuf"])
            for nm in ("abuf", "a2buf", "ixbuf"):
                for pi in range(len(PIECES)):
                    p.readers.setdefault((nm, pi), []).append(p.last_write["hbuf"])
            if j == 0:
                self.tap("hbuf", hb, ["hbuf"])
            CP("pool", hfinL[:, j, 0:1], hb[:, TP - 1:TP], ["hbuf"], ("hfinL", j, 0))
            CP("pool", hfinL[:, j, 1:17], hb[:, TP + 3:T:4], ["hbuf"], ("hfinL", j, 1))
            for pi, (c0, w) in enumerate(PIECES):
                post_queue.append((j, pi, c0, w))

        npq = [0]

        def post_piece(j, pi, c0, w):
            sl = j % 2
            i2 = npq[0] % 2
            npq[0] += 1
            glv = glp[i2][:, 0:w]; gsv = gsp[i2][:, 0:w]; gbv = gbp[i2][:, 0:w]; t1v = t1p[i2][:, 0:w]
            groups = [("gl", wsg[sl], 0, 8, xT, ("wsg", sl, 0)), ("gs", wsg[sl], 1, 8, xT, ("wsg", sl, 1)),
                      ("gb", wgl[sl], 1, 4, z2, ("wgl", sl, 1)), ("ga", wgl[sl], 0, 4, z2, ("wgl", sl, 0))]
            for gi_, (nm, wt, idx, nk, src, wkey) in enumerate(groups):
                bb = POSTB[gi_ % 2]
                for k in range(nk):
                    rk = self.xT_keys(c0, w) if src is xT else [("z2", k, c0)]
                    p.op("pe", lambda e, k=k, bb=bb, wt=wt, idx=idx, src=src, nk=nk, c0=c0, w=w: e.matmul(
                        self.bk(bb)[:, 0:w], wt[:, k, idx, :], src[:, k, c0:c0 + w], start=(k == 0), stop=(k == nk - 1)),
                        reads=[wkey] + rk, writes=[("bank", bb)], inc=(k == nk - 1))
                psv = self.bk(bb)[:, 0:w]
                if nm == "gl":
                    ACT(glv, psv, AF.Tanh, [("bank", bb)], ("glp", i2), scale=0.5)
                elif nm == "gs":
                    ACT(gsv, psv, AF.Tanh, [("bank", bb)], ("gsp", i2), scale=0.5)
                elif nm == "gb":
                    ACT(gbv, psv, AF.Tanh, [("bank", bb)], ("gbp", i2), scale=0.25)
                else:
                    ga_ps, ga_b = psv, bb
            STT("dve", t1v, gbv, 1.0, ga_ps, ALU.add, ALU.mult, [("gbp", i2), ("bank", ga_b)], ("t1p", i2))
            STT("dve", t1v, gsv, 1.0, t1v, ALU.add, ALU.mult, [("gsp", i2), ("t1p", i2)], ("t1p", i2))
            STT("dve", glv, glv, 1.0, hb[:, c0:c0 + w], ALU.add, ALU.mult, [("glp", i2), "hbuf"], ("glp", i2))
            STT("dve", merged2[:, j, c0:c0 + w], t1v, 0.25, glv, ALU.mult, ALU.add, [("t1p", i2), ("glp", i2)], ("merged2", j, c0))
            if pi == len(PIECES) - 1 and j + 2 < 8:
                load_post(j + 2)

        load_pre(0); load_post(0); load_post(1)
        N_ = len(items)
        stages = [(P5, 5), (P4, 4), (P1, 1), (P2, 2), (P3, 3), (P0, 0)]
        for t in range(N_ + 5):
            for fn, lag in stages:
                if 0 <= t - lag < N_:
                    fn(t - lag, items[t - lag])
            if post_queue:
                post_piece(*post_queue.pop(0))
        while post_queue:
            post_piece(*post_queue.pop(0))
        self.tap("merged2", merged2[:, 0, :], [("merged2", 0, c0) for c0, _ in PIECES])
        self.out_dma(O["lru_conv"], tail_sb, [("tail", j) for j in range(8)])
        self.out_dma(O["lru_h"], hfinL, [("hfinL", j, i) for j in range(8) for i in range(2)])

    def ln_tile(self, ps_flat, res, gB, bB, ytok, outv, rows, kps, kres, kg, kb, ky, kout, st6, mv, sd):
        p = self.p
        TT, TS, STT, ACT, CP = self.TT, self.TS, self.STT, self.ACT, self.CP
        yv = ytok[0:rows, :]
        p.op("act", lambda e: e.activation(out=yv, in_=ps_flat[0:rows, :], func=AF.Copy, scale=0.5), reads=kps, writes=[ky])
        STT("dve", yv, res[0:rows, :], ALPHA, yv, ALU.mult, ALU.add, [kres, ky], ky)
        for h in range(2):
            p.op("dve", lambda e, h=h: e.bn_stats(out=st6[0:rows, h, :], in_=yv[:, 512 * h:512 * (h + 1)]), reads=[ky], writes=[ky + ("st", h)])
        p.op("dve", lambda e: e.bn_aggr(out=mv[0:rows, :], in_=st6[0:rows, :, :].rearrange("p a b -> p (a b)")),
             reads=[ky + ("st", 0), ky + ("st", 1)], writes=[ky + ("mv",)])
        ACT(sd[0:rows, :], mv[0:rows, 1:2], AF.Sqrt, [ky + ("mv",)], ky + ("sd",), bias=LN_EPS)
        p.op("dve", lambda e: e.reciprocal(out=sd[0:rows, :], in_=sd[0:rows, :]), reads=[ky + ("sd",)], writes=[ky + ("sd",)])
        TS("dve", yv, yv, mv[0:rows, 0:1], ALU.subtract, [ky, ky + ("mv",), ky + ("sd",)], ky, s2=sd[0:rows, 0:1], op1=ALU.mult)
        TT("pool", yv, yv, gB[0:rows, :], ALU.mult, [ky, kg], ky)
        TT("dve", outv[0:rows, :], yv, bB[0:rows, :], ALU.add, [ky, kb], kout)

    def mix_stage(self):
        nc, p, I, O = self.nc, self.p, self.I, self.O
        R0, R1 = self.R0, self.R1
        TT, TS, STT, ACT, CP, MS = self.TT, self.TS, self.STT, self.ACT, self.CP, self.MS
        merged2 = self.merged2
        x1T = self.xT
        self.x1T = x1T
        lnc = self.lnc
        for i, nm in enumerate(("ln1_g", "ln1_b", "ln2_g", "ln2_b")):
            p.dma("sp", lnc[:, i, :], I[nm].partition_broadcast(128), "d_lnc%d" % i, writes=[("lnc", i)])
        wout = R1.take([128, 8, D], BF16)
        for h in range(2):
            p.dma("pool", wout[:, :, 512 * h:512 * (h + 1)], I["w_out"][:, 512 * h:512 * (h + 1)].rearrange("(k p) n -> p k n", p=128),
                  "d_wout%d" % h, writes=[("wout", h)])
        NX, NYT, NX1, NXB, NS_ = 3, 4, 3, 2, 4
        xtok = [R1.take([128, D]) for i in range(NX)]
        ytok = [R1.take([128, D]) for i in range(NYT)]
        x1tok = [R1.take([128, D]) for i in range(NX1)]
        x1b = [R1.take([128, D], BF16) for i in range(NXB)]
        st6 = [R1.take([128, 2, 6]) for i in range(NS_)]
        mv = [R1.take([128, 2]) for i in range(NS_)]
        sd = [R1.take([128, 1]) for i in range(NS_)]
        ntt = (T + 127) // 128
        TB = [4, 5, 6, 7]
        g1, b1 = lnc[:, 0, :], lnc[:, 1, :]
        items = [dict(tt=tt, r0=tt * 128, rows=min(128, T - tt * 128)) for tt in range(ntt)]

        def M0(t, it):
            r0, rows = it["r0"], it["rows"]
            s = t % NX; pp = t % 2
            p.dma("sp", xtok[s][0:rows, :], I["x"][r0:r0 + rows, :], "d_xtok%d" % s, writes=[("xtok", s)])
            for h in range(2):
                for k in range(8):
                    p.op("pe", lambda e, k=k, h=h, pp=pp, r0=r0, rows=rows: e.matmul(
                        self.ps[pp][0:rows, h, :], merged2[:, k, r0:r0 + rows], wout[:, k, 512 * h:512 * (h + 1)],
                        start=(k == 0), stop=(k == 7)),
                        reads=[("wout", h)] + [("merged2", k, c0) for (c0, w) in self.PIECES if c0 <= r0 < c0 + w],
                        writes=[("bank", 2 * pp + h)], inc=(k == 7))

        def M1(t, it):
            rows = it["rows"]
            pp = t % 2; sy = t % NYT; sx = t % NX; ss = t % NS_
            psf = self.ps[pp][:].rearrange("p a c -> p (a c)")
            yv = ytok[sy][0:rows, :]
            ky = ("ytok", sy)
            p.op("act", lambda e: e.activation(out=yv, in_=psf[0:rows, :], func=AF.Copy, scale=0.5),
                 reads=[("bank", 2 * pp), ("bank", 2 * pp + 1)], writes=[ky])
            STT("dve", yv, xtok[sx][0:rows, :], ALPHA, yv, ALU.mult, ALU.add, [("xtok", sx), ky], ky)
            for h in range(2):
                p.op("dve", lambda e, h=h: e.bn_stats(out=st6[ss][0:rows, h, :], in_=yv[:, 512 * h:512 * (h + 1)]),
                     reads=[ky], writes=[("st6", ss, h)])
            p.op("dve", lambda e: e.bn_aggr(out=mv[ss][0:rows, :], in_=st6[ss][0:rows, :, :].rearrange("p a b -> p (a b)")),
                 reads=[("st6", ss, 0), ("st6", ss, 1)], writes=[("mv", ss)])

        def M2(t, it):
            rows = it["rows"]
            sy = t % NYT; ss = t % NS_
            yv = ytok[sy][0:rows, :]
            ky = ("ytok", sy)
            ACT(sd[ss][0:rows, :], mv[ss][0:rows, 1:2], AF.Sqrt, [("mv", ss)], ("sd", ss), bias=LN_EPS)
            p.op("dve", lambda e: e.reciprocal(out=sd[ss][0:rows, :], in_=sd[ss][0:rows, :]), reads=[("sd", ss)], writes=[("sd", ss)])
            TS("dve", yv, yv, mv[ss][0:rows, 0:1], ALU.subtract, [ky, ("mv", ss), ("sd", ss)], ky, s2=sd[ss][0:rows, 0:1], op1=ALU.mult)

        def M3a(t, it):
            tt, r0, rows = it["tt"], it["r0"], it["rows"]
            sy = t % NYT; s1 = t % NX1
            yv = ytok[sy][0:rows, :]
            ky = ("ytok", sy)
            TT("pool", yv, yv, g1[0:rows, :], ALU.mult, [ky, ("lnc", 0)], ky)
            TT("dve", x1tok[s1][0:rows, :], yv, b1[0:rows, :], ALU.add, [ky, ("lnc", 1)], ("x1tok", s1))
            if tt == 0:
                self.tap("x1tok", x1tok[s1], [("x1tok", s1)])
            p.dma("sp", self.x1_scr[r0:r0 + rows, :], x1tok[s1][0:rows, :], "d_x1w%d" % s1, reads=[("x1tok", s1)], writes=[("x1scr", tt)])

        def M3b(t, it):
            rows = it["rows"]
            s1 = t % NX1; sb = t % NXB
            CP("act", x1b[sb][0:rows, :], x1tok[s1][0:rows, :], [("x1tok", s1)], ("x1b", sb))

        def M4(t, it):
            rows = it["rows"]
            sb = t % NXB
            b = TB[t % 4]
            it["b"] = b
            pt = self.bk(b).bitcast(BF16)
            for k in range(8):
                p.op("pe", lambda e, k=k, pt=pt, sb=sb, rows=rows: e.transpose(
                    pt[:, k * 128:k * 128 + rows], x1b[sb][0:rows, k * 128:(k + 1) * 128], self.ident_b[0:rows, 0:rows]),
                    reads=[("x1b", sb), "ident_b"], writes=[("bank", b)], inc=(k == 7))

        def M5(t, it):
            tt, r0, rows, b = it["tt"], it["r0"], it["rows"], it["b"]
            pt = self.bk(b).bitcast(BF16)
            src = pt.rearrange("p (k c) -> p k c", c=128)[:, :, 0:rows]
            CP("act", x1T[:, :, r0:r0 + rows], src, [("bank", b)], ("x1T", tt))

        stages = [(M3a, 3), (M1, 1), (M2, 2), (M3b, 3), (M5, 5), (M4, 4), (M0, 0)]
        N_ = len(items)
        for t in range(N_ + 5):
            for fn, lag in stages:
                if 0 <= t - lag < N_:
                    fn(t - lag, items[t - lag])
        self.tap("x1T", x1T[:, 0, :], [("x1T", tt) for tt in range(ntt)])

    def x1T_keys(self, c0, w):
        return [("x1T", tt) for tt in range(c0 // 128, (c0 + w - 1) // 128 + 1)]

    def ffn_stage(self):
        nc, p, I, O = self.nc, self.p, self.I, self.O
        R0, R1 = self.R0, self.R1
        TT, TS, STT, ACT, CP, MS = self.TT, self.TS, self.STT, self.ACT, self.CP, self.MS
        NCK = dict(allow_slow_non_contiguous=True)
        x1T, lnc = self.x1T, self.lnc
        NJ = DFF // 128
        QW = 576
        Fc = self.Fc
        FK = self.Fc_keys
        wdn = R1.take([128, NJ, D], BF16)
        for c in range(6):
            p.dma("pool", wdn[:, 4 * c:4 * c + 4, :], I["w_down"][512 * c:512 * (c + 1), :].rearrange("(k p) n -> p k n", p=128),
                  "d_wdn%d" % c, writes=[("wdn", c)])
        Gq = R1.take([128, NJ, QW], BF16)
        NSL = 4
        wup = [R1.take([128, 8, 2, 128], BF16) for i in range(NSL)]
        halo = R1.take([128, NJ, 2])
        MS("pool", halo, 0.0, "halo_init")
        NY, NQ, NG = 5, 3, 2
        a_sb = [R1.take([128, 2 + 512]) for i in range(2)]
        as_sb = R1.take([128, 16, 6])
        y0 = [R1.take([128, 512]) for i in range(NY)]
        qq = [R1.take([128, 512]) for i in range(NQ)]
        ag = [R1.take([128, 512]) for i in range(NG)]
        tailf = [R1.take([NTAIL, 512]) for i in range(2)]
        NB = 2
        x1tok = [R1.take([128, D]) for i in range(NB)]
        ytok = [R1.take([128, D]) for i in range(NB)]
        otok = ytok
        st6 = [R1.take([128, 2, 6]) for i in range(NB)]
        mv = [R1.take([128, 2]) for i in range(NB)]
        sd = [R1.take([128, 1]) for i in range(NB)]
        nld = [0]

        def load_wup(j):
            sl = nld[0] % NSL
            nld[0] += 1
            p.dma("pool", wup[sl].rearrange("p k g n -> p k (g n)"),
                  I["w_up"][:, 256 * j:256 * (j + 1)].rearrange("(k p) n -> p k n", p=128), "d_wup%d" % sl,
                  writes=[("wup", sl, 0), ("wup", sl, 1)])
            return sl

        npc = [0]
        ntile = [0]
        for n in range(4):
            q0 = 512 * n
            pieces = [(q0, 512, 0)] + ([(TP, NS, 512)] if n == 3 else [])
            RA = [0, 1]
            RG = [2, 3, 4, 5, 6]
            TAILB = 7
            items = []
            for j in range(NJ):
                for pi_, (c0, w, lc0) in enumerate(pieces):
                    items.append(dict(j=j, c0=c0, w=w, lc0=lc0, first=(pi_ == 0), last=(pi_ == len(pieces) - 1)))
            pending = [load_wup(0), load_wup(1), load_wup(2)]
            cur_sl = {}

            def S0(t, it):
                j, c0, w = it["j"], it["c0"], it["w"]
                if it["first"]:
                    cur_sl[j] = pending.pop(0)
                    if j + 3 < NJ:
                        pending.append(load_wup(j + 3))
                sl = cur_sl[j]
                bA = RA[t % 2]; bG = RG[t % 5]
                it["bA"], it["bG"] = bA, bG
                for (bb, gi) in ((bA, 0), (bG, 1)):
                    for k in range(8):
                        p.op("pe", lambda e, k=k, bb=bb, gi=gi, sl=sl, c0=c0, w=w: e.matmul(
                            self.bk(bb)[:, 0:w], wup[sl][:, k, gi, :], x1T[:, k, c0:c0 + w], start=(k == 0), stop=(k == 7)),
                            reads=[("wup", sl, gi)] + self.x1T_keys(c0, w), writes=[("bank", bb)], inc=(k == 7))
                if n == 3 and it["last"]:
                    for k in range(8):
                        p.op("pe", lambda e, k=k, sl=sl: e.matmul(self.bk(TAILB)[0:NTAIL, 0:128], x1T[:, k, TAIL0:T], wup[sl][:, k, 0, :],
                                                                  start=(k == 0), stop=(k == 7)),
                             reads=[("wup", sl, 0)] + self.x1T_keys(TAIL0, NTAIL), writes=[("bank", TAILB)], inc=(k == 7))

            def S1(t, it):
                j, c0, w, bA = it["j"], it["c0"], it["w"], it["bA"]
                samp = (w == NS)
                iy = t % NY; ia = t % 2
                fw = [Fc[:, j, 32 + k:33 + k] for k in range(3)]
                aps = self.bk(bA)[:, 0:w]
                yv = y0[iy][:, 0:w]
                ACT(yv, aps, AF.Identity, [("bank", bA)] + FK, ("y0", iy), scale=fw[2], bias=Fc[:, j, 35:36])
                if not samp:
                    ab = a_sb[ia]
                    CP("act", ab[:, 2:2 + w], aps, [("bank", bA)], ("a_sb", ia))
                    CP("pool", ab[:, 0:2], halo[:, j, :], ["halo_init", ("halo", j)], ("a_sbh", ia))
                    ak = [("a_sb", ia), ("a_sbh", ia), ("y0", iy)]
                    STT("dve", yv, ab[:, 1:1 + w], fw[1], yv, ALU.mult, ALU.add, ak, ("y0", iy))
                    STT("dve", yv, ab[:, 0:w], fw[0], yv, ALU.mult, ALU.add, ak, ("y0", iy))
                    CP("pool", halo[:, j, :], ab[:, w:w + 2], [("a_sb", ia), ("a_sbh", ia)], ("halo", j))
                else:
                    CP("act", as_sb[:, :, 2:6], aps.rearrange("p (b s) -> p b s", s=4), [("bank", bA)], ("as_sb", "new"))
                    CP("pool", as_sb[:, :, 0:2], Fc[:, j, 0:32].rearrange("p (b k) -> p b k", k=2), FK, ("as_sb", "st"))
                    ak = [("as_sb", "new"), ("as_sb", "st"), ("y0", iy)]
                    y3 = yv.rearrange("p (b s) -> p b s", s=4)
                    STT("dve", y3, as_sb[:, :, 1:5], fw[1], y3, ALU.mult, ALU.add, ak, ("y0", iy))
                    STT("dve", y3, as_sb[:, :, 0:4], fw[0], y3, ALU.mult, ALU.add, ak, ("y0", iy))
                if n == 3 and it["last"]:
                    tb = (j // 4) % 2
                    CP("act", tailf[tb][:, 128 * (j % 4):128 * (j % 4 + 1)], self.bk(TAILB)[0:NTAIL, 0:128], [("bank", TAILB)], ("tailf", tb, j % 4))
                    if j % 4 == 3:
                        sem = "o_tf%d" % tb
                        p.dma("sp", O["ffn_conv"][:, 512 * (j // 4):512 * (j // 4 + 1)], tailf[tb], sem,
                              reads=[("tailf", tb, i) for i in range(4)])
                        self.out_sems[sem] = p.count[sem]

            def S2(t, it):
                w = it["w"]
                iy = t % NY; iq = t % NQ
                yv = y0[iy][:, 0:w]; qv = qq[iq][:, 0:w]
                ACT(qv, yv, AF.Square, [("y0", iy)], ("qq", iq))
                ACT(qv, qv, AF.Identity, [("qq", iq)], ("qq", iq), scale=GC, bias=1.0)

            def S3(t, it):
                w = it["w"]
                iy = t % NY; iq = t % NQ; ig = t % NG
                TT("pool", ag[ig][:, 0:w], qq[iq][:, 0:w], y0[iy][:, 0:w], ALU.mult, [("qq", iq), ("y0", iy)], ("ag", ig))

            def S4(t, it):
                j, w, lc0, bG = it["j"], it["w"], it["lc0"], it["bG"]
                iy = t % NY; iq = t % NQ; ig = t % NG
                yv = y0[iy][:, 0:w]; qv = qq[iq][:, 0:w]; av = ag[ig][:, 0:w]
                gps = self.bk(bG)[:, 0:w]
                ACT(qv, av, AF.Tanh, [("ag", ig)], ("qq", iq), scale=GK)
                STT("dve", av, qv, 1.0, yv, ALU.add, ALU.mult, [("qq", iq), ("y0", iy)], ("ag", ig))
                TT("dve", Gq[:, j, lc0:lc0 + w], av, gps, ALU.mult, [("ag", ig), ("bank", bG)], ("Gq", j, lc0))

            N_ = len(items)
            stages = [(S4, 4), (S1, 1), (S2, 2), (S3, 3), (S0, 0)]
            for t in range(N_ + 4):
                for fn, lag in stages:
                    if 0 <= t - lag < N_:
                        fn(t - lag, items[t - lag])
            if n == 0:
                self.tap("Gq", Gq[:, 0, :], [("Gq", 0, 0)])
            tts = [4 * n + i for i in range(4)] + ([16] if n == 3 else [])
            for tt in tts:
                r0 = tt * 128
                rows = min(128, T - r0)
                lc = r0 - q0 if tt < 16 else 512
                s = ntile[0] % NB
                pp = ntile[0] % 2
                ntile[0] += 1
                p.dma("sp", x1tok[s][0:rows, :], self.x1_scr[r0:r0 + rows, :], "d_x1r%d" % s, reads=[("x1scr", tt)], writes=[("x1tok2", s)])
                psf = self.ps[pp][:].rearrange("p a c -> p (a c)")
                gkeys = [("Gq", j, 512 if tt == 16 else 0) for j in range(NJ)]
                for h in range(2):
                    for k in range(NJ):
                        p.op("pe", lambda e, k=k, h=h, pp=pp, lc=lc, rows=rows: e.matmul(
                            self.ps[pp][0:rows, h, :], Gq[:, k, lc:lc + rows], wdn[:, k, 512 * h:512 * (h + 1)],
                            start=(k == 0), stop=(k == NJ - 1)),
                            reads=[("wdn", k // 4), ("Gq", k, 512 if tt == 16 else 0)], writes=[("bank", 2 * pp + h)], inc=(k == NJ - 1))
                self.ln_tile(psf, x1tok[s], lnc[:, 2, :], lnc[:, 3, :], ytok[s], otok[s], rows,
                             [("bank", 2 * pp), ("bank", 2 * pp + 1)], ("x1tok2", s), ("lnc", 2), ("lnc", 3), ("ytok2", s), ("ytok2", s),
                             st6[s], mv[s], sd[s])
                sem = "o_y%d" % s
                p.dma("sp", O["y"][r0:r0 + rows, :], otok[s][0:rows, :], sem, reads=[("ytok2", s)])
                self.out_sems[sem] = p.count[sem]

    def finish(self):
        fw = [(s, v) for s, v in self.out_sems.items()]
        self.p.emit(final_waits=fw)
        self.st.close()
        print("arena peaks: R0 %d/%d words, R1 %d/%d words" % (self.R0.peak, self.R0.words, self.R1.peak, self.R1.words))
        return self.nc


def shard_inputs(inputs, c):
    f = lambda a: np.ascontiguousarray(a, dtype=np.float32)
    m = {}
    m["x"] = f(np.concatenate([inputs["x_prompt"][c], inputs["x_sample"][NSQ * c:NSQ * (c + 1)].reshape(NS, D)], axis=0))
    m["st_lru_conv"] = f(inputs["state_lru_conv"][0, NSQ * c:NSQ * (c + 1)].reshape(NSQ * 3, D))
    m["st_lru_h"] = f(inputs["state_lru_h"][0, NSQ * c:NSQ * (c + 1)])
    m["st_s5_re"] = f(inputs["state_s5_re"][0, NSQ * c:NSQ * (c + 1)].reshape(NSQ, 2048))
    m["st_s5_im"] = f(inputs["state_s5_im"][0, NSQ * c:NSQ * (c + 1)].reshape(NSQ, 2048))
    m["st_ffn_conv"] = f(inputs["state_ffn_conv"][0, NSQ * c:NSQ * (c + 1)].reshape(NSQ * 2, DFF))
    for k in ("w_in", "lru_conv_w", "lru_conv_b", "lru_wa", "lru_ba", "lru_wx", "lru_bx", "lru_lambda", "s5_a_re", "s5_a_im",
              "s5_log_dt", "s5_b_re", "s5_b_im", "s5_c_re", "s5_c_im", "s5_d", "w_glu", "w_out", "ln1_g", "ln1_b", "w_up",
              "ffn_conv_w", "ffn_conv_b", "w_down", "ln2_g", "ln2_b"):
        m[k] = f(inputs[k][0])
    wu = m["w_up"]
    m["w_up"] = f(np.stack([wu[:, :DFF].reshape(D, DFF // 128, 128), wu[:, DFF:].reshape(D, DFF // 128, 128)], axis=2).reshape(D, 2 * DFF))
    wi = m["w_in"]
    gl = wi[:, 1536:2560].reshape(D, 8, 128); gs = wi[:, 2560:3584].reshape(D, 8, 128)
    m["w_in"] = f(np.concatenate([wi[:, :1536], np.stack([gl, gs], axis=2).reshape(D, 2048)], axis=1))
    wg = m["w_glu"]
    m["w_glu"] = f(np.stack([wg[:, :1024].reshape(512, 8, 128), wg[:, 1024:].reshape(512, 8, 128)], axis=2).reshape(512, 2048))
    return m


_NC_CACHE = {}


def _get_nc():
    if "nc" not in _NC_CACHE:
        _NC_CACHE["nc"] = Builder().build()
    return _NC_CACHE["nc"]


def kernel(**inputs):
    nc = _get_nc()
    in_maps = [shard_inputs(inputs, c) for c in range(NCORES)]
    res = run_bass_kernel_spmd(nc, in_maps, core_ids=list(range(NCORES)))
    R = res.results
    B = NCORES
    y_p = np.zeros((B, TP, D), np.float32); y_s = np.zeros((B * NSQ, 4, D), np.float32)
    p_conv = np.zeros((1, B, 3, D), np.float32); p_h = np.zeros((1, B, D), np.float32)
    p_re = np.zeros((1, B, 32, 64), np.float32); p_im = np.zeros((1, B, 32, 64), np.float32)
    p_ffn = np.zeros((1, B, 2, DFF), np.float32)
    s_conv = np.zeros((1, B * NSQ, 3, D), np.float32); s_h = np.zeros((1, B * NSQ, D), np.float32)
    s_re = np.zeros((1, B * NSQ, 32, 64), np.float32); s_im = np.zeros((1, B * NSQ, 32, 64), np.float32)
    s_ffn = np.zeros((1, B * NSQ, 2, DFF), np.float32)
    for c in range(B):
        r = R[c]
        sl = slice(NSQ * c, NSQ * (c + 1))
        y_p[c] = r["y"][0:TP]
        y_s[sl] = r["y"][TP:].reshape(NSQ, 4, D)
        lc = r["o_lru_conv"]
        p_conv[0, c] = lc[0:3]
        s_conv[0, sl] = lc[3:].reshape(NSQ, 4, D)[:, 1:4]
        lh = r["o_lru_h"].transpose(2, 1, 0).reshape(17, D)
        p_h[0, c] = lh[0]
        s_h[0, sl] = lh[1:17]
        s5 = r["o_s5"].reshape(2, 64, 2, 16, 17).transpose(2, 4, 3, 0, 1)
        s5 = s5.reshape(2, 17, 32, 64)
        p_re[0, c] = s5[0, 0]
        p_im[0, c] = s5[1, 0]
        s_re[0, sl] = s5[0, 1:17]
        s_im[0, sl] = s5[1, 1:17]
        fc = r["o_ffn_conv"]
        p_ffn[0, c] = fc[1:3]
        s_ffn[0, sl] = fc[3:].reshape(NSQ, 4, DFF)[:, 2:4]
    return (y_p, y_s, p_conv, p_h, p_re, p_im, p_ffn, s_conv, s_h, s_re, s_im, s_ffn)
```

```python
import math
import contextlib
import numpy as np
import concourse.bass as bass
import concourse.mybir as mybir
from concourse.bass_utils import run_bass_kernel_spmd

F32 = mybir.dt.float32
BF16 = mybir.dt.bfloat16
AF = mybir.ActivationFunctionType
ALU = mybir.AluOpType

ENGINES = ("pe", "act", "dve", "pool", "sp")
NCORES = 8
TP = 2048
NSQ = 16
NS = 64
T = TP + NS
TAIL0 = TP - 3
NTAIL = T - TAIL0
D = 1024
DFF = 3072
ALPHA = 2.0 ** 0.25
LN_EPS = 1e-5
LCH = 8
NCH = TP // LCH
NLEV = 8
PAD = 128
PI = math.pi
GK = math.sqrt(2.0 / math.pi)
GC = 0.044715


class Prog:
    def __init__(self, nc):
        self.nc = nc
        self.streams = {e: [] for e in ENGINES}
        self.count = {}
        self.waited = {e: {} for e in ENGINES}
        self.last_write = {}
        self.readers = {}
        self.sem_names = set()
        self.epoch = "0"
        self.pending = {}

    def barrier(self):
        snap = list(self.count.items())
        for e in ENGINES:
            self.pending.setdefault(e, []).extend(snap)

    def op(self, eng, fn, reads=(), writes=(), inc=True, sem=None, amount=1):
        if sem is None:
            sem = "s_%s_%s" % (eng, self.epoch)
        self.sem_names.add(sem)
        deps = list(self.pending.pop(eng, ()))
        for k in reads:
            ev = self.last_write.get(k)
            if ev is not None:
                deps.append(ev)
        for k in writes:
            ev = self.last_write.get(k)
            if ev is not None:
                deps.append(ev)
            deps.extend(self.readers.get(k, ()))
        waits = {}
        for (s, v) in deps:
            if eng == "pe" and s.startswith("s_pe_"):
                continue
            if self.waited[eng].get(s, 0) >= v:
                continue
            if waits.get(s, 0) < v:
                waits[s] = v
        for s, v in waits.items():
            self.waited[eng][s] = v
        cur = self.count.get(sem, 0)
        val = cur + amount
        if inc:
            self.count[sem] = val
        ev = (sem, val)
        self.streams[eng].append((fn, list(waits.items()), (sem, amount) if inc else None))
        for k in reads:
            self.readers.setdefault(k, []).append(ev)
        for k in writes:
            self.last_write[k] = ev
            self.readers[k] = []
        return ev

    def dma(self, queue, out, in_, sem, reads=(), writes=(), **kw):
        def fn(e):
            return e.dma_start(out=out, in_=in_, **kw)
        return self.op(queue, fn, reads=reads, writes=writes, inc=True, sem=sem, amount=16)

    def emit(self, final_waits=()):
        nc = self.nc
        names = sorted(self.sem_names)
        with contextlib.ExitStack() as st:
            sems = {n: st.enter_context(nc.semaphore(n)) for n in names}
            block = st.enter_context(nc.Block())
            streams = self.streams

            def run(engh, lst, last):
                for fn, waits, inc in lst:
                    for s, v in waits:
                        engh.wait_ge(sems[s], v)
                    ins = fn(engh)
                    if inc is not None:
                        ins.then_inc(sems[inc[0]], inc[1])
                if last:
                    for s, v in final_waits:
                        engh.wait_ge(sems[s], v)

            @block.tensor
            def _(e):
                run(e, streams["pe"], False)

            @block.scalar
            def _(e):
                run(e, streams["act"], False)

            @block.vector
            def _(e):
                run(e, streams["dve"], False)

            @block.gpsimd
            def _(e):
                run(e, streams["pool"], False)

            @block.sync
            def _(e):
                run(e, streams["sp"], True)


class Arena:
    def __init__(self, tensor, words):
        self.t = tensor
        self.words = words
        self.pos = 0
        self.peak = 0

    def take(self, shape, dt=F32):
        n = 1
        for s in shape[1:]:
            n *= s
        esz = 4 if dt == F32 else 2
        w = (n * esz + 3) // 4
        w = (w + 7) // 8 * 8
        assert self.pos + w <= self.words, "arena overflow: need %d have %d" % (self.pos + w, self.words)
        v = self.t[0:shape[0], self.pos:self.pos + w]
        self.pos += w
        self.peak = max(self.peak, self.pos)
        if dt != F32:
            v = v.bitcast(dt)
        v = v[:, 0:n]
        if len(shape) > 2:
            names = " ".join("d%d" % i for i in range(len(shape) - 1))
            kw = {"d%d" % i: shape[i + 1] for i in range(len(shape) - 2)}
            v = v.rearrange("p (%s) -> p %s" % (names, names), **kw)
        return v


class StopBuild(Exception):
    pass


class Builder:
    def chk_stop(self, name, reads=()):
        if self.stop_after == name:
            self.p.barrier()
            d = self.dout("dbg_stop", [128, 4])
            t = self.R0.take([128, 4])
            self.MS("dve", t, 1.0, "stoptile")
            self.out_dma(d, t, ["stoptile"])
            raise StopBuild()

    def __init__(self, debug=(), stop_after=None):
        self.debug = set(debug)
        self.stop_after = stop_after
        self.nc = bass.Bass("TRN2", target_bir_lowering=False)
        self.p = Prog(self.nc)
        self.st = contextlib.ExitStack()
        self.out_sems = {}
        self.nout = 0
        self.free_banks = list(range(8))
        self.nbank = 0
        self.ntmp = 0

    def din(self, name, shape):
        return self.nc.dram_tensor(name, list(shape), F32, kind="ExternalInput").ap()

    def dout(self, name, shape, dt=F32):
        return self.nc.dram_tensor(name, list(shape), dt, kind="ExternalOutput").ap()

    def sb(self, name, shape, dt=F32):
        t = self.st.enter_context(self.nc.sbuf_tensor(name, list(shape), dt))
        return t[:]

    def out_dma(self, out, in_, reads, queue="sp", **kw):
        sem = "o_%d" % (self.nout % 8)
        self.nout += 1
        self.p.dma(queue, out, in_, sem, reads=reads, **kw)
        self.out_sems[sem] = self.p.count[sem]

    def tap(self, name, ap, reads):
        if name not in self.debug:
            return
        d = self.dout("dbg_" + name, list(ap.shape), ap.dtype)
        self.out_dma(d, ap, reads)

    def bank(self):
        b = self.free_banks[self.nbank % len(self.free_banks)]
        self.nbank += 1
        return b

    def bk(self, b):
        return self.ps[b // 2][:, b % 2, :]

    def TT(self, eng, out, a, b_, op, r, w):
        self.p.op(eng, lambda e: e.tensor_tensor(out=out, in0=a, in1=b_, op=op), reads=r, writes=[w])

    def TS(self, eng, out, a, s1, op0, r, w, s2=None, op1=None):
        if op1 is None:
            self.p.op(eng, lambda e: e.tensor_scalar(out=out, in0=a, scalar1=s1, scalar2=None, op0=op0), reads=r, writes=[w])
        else:
            self.p.op(eng, lambda e: e.tensor_scalar(out=out, in0=a, scalar1=s1, scalar2=s2, op0=op0, op1=op1), reads=r, writes=[w])

    def STT(self, eng, out, a, sc, b_, op0, op1, r, w):
        self.p.op(eng, lambda e: e.scalar_tensor_tensor(out=out, in0=a, scalar=sc, in1=b_, op0=op0, op1=op1), reads=r, writes=[w])

    def ACT(self, out, a, func, r, w, scale=1.0, bias=0.0):
        self.p.op("act", lambda e: e.activation(out=out, in_=a, func=func, scale=scale, bias=bias), reads=r, writes=[w])

    def CP(self, eng, out, a, r, w):
        if eng == "act":
            self.p.op("act", lambda e: e.copy(out=out, in_=a), reads=r, writes=[w])
        else:
            self.p.op(eng, lambda e: e.tensor_copy(out=out, in_=a), reads=r, writes=[w])

    def MS(self, eng, out, val, w, r=()):
        self.p.op(eng, lambda e: e.memset(out, val), reads=list(r), writes=[w])

    def build(self):
        nc, p = self.nc, self.p
        din = self.din
        I = {}
        for name, shape in (("x", [T, D]), ("st_lru_conv", [NSQ * 3, D]), ("st_lru_h", [NSQ, D]), ("st_s5_re", [NSQ, 2048]),
                            ("st_s5_im", [NSQ, 2048]), ("st_ffn_conv", [NSQ * 2, DFF]), ("w_in", [D, 3584]),
                            ("lru_conv_w", [4, D]), ("lru_conv_b", [D]), ("lru_wa", [16, 64, 64]), ("lru_ba", [D]),
                            ("lru_wx", [16, 64, 64]), ("lru_bx", [D]), ("lru_lambda", [D]), ("s5_a_re", [32, 64]),
                            ("s5_a_im", [32, 64]), ("s5_log_dt", [32]), ("s5_b_re", [32, 64, 16]), ("s5_b_im", [32, 64, 16]),
                            ("s5_c_re", [32, 16, 64]), ("s5_c_im", [32, 16, 64]), ("s5_d", [512]), ("w_glu", [512, 2048]),
                            ("w_out", [D, D]), ("ln1_g", [D]), ("ln1_b", [D]), ("w_up", [D, 2 * DFF]), ("ffn_conv_w", [3, DFF]),
                            ("ffn_conv_b", [DFF]), ("w_down", [DFF, D]), ("ln2_g", [D]), ("ln2_b", [D])):
            I[name] = din(name, shape)
        self.I = I
        O = {}
        O["y"] = self.dout("y", [T, D])
        O["lru_conv"] = self.dout("o_lru_conv", [NTAIL, D])
        O["lru_h"] = self.dout("o_lru_h", [128, 8, 17])
        O["s5"] = self.dout("o_s5", [128, 2, 16, 17])
        O["ffn_conv"] = self.dout("o_ffn_conv", [NTAIL, DFF])
        self.O = O
        self.x1_scr = nc.dram_tensor("x1_scr", [T, D], F32, kind="Internal").ap()

        self.ps = [self.st.enter_context(nc.psum_tensor("ps%d" % i, [128, 2, 512], F32)) for i in range(4)]
        R0W = 17664
        R1W = 35456
        self.R0 = Arena(self.st.enter_context(nc.sbuf_tensor("R0", [128, R0W], F32)), R0W)
        self.R1 = Arena(self.st.enter_context(nc.sbuf_tensor("R1", [128, R1W], F32)), R1W)
        R0, R1 = self.R0, self.R1

        ident_f = R0.take([128, 128]); ident_b = R0.take([128, 128], BF16)
        self.ident_f, self.ident_b = ident_f, ident_b
        self.MS("pool", ident_f, 0.0, "ident_f")
        p.op("pool", lambda e: e.affine_select(out=ident_f, in_=ident_f, pattern=[[-1, 128]],
                                               compare_op=ALU.not_equal, fill=1.0, base=0, channel_multiplier=1),
             reads=["ident_f"], writes=["ident_f"])
        self.CP("pool", ident_b, ident_f, ["ident_f"], "ident_b")

        self.PIECES = [(0, 512), (512, 512), (1024, 512), (1536, 512), (2048, 64)]
        xT = R0.take([128, 8, T], BF16)
        self.xT = xT
        zblk = R0.take([128, 4224])
        self.z2 = zblk.bitcast(BF16)[:, 0:4 * T].rearrange("p (a t) -> p a t", a=4)
        self.lnc = zblk[:, 0:4096].rearrange("p (a n) -> p a n", a=4)
        self.Hfin = R0.take([128, 2, 16, 17])
        self.LRUc = R0.take([128, 8, 72])
        self.Fc = R0.take([128, DFF // 128, 36])
        mark0 = R1.pos
        u_bf = R1.take([128, 4, T], BF16)
        self.W1 = R1.take([128, 4, LCH, 2, 128], BF16)
        self.W4a = R1.take([128, 16, LCH, 2, 32], BF16)
        self.W4b = R1.take([128, 16, LCH, 2, 32], BF16)
        self.h0S = R1.take([128, 2, 16, 16]); self.h0S_bf = R1.take([128, 2, 16, 16], BF16)
        self.RS = Arena(R1.take([128, 2368]), 2368)
        mark1 = R1.pos
        self.free_banks = [4, 5, 6, 7]
        self.param_loads()
        xb = [R1.take([128, D], BF16) for i in range(4)]
        ntt = (T + 127) // 128
        for tt in range(ntt):
            r0 = tt * 128
            rows = min(128, T - r0)
            slot = tt % 4
            p.dma("pool", xb[slot][0:rows, :], I["x"][r0:r0 + rows, :], "d_xb%d" % slot, writes=[("xb", slot)])
            b = self.bank()
            pt = self.bk(b).bitcast(BF16)
            for k in range(8):
                p.op("pe", lambda e, k=k, pt=pt, slot=slot, rows=rows: e.transpose(
                    pt[:, k * 128:k * 128 + rows], xb[slot][0:rows, k * 128:(k + 1) * 128], ident_b[0:rows, 0:rows]),
                    reads=[("xb", slot), "ident_b"], writes=[("bank", b)], inc=(k == 7))
            src = pt.rearrange("p (k c) -> p k c", c=128)[:, :, 0:rows]
            self.CP("act" if tt % 2 == 0 else "dve", xT[:, :, r0:r0 + rows], src, [("bank", b)], ("xT", tt))
            if tt == 3:
                self.param_transposes()
        self.tap("xT", xT[:, 0, :], [("xT", tt) for tt in range(ntt)])
        if self.stop_after == "p0":
            return self.finish()
        try:
            self.s5_prep()
        except StopBuild:
            return self.finish()
        self.tap("W1", self.W1[:, 0, :, :, :], self.W1_keys)
        self.tap("W4a", self.W4a[:, 0, :, :, :], self.W4_keys)
        self.tap("W4b", self.W4b[:, 0, :, :, :], self.W4_keys)
        self.tap("h0S", self.h0S, self.h0S_keys)
        self.tap("mur", self.mur, self.mu_keys)
        if self.stop_after == "prep":
            return self.finish()
        p.barrier()
        R1.pos = mark1
        wslot_u = [R1.take([128, 8, 128], BF16) for i in range(2)]
        for qh in range(4):
            slot = qh % 2
            p.dma("pool", wslot_u[slot], I["w_in"][:, 1024 + 128 * qh:1024 + 128 * (qh + 1)].rearrange("(k p) n -> p k n", p=128),
                  "d_wu%d" % slot, writes=[("wslot_u", slot)])
            for (c0, w) in self.PIECES:
                b = self.bank()
                for k in range(8):
                    p.op("pe", lambda e, k=k, b=b, slot=slot, c0=c0, w=w: e.matmul(
                        self.bk(b)[:, 0:w], wslot_u[slot][:, k, :], xT[:, k, c0:c0 + w], start=(k == 0), stop=(k == 7)),
                        reads=[("wslot_u", slot)] + self.xT_keys(c0, w), writes=[("bank", b)], inc=(k == 7))
                self.CP("act", u_bf[:, qh, c0:c0 + w], self.bk(b)[:, 0:w], [("bank", b)], ("u_bf", qh, c0))
        self.tap("u_bf", u_bf[:, 0, :], [("u_bf", 0, c0) for c0, _ in self.PIECES])
        if self.stop_after == "u":
            return self.finish()
        self.s5_main(u_bf)
        self.tap("z2", self.z2[:, 0, :], [("z2", 0, c0) for c0, _ in self.PIECES])
        self.tap("Hfin", self.Hfin, [("Hfin", q) for q in range(16)] + [("Hfin0", q) for q in range(16)])
        hk = [("Hfin", q) for q in range(16)] + [("Hfin0", q) for q in range(16)]
        self.out_dma(O["s5"], self.Hfin, hk)
        if self.stop_after == "s5":
            return self.finish()
        p.barrier()
        R1.pos = mark0
        p.epoch = "1"
        self.lru_stage()
        if self.stop_after == "lru":
            return self.finish()
        p.barrier()
        R1.pos = self.merged_end
        p.epoch = "2"
        self.mix_stage()
        if self.stop_after == "mix":
            return self.finish()
        p.barrier()
        R1.pos = 0
        p.epoch = "3"
        self.ffn_stage()
        return self.finish()

    def xT_keys(self, c0, w):
        return [("xT", tt) for tt in range(c0 // 128, (c0 + w - 1) // 128 + 1)]

    def param_loads(self):
        nc, p, I = self.nc, self.p, self.I
        R1 = self.R1
        MS = self.MS
        NCK = dict(allow_slow_non_contiguous=True)
        NJ = DFF // 128
        self.LF = R1.take([128, D + DFF])
        Lp = self.LF[:, 0:D]; Fp = self.LF[:, D:D + DFF]
        hs1 = R1.take([128, 2048])
        hsp = [hs1, hs1]
        self.Lp, self.Fp, self.hsp = Lp, Fp, hsp
        MS("pool", Lp, 0.0, "Lp"); MS("pool", Fp, 0.0, "Fp"); MS("pool", hs1, 0.0, "hsp")
        row = lambda nm: I[nm].rearrange("(o n) -> o n", o=1)
        for (r0, r1, src) in ((0, 48, I["st_lru_conv"]), (48, 64, I["st_lru_h"]), (64, 68, I["lru_conv_w"]), (68, 69, row("lru_conv_b")),
                              (69, 70, row("lru_ba")), (70, 71, row("lru_bx")), (71, 72, row("lru_lambda"))):
            p.dma("sp", Lp[r0:r1, :], src, "d_Lp", writes=["Lp"])
        for (r0, r1, src) in ((0, 32, I["st_ffn_conv"]), (32, 35, I["ffn_conv_w"]), (35, 36, row("ffn_conv_b"))):
            p.dma("sp", Fp[r0:r1, :], src, "d_Fp", writes=["Fp"])
        self.hs_loaded = False
        sh3 = [128, 16, 32]
        self.Bnr = R1.take(sh3); self.Bni = R1.take(sh3)
        self.Cnr = R1.take([128, 4, 128]); self.Cni = R1.take([128, 4, 128])
        for t_, k_ in ((self.Bnr, "Bnr"), (self.Bni, "Bni"), (self.Cnr, "Cnr"), (self.Cni, "Cni")):
            MS("pool", t_, 0.0, k_)
        for (dst, nm, key) in ((self.Bnr, "s5_b_re", "Bnr"), (self.Bni, "s5_b_im", "Bni")):
            v = I[nm].rearrange("(q two) p c -> two p q c", two=2)
            for two in range(2):
                p.dma("sp", dst[64 * two:64 * two + 64, :, 16 * two:16 * two + 16], v[two], "d_prep", writes=[key])
        for (dst, nm, key) in ((self.Cnr, "s5_c_re", "Cnr"), (self.Cni, "s5_c_im", "Cni")):
            v = I[nm].rearrange("(qh ql two) c p -> ql two c qh p", qh=4, ql=4, two=2)
            for ql in range(4):
                for two in range(2):
                    p0 = 32 * ql + 16 * two
                    p.dma("sp", dst[p0:p0 + 16, :, 64 * two:64 * two + 64], v[ql, two], "d_prep", writes=[key])
        shS = [128, 16]
        self.aSr = R1.take(shS); self.aSi = R1.take(shS); self.ldS = R1.take(shS)
        p.dma("sp", self.aSr, I["s5_a_re"].rearrange("(q two) p -> (two p) q", two=2), "d_prep", writes=["aSr"], **NCK)
        p.dma("sp", self.aSi, I["s5_a_im"].rearrange("(q two) p -> (two p) q", two=2), "d_prep", writes=["aSi"], **NCK)
        v = I["s5_log_dt"].rearrange("(q two) -> two q", two=2)
        for two in range(2):
            p.dma("sp", self.ldS[64 * two:64 * two + 64, :], v[two].partition_broadcast(64), "d_prep", writes=["ldS"], **NCK)
        self.dS5 = self.R0.take([128, 4])
        p.dma("sp", self.dS5, I["s5_d"].rearrange("(t p) -> p t", p=128), "d_prep", writes=["dS5"], **NCK)
        for sem, keys in (("d_Lp", ["Lp"]), ("d_Fp", ["Fp"]),
                          ("d_prep", ["Bnr", "Bni", "Cnr", "Cni", "aSr", "aSi", "ldS", "dS5"])):
            for k_ in keys:
                p.last_write[k_] = (sem, p.count[sem])

    def tr_group(self, srcs, evac):
        p = self.p
        for g0 in range(0, len(srcs), 4):
            grp = srcs[g0:g0 + 4]
            b = self.bank()
            bv = self.bk(b).rearrange("p (a c) -> p a c", a=4)
            for i, (ap, keys) in enumerate(grp):
                p.op("pe", lambda e, i=i, ap=ap, bv=bv: e.transpose(bv[:, i, :], ap, self.ident_f),
                     reads=list(keys) + ["ident_f"], writes=[("bank", b)], inc=(i == len(grp) - 1))
            evac(bv, g0, len(grp), b)

    def param_transposes(self):
        p = self.p
        CP = self.CP
        NJ = DFF // 128
        Lp, Fp, hsp = self.Lp, self.Fp, self.hsp

        def ev_L(bv, g0, n, b):
            CP("act", self.LRUc[:, g0:g0 + n, :], bv[:, 0:n, 0:72], [("bank", b)], ("LRUc", g0 // 4))
        self.tr_group([(Lp[:, 128 * j:128 * (j + 1)], ["Lp"]) for j in range(8)], ev_L)

        def ev_F(bv, g0, n, b):
            CP("dve", self.Fc[:, g0:g0 + n, :], bv[:, 0:n, 0:36], [("bank", b)], ("Fc", g0 // 4))
        self.tr_group([(Fp[:, 128 * j:128 * (j + 1)], ["Fp"]) for j in range(NJ)], ev_F)
        self.LRUc_keys = [("LRUc", i) for i in range(2)]
        self.Fc_keys = [("Fc", i) for i in range(NJ // 4)]
        for part in range(2):
            p.dma("sp", hsp[part][0:16, :], self.I[("st_s5_re", "st_s5_im")[part]], "d_hsp", writes=["hsp"])
            def ev_h(bv, g0, n, b, part=part):
                CP("act", self.h0S[:, part, g0:g0 + n, :], bv[:, 0:n, 0:16], [("bank", b)], ("h0S", part, g0 // 4))
            self.tr_group([(hsp[part][:, 128 * q:128 * (q + 1)], ["hsp"]) for q in range(16)], ev_h)
        self.h0S_keys = [("h0S", part, i) for part in range(2) for i in range(4)]
        sh3 = [128, 16, 32]
        self.CTr = self.R1.take(sh3); self.CTi = self.R1.take(sh3)
        for (src, dst, k_src, k_dst) in ((self.Cnr, self.CTr, "Cnr", "CTr"), (self.Cni, self.CTi, "Cni", "CTi")):
            def ev_c(bv, g0, n, b, dst=dst, k_dst=k_dst):
                CP("dve", dst[:, 4 * g0:4 * (g0 + n), :].rearrange("p (a q) c -> p a (q c)", a=n), bv[:, 0:n, :], [("bank", b)], (k_dst, g0))
            self.tr_group([(src[:, qh, :], [k_src]) for qh in range(4)], ev_c)
        self.CT_keys = {"CTr": [("CTr", 0)], "CTi": [("CTi", 0)]}

    def s5_prep(self):
        nc, p, I = self.nc, self.p, self.I
        R0, R1, RS = self.R0, self.R1, self.RS
        TT, TS, STT, ACT, CP, MS = self.TT, self.TS, self.STT, self.ACT, self.CP, self.MS
        W1, W4a, W4b = self.W1, self.W4a, self.W4b
        aSr, aSi, ldS = self.aSr, self.aSi, self.ldS
        CTr, CTi = self.CTr, self.CTi
        kCTr, kCTi = self.CT_keys["CTr"], self.CT_keys["CTi"]
        shS = [128, 16]
        sh3 = [128, 16, 32]
        I32 = mybir.dt.int32

        def sincos_base(theta, t, kf, A, C2, cs, sn, k_th, k_t, k_kf, k_A, k_C2, k_cs, k_sn):
            TS("dve", t, theta, 1.0 / (2 * PI), ALU.mult, [k_th], k_t, s2=16.0, op1=ALU.add)
            CP("dve", kf.bitcast(I32), t, [k_t], k_kf)
            CP("dve", kf, kf.bitcast(I32), [k_kf], k_kf)
            TT("dve", t, t, kf, ALU.subtract, [k_t, k_kf], k_t)
            ACT(A, t, AF.Sin, [k_t], k_A, scale=PI)
            ACT(C2, t, AF.Sin, [k_t], k_C2, scale=PI / 2)
            TT("dve", C2, C2, C2, ALU.mult, [k_C2], k_C2)
            TS("dve", C2, C2, -2.0, ALU.mult, [k_C2], k_C2, s2=1.0, op1=ALU.add)
            STT("dve", sn, A, 2.0, C2, ALU.mult, ALU.mult, [k_A, k_C2], k_sn)
            TT("dve", cs, A, A, ALU.mult, [k_A], k_cs)
            TS("dve", cs, cs, -2.0, ALU.mult, [k_cs], k_cs, s2=1.0, op1=ALU.add)

        def tS():
            return RS.take(shS)

        dtS = tS(); adtS = tS(); thS = tS()
        ACT(dtS, ldS, AF.Exp, ["ldS"], "dtS")
        TT("dve", adtS, aSr, dtS, ALU.mult, ["aSr", "dtS"], "adtS")
        TT("dve", thS, aSi, dtS, ALU.mult, ["aSi", "dtS"], "thS")
        self.rhoS = tS()
        ACT(self.rhoS, adtS, AF.Exp, ["adtS"], "rhoS")
        cosS = []; sinS = []; nsinS = []
        lamr = {}; lami = {}; lrotr = {}; lroti = {}; rpow = {}
        c1S = tS(); s1S = tS(); w1 = tS(); w2 = tS(); w3 = tS(); w4 = tS()
        sincos_base(thS, w1, w2, w3, w4, c1S, s1S, "thS", "Sw1", "Sw2", "Sw3", "Sw4", "S1cs", "S1sn")
        ua = w1; ub = w2
        for s in range(2 * LCH):
            if s == 0:
                cs = tS(); sn = tS()
                MS("dve", cs, 1.0, "S0cs"); MS("dve", sn, 0.0, "S0sn")
            elif s == 1:
                cs, sn = c1S, s1S
            else:
                cs = tS(); sn = tS()
                pc, ps_ = cosS[s - 1], sinS[s - 1]
                kpc, kps = "S%dcs" % (s - 1), "S%dsn" % (s - 1)
                TT("dve", ua, pc, c1S, ALU.mult, [kpc, "S1cs", "Sw1"], "Sw1")
                TT("dve", ub, ps_, s1S, ALU.mult, [kps, "S1sn", "Sw2"], "Sw2")
                TT("dve", cs, ua, ub, ALU.subtract, ["Sw1", "Sw2"], "S%dcs" % s)
                TT("dve", ua, ps_, c1S, ALU.mult, [kps, "S1cs"], "Sw1")
                TT("dve", ub, pc, s1S, ALU.mult, [kpc, "S1sn"], "Sw2")
                TT("dve", sn, ua, ub, ALU.add, ["Sw1", "Sw2"], "S%dsn" % s)
            cosS.append(cs); sinS.append(sn)
            if s <= LCH:
                ns = tS()
                TS("dve", ns, sn, -1.0, ALU.mult, ["S%dsn" % s], "nS%dsn" % s)
                nsinS.append(ns)
            if 1 <= s <= LCH:
                r = tS()
                ACT(r, adtS, AF.Exp, ["adtS"], "rpow%d" % s, scale=float(s))
                rpow[s] = r
                if s in (1, 4):
                    a = tS(); b_ = tS()
                    TT("dve", a, r, cs, ALU.mult, ["rpow%d" % s, "S%dcs" % s], "lamr%d" % s)
                    TT("dve", b_, r, sn, ALU.mult, ["rpow%d" % s, "S%dsn" % s], "lami%d" % s)
                    lamr[s] = a; lami[s] = b_
        for s in range(1, LCH + 1):
            a = tS(); b_ = tS()
            ks = s + LCH - 1
            TT("dve", a, rpow[s], cosS[ks], ALU.mult, ["rpow%d" % s, "S%dcs" % ks], "lrotr%d" % s)
            TT("dve", b_, rpow[s], sinS[ks], ALU.mult, ["rpow%d" % s, "S%dsn" % ks], "lroti%d" % s)
            lrotr[s] = a; lroti[s] = b_
        self.cosS, self.sinS, self.nsinS, self.lamr, self.lami = cosS, sinS, nsinS, lamr, lami
        self.nlami4 = tS()
        TS("dve", self.nlami4, lami[4], -1.0, ALU.mult, ["lami4"], "nlami4")
        ta = R1.take(sh3); tb = R1.take(sh3)

        def bc(t):
            return t.unsqueeze(2).to_broadcast(sh3)

        Bnr, Bni = self.Bnr, self.Bni
        nr = tS(); den = tS(); u1 = tS(); cfr = tS(); cfi = tS()
        TS("dve", nr, lamr[1], -1.0, ALU.add, ["lamr1"], "nr")
        TT("dve", den, aSr, aSr, ALU.mult, ["aSr"], "den")
        TT("dve", u1, aSi, aSi, ALU.mult, ["aSi"], "u1")
        TT("dve", den, den, u1, ALU.add, ["den", "u1"], "den")
        p.op("dve", lambda e: e.reciprocal(out=den, in_=den), reads=["den"], writes=["den"])
        TT("dve", cfr, nr, aSr, ALU.mult, ["nr", "aSr"], "cfr")
        TT("dve", u1, lami[1], aSi, ALU.mult, ["lami1", "aSi", "den"], "u1")
        TT("dve", cfr, cfr, u1, ALU.add, ["cfr", "u1"], "cfr")
        TT("dve", cfr, cfr, den, ALU.mult, ["cfr", "den"], "cfr")
        TT("dve", cfi, lami[1], aSr, ALU.mult, ["lami1", "aSr"], "cfi")
        TT("dve", u1, nr, aSi, ALU.mult, ["nr", "aSi", "cfr"], "u1")
        TT("dve", cfi, cfi, u1, ALU.subtract, ["cfi", "u1"], "cfi")
        TT("dve", cfi, cfi, den, ALU.mult, ["cfi", "den"], "cfi")
        BbR = R1.take(sh3); BbI = R1.take(sh3)
        tc_ = R1.take(sh3); td_ = R1.take(sh3)
        TT("pool", tc_, Bnr, bc(cfr), ALU.mult, ["Bnr", "cfr"], "w1tc")
        TT("pool", td_, Bni, bc(cfi), ALU.mult, ["Bni", "cfi"], "w1td")
        TT("pool", BbR, tc_, td_, ALU.subtract, ["w1tc", "w1td"], "BbR")
        TT("pool", tc_, Bni, bc(cfr), ALU.mult, ["Bni", "cfr"], "w1tc")
        TT("pool", td_, Bnr, bc(cfi), ALU.mult, ["Bnr", "cfi"], "w1td")
        TT("pool", BbI, tc_, td_, ALU.add, ["w1tc", "w1td"], "BbI")
        W1S = self.LF.bitcast(BF16).rearrange("p (a s b c) -> p a s b c", a=4, s=LCH, b=2)
        MS("pool", W1S[:, 0, 0, 0, 0:2], 0.0, "LFfree", r=["Lp", "Fp"])
        p.last_write["Lp"] = p.last_write["LFfree"]; p.last_write["Fp"] = p.last_write["LFfree"]
        v4 = lambda t: t.rearrange("p (a b) c -> p a (b c)", a=4)
        for s in range(LCH):
            kc, ks = "S%dcs" % s, "S%dsn" % s
            TT("pool", tc_, BbR, bc(cosS[s]), ALU.mult, ["BbR", kc], "w1tc")
            TT("pool", td_, BbI, bc(sinS[s]), ALU.mult, ["BbI", ks], "w1td")
            TT("pool", W1S[:, :, s, 0, :], v4(tc_), v4(td_), ALU.add, ["w1tc", "w1td", "LFfree"], ("W1S", s, 0))
            TT("pool", tc_, BbI, bc(cosS[s]), ALU.mult, ["BbI", kc], "w1tc")
            TT("pool", td_, BbR, bc(sinS[s]), ALU.mult, ["BbR", ks], "w1td")
            TT("pool", W1S[:, :, s, 1, :], v4(tc_), v4(td_), ALU.subtract, ["w1tc", "w1td", "LFfree"], ("W1S", s, 1))
        for qh in range(4):
            for h in range(2):
                b = self.bank()
                pt = self.bk(b).bitcast(BF16)
                n = 0
                for s in range(4 * h, 4 * h + 4):
                    for part in range(2):
                        p.op("pe", lambda e, pt=pt, n=n, qh=qh, s=s, part=part: e.transpose(
                            pt[:, 128 * n:128 * (n + 1)], W1S[:, qh, s, part, :], self.ident_b),
                            reads=[("W1S", s, part), "ident_b"], writes=[("bank", b)], inc=(n == 7))
                        n += 1
                CP("act", W1[:, qh, 4 * h:4 * h + 4, :, :].rearrange("p a b c -> p (a b c)"), pt, [("bank", b)], ("W1", qh, h))
        self.W1_keys = [("W1", qh, h) for qh in range(4) for h in range(2)]
        c7 = cosS[LCH - 1]; s7 = sinS[LCH - 1]
        kc7, ks7 = "S%dcs" % (LCH - 1), "S%dsn" % (LCH - 1)
        sh16 = [128, 16, 16]
        bc16 = lambda t: t.unsqueeze(2).to_broadcast(sh16)
        tav = ta[:, :, 0:16]; tbv = tb[:, :, 0:16]
        h0r = self.h0S[:, 0, :, :]; h0i = self.h0S[:, 1, :, :]
        TT("dve", tav, h0r, bc16(c7), ALU.mult, self.h0S_keys + [kc7, "w4ta"], "w4ta")
        TT("dve", tbv, h0i, bc16(s7), ALU.mult, self.h0S_keys + [ks7, "w4tb"], "w4tb")
        TT("dve", self.h0S_bf[:, 0, :, :], tav, tbv, ALU.add, ["w4ta", "w4tb"], "h0S_bf")
        TT("dve", tav, h0i, bc16(c7), ALU.mult, self.h0S_keys + [kc7], "w4ta")
        TT("dve", tbv, h0r, bc16(s7), ALU.mult, self.h0S_keys + [ks7], "w4tb")
        TT("dve", self.h0S_bf[:, 1, :, :], tav, tbv, ALU.subtract, ["w4ta", "w4tb", "h0S_bf"], "h0S_bf")
        for s in range(LCH):
            for (Wt, cr_t, ci_t, kr, ki, nm) in (
                (W4a, cosS[s], sinS[s], "S%dcs" % s, "S%dsn" % s, "W4a"),
                (W4b, lrotr[s + 1], lroti[s + 1], "lrotr%d" % (s + 1), "lroti%d" % (s + 1), "W4b"),
            ):
                TT("dve", ta, CTr, bc(cr_t), ALU.mult, kCTr + [kr], "w4ta")
                TT("dve", tb, CTi, bc(ci_t), ALU.mult, kCTi + [ki], "w4tb")
                TT("dve", Wt[:, :, s, 0, :], ta, tb, ALU.subtract, ["w4ta", "w4tb"], (nm, s, 0))
                TT("dve", ta, CTr, bc(ci_t), ALU.mult, kCTr + [ki], "w4ta")
                TT("dve", tb, CTi, bc(cr_t), ALU.mult, kCTi + [kr], "w4tb")
                TT("dve", ta, ta, tb, ALU.add, ["w4ta", "w4tb"], "w4ta")
                TS("dve", Wt[:, :, s, 1, :], ta, -1.0, ALU.mult, ["w4ta"], (nm, s, 1))
        self.W4_keys = [(nm, s, part) for nm in ("W4a", "W4b") for s in range(LCH) for part in range(2)]
        self.mur = RS.take([128, 16, NLEV]); self.mui = RS.take([128, 16, NLEV]); self.muni = RS.take([128, 16, NLEV])
        angc = RS.take([128, 16, NLEV]); angs = RS.take([128, 16, NLEV]); rmag = RS.take([128, 16, NLEV])
        CP("dve", angc[:, :, 0], cosS[LCH], ["S%dcs" % LCH], ("angc", 0))
        CP("dve", angs[:, :, 0], sinS[LCH], ["S%dsn" % LCH], ("angs", 0))
        sa = tS(); sb_ = tS()
        for j in range(NLEV):
            if j >= 1:
                TT("dve", sa, angc[:, :, j - 1], angc[:, :, j - 1], ALU.mult, [("angc", j - 1)], "sq_a")
                TT("dve", sb_, angs[:, :, j - 1], angs[:, :, j - 1], ALU.mult, [("angs", j - 1)], "sq_b")
                TT("dve", angc[:, :, j], sa, sb_, ALU.subtract, ["sq_a", "sq_b"], ("angc", j))
                TT("dve", sa, angc[:, :, j - 1], angs[:, :, j - 1], ALU.mult, [("angc", j - 1), ("angs", j - 1)], "sq_a")
                TS("dve", angs[:, :, j], sa, 2.0, ALU.mult, ["sq_a"], ("angs", j))
            ACT(rmag[:, :, j], adtS, AF.Exp, ["adtS"], ("rmag", j), scale=float(LCH * (1 << j)))
            TT("dve", self.mur[:, :, j], rmag[:, :, j], angc[:, :, j], ALU.mult, [("rmag", j), ("angc", j)], ("mur", j))
            TT("dve", self.mui[:, :, j], rmag[:, :, j], angs[:, :, j], ALU.mult, [("rmag", j), ("angs", j)], ("mui", j))
        for j in range(NLEV):
            TS("dve", self.muni[:, :, j], self.mui[:, :, j], -1.0, ALU.mult, [("mui", j)], ("muni", j))
        self.mu_keys = [(nm, j) for nm in ("mur", "mui", "muni") for j in range(NLEV)]
        self.rp8 = RS.take([128, 16, LCH]); self.rp4 = RS.take([128, 16, 4])
        CP("dve", self.rp8, self.rhoS.unsqueeze(2).to_broadcast([128, 16, LCH]), ["rhoS"], "rp8")
        MS("dve", self.rp8[:, :, 0:1], 0.0, "rp8", r=["rp8"])
        CP("dve", self.rp4, self.rhoS.unsqueeze(2).to_broadcast([128, 16, 4]), ["rhoS"], "rp4")
        MS("dve", self.rp4[:, :, 0:1], 0.0, "rp4", r=["rp4"])
    def s5_main(self, u_bf):
        nc, p = self.nc, self.p
        R0, R1 = self.R0, self.R1
        TT, TS, STT, ACT, CP, MS = self.TT, self.TS, self.STT, self.ACT, self.CP, self.MS
        W1, W4a, W4b = self.W1, self.W4a, self.W4b
        cosS, sinS, nsinS, lamr, lami = self.cosS, self.sinS, self.nsinS, self.lamr, self.lami
        z2 = self.z2
        NSET = 2
        pat = [R1.take([128, 512]) for i in range(NSET)]
        pats = [R1.take([128, 64]) for i in range(NSET)]
        gzf = [R1.take([128, 512]) for i in range(4)]
        gzfs = [R1.take([128, 2, 64]) for i in range(2)]
        gzb = [R1.take([128, 2, T], BF16) for i in range(NSET)]
        HA = [R1.take([128, 2, PAD + NCH]) for i in range(NSET)]
        HB = [R1.take([128, 2, PAD + NCH]) for i in range(NSET)]
        Hpb = [R1.take([128, 2, NCH], BF16) for i in range(NSET)]
        for i in range(NSET):
            MS("pool", HA[i], 0.0, ("HA", i))
            MS("pool", HB[i], 0.0, ("HB", i))
        yf = R1.take([128, 512]); gq = R1.take([128, 512]); ga = R1.take([128, 512]); gt = gq
        VB = [0, 1]
        nunit = [0]
        SVB = 2
        YB = {0: 4, 512: 5, 1024: 6, 1536: 7}
        PIECES = self.PIECES
        nseg = [0]

        def stageA(q):
            qh, ql = q // 4, q % 4
            hb = q % NSET
            kw = dict(tile_position=(96, 0)) if ql == 3 else {}
            p.op("act", lambda e: e.activation(out=pat[hb].rearrange("p (c s) -> p c s", s=LCH),
                                               in_=self.rp8[:, q:q + 1, :].to_broadcast([128, 512 // LCH, LCH]), func=AF.Copy),
                 reads=["rp8"], writes=[("pat", hb)])
            p.op("act", lambda e: e.activation(out=pats[hb].rearrange("p (c s) -> p c s", s=4),
                                               in_=self.rp4[:, q:q + 1, :].to_broadcast([128, 64 // 4, 4]), func=AF.Copy),
                 reads=["rp4"], writes=[("pats", hb)])
            for (c0, w) in PIECES:
                samp = (w == 64)
                L = 4 if samp else LCH
                sl = nseg[0] % 2
                nseg[0] += 1
                for part in range(2):
                    if samp:
                        vb = SVB
                        vv = self.bk(SVB)[:, 64 * part:64 * part + 64]
                    else:
                        vb = VB[nunit[0] % 2]
                        vv = self.bk(vb)
                    ug = nunit[0] % 4
                    nunit[0] += 1
                    for s in range(L):
                        p.op("pe", lambda e, vv=vv, s=s, part=part, qh=qh, ql=ql, c0=c0, w=w, L=L, kw=kw: e.matmul(
                            vv[:, s:w:L], W1[32 * ql:32 * ql + 32, qh, s, part, :],
                            u_bf[32 * ql:32 * ql + 32, qh, c0 + s:c0 + w:L], start=True, stop=True, **kw),
                            reads=self.W1_keys + [("u_bf", qh, c0)], writes=[("bank", vb)], inc=(s == L - 1))
                    if not samp:
                        go = gzf[ug]; gkey = ("gzf", ug)
                        p.op("dve", lambda e, go=go, hb=hb, vv=vv: e.tensor_tensor_scan(
                            out=go, data0=pat[hb], data1=vv, initial=0.0, op0=ALU.mult, op1=ALU.add),
                            reads=[("pat", hb), ("bank", vb)], writes=[gkey])
                        CP("act", gzb[hb][:, part, c0:c0 + w], go, [gkey], ("gzb", hb, c0, part))
                        k0 = PAD + c0 // LCH
                        nchunk = w // LCH
                        CP("pool", HA[hb][:, part, k0:k0 + nchunk], go[:, LCH - 1:512:LCH], [gkey], ("HA", hb))
                    else:
                        go = gzfs[sl][:, part, :]; gkey = ("gzfs", sl, part)
                        p.op("dve", lambda e, go=go, hb=hb, vv=vv: e.tensor_tensor_scan(
                            out=go, data0=pats[hb], data1=vv, initial=0.0, op0=ALU.mult, op1=ALU.add),
                            reads=[("pats", hb), ("bank", vb)], writes=[gkey])
                        CP("act", gzb[hb][:, part, c0:c0 + w], go, [gkey], ("gzb", hb, c0, part))
                if samp:
                    go = gzfs[sl]
                    gkeys = [("gzfs", sl, 0), ("gzfs", sl, 1)]
                    er = go[:, 0, 3:64:4]; ei = go[:, 1, 3:64:4]
                    c3 = cosS[3][:, q:q + 1]; s3 = sinS[3][:, q:q + 1]; ns3 = nsinS[3][:, q:q + 1]
                    l4r = lamr[4][:, q:q + 1]; l4i = lami[4][:, q:q + 1]; nl4i = self.nlami4[:, q:q + 1]
                    kk = gkeys + ["S3cs", "S3sn", "nS3sn", "lamr4", "lami4", "nlami4"] + self.h0S_keys
                    fr = self.Hfin[:, 0, q, 1:17]; fi = self.Hfin[:, 1, q, 1:17]
                    h0r = self.h0S[:, 0, q, :]; h0i = self.h0S[:, 1, q, :]
                    fk = ("Hfin", q)
                    TS("dve", fr, er, c3, ALU.mult, kk, fk)
                    STT("dve", fr, ei, ns3, fr, ALU.mult, ALU.add, kk + [fk], fk)
                    STT("dve", fr, h0r, l4r, fr, ALU.mult, ALU.add, kk + [fk], fk)
                    STT("dve", fr, h0i, nl4i, fr, ALU.mult, ALU.add, kk + [fk], fk)
                    TS("dve", fi, er, s3, ALU.mult, kk + [fk], fk)
                    STT("dve", fi, ei, c3, fi, ALU.mult, ALU.add, kk + [fk], fk)
                    STT("dve", fi, h0r, l4i, fi, ALU.mult, ALU.add, kk + [fk], fk)
                    STT("dve", fi, h0i, l4r, fi, ALU.mult, ALU.add, kk + [fk], fk)

        def stageB(q):
            hb = q % NSET
            src, dst = HA[hb], HB[hb]
            skey, dkey = ("HA", hb), ("HB", hb)
            for j in range(NLEV):
                d = 1 << j
                mr = self.mur[:, q, j:j + 1]; mi = self.mui[:, q, j:j + 1]; mni = self.muni[:, q, j:j + 1]
                mk = [("mur", j), ("mui", j), ("muni", j)]
                S0 = src[:, 0, PAD:PAD + NCH]; S1 = src[:, 1, PAD:PAD + NCH]
                Z0 = src[:, 0, PAD - d:PAD + NCH - d]; Z1 = src[:, 1, PAD - d:PAD + NCH - d]
                D0 = dst[:, 0, PAD:PAD + NCH]; D1 = dst[:, 1, PAD:PAD + NCH]
                STT("dve", D0, Z1, mni, S0, ALU.mult, ALU.add, [skey] + mk, dkey)
                STT("dve", D1, Z0, mi, S1, ALU.mult, ALU.add, [skey, dkey] + mk, dkey)
                Zb = src[:, :, PAD - d:PAD + NCH - d]; Db = dst[:, :, PAD:PAD + NCH]
                STT("dve", Db, Zb, mr, Db, ALU.mult, ALU.add, [skey, dkey] + mk, dkey)
                src, dst = dst, src
                skey, dkey = dkey, skey
            CP("act", Hpb[hb], HA[hb][:, :, PAD - 1:PAD - 1 + NCH], [("HA", hb)], ("Hpb", hb))
            hr_ = HA[hb][:, 0, PAD + NCH - 1:PAD + NCH]; hi_ = HA[hb][:, 1, PAD + NCH - 1:PAD + NCH]
            c7 = cosS[LCH - 1][:, q:q + 1]; s7 = sinS[LCH - 1][:, q:q + 1]; ns7 = nsinS[LCH - 1][:, q:q + 1]
            kk7 = [("HA", hb), "S%dcs" % (LCH - 1), "S%dsn" % (LCH - 1), "nS%dsn" % (LCH - 1)]
            fr0 = self.Hfin[:, 0, q, 0:1]; fi0 = self.Hfin[:, 1, q, 0:1]
            TS("dve", fr0, hr_, c7, ALU.mult, kk7, ("Hfin0", q))
            STT("dve", fr0, hi_, ns7, fr0, ALU.mult, ALU.add, kk7 + [("Hfin0", q)], ("Hfin0", q))
            TS("dve", fi0, hr_, s7, ALU.mult, kk7 + [("Hfin0", q)], ("Hfin0", q))
            STT("dve", fi0, hi_, c7, fi0, ALU.mult, ALU.add, kk7 + [("Hfin0", q)], ("Hfin0", q))

        def stageC(q):
            qh, ql = q // 4, q % 4
            hb = q % NSET
            kw = dict(tile_position=(0, 96)) if ql == 3 else {}
            for (c0, w) in PIECES:
                samp = (w == 64)
                L = 4 if samp else LCH
                if samp:
                    ybank = SVB
                    yall = self.bk(SVB)[:, 128:192]
                else:
                    ybank = YB[c0]
                    yall = self.bk(ybank)
                for s in range(L):
                    outv = yall[32 * ql:32 * ql + 32, s:w:L]
                    if samp:
                        hr = self.h0S_bf[:, 0, q, :]; hi = self.h0S_bf[:, 1, q, :]
                        hkeys = ["h0S_bf"]
                    else:
                        k0 = c0 // LCH
                        hr = Hpb[hb][:, 0, k0:k0 + w // L]; hi = Hpb[hb][:, 1, k0:k0 + w // L]
                        hkeys = [("Hpb", hb)]
                    ops = [
                        (W4a[:, q, s, 0, :], gzb[hb][:, 0, c0 + s:c0 + w:L]),
                        (W4a[:, q, s, 1, :], gzb[hb][:, 1, c0 + s:c0 + w:L]),
                        (W4b[:, q, s, 0, :], hr),
                        (W4b[:, q, s, 1, :], hi),
                    ]
                    for i, (lh, rh) in enumerate(ops):
                        last = (i == 3 and s == L - 1)
                        p.op("pe", lambda e, outv=outv, lh=lh, rh=rh, i=i, kw=kw: e.matmul(
                            outv, lh, rh, start=(i == 0), stop=(i == 3), **kw),
                            reads=self.W4_keys + [("gzb", hb, c0, 0), ("gzb", hb, c0, 1)] + hkeys, writes=[("bank", ybank)], inc=last)

        def stageY(qh):
            for (c0, w) in PIECES:
                samp = (w == 64)
                if samp:
                    ybank = SVB; ysrc = self.bk(SVB)[:, 128:192]
                else:
                    ybank = YB[c0]; ysrc = self.bk(ybank)
                yv = yf[:, 0:w]; qv = gq[:, 0:w]; av = ga[:, 0:w]; tv = gt[:, 0:w]
                STT("dve", yv, u_bf[:, qh, c0:c0 + w], self.dS5[:, qh:qh + 1], ysrc, ALU.mult, ALU.add,
                    [("u_bf", qh, c0), "dS5", ("bank", ybank)], "s5yf")
                if qh == 0:
                    self.tap("ys5_%d" % c0, yv, ["s5yf"])
                ACT(qv, yv, AF.Square, ["s5yf"], "s5gq")
                ACT(qv, qv, AF.Identity, ["s5gq"], "s5gq", scale=GC, bias=1.0)
                TT("pool", av, qv, yv, ALU.mult, ["s5gq", "s5yf"], "s5ga")
                ACT(qv, av, AF.Tanh, ["s5ga"], "s5gq", scale=GK)
                STT("dve", z2[:, qh, c0:c0 + w], qv, 1.0, yv, ALU.add, ALU.mult, ["s5gq", "s5yf"], ("z2", qh, c0))

        for qh in range(4):
            qs = [4 * qh + i for i in range(4)]
            stageA(qs[0]); stageB(qs[0])
            for i in range(1, 4):
                stageA(qs[i])
                stageC(qs[i - 1])
                stageB(qs[i])
            stageC(qs[3])
            stageY(qh)

    def gelu2(self, yv, qv, av, tv, outv, ky, kq, ka, kt, kout):
        self.ACT(qv, yv, AF.Square, [ky], kq)
        self.STT("dve", av, qv, GK * GC, yv, ALU.mult, ALU.mult, [kq, ky], ka)
        self.STT("dve", av, yv, GK, av, ALU.mult, ALU.add, [ky, ka], ka)
        self.ACT(tv, av, AF.Tanh, [ka], kt)
        self.STT("dve", outv, tv, 1.0, yv, ALU.add, ALU.mult, [kt, ky], kout)

    def lru_stage(self):
        nc, p, I, O = self.nc, self.p, self.I, self.O
        R0, R1 = self.R0, self.R1
        TT, TS, STT, ACT, CP, MS = self.TT, self.TS, self.STT, self.ACT, self.CP, self.MS
        NCK = dict(allow_slow_non_contiguous=True)
        xT, z2 = self.xT, self.z2
        PIECES = self.PIECES
        self.free_banks = list(range(8))
        LRUc = self.LRUc
        LK = self.LRUc_keys
        baT = R0.take([128, 8]); bxT = R0.take([128, 8]); sc8 = R0.take([128, 8]); hsc8 = R0.take([128, 8])
        TS("dve", baT, LRUc[:, :, 69], 0.5, ALU.mult, LK, "baT")
        TS("dve", bxT, LRUc[:, :, 70], 0.5, ALU.mult, LK, "bxT")
        ACT(sc8, LRUc[:, :, 71], AF.Exp, LK, "sc8", scale=-1.0)
        ACT(sc8, sc8, AF.Ln, ["sc8"], "sc8", bias=1.0)
        TS("dve", hsc8, sc8, -4.0, ALU.mult, ["sc8"], "hsc8")
        TS("dve", sc8, sc8, -8.0, ALU.mult, ["sc8", "hsc8"], "sc8")
        Wg = R0.take([128, 8, 2, 128], BF16)
        MS("pool", Wg, 0.0, "Wg")
        for gi, nm in ((0, "lru_wa"), (1, "lru_wx")):
            v = I[nm].rearrange("(j two) i o -> two i j o", two=2)
            for par in range(2):
                p.dma("pool", Wg[64 * par:64 * par + 64, :, gi, 64 * par:64 * par + 64], v[par], "d_wg", writes=["Wg"])
        p.last_write["Wg"] = ("d_wg", p.count["d_wg"])
        hfinL = R0.take([128, 8, 17])
        self.merged2 = R1.take([128, 8, T], BF16)
        merged2 = self.merged2
        self.merged_end = R1.pos
        xl_sb = R1.take([128, 3 + TP]); xs_sb = R1.take([128, 16, 7])
        abuf = R1.take([128, T]); a2buf = R1.take([128, T]); ixbuf = R1.take([128, T])
        hbuf = [R1.take([128, T])]
        tail_sb = R1.take([NTAIL, D])
        MS("pool", xl_sb[:, 0:3], 0.0, ("xl", -1))
        NP = 2
        ytmp = [R1.take([128, 512]) for i in range(2)]
        NXC = 5
        xc = [R1.take([128, 512]) for i in range(NXC)]
        xcb = [R1.take([128, 512], BF16) for i in range(2)]
        rp_ = [R1.take([128, 512]) for i in range(2)]
        ip_ = [R1.take([128, 512]) for i in range(2)]
        glp = [R1.take([128, 512]) for i in range(2)]
        gsp = [R1.take([128, 512]) for i in range(2)]
        gbp = [R1.take([128, 512]) for i in range(2)]
        t1p = [R1.take([128, 512]) for i in range(2)]
        t16 = R1.take([128, 16])
        wsl = [R1.take([128, 8, 128], BF16) for i in range(2)]
        wsg = [R1.take([128, 8, 2, 128], BF16) for i in range(2)]
        wgl = [R1.take([128, 4, 2, 128], BF16) for i in range(2)]
        hb = hbuf[0]
        XB = [0, 1]; GAB = [2, 3]; GXB = [4, 5]; POSTB = [6, 7]
        allp = lambda nm: [(nm, pi) for pi in range(len(PIECES))]

        def load_pre(j):
            sl = j % 2
            p.dma("pool", wsl[sl], I["w_in"][:, 128 * j:128 * (j + 1)].rearrange("(k p) n -> p k n", p=128), "d_wsl%d" % sl,
                  writes=[("wsl", sl)])

        def load_post(j):
            sl = j % 2
            p.dma("pool", wsg[sl].rearrange("p k g n -> p k (g n)"),
                  I["w_in"][:, 1536 + 256 * j:1536 + 256 * (j + 1)].rearrange("(k p) n -> p k n", p=128), "d_wsg%d" % sl,
                  writes=[("wsg", sl, 0), ("wsg", sl, 1)])
            p.dma("pool", wgl[sl].rearrange("p k g n -> p k (g n)"),
                  I["w_glu"][:, 256 * j:256 * (j + 1)].rearrange("(k p) n -> p k n", p=128), "d_wgl%d" % sl,
                  writes=[("wgl", sl, 0), ("wgl", sl, 1)])

        items = [dict(j=j, pi=pi, c0=c0, w=w) for j in range(8) for pi, (c0, w) in enumerate(PIECES)]

        def P0(t, it):
            j, c0, w = it["j"], it["c0"], it["w"]
            sl = j % 2
            if it["pi"] == 0 and j + 1 < 8:
                load_pre(j + 1)
            b = XB[t % 2]
            for k in range(8):
                p.op("pe", lambda e, k=k, b=b, sl=sl, c0=c0, w=w: e.matmul(
                    self.bk(b)[:, 0:w], wsl[sl][:, k, :], xT[:, k, c0:c0 + w], start=(k == 0), stop=(k == 7)),
                    reads=[("wsl", sl)] + self.xT_keys(c0, w), writes=[("bank", b)], inc=(k == 7))
            if it["pi"] == len(PIECES) - 1:
                tb = POSTB[1]
                for k in range(8):
                    p.op("pe", lambda e, k=k, tb=tb, sl=sl: e.matmul(self.bk(tb)[0:NTAIL, 0:128], xT[:, k, TAIL0:T], wsl[sl][:, k, :],
                                                                     start=(k == 0), stop=(k == 7)),
                         reads=[("wsl", sl)] + self.xT_keys(TAIL0, NTAIL), writes=[("bank", tb)], inc=(k == 7))
                CP("act", tail_sb[:, 128 * j:128 * (j + 1)], self.bk(tb)[0:NTAIL, 0:128], [("bank", tb)], ("tail", j))

        def P1(t, it):
            j, pi, c0, w = it["j"], it["pi"], it["c0"], it["w"]
            b = XB[t % 2]
            ps = self.bk(b)[:, 0:w]
            yv = ytmp[t % 2][:, 0:w]
            ACT(yv, ps, AF.Identity, [("bank", b)] + LK, ("ytmp", t % 2), scale=LRUc[:, j, 67:68], bias=LRUc[:, j, 68:69])
            if w != 64:
                CP("act", xl_sb[:, 3 + c0:3 + c0 + w], ps, [("bank", b)], ("xl", pi))
            else:
                CP("pool", xs_sb[:, :, 0:3], LRUc[:, j, 0:48].rearrange("p (b k) -> p b k", k=3), LK, ("xs", "st"))
                CP("act", xs_sb[:, :, 3:7], ps.rearrange("p (b s) -> p b s", s=4), [("bank", b)], ("xs", "new"))

        def P2(t, it):
            j, pi, c0, w = it["j"], it["pi"], it["c0"], it["w"]
            cw = [LRUc[:, j, 64 + k:65 + k] for k in range(4)]
            yk = ("ytmp", t % 2); xk_ = ("xc", t % NXC)
            yv = ytmp[t % 2][:, 0:w]; xcv = xc[t % NXC][:, 0:w]
            if w != 64:
                xk = [("xl", pi), ("xl", pi - 1), yk] + LK
                STT("dve", yv, xl_sb[:, c0 + 2:c0 + 2 + w], cw[2], yv, ALU.mult, ALU.add, xk, yk)
                STT("dve", yv, xl_sb[:, c0 + 1:c0 + 1 + w], cw[1], yv, ALU.mult, ALU.add, xk, yk)
                STT("dve", xcv, xl_sb[:, c0:c0 + w], cw[0], yv, ALU.mult, ALU.add, xk, xk_)
            else:
                xk = [("xs", "st"), ("xs", "new"), yk] + LK
                y3 = yv.rearrange("p (b s) -> p b s", s=4); xc3 = xcv.rearrange("p (b s) -> p b s", s=4)
                STT("dve", y3, xs_sb[:, :, 2:6], cw[2], y3, ALU.mult, ALU.add, xk, yk)
                STT("dve", y3, xs_sb[:, :, 1:5], cw[1], y3, ALU.mult, ALU.add, xk, yk)
                STT("dve", xc3, xs_sb[:, :, 0:4], cw[0], y3, ALU.mult, ALU.add, xk, xk_)
            if j == 0:
                self.tap("xc_%d" % c0, xcv, [xk_])
            CP("pool", xcb[t % 2][:, 0:w], xcv, [xk_], ("xcb", t % 2))

        def P3(t, it):
            j, w = it["j"], it["w"]
            for (bb, gi) in ((GAB[t % 2], 0), (GXB[t % 2], 1)):
                p.op("pe", lambda e, bb=bb, gi=gi, j=j, t=t, w=w: e.matmul(self.bk(bb)[:, 0:w], Wg[:, j, gi, :], xcb[t % 2][:, 0:w],
                                                                          start=True, stop=True),
                     reads=["Wg", ("xcb", t % 2)], writes=[("bank", bb)])

        def P4(t, it):
            j, w = it["j"], it["w"]
            ACT(rp_[t % 2][:, 0:w], self.bk(GAB[t % 2])[:, 0:w], AF.Tanh, [("bank", GAB[t % 2]), "baT"], ("rp", t % 2),
                scale=0.5, bias=baT[:, j:j + 1])
            ACT(ip_[t % 2][:, 0:w], self.bk(GXB[t % 2])[:, 0:w], AF.Tanh, [("bank", GXB[t % 2]), "bxT"], ("ip", t % 2),
                scale=0.5, bias=bxT[:, j:j + 1])

        def P5(t, it):
            j, pi, c0, w = it["j"], it["pi"], it["c0"], it["w"]
            rv = rp_[t % 2][:, 0:w]; iv = ip_[t % 2][:, 0:w]; xcv = xc[t % NXC][:, 0:w]
            ACT(abuf[:, c0:c0 + w], rv, AF.Exp, [("rp", t % 2), "hsc8"], ("abuf", pi), scale=hsc8[:, j:j + 1], bias=hsc8[:, j:j + 1])
            TT("pool", a2buf[:, c0:c0 + w], abuf[:, c0:c0 + w], abuf[:, c0:c0 + w], ALU.mult, [("abuf", pi)], ("a2buf", pi))
            STT("dve", ixbuf[:, c0:c0 + w], iv, 1.0, xcv, ALU.add, ALU.mult, [("ip", t % 2), ("xc", t % NXC)], ("ixbuf", pi))
            if pi == len(PIECES) - 1:
                mid_tile(j)

        post_queue = []

        def mid_tile(j):
            sl = j % 2
            ACT(a2buf, a2buf, AF.Sqrt, allp("a2buf"), "mh", scale=-0.25, bias=0.25)
            TT("dve", ixbuf, a2buf, ixbuf, ALU.mult, ["mh"] + allp("ixbuf"), "bterm")
            TT("dve", t16, abuf[:, TP:T:4], LRUc[:, j, 48:64], ALU.mult, allp("abuf") + LK, "t16")
            TT("dve", ixbuf[:, TP:T:4], ixbuf[:, TP:T:4], t16, ALU.add, ["bterm", "t16"], "bterm")
            MS("dve", abuf[:, TP:T:4], 0.0, "afix", r=allp("abuf") + allp("a2buf") + ["t16"])
            p.op("dve", lambda e: e.tensor_tensor_scan(out=hb, data0=abuf, data1=ixbuf, initial=0.0, op0=ALU.mult, op1=ALU.add),
                 reads=allp("abuf") + ["afix", "bterm"], writes=["hbuf"])
            for nm in ("abuf", "a2buf", "ixbuf"):
                for pi in range(len(PIECES)):
                    p.readers.setdefault((nm, pi), []).append(p.last_write["hbuf"])
            if j == 0:
                self.tap("hbuf", hb, ["hbuf"])
            CP("pool", hfinL[:, j, 0:1], hb[:, TP - 1:TP], ["hbuf"], ("hfinL", j, 0))
            CP("pool", hfinL[:, j, 1:17], hb[:, TP + 3:T:4], ["hbuf"], ("hfinL", j, 1))
            for pi, (c0, w) in enumerate(PIECES):
                post_queue.append((j, pi, c0, w))

        npq = [0]

        def post_piece(j, pi, c0, w):
            sl = j % 2
            i2 = npq[0] % 2
            npq[0] += 1
            glv = glp[i2][:, 0:w]; gsv = gsp[i2][:, 0:w]; gbv = gbp[i2][:, 0:w]; t1v = t1p[i2][:, 0:w]
            groups = [("gl", wsg[sl], 0, 8, xT, ("wsg", sl, 0)), ("gs", wsg[sl], 1, 8, xT, ("wsg", sl, 1)),
                      ("gb", wgl[sl], 1, 4, z2, ("wgl", sl, 1)), ("ga", wgl[sl], 0, 4, z2, ("wgl", sl, 0))]
            for gi_, (nm, wt, idx, nk, src, wkey) in enumerate(groups):
                bb = POSTB[gi_ % 2]
                for k in range(nk):
                    rk = self.xT_keys(c0, w) if src is xT else [("z2", k, c0)]
                    p.op("pe", lambda e, k=k, bb=bb, wt=wt, idx=idx, src=src, nk=nk, c0=c0, w=w: e.matmul(
                        self.bk(bb)[:, 0:w], wt[:, k, idx, :], src[:, k, c0:c0 + w], start=(k == 0), stop=(k == nk - 1)),
                        reads=[wkey] + rk, writes=[("bank", bb)], inc=(k == nk - 1))
                psv = self.bk(bb)[:, 0:w]
                if nm == "gl":
                    ACT(glv, psv, AF.Tanh, [("bank", bb)], ("glp", i2), scale=0.5)
                elif nm == "gs":
                    ACT(gsv, psv, AF.Tanh, [("bank", bb)], ("gsp", i2), scale=0.5)
                elif nm == "gb":
                    ACT(gbv, psv, AF.Tanh, [("bank", bb)], ("gbp", i2), scale=0.25)
                else:
                    ga_ps, ga_b = psv, bb
            STT("dve", t1v, gbv, 1.0, ga_ps, ALU.add, ALU.mult, [("gbp", i2), ("bank", ga_b)], ("t1p", i2))
            STT("dve", t1v, gsv, 1.0, t1v, ALU.add, ALU.mult, [("gsp", i2), ("t1p", i2)], ("t1p", i2))
            STT("dve", glv, glv, 1.0, hb[:, c0:c0 + w], ALU.add, ALU.mult, [("glp", i2), "hbuf"], ("glp", i2))
            STT("dve", merged2[:, j, c0:c0 + w], t1v, 0.25, glv, ALU.mult, ALU.add, [("t1p", i2), ("glp", i2)], ("merged2", j, c0))
            if pi == len(PIECES) - 1 and j + 2 < 8:
                load_post(j + 2)

        load_pre(0); load_post(0); load_post(1)
        N_ = len(items)
        stages = [(P5, 5), (P4, 4), (P1, 1), (P2, 2), (P3, 3), (P0, 0)]
        for t in range(N_ + 5):
            for fn, lag in stages:
                if 0 <= t - lag < N_:
                    fn(t - lag, items[t - lag])
            if post_queue:
                post_piece(*post_queue.pop(0))
        while post_queue:
            post_piece(*post_queue.pop(0))
        self.tap("merged2", merged2[:, 0, :], [("merged2", 0, c0) for c0, _ in PIECES])
        self.out_dma(O["lru_conv"], tail_sb, [("tail", j) for j in range(8)])
        self.out_dma(O["lru_h"], hfinL, [("hfinL", j, i) for j in range(8) for i in range(2)])

    def ln_tile(self, ps_flat, res, gB, bB, ytok, outv, rows, kps, kres, kg, kb, ky, kout, st6, mv, sd):
        p = self.p
        TT, TS, STT, ACT, CP = self.TT, self.TS, self.STT, self.ACT, self.CP
        yv = ytok[0:rows, :]
        p.op("act", lambda e: e.activation(out=yv, in_=ps_flat[0:rows, :], func=AF.Copy, scale=0.5), reads=kps, writes=[ky])
        STT("dve", yv, res[0:rows, :], ALPHA, yv, ALU.mult, ALU.add, [kres, ky], ky)
        for h in range(2):
            p.op("dve", lambda e, h=h: e.bn_stats(out=st6[0:rows, h, :], in_=yv[:, 512 * h:512 * (h + 1)]), reads=[ky], writes=[ky + ("st", h)])
        p.op("dve", lambda e: e.bn_aggr(out=mv[0:rows, :], in_=st6[0:rows, :, :].rearrange("p a b -> p (a b)")),
             reads=[ky + ("st", 0), ky + ("st", 1)], writes=[ky + ("mv",)])
        ACT(sd[0:rows, :], mv[0:rows, 1:2], AF.Sqrt, [ky + ("mv",)], ky + ("sd",), bias=LN_EPS)
        p.op("dve", lambda e: e.reciprocal(out=sd[0:rows, :], in_=sd[0:rows, :]), reads=[ky + ("sd",)], writes=[ky + ("sd",)])
        TS("dve", yv, yv, mv[0:rows, 0:1], ALU.subtract, [ky, ky + ("mv",), ky + ("sd",)], ky, s2=sd[0:rows, 0:1], op1=ALU.mult)
        TT("pool", yv, yv, gB[0:rows, :], ALU.mult, [ky, kg], ky)
        TT("dve", outv[0:rows, :], yv, bB[0:rows, :], ALU.add, [ky, kb], kout)

    def mix_stage(self):
        nc, p, I, O = self.nc, self.p, self.I, self.O
        R0, R1 = self.R0, self.R1
        TT, TS, STT, ACT, CP, MS = self.TT, self.TS, self.STT, self.ACT, self.CP, self.MS
        merged2 = self.merged2
        x1T = self.xT
        self.x1T = x1T
        lnc = self.lnc
        for i, nm in enumerate(("ln1_g", "ln1_b", "ln2_g", "ln2_b")):
            p.dma("sp", lnc[:, i, :], I[nm].partition_broadcast(128), "d_lnc%d" % i, writes=[("lnc", i)])
        wout = R1.take([128, 8, D], BF16)
        for h in range(2):
            p.dma("pool", wout[:, :, 512 * h:512 * (h + 1)], I["w_out"][:, 512 * h:512 * (h + 1)].rearrange("(k p) n -> p k n", p=128),
                  "d_wout%d" % h, writes=[("wout", h)])
        NX, NYT, NX1, NXB, NS_ = 3, 4, 3, 2, 4
        xtok = [R1.take([128, D]) for i in range(NX)]
        ytok = [R1.take([128, D]) for i in range(NYT)]
        x1tok = [R1.take([128, D]) for i in range(NX1)]
        x1b = [R1.take([128, D], BF16) for i in range(NXB)]
        st6 = [R1.take([128, 2, 6]) for i in range(NS_)]
        mv = [R1.take([128, 2]) for i in range(NS_)]
        sd = [R1.take([128, 1]) for i in range(NS_)]
        ntt = (T + 127) // 128
        TB = [4, 5, 6, 7]
        g1, b1 = lnc[:, 0, :], lnc[:, 1, :]
        items = [dict(tt=tt, r0=tt * 128, rows=min(128, T - tt * 128)) for tt in range(ntt)]

        def M0(t, it):
            r0, rows = it["r0"], it["rows"]
            s = t % NX; pp = t % 2
            p.dma("sp", xtok[s][0:rows, :], I["x"][r0:r0 + rows, :], "d_xtok%d" % s, writes=[("xtok", s)])
            for h in range(2):
                for k in range(8):
                    p.op("pe", lambda e, k=k, h=h, pp=pp, r0=r0, rows=rows: e.matmul(
                        self.ps[pp][0:rows, h, :], merged2[:, k, r0:r0 + rows], wout[:, k, 512 * h:512 * (h + 1)],
                        start=(k == 0), stop=(k == 7)),
                        reads=[("wout", h)] + [("merged2", k, c0) for (c0, w) in self.PIECES if c0 <= r0 < c0 + w],
                        writes=[("bank", 2 * pp + h)], inc=(k == 7))

        def M1(t, it):
            rows = it["rows"]
            pp = t % 2; sy = t % NYT; sx = t % NX; ss = t % NS_
            psf = self.ps[pp][:].rearrange("p a c -> p (a c)")
            yv = ytok[sy][0:rows, :]
            ky = ("ytok", sy)
            p.op("act", lambda e: e.activation(out=yv, in_=psf[0:rows, :], func=AF.Copy, scale=0.5),
                 reads=[("bank", 2 * pp), ("bank", 2 * pp + 1)], writes=[ky])
            STT("dve", yv, xtok[sx][0:rows, :], ALPHA, yv, ALU.mult, ALU.add, [("xtok", sx), ky], ky)
            for h in range(2):
                p.op("dve", lambda e, h=h: e.bn_stats(out=st6[ss][0:rows, h, :], in_=yv[:, 512 * h:512 * (h + 1)]),
                     reads=[ky], writes=[("st6", ss, h)])
            p.op("dve", lambda e: e.bn_aggr(out=mv[ss][0:rows, :], in_=st6[ss][0:rows, :, :].rearrange("p a b -> p (a b)")),
                 reads=[("st6", ss, 0), ("st6", ss, 1)], writes=[("mv", ss)])

        def M2(t, it):
            rows = it["rows"]
            sy = t % NYT; ss = t % NS_
            yv = ytok[sy][0:rows, :]
            ky = ("ytok", sy)
            ACT(sd[ss][0:rows, :], mv[ss][0:rows, 1:2], AF.Sqrt, [("mv", ss)], ("sd", ss), bias=LN_EPS)
            p.op("dve", lambda e: e.reciprocal(out=sd[ss][0:rows, :], in_=sd[ss][0:rows, :]), reads=[("sd", ss)], writes=[("sd", ss)])
            TS("dve", yv, yv, mv[ss][0:rows, 0:1], ALU.subtract, [ky, ("mv", ss), ("sd", ss)], ky, s2=sd[ss][0:rows, 0:1], op1=ALU.mult)

        def M3a(t, it):
            tt, r0, rows = it["tt"], it["r0"], it["rows"]
            sy = t % NYT; s1 = t % NX1
            yv = ytok[sy][0:rows, :]
            ky = ("ytok", sy)
            TT("pool", yv, yv, g1[0:rows, :], ALU.mult, [ky, ("lnc", 0)], ky)
            TT("dve", x1tok[s1][0:rows, :], yv, b1[0:rows, :], ALU.add, [ky, ("lnc", 1)], ("x1tok", s1))
            if tt == 0:
                self.tap("x1tok", x1tok[s1], [("x1tok", s1)])
            p.dma("sp", self.x1_scr[r0:r0 + rows, :], x1tok[s1][0:rows, :], "d_x1w%d" % s1, reads=[("x1tok", s1)], writes=[("x1scr", tt)])

        def M3b(t, it):
            rows = it["rows"]
            s1 = t % NX1; sb = t % NXB
            CP("act", x1b[sb][0:rows, :], x1tok[s1][0:rows, :], [("x1tok", s1)], ("x1b", sb))

        def M4(t, it):
            rows = it["rows"]
            sb = t % NXB
            b = TB[t % 4]
            it["b"] = b
            pt = self.bk(b).bitcast(BF16)
            for k in range(8):
                p.op("pe", lambda e, k=k, pt=pt, sb=sb, rows=rows: e.transpose(
                    pt[:, k * 128:k * 128 + rows], x1b[sb][0:rows, k * 128:(k + 1) * 128], self.ident_b[0:rows, 0:rows]),
                    reads=[("x1b", sb), "ident_b"], writes=[("bank", b)], inc=(k == 7))

        def M5(t, it):
            tt, r0, rows, b = it["tt"], it["r0"], it["rows"], it["b"]
            pt = self.bk(b).bitcast(BF16)
            src = pt.rearrange("p (k c) -> p k c", c=128)[:, :, 0:rows]
            CP("act", x1T[:, :, r0:r0 + rows], src, [("bank", b)], ("x1T", tt))

        stages = [(M3a, 3), (M1, 1), (M2, 2), (M3b, 3), (M5, 5), (M4, 4), (M0, 0)]
        N_ = len(items)
        for t in range(N_ + 5):
            for fn, lag in stages:
                if 0 <= t - lag < N_:
                    fn(t - lag, items[t - lag])
        self.tap("x1T", x1T[:, 0, :], [("x1T", tt) for tt in range(ntt)])

    def x1T_keys(self, c0, w):
        return [("x1T", tt) for tt in range(c0 // 128, (c0 + w - 1) // 128 + 1)]

    def ffn_stage(self):
        nc, p, I, O = self.nc, self.p, self.I, self.O
        R0, R1 = self.R0, self.R1
        TT, TS, STT, ACT, CP, MS = self.TT, self.TS, self.STT, self.ACT, self.CP, self.MS
        NCK = dict(allow_slow_non_contiguous=True)
        x1T, lnc = self.x1T, self.lnc
        NJ = DFF // 128
        QW = 576
        Fc = self.Fc
        FK = self.Fc_keys
        wdn = R1.take([128, NJ, D], BF16)
        for c in range(6):
            p.dma("pool", wdn[:, 4 * c:4 * c + 4, :], I["w_down"][512 * c:512 * (c + 1), :].rearrange("(k p) n -> p k n", p=128),
                  "d_wdn%d" % c, writes=[("wdn", c)])
        Gq = R1.take([128, NJ, QW], BF16)
        NSL = 4
        wup = [R1.take([128, 8, 2, 128], BF16) for i in range(NSL)]
        halo = R1.take([128, NJ, 2])
        MS("pool", halo, 0.0, "halo_init")
        NY, NQ, NG = 5, 3, 2
        a_sb = [R1.take([128, 2 + 512]) for i in range(2)]
        as_sb = R1.take([128, 16, 6])
        y0 = [R1.take([128, 512]) for i in range(NY)]
        qq = [R1.take([128, 512]) for i in range(NQ)]
        ag = [R1.take([128, 512]) for i in range(NG)]
        tailf = [R1.take([NTAIL, 512]) for i in range(2)]
        NB = 2
        x1tok = [R1.take([128, D]) for i in range(NB)]
        ytok = [R1.take([128, D]) for i in range(NB)]
        otok = ytok
        st6 = [R1.take([128, 2, 6]) for i in range(NB)]
        mv = [R1.take([128, 2]) for i in range(NB)]
        sd = [R1.take([128, 1]) for i in range(NB)]
        nld = [0]

        def load_wup(j):
            sl = nld[0] % NSL
            nld[0] += 1
            p.dma("pool", wup[sl].rearrange("p k g n -> p k (g n)"),
                  I["w_up"][:, 256 * j:256 * (j + 1)].rearrange("(k p) n -> p k n", p=128), "d_wup%d" % sl,
                  writes=[("wup", sl, 0), ("wup", sl, 1)])
            return sl

        npc = [0]
        ntile = [0]
        for n in range(4):
            q0 = 512 * n
            pieces = [(q0, 512, 0)] + ([(TP, NS, 512)] if n == 3 else [])
            RA = [0, 1]
            RG = [2, 3, 4, 5, 6]
            TAILB = 7
            items = []
            for j in range(NJ):
                for pi_, (c0, w, lc0) in enumerate(pieces):
                    items.append(dict(j=j, c0=c0, w=w, lc0=lc0, first=(pi_ == 0), last=(pi_ == len(pieces) - 1)))
            pending = [load_wup(0), load_wup(1), load_wup(2)]
            cur_sl = {}

            def S0(t, it):
                j, c0, w = it["j"], it["c0"], it["w"]
                if it["first"]:
                    cur_sl[j] = pending.pop(0)
                    if j + 3 < NJ:
                        pending.append(load_wup(j + 3))
                sl = cur_sl[j]
                bA = RA[t % 2]; bG = RG[t % 5]
                it["bA"], it["bG"] = bA, bG
                for (bb, gi) in ((bA, 0), (bG, 1)):
                    for k in range(8):
                        p.op("pe", lambda e, k=k, bb=bb, gi=gi, sl=sl, c0=c0, w=w: e.matmul(
                            self.bk(bb)[:, 0:w], wup[sl][:, k, gi, :], x1T[:, k, c0:c0 + w], start=(k == 0), stop=(k == 7)),
                            reads=[("wup", sl, gi)] + self.x1T_keys(c0, w), writes=[("bank", bb)], inc=(k == 7))
                if n == 3 and it["last"]:
                    for k in range(8):
                        p.op("pe", lambda e, k=k, sl=sl: e.matmul(self.bk(TAILB)[0:NTAIL, 0:128], x1T[:, k, TAIL0:T], wup[sl][:, k, 0, :],
                                                                  start=(k == 0), stop=(k == 7)),
                             reads=[("wup", sl, 0)] + self.x1T_keys(TAIL0, NTAIL), writes=[("bank", TAILB)], inc=(k == 7))

            def S1(t, it):
                j, c0, w, bA = it["j"], it["c0"], it["w"], it["bA"]
                samp = (w == NS)
                iy = t % NY; ia = t % 2
                fw = [Fc[:, j, 32 + k:33 + k] for k in range(3)]
                aps = self.bk(bA)[:, 0:w]
                yv = y0[iy][:, 0:w]
                ACT(yv, aps, AF.Identity, [("bank", bA)] + FK, ("y0", iy), scale=fw[2], bias=Fc[:, j, 35:36])
                if not samp:
                    ab = a_sb[ia]
                    CP("act", ab[:, 2:2 + w], aps, [("bank", bA)], ("a_sb", ia))
                    CP("pool", ab[:, 0:2], halo[:, j, :], ["halo_init", ("halo", j)], ("a_sbh", ia))
                    ak = [("a_sb", ia), ("a_sbh", ia), ("y0", iy)]
                    STT("dve", yv, ab[:, 1:1 + w], fw[1], yv, ALU.mult, ALU.add, ak, ("y0", iy))
                    STT("dve", yv, ab[:, 0:w], fw[0], yv, ALU.mult, ALU.add, ak, ("y0", iy))
                    CP("pool", halo[:, j, :], ab[:, w:w + 2], [("a_sb", ia), ("a_sbh", ia)], ("halo", j))
                else:
                    CP("act", as_sb[:, :, 2:6], aps.rearrange("p (b s) -> p b s", s=4), [("bank", bA)], ("as_sb", "new"))
                    CP("pool", as_sb[:, :, 0:2], Fc[:, j, 0:32].rearrange("p (b k) -> p b k", k=2), FK, ("as_sb", "st"))
                    ak = [("as_sb", "new"), ("as_sb", "st"), ("y0", iy)]
                    y3 = yv.rearrange("p (b s) -> p b s", s=4)
                    STT("dve", y3, as_sb[:, :, 1:5], fw[1], y3, ALU.mult, ALU.add, ak, ("y0", iy))
                    STT("dve", y3, as_sb[:, :, 0:4], fw[0], y3, ALU.mult, ALU.add, ak, ("y0", iy))
                if n == 3 and it["last"]:
                    tb = (j // 4) % 2
                    CP("act", tailf[tb][:, 128 * (j % 4):128 * (j % 4 + 1)], self.bk(TAILB)[0:NTAIL, 0:128], [("bank", TAILB)], ("tailf", tb, j % 4))
                    if j % 4 == 3:
                        sem = "o_tf%d" % tb
                        p.dma("sp", O["ffn_conv"][:, 512 * (j // 4):512 * (j // 4 + 1)], tailf[tb], sem,
                              reads=[("tailf", tb, i) for i in range(4)])
                        self.out_sems[sem] = p.count[sem]

            def S2(t, it):
                w = it["w"]
                iy = t % NY; iq = t % NQ
                yv = y0[iy][:, 0:w]; qv = qq[iq][:, 0:w]
                ACT(qv, yv, AF.Square, [("y0", iy)], ("qq", iq))
                ACT(qv, qv, AF.Identity, [("qq", iq)], ("qq", iq), scale=GC, bias=1.0)

            def S3(t, it):
                w = it["w"]
                iy = t % NY; iq = t % NQ; ig = t % NG
                TT("pool", ag[ig][:, 0:w], qq[iq][:, 0:w], y0[iy][:, 0:w], ALU.mult, [("qq", iq), ("y0", iy)], ("ag", ig))

            def S4(t, it):
                j, w, lc0, bG = it["j"], it["w"], it["lc0"], it["bG"]
                iy = t % NY; iq = t % NQ; ig = t % NG
                yv = y0[iy][:, 0:w]; qv = qq[iq][:, 0:w]; av = ag[ig][:, 0:w]
                gps = self.bk(bG)[:, 0:w]
                ACT(qv, av, AF.Tanh, [("ag", ig)], ("qq", iq), scale=GK)
                STT("dve", av, qv, 1.0, yv, ALU.add, ALU.mult, [("qq", iq), ("y0", iy)], ("ag", ig))
                TT("dve", Gq[:, j, lc0:lc0 + w], av, gps, ALU.mult, [("ag", ig), ("bank", bG)], ("Gq", j, lc0))

            N_ = len(items)
            stages = [(S4, 4), (S3, 3), (S1, 1), (S2, 2), (S0, 0)]
            for t in range(N_ + 4):
                for fn, lag in stages:
                    if 0 <= t - lag < N_:
                        fn(t - lag, items[t - lag])
            if n == 0:
                self.tap("Gq", Gq[:, 0, :], [("Gq", 0, 0)])
            tts = [4 * n + i for i in range(4)] + ([16] if n == 3 else [])
            for tt in tts:
                r0 = tt * 128
                rows = min(128, T - r0)
                lc = r0 - q0 if tt < 16 else 512
                s = ntile[0] % NB
                pp = ntile[0] % 2
                ntile[0] += 1
                p.dma("sp", x1tok[s][0:rows, :], self.x1_scr[r0:r0 + rows, :], "d_x1r%d" % s, reads=[("x1scr", tt)], writes=[("x1tok2", s)])
                psf = self.ps[pp][:].rearrange("p a c -> p (a c)")
                gkeys = [("Gq", j, 512 if tt == 16 else 0) for j in range(NJ)]
                for h in range(2):
                    for k in range(NJ):
                        p.op("pe", lambda e, k=k, h=h, pp=pp, lc=lc, rows=rows: e.matmul(
                            self.ps[pp][0:rows, h, :], Gq[:, k, lc:lc + rows], wdn[:, k, 512 * h:512 * (h + 1)],
                            start=(k == 0), stop=(k == NJ - 1)),
                            reads=[("wdn", k // 4), ("Gq", k, 512 if tt == 16 else 0)], writes=[("bank", 2 * pp + h)], inc=(k == NJ - 1))
                self.ln_tile(psf, x1tok[s], lnc[:, 2, :], lnc[:, 3, :], ytok[s], otok[s], rows,
                             [("bank", 2 * pp), ("bank", 2 * pp + 1)], ("x1tok2", s), ("lnc", 2), ("lnc", 3), ("ytok2", s), ("ytok2", s),
                             st6[s], mv[s], sd[s])
                sem = "o_y%d" % s
                p.dma("sp", O["y"][r0:r0 + rows, :], otok[s][0:rows, :], sem, reads=[("ytok2", s)])
                self.out_sems[sem] = p.count[sem]

    def finish(self):
        fw = [(s, v) for s, v in self.out_sems.items()]
        self.p.emit(final_waits=fw)
        self.st.close()
        print("arena peaks: R0 %d/%d words, R1 %d/%d words" % (self.R0.peak, self.R0.words, self.R1.peak, self.R1.words))
        return self.nc


def shard_inputs(inputs, c):
    f = lambda a: np.ascontiguousarray(a, dtype=np.float32)
    m = {}
    m["x"] = f(np.concatenate([inputs["x_prompt"][c], inputs["x_sample"][NSQ * c:NSQ * (c + 1)].reshape(NS, D)], axis=0))
    m["st_lru_conv"] = f(inputs["state_lru_conv"][0, NSQ * c:NSQ * (c + 1)].reshape(NSQ * 3, D))
    m["st_lru_h"] = f(inputs["state_lru_h"][0, NSQ * c:NSQ * (c + 1)])
    m["st_s5_re"] = f(inputs["state_s5_re"][0, NSQ * c:NSQ * (c + 1)].reshape(NSQ, 2048))
    m["st_s5_im"] = f(inputs["state_s5_im"][0, NSQ * c:NSQ * (c + 1)].reshape(NSQ, 2048))
    m["st_ffn_conv"] = f(inputs["state_ffn_conv"][0, NSQ * c:NSQ * (c + 1)].reshape(NSQ * 2, DFF))
    for k in ("w_in", "lru_conv_w", "lru_conv_b", "lru_wa", "lru_ba", "lru_wx", "lru_bx", "lru_lambda", "s5_a_re", "s5_a_im",
              "s5_log_dt", "s5_b_re", "s5_b_im", "s5_c_re", "s5_c_im", "s5_d", "w_glu", "w_out", "ln1_g", "ln1_b", "w_up",
              "ffn_conv_w", "ffn_conv_b", "w_down", "ln2_g", "ln2_b"):
        m[k] = f(inputs[k][0])
    wu = m["w_up"]
    m["w_up"] = f(np.stack([wu[:, :DFF].reshape(D, DFF // 128, 128), wu[:, DFF:].reshape(D, DFF // 128, 128)], axis=2).reshape(D, 2 * DFF))
    wi = m["w_in"]
    gl = wi[:, 1536:2560].reshape(D, 8, 128); gs = wi[:, 2560:3584].reshape(D, 8, 128)
    m["w_in"] = f(np.concatenate([wi[:, :1536], np.stack([gl, gs], axis=2).reshape(D, 2048)], axis=1))
    wg = m["w_glu"]
    m["w_glu"] = f(np.stack([wg[:, :1024].reshape(512, 8, 128), wg[:, 1024:].reshape(512, 8, 128)], axis=2).reshape(512, 2048))
    return m


_NC_CACHE = {}


def _get_nc():
    if "nc" not in _NC_CACHE:
        _NC_CACHE["nc"] = Builder().build()
    return _NC_CACHE["nc"]


def kernel(**inputs):
    nc = _get_nc()
    in_maps = [shard_inputs(inputs, c) for c in range(NCORES)]
    res = run_bass_kernel_spmd(nc, in_maps, core_ids=list(range(NCORES)))
    R = res.results
    B = NCORES
    y_p = np.zeros((B, TP, D), np.float32); y_s = np.zeros((B * NSQ, 4, D), np.float32)
    p_conv = np.zeros((1, B, 3, D), np.float32); p_h = np.zeros((1, B, D), np.float32)
    p_re = np.zeros((1, B, 32, 64), np.float32); p_im = np.zeros((1, B, 32, 64), np.float32)
    p_ffn = np.zeros((1, B, 2, DFF), np.float32)
    s_conv = np.zeros((1, B * NSQ, 3, D), np.float32); s_h = np.zeros((1, B * NSQ, D), np.float32)
    s_re = np.zeros((1, B * NSQ, 32, 64), np.float32); s_im = np.zeros((1, B * NSQ, 32, 64), np.float32)
    s_ffn = np.zeros((1, B * NSQ, 2, DFF), np.float32)
    for c in range(B):
        r = R[c]
        sl = slice(NSQ * c, NSQ * (c + 1))
        y_p[c] = r["y"][0:TP]
        y_s[sl] = r["y"][TP:].reshape(NSQ, 4, D)
        lc = r["o_lru_conv"]
        p_conv[0, c] = lc[0:3]
        s_conv[0, sl] = lc[3:].reshape(NSQ, 4, D)[:, 1:4]
        lh = r["o_lru_h"].transpose(2, 1, 0).reshape(17, D)
        p_h[0, c] = lh[0]
        s_h[0, sl] = lh[1:17]
        s5 = r["o_s5"].reshape(2, 64, 2, 16, 17).transpose(2, 4, 3, 0, 1)
        s5 = s5.reshape(2, 17, 32, 64)
        p_re[0, c] = s5[0, 0]
        p_im[0, c] = s5[1, 0]
        s_re[0, sl] = s5[0, 1:17]
        s_im[0, sl] = s5[1, 1:17]
        fc = r["o_ffn_conv"]
        p_ffn[0, c] = fc[1:3]
        s_ffn[0, sl] = fc[3:].reshape(NSQ, 4, DFF)[:, 2:4]
    return (y_p, y_s, p_conv, p_h, p_re, p_im, p_ffn, s_conv, s_h, s_re, s_im, s_ffn)
```
